# Optimizing a Trainium2 kernel written in Bass

```python
import math
import jax, jax.numpy as jnp
from jax import lax
import numpy as np

D_MODEL = 2048
BATCH = 4
SEQ = 4096
DEPTH = 4

N_MIXERS = 3
FOX_HEADS = 16
FOX_HEAD_DIM = D_MODEL // FOX_HEADS
FOX_Q_BLOCK = 128
GLA_HEADS = 4
GLA_KEY_DIM = D_MODEL // 2
GLA_VALUE_DIM = D_MODEL
GLA_DK = GLA_KEY_DIM // GLA_HEADS
GLA_DV = GLA_VALUE_DIM // GLA_HEADS
GLA_GATE_RANK = 16
GLA_GATE_TAU = 16.0
GLA_CHUNK = 64
CONV_WIDTH = 3
D_FF = 5632
LN_EPS = 1e-5
RMS_EPS = 1e-5
DEEPNORM_ALPHA = (2 * DEPTH) ** 0.25
DEEPNORM_BETA = (8 * DEPTH) ** -0.25
N_FOX = (DEPTH + 2) // 3
N_GLA = (DEPTH + 1) // 3
N_CONV = DEPTH // 3

kernel_name = 'hybrid_fox_gla_shortconv_deepnorm_adaln'


def layer_norm(x, g, b):
    xf = x.astype(jnp.float32)
    mu = jnp.mean(xf, axis=-1, keepdims=True)
    var = jnp.mean(jnp.square(xf - mu), axis=-1, keepdims=True)
    return ((xf - mu) * lax.rsqrt(var + LN_EPS) * g + b).astype(x.dtype)


def causal_dwconv(x, w):
    s = x.shape[1]
    xp = jnp.pad(x, ((0, 0), (CONV_WIDTH - 1, 0), (0, 0)))
    y = xp[:, 0:s] * w[0]
    for k in range(1, CONV_WIDTH):
        y = y + xp[:, k:k + s] * w[k]
    return y


def fox_mixer(h, wq, wk, wv, wg, wf, bf, wo):
    b, s, _ = h.shape
    nq = s // FOX_Q_BLOCK

    def heads(w):
        return (h @ w).reshape(b, s, FOX_HEADS, FOX_HEAD_DIM).transpose(0, 2, 1, 3)

    q = heads(wq) * (FOX_HEAD_DIM ** -0.5)
    k = heads(wk)
    v = heads(wv)
    log_f = jax.nn.log_sigmoid((h @ wf + bf).astype(jnp.float32))
    cum = jnp.cumsum(log_f, axis=1).transpose(0, 2, 1)
    pos = jnp.arange(s)
    q_blocks = q.reshape(b, FOX_HEADS, nq, FOX_Q_BLOCK, FOX_HEAD_DIM).transpose(2, 0, 1, 3, 4)
    f_blocks = cum.reshape(b, FOX_HEADS, nq, FOX_Q_BLOCK).transpose(2, 0, 1, 3)
    p_blocks = pos.reshape(nq, FOX_Q_BLOCK)

    def attend(args):
        q_blk, f_blk, p_blk = args
        logits = jnp.einsum('bhqd,bhkd->bhqk', q_blk, k).astype(jnp.float32)
        logits = logits + f_blk[..., None] - cum[:, :, None, :]
        logits = jnp.where(pos[None, :] <= p_blk[:, None], logits, -jnp.inf)
        probs = jax.nn.softmax(logits, axis=-1)
        return jnp.einsum('bhqk,bhkd->bhqd', probs.astype(v.dtype), v)

    o = lax.map(attend, (q_blocks, f_blocks, p_blocks))
    o = o.transpose(1, 0, 3, 2, 4).reshape(b, s, FOX_HEADS * FOX_HEAD_DIM)
    o = o * jax.nn.sigmoid(h @ wg)
    return o @ wo


def gla_mixer(h, wq, wk, wv, wa1, wa2, ba, wr, norm_g, wo):
    b, s, _ = h.shape
    nc = s // GLA_CHUNK
    f32 = jnp.float32

    def heads(t, d):
        return t.astype(f32).reshape(b, nc, GLA_CHUNK, GLA_HEADS, d).transpose(0, 3, 1, 2, 4)

    q = heads(h @ wq, GLA_DK) * (GLA_DK ** -0.5)
    k = heads(h @ wk, GLA_DK)
    v = heads(h @ wv, GLA_DV)
    log_a = jax.nn.log_sigmoid(((h @ wa1) @ wa2 + ba).astype(f32)) / GLA_GATE_TAU
    log_a = heads(log_a, GLA_DK)
    cb = jnp.cumsum(log_a, axis=3)
    cb_last = cb[:, :, :, -1]
    q_dec = q * jnp.exp(cb)
    k_inv = k * jnp.exp(-cb)
    k_dec = k * jnp.exp(cb_last[:, :, :, None, :] - cb)
    causal = jnp.tril(jnp.ones((GLA_CHUNK, GLA_CHUNK), dtype=bool))
    att = jnp.einsum('bhnid,bhnjd->bhnij', q_dec, k_inv)
    att = jnp.where(causal, att, 0.0)
    o_intra = jnp.einsum('bhnij,bhnjv->bhniv', att, v)

    def step(state, xs):
        qd, kd, vc, bl = xs
        o_c = jnp.einsum('bhcd,bhdv->bhcv', qd, state)
        state = jnp.exp(bl)[..., None] * state + jnp.einsum('bhcd,bhcv->bhdv', kd, vc)
        return state, o_c

    def chunk_major(t):
        return jnp.moveaxis(t, 2, 0)

    s0 = jnp.zeros((b, GLA_HEADS, GLA_DK, GLA_DV), f32)
    _, o_inter = lax.scan(step, s0, (chunk_major(q_dec), chunk_major(k_dec),
                                     chunk_major(v), chunk_major(cb_last)))
    o = o_intra + jnp.moveaxis(o_inter, 0, 2)
    o = o * lax.rsqrt(jnp.mean(o * o, axis=-1, keepdims=True) + RMS_EPS) * norm_g
    o = o.transpose(0, 2, 3, 1, 4).reshape(b, s, GLA_VALUE_DIM).astype(h.dtype)
    o = o * jax.nn.silu(h @ wr)
    return o @ wo


def short_conv_mixer(h, w_in, conv_w, w_out):
    gate_b, gate_c, u = jnp.split(h @ w_in, 3, axis=-1)
    return (gate_b * causal_dwconv(gate_c * u, conv_w)) @ w_out


def conv_ffn(h, w_up, conv_w, conv_b, w_down):
    z = causal_dwconv(h @ w_up, conv_w) + conv_b
    a, u = jnp.split(z, 2, axis=-1)
    return (jax.nn.silu(a) * u) @ w_down


def setup_inputs(seed: int = 0) -> dict:
    key = jax.random.key(seed)
    keys = list(jax.random.split(key, 40))
    counter = [0]

    def nrm(shape, scale):
        kk = keys[counter[0]]
        counter[0] += 1
        return jax.random.normal(kk, shape, jnp.float32) * scale

    D = D_MODEL
    beta = DEEPNORM_BETA
    x = nrm((BATCH, SEQ, D), 1.0)
    c = nrm((BATCH, D), 1.0)
    ada_w = nrm((DEPTH, D, 6 * D), D ** -0.5)
    ada_b = nrm((DEPTH, 6 * D), 0.02)
    ln1_g = 1.0 + nrm((DEPTH, D), 0.02)
    ln1_b = nrm((DEPTH, D), 0.02)
    ln2_g = 1.0 + nrm((DEPTH, D), 0.02)
    ln2_b = nrm((DEPTH, D), 0.02)
    fox_wq = nrm((N_FOX, D, D), D ** -0.5)
    fox_wk = nrm((N_FOX, D, D), D ** -0.5)
    fox_wv = nrm((N_FOX, D, D), D ** -0.5)
    fox_wg = nrm((N_FOX, D, D), D ** -0.5)
    fox_wf = nrm((N_FOX, D, FOX_HEADS), D ** -0.5)
    fox_bf = 2.0 + nrm((N_FOX, FOX_HEADS), 0.5)
    fox_wo = nrm((N_FOX, D, D), D ** -0.5 * beta)
    gla_wq = nrm((N_GLA, D, GLA_KEY_DIM), D ** -0.5)
    gla_wk = nrm((N_GLA, D, GLA_KEY_DIM), D ** -0.5)
    gla_wv = nrm((N_GLA, D, GLA_VALUE_DIM), D ** -0.5)
    gla_wa1 = nrm((N_GLA, D, GLA_GATE_RANK), D ** -0.5)
    gla_wa2 = nrm((N_GLA, GLA_GATE_RANK, GLA_KEY_DIM), GLA_GATE_RANK ** -0.5)
    gla_ba = nrm((N_GLA, GLA_KEY_DIM), 0.1)
    gla_wr = nrm((N_GLA, D, GLA_VALUE_DIM), D ** -0.5)
    gla_norm_g = 1.0 + nrm((N_GLA, GLA_DV), 0.02)
    gla_wo = nrm((N_GLA, GLA_VALUE_DIM, D), GLA_VALUE_DIM ** -0.5 * beta)
    conv_w_in = nrm((N_CONV, D, 3 * D), D ** -0.5)
    conv_w = nrm((N_CONV, CONV_WIDTH, D), CONV_WIDTH ** -0.5)
    conv_w_out = nrm((N_CONV, D, D), D ** -0.5 * beta)
    ffn_w_up = nrm((DEPTH, D, 2 * D_FF), D ** -0.5)
    ffn_conv_w = nrm((DEPTH, CONV_WIDTH, 2 * D_FF), CONV_WIDTH ** -0.5)
    ffn_conv_b = nrm((DEPTH, 2 * D_FF), 0.02)
    ffn_w_down = nrm((DEPTH, D_FF, D), D_FF ** -0.5 * beta)
    return {'x': x, 'c': c, 'ada_w': ada_w, 'ada_b': ada_b,
            'ln1_g': ln1_g, 'ln1_b': ln1_b, 'ln2_g': ln2_g, 'ln2_b': ln2_b,
            'fox_wq': fox_wq, 'fox_wk': fox_wk, 'fox_wv': fox_wv, 'fox_wg': fox_wg,
            'fox_wf': fox_wf, 'fox_bf': fox_bf, 'fox_wo': fox_wo,
            'gla_wq': gla_wq, 'gla_wk': gla_wk, 'gla_wv': gla_wv, 'gla_wa1': gla_wa1,
            'gla_wa2': gla_wa2, 'gla_ba': gla_ba, 'gla_wr': gla_wr,
            'gla_norm_g': gla_norm_g, 'gla_wo': gla_wo,
            'conv_w_in': conv_w_in, 'conv_w': conv_w, 'conv_w_out': conv_w_out,
            'ffn_w_up': ffn_w_up, 'ffn_conv_w': ffn_conv_w, 'ffn_conv_b': ffn_conv_b,
            'ffn_w_down': ffn_w_down}


def reference(x, c, ada_w, ada_b, ln1_g, ln1_b, ln2_g, ln2_b,
              fox_wq, fox_wk, fox_wv, fox_wg, fox_wf, fox_bf, fox_wo,
              gla_wq, gla_wk, gla_wv, gla_wa1, gla_wa2, gla_ba, gla_wr,
              gla_norm_g, gla_wo,
              conv_w_in, conv_w, conv_w_out,
              ffn_w_up, ffn_conv_w, ffn_conv_b, ffn_w_down):
    cond = jax.nn.silu(c)
    for i in range(DEPTH):
        mod = cond @ ada_w[i] + ada_b[i]
        sh1, sc1, g1, sh2, sc2, g2 = [m[:, None, :] for m in jnp.split(mod, 6, axis=-1)]
        h = x * (1.0 + sc1) + sh1
        kind = i % N_MIXERS
        j = i // N_MIXERS
        if kind == 0:
            y = fox_mixer(h, fox_wq[j], fox_wk[j], fox_wv[j], fox_wg[j],
                          fox_wf[j], fox_bf[j], fox_wo[j])
        elif kind == 1:
            y = gla_mixer(h, gla_wq[j], gla_wk[j], gla_wv[j], gla_wa1[j], gla_wa2[j],
                          gla_ba[j], gla_wr[j], gla_norm_g[j], gla_wo[j])
        else:
            y = short_conv_mixer(h, conv_w_in[j], conv_w[j], conv_w_out[j])
        x = layer_norm(DEEPNORM_ALPHA * x + g1 * y, ln1_g[i], ln1_b[i])
        h = x * (1.0 + sc2) + sh2
        y = conv_ffn(h, ffn_w_up[i], ffn_conv_w[i], ffn_conv_b[i], ffn_w_down[i])
        x = layer_norm(DEEPNORM_ALPHA * x + g2 * y, ln2_g[i], ln2_b[i])
    return x
```

```python
import numpy as np
from contextlib import ExitStack
import ml_dtypes
import concourse.bass as bass
import concourse.mybir as mybir
from concourse.bass_utils import run_bass_kernel_spmd

F32 = mybir.dt.float32
BF16 = mybir.dt.bfloat16
AF = mybir.ActivationFunctionType
ALU = mybir.AluOpType
AX = mybir.AxisListType

D = 2048
S = 4096
NB = 4
DEPTH = 4
DFF = 5632
KC = D // 128
TOK = 2048
ALPHA = (2 * DEPTH) ** 0.25
LN_EPS = 1e-5
RMS_EPS = 1e-5
NCORES = 8
CC_INC = 16
PAIRS = [[0, 1], [2, 3], [4, 5], [6, 7]]
ALL8 = [list(range(8))]


class Tok:
    __slots__ = ("name", "w", "r")

    def __init__(self, name=""):
        self.name = name
        self.w = None
        self.r = []


class Ctx:
    def __init__(self, nc, es):
        self.nc = nc
        self.es = es
        self.eng = {"pe": nc.tensor, "act": nc.scalar, "dve": nc.vector,
                    "pool": nc.gpsimd, "sp": nc.sync}
        self.sem = {}
        self.cnt = {}
        for k in ("pe", "act", "dve", "pool"):
            self.sem[k] = es.enter_context(nc.semaphore("s_" + k))
            self.cnt[k] = 0
        self.seen = {k: {} for k in self.eng}
        self.n_inst = 0
        self.out_toks = []
        self.root_es = es
        self.prefix = ""
        self.kind = ""
        self.bg = []
        self.bg_final = {}

    def pump(self, n=1):
        while n > 0 and self.bg:
            self.bg.pop(0)()
            n -= 1

    def pump_all(self):
        self.pump(len(self.bg))
        for tok, dep in self.bg_final.items():
            tok.w = dep
        self.bg_final = {}

    def sb(self, name, shape, dt):
        return self.es.enter_context(self.nc.sbuf_tensor(self.prefix + name, list(shape), dt))

    def ps(self, name, shape, dt=F32):
        return self.es.enter_context(self.nc.psum_tensor(self.prefix + name, list(shape), dt))

    def dma_sem(self, name):
        key = "d_" + self.kind + name
        if key not in self.sem:
            self.sem[key] = self.root_es.enter_context(self.nc.semaphore(key))
            self.cnt[key] = 0
        return key

    def barrier(self):
        for e in self.eng:
            seen = self.seen[e]
            for k, cnt in self.cnt.items():
                if k.startswith("d_bg"):
                    continue
                if cnt > 0 and seen.get(k, 0) < cnt:
                    self.eng[e].wait_ge(self.sem[k], cnt)
                    seen[k] = cnt
                    self.n_inst += 1

    def phase(self, kind, inst):
        ctx = self

        class _P:
            def __enter__(self_):
                ctx.barrier()
                self_.es = ExitStack()
                self_.es.__enter__()
                self_.old = (ctx.es, ctx.prefix, ctx.kind)
                ctx.es, ctx.prefix, ctx.kind = self_.es, f"{inst}_", kind + "_"
                return ctx

            def __exit__(self_, *a):
                ctx.barrier()
                ctx.es, ctx.prefix, ctx.kind = self_.old
                return self_.es.__exit__(*a)

        return _P()

    def collective(self, kind, src, dst, groups):
        key = self.dma_sem("cc")
        ins = self.nc.gpsimd.collective_compute(kind, ALU.bypass, replica_groups=groups,
                                                ins=[src], outs=[dst])
        self.n_inst += 1
        self.cnt[key] += CC_INC
        ins.then_inc(self.sem[key], CC_INC)
        return ins

    def _need(self, e, reads, writes):
        need = {}

        def add(dep):
            if dep is None:
                return
            k, c = dep
            if need.get(k, 0) < c:
                need[k] = c

        for t in reads:
            add(t.w)
        for t in writes:
            if t.w is not None and t.w[0] != e:
                add(t.w)
            for d in t.r:
                if d[0] != e:
                    add(d)
        seen = self.seen[e]
        engine = self.eng[e]
        for k, c in need.items():
            if seen.get(k, 0) < c:
                engine.wait_ge(self.sem[k], c)
                seen[k] = c
                self.n_inst += 1

    def _done(self, key, cnt, reads, writes):
        dep = (key, cnt)
        for t in writes:
            t.w = dep
            t.r = []
        for t in reads:
            t.r.append(dep)
            if len(t.r) > 16:
                best = {}
                for k, c in t.r:
                    if best.get(k, 0) < c:
                        best[k] = c
                t.r = list(best.items())

    def op(self, e, fn, reads=(), writes=(), track=True):
        self._need(e, reads, writes)
        ins = fn(self.eng[e])
        self.n_inst += 1
        if track:
            self.cnt[e] += 1
            ins.then_inc(self.sem[e], 1)
            self._done(e, self.cnt[e], reads, writes)
        return ins

    def dma(self, q, semkey, out, in_, reads=(), writes=(), **kw):
        self._need(q, reads, writes)
        ins = self.eng[q].dma_start(out=out, in_=in_, **kw)
        self.n_inst += 1
        self.cnt[semkey] += 16
        ins.then_inc(self.sem[semkey], 16)
        self._done(semkey, self.cnt[semkey], reads, writes)
        return ins

    def finish(self):
        self._need("sp", self.out_toks, ())


class Ring:
    def __init__(self, c, name, n, shape, dt, dma=True):
        self.bufs = [c.sb(f"{name}{i}", shape, dt) for i in range(n)]
        self.toks = [Tok(f"{name}{i}") for i in range(n)]
        self.sems = [c.dma_sem(f"{name}{i}") for i in range(n)] if dma else None
        self.n = n
        self.i = -1

    def next(self):
        self.i = (self.i + 1) % self.n
        return self.cur()

    def cur(self):
        i = self.i
        return self.bufs[i], self.toks[i], (self.sems[i] if self.sems else None)


def mm_group(c, out_ap, out_tok, pairs, reads):
    n = len(pairs)
    for i, (l, r) in enumerate(pairs):
        last = i == n - 1
        c.op("pe", lambda e: e.matmul(out_ap, l, r, start=(i == 0), stop=last),
             reads=reads, writes=[out_tok], track=last)


class LNState:
    def __init__(self, c, widths=(512,)):
        ng = len(widths)
        width = max(widths)
        self.ones = c.sb("ln_ones", [128, 128], BF16)
        self.t_ones = Tok("ones")
        c.op("dve", lambda e: e.memset(self.ones[:], 1.0), writes=[self.t_ones])
        self.S1 = [c.ps(f"ln_s1_{g}", [128, 512]) for g in range(ng)]
        self.S2 = [c.ps(f"ln_s2_{g}", [128, 512]) for g in range(ng)]
        self.t_S1 = [Tok(f"s1_{g}") for g in range(ng)]
        self.t_S2 = [Tok(f"s2_{g}") for g in range(ng)]
        self.zb = Ring(c, "ln_zb", 2, [128, width], BF16, dma=False)
        self.sq = Ring(c, "ln_sq", 2, [128, width], BF16, dma=False)
        self.m = c.sb("ln_m", [128, width], F32)
        self.msq = c.sb("ln_msq", [128, width], F32)
        self.rstd = [c.sb(f"ln_rstd{g}", [128, widths[g]], F32) for g in range(ng)]
        self.nmr = [c.sb(f"ln_nmr{g}", [128, widths[g]], F32) for g in range(ng)]
        self.t_m = Tok("m")
        self.m2 = c.sb("ln_m2", [128, width], F32)
        self.t_m2 = Tok("m2")
        self.t_msq = Tok("msq")
        self.t_rstd = [Tok() for g in range(ng)]
        self.t_nmr = [Tok() for g in range(ng)]
        self.tt = Ring(c, "ln_tt", 2, [128, width], F32, dma=False)

    def accum(self, c, g, n, z_ap, z_tok, W):
        zb, t_zb, _ = self.zb.next()
        sq, t_sq, _ = self.sq.next()
        c.op("act", lambda e: e.activation(out=zb[:, 0:W], in_=z_ap, func=AF.Identity),
             reads=[z_tok], writes=[t_zb])
        c.op("act", lambda e: e.activation(out=sq[:, 0:W], in_=z_ap, func=AF.Square),
             reads=[z_tok], writes=[t_sq])
        last = n == KC - 1
        c.op("pe", lambda e: e.matmul(self.S1[g][:, 0:W], self.ones[:], zb[:, 0:W],
                                      start=(n == 0), stop=last),
             reads=[self.t_ones, t_zb], writes=[self.t_S1[g]])
        c.op("pe", lambda e: e.matmul(self.S2[g][:, 0:W], self.ones[:], sq[:, 0:W],
                                      start=(n == 0), stop=last),
             reads=[self.t_ones, t_sq], writes=[self.t_S2[g]])

    def finalize(self, c, g, W, eps):
        m, msq, rstd, nmr, m2 = self.m, self.msq, self.rstd[g], self.nmr[g], self.m2
        c.op("act", lambda e: e.activation(out=m[:, 0:W], in_=self.S1[g][:, 0:W],
                                           func=AF.Identity, scale=1.0 / D),
             reads=[self.t_S1[g]], writes=[self.t_m])
        c.op("dve", lambda e: e.tensor_tensor(out=msq[:, 0:W], in0=m[:, 0:W], in1=m[:, 0:W],
                                              op=ALU.mult),
             reads=[self.t_m], writes=[self.t_msq])
        c.op("dve", lambda e: e.scalar_tensor_tensor(out=msq[:, 0:W], in0=self.S2[g][:, 0:W],
                                                     scalar=1.0 / D, in1=msq[:, 0:W],
                                                     op0=ALU.mult, op1=ALU.subtract),
             reads=[self.t_S2[g], self.t_msq], writes=[self.t_msq])
        c.op("dve", lambda e: e.tensor_scalar(out=msq[:, 0:W], in0=msq[:, 0:W],
                                              scalar1=float(eps), scalar2=None, op0=ALU.add),
             reads=[self.t_msq], writes=[self.t_msq])
        c.op("dve", lambda e: e.reciprocal(out=m2[:, 0:W], in_=msq[:, 0:W]),
             reads=[self.t_msq], writes=[self.t_m2])
        c.op("act", lambda e: e.activation(out=rstd[:, 0:W], in_=m2[:, 0:W], func=AF.Sqrt),
             reads=[self.t_m2], writes=[self.t_rstd[g]])
        c.op("dve", lambda e: e.scalar_tensor_tensor(out=nmr[:, 0:W], in0=m[:, 0:W],
                                                     scalar=-1.0, in1=rstd[:, 0:W],
                                                     op0=ALU.mult, op1=ALU.mult),
             reads=[self.t_m, self.t_rstd[g]], writes=[self.t_nmr[g]])

    def normalize(self, c, g, W, z_ap, z_tok):
        tt, t_tt, _ = self.tt.next()
        c.op("dve", lambda e: e.tensor_tensor(out=tt[:, 0:W], in0=z_ap, in1=self.rstd[g][:, 0:W],
                                              op=ALU.mult),
             reads=[z_tok, self.t_rstd[g]], writes=[t_tt])
        c.op("pool", lambda e: e.tensor_tensor(out=tt[:, 0:W], in0=tt[:, 0:W],
                                               in1=self.nmr[g][:, 0:W], op=ALU.add),
             reads=[t_tt, self.t_nmr[g]], writes=[t_tt])
        return tt[:, 0:W], t_tt


def load_vecs(c, name, dram_ap, ncols, q="sp"):
    t = c.sb("v_" + name, [128, ncols], F32)
    tok = Tok(name)
    sem = c.dma_sem(name)
    c.dma(q, sem, t[:], dram_ap, writes=[tok])
    return t, tok


FFN_PASSES = [[(0, 510)], [(510, 510)], [(1020, 510)], [(1530, 510), (2040, 8)]]
NJ = DFF // 128


def emit_ffn(c, h2x, x1T, wub, wdb, t_wconv, cw, cb, mod, lng, lnb, x2T, FFN_PASSES):
    h2v = h2x.rearrange("(kc p) t -> p kc t", p=128)
    x1v = x1T.rearrange("(kc p) t -> p kc t", p=128)
    x2v = x2T.rearrange("(kc p) t -> p kc t", p=128)

    cwt, t_cw = load_vecs(c, "cw", cw, 3 * 88)
    cbt, t_cb = load_vecs(c, "cb", cb, 88)
    modt, t_mod = mod
    lngt, t_lng = load_vecs(c, "lng", lng, 16)
    lnbt, t_lnb = load_vecs(c, "lnb", lnb, 16)
    g2a = c.sb("g2a", [128, 16], F32)
    t_g2a = Tok("g2a")
    c.op("dve", lambda e: e.tensor_scalar(out=g2a[:], in0=modt[:, 80:96], scalar1=1.0 / ALPHA,
                                          scalar2=None, op0=ALU.mult),
         reads=[t_mod], writes=[t_g2a])

    WR = max([g[1][1] for g in FFN_PASSES if len(g) > 1] + [2])
    ln = LNState(c, widths=(510, WR))
    work = [c.ps(f"wk{i}", [128, 512]) for i in range(4)]
    t_work = [Tok(f"wk{i}") for i in range(4)]
    hx = c.sb("hx", [128, KC, 514 + WR], BF16)
    t_hx = Tok("hx")
    s_hx = c.dma_sem("hx")
    gT = c.sb("gT", [128, NJ, 510 + WR], BF16)
    t_g = [Tok(f"g{j}") for j in range(NJ)]
    xt = c.sb("xt", [128, KC, 510 + WR], F32)
    t_xt = [Tok(f"xt{n}") for n in range(KC)]
    s_xt = c.dma_sem("xt")
    s_out = c.dma_sem("out")
    wu = Ring(c, "wu", 3, [128, KC, 256], BF16)
    wd = Ring(c, "wd", 2, [128, NJ, 128], BF16)
    scr = {k: Ring(c, "scr_" + k, 2, [128, 510], F32, dma=False) for k in ("a", "u", "sa")}
    t_outdram = Tok("x2T")
    c.out_toks.append(t_outdram)

    def load_wu(j):
        buf, tok, sem = wu.next()
        c.dma("sp", sem, buf[:], wub[j].rearrange("p (kc n) -> p kc n", kc=KC), reads=[t_wconv], writes=[tok])
        return buf, tok

    def load_wd(n):
        buf, tok, sem = wd.next()
        c.dma("sp", sem, buf[:], wdb[n].rearrange("p (kc n) -> p kc n", kc=NJ), reads=[t_wconv], writes=[tok])
        return buf, tok

    def pass_cols(groups):
        cols = []
        c0 = 0
        gc0 = 0
        for (s, W) in groups:
            cols.append((s, W, c0, gc0))
            c0 += 512
            gc0 += W
        return cols

    def load_hx(cols):
        for (s, W, c0, gc0) in cols:
            c.dma("pool", s_hx, hx[:, :, c0:c0 + W + 2], h2v[:, :, s:s + W + 2], writes=[t_hx])

    PF = 2
    wk_i = 0
    load_hx(pass_cols(FFN_PASSES[0]))
    for ip, groups in enumerate(FFN_PASSES):
        cols = pass_cols(groups)
        for (s, W, c0, gc0) in cols:
            c.dma("pool", s_xt, xt[:, :, gc0:gc0 + W], x1v[:, :, s:s + W], writes=t_xt)
        pend = [load_wu(j) for j in range(min(PF, NJ))]
        for j in range(NJ):
            if j + PF < NJ:
                pend.append(load_wu(j + PF))
            if j % 2 == 0:
                c.pump(1)
            wbuf, t_w = pend.pop(0)
            for (s, W, c0, gc0) in cols:
                N = W + 2
                res = {}
                for hi, half in enumerate(("a", "u")):
                    pb = work[wk_i % 4]
                    t_pb = t_work[wk_i % 4]
                    wk_i += 1
                    mm_group(c, pb[:, 0:N], t_pb,
                             [(wbuf[:, k, hi * 128:(hi + 1) * 128], hx[:, k, c0:c0 + N])
                              for k in range(KC)], reads=[t_w, t_hx])
                    b1, tk1, _ = scr[half].next()
                    res[half] = (pb, t_pb, j + hi * NJ, b1, tk1)
                for half in ("a", "u"):
                    pb, t_pb, jj, b1, tk1 = res[half]
                    c.op("act", lambda e: e.activation(out=b1[:, 0:W], in_=pb[:, 2:N], func=AF.Identity,
                                                       bias=cbt[:, jj:jj + 1],
                                                       scale=cwt[:, 2 * 88 + jj:2 * 88 + jj + 1]),
                         reads=[t_pb, t_cw, t_cb], writes=[tk1])
                for half in ("a", "u"):
                    pb, t_pb, jj, b1, tk1 = res[half]
                    c.op("dve", lambda e: e.scalar_tensor_tensor(
                        out=b1[:, 0:W], in0=pb[:, 1:N - 1], scalar=cwt[:, 88 + jj:88 + jj + 1],
                        in1=b1[:, 0:W], op0=ALU.mult, op1=ALU.add),
                         reads=[t_pb, tk1, t_cw], writes=[tk1])
                for half in ("a", "u"):
                    pb, t_pb, jj, b1, tk1 = res[half]
                    c.op("dve", lambda e: e.scalar_tensor_tensor(
                        out=b1[:, 0:W], in0=pb[:, 0:N - 2], scalar=cwt[:, jj:jj + 1],
                        in1=b1[:, 0:W], op0=ALU.mult, op1=ALU.add),
                         reads=[t_pb, tk1, t_cw], writes=[tk1])
                sa, t_sa, _ = scr["sa"].next()
                ba, tka = res["a"][3], res["a"][4]
                bu, tku = res["u"][3], res["u"][4]
                c.op("act", lambda e: e.activation(out=sa[:, 0:W], in_=ba[:, 0:W], func=AF.Silu),
                     reads=[tka], writes=[t_sa])
                c.op("pool", lambda e: e.tensor_tensor(out=gT[:, j, gc0:gc0 + W], in0=sa[:, 0:W],
                                                       in1=bu[:, 0:W], op=ALU.mult),
                     reads=[t_sa, tku], writes=[t_g[j]])
        if ip + 1 < len(FFN_PASSES):
            load_hx(pass_cols(FFN_PASSES[ip + 1]))
        pend = [load_wd(0)]
        for n in range(KC):
            if n + 1 < KC:
                pend.append(load_wd(n + 1))
            wbuf, t_w = pend.pop(0)
            for gi, (s, W, c0, gc0) in enumerate(cols):
                pb = work[wk_i % 4]
                t_pb = t_work[wk_i % 4]
                wk_i += 1
                mm_group(c, pb[:, 0:W], t_pb,
                         [(wbuf[:, k, :], gT[:, k, gc0:gc0 + W]) for k in range(NJ)],
                         reads=[t_w] + t_g)
                c.op("dve", lambda e: e.scalar_tensor_tensor(
                    out=xt[:, n, gc0:gc0 + W], in0=pb[:, 0:W], scalar=g2a[:, n:n + 1],
                    in1=xt[:, n, gc0:gc0 + W], op0=ALU.mult, op1=ALU.add),
                     reads=[t_pb, t_g2a, t_xt[n]], writes=[t_xt[n]])
                ln.accum(c, gi, n, xt[:, n, gc0:gc0 + W], t_xt[n], W)
        for gi, (s, W, c0, gc0) in enumerate(cols):
            ln.finalize(c, gi, W, LN_EPS / (ALPHA * ALPHA))
            for n in range(KC):
                t_ap, t_tok = ln.normalize(c, gi, W, xt[:, n, gc0:gc0 + W], t_xt[n])
                c.op("act", lambda e: e.activation(out=xt[:, n, gc0:gc0 + W], in_=t_ap, func=AF.Identity,
                                                   bias=lnbt[:, n:n + 1], scale=lngt[:, n:n + 1]),
                     reads=[t_tok, t_lng, t_lnb], writes=[t_xt[n]])
            c.dma("pool", s_out, x2v[:, :, s:s + W], xt[:, :, gc0:gc0 + W], reads=t_xt,
                  writes=[t_outdram])


def emit_oproj(c, ogT, xT, wo, mod, lng, lnb, x1T, h2T, ntok=TOK, h2off=0):
    ogv = ogT.rearrange("(kc p) t -> p kc t", p=128)
    xv = xT.rearrange("(kc p) t -> p kc t", p=128)
    x1v = x1T.rearrange("(kc p) t -> p kc t", p=128)
    h2v = h2T.rearrange("(kc p) t -> p kc t", p=128)
    wob, t_wconv = wo
    modt, t_mod = mod
    lngt, t_lng = load_vecs(c, "lng", lng, 16)
    lnbt, t_lnb = load_vecs(c, "lnb", lnb, 16)
    g1a = c.sb("g1a", [128, 16], F32)
    gs = c.sb("gs", [128, 16], F32)
    bs = c.sb("bs", [128, 16], F32)
    t_g1a, t_gs, t_bs = Tok(), Tok(), Tok()
    c.op("dve", lambda e: e.tensor_scalar(out=g1a[:], in0=modt[:, 32:48], scalar1=1.0 / ALPHA,
                                          scalar2=None, op0=ALU.mult),
         reads=[t_mod], writes=[t_g1a])
    c.op("dve", lambda e: e.scalar_tensor_tensor(out=gs[:], in0=modt[:, 64:80], scalar=1.0, in1=lngt[:],
                                                 op0=ALU.add, op1=ALU.mult),
         reads=[t_mod, t_lng], writes=[t_gs])
    c.op("dve", lambda e: e.scalar_tensor_tensor(out=bs[:], in0=modt[:, 64:80], scalar=1.0, in1=lnbt[:],
                                                 op0=ALU.add, op1=ALU.mult),
         reads=[t_mod, t_lnb], writes=[t_bs])
    c.op("dve", lambda e: e.tensor_tensor(out=bs[:], in0=bs[:], in1=modt[:, 48:64], op=ALU.add),
         reads=[t_mod, t_bs], writes=[t_bs])

    ln = LNState(c, widths=(512,))
    work = [c.ps(f"wk{i}", [128, 512]) for i in range(4)]
    t_work = [Tok(f"wk{i}") for i in range(4)]
    og = Ring(c, "og", 2, [128, KC, 512], BF16)
    xr = Ring(c, "xr", 2, [128, KC, 512], F32)
    h2r = Ring(c, "h2r", 1, [128, KC, 512], BF16, dma=False)
    wor = Ring(c, "wo", 3, [128, KC, 128], BF16)
    s_o1 = c.dma_sem("o1")
    s_o2 = c.dma_sem("o2")
    t_o1, t_o2 = Tok("x1T"), Tok("h2T")
    c.out_toks += [t_o1, t_o2]
    NT = ntok // 512
    wk_i = 0
    if h2off:
        zt = c.sb("zt", [128, KC, h2off], BF16)
        t_zt = Tok()
        c.op("dve", lambda e: e.memset(zt[:], 0.0), writes=[t_zt])
        c.dma("sp", s_o2, h2v[:, :, 0:h2off], zt[:], reads=[t_zt], writes=[t_o2])

    def load_tile(t):
        ob, t_ob, s_ob = og.next()
        c.dma("sp", s_ob, ob[:], ogv[:, :, t * 512:(t + 1) * 512], writes=[t_ob])
        xb, t_xb, s_xb = xr.next()
        c.dma("sp", s_xb, xb[:], xv[:, :, t * 512:(t + 1) * 512], writes=[t_xb])
        return ob, t_ob, xb, t_xb

    def load_wo(n):
        buf, tok, sem = wor.next()
        c.dma("pool", sem, buf[:], wob[n].rearrange("p (kc n) -> p kc n", kc=KC), reads=[t_wconv], writes=[tok])
        return buf, tok

    nxt = load_tile(0)
    for t in range(NT):
        ob, t_ob, xb, t_xb = nxt
        pend = [load_wo(0), load_wo(1)]
        for n in range(KC):
            if n + 2 < KC:
                pend.append(load_wo(n + 2))
            wbuf, t_w = pend.pop(0)
            pb = work[wk_i % 4]
            t_pb = t_work[wk_i % 4]
            wk_i += 1
            mm_group(c, pb[:], t_pb, [(wbuf[:, k, :], ob[:, k, :]) for k in range(KC)],
                     reads=[t_w, t_ob])
            c.op("dve", lambda e: e.scalar_tensor_tensor(
                out=xb[:, n, :], in0=pb[:], scalar=g1a[:, n:n + 1], in1=xb[:, n, :],
                op0=ALU.mult, op1=ALU.add),
                 reads=[t_pb, t_g1a, t_xb], writes=[t_xb])
            ln.accum(c, 0, n, xb[:, n, :], t_xb, 512)
        if t + 1 < NT:
            nxt = load_tile(t + 1)
        ln.finalize(c, 0, 512, LN_EPS / (ALPHA * ALPHA))
        hb, t_hb, _ = h2r.next()
        for n in range(KC):
            t_ap, t_tok = ln.normalize(c, 0, 512, xb[:, n, :], t_xb)
            c.op("act", lambda e: e.activation(out=xb[:, n, :], in_=t_ap, func=AF.Identity,
                                               bias=lnbt[:, n:n + 1], scale=lngt[:, n:n + 1]),
                 reads=[t_tok, t_lng, t_lnb], writes=[t_xb])
            c.op("act", lambda e: e.activation(out=hb[:, n, :], in_=t_ap, func=AF.Identity,
                                               bias=bs[:, n:n + 1], scale=gs[:, n:n + 1]),
                 reads=[t_tok, t_gs, t_bs], writes=[t_hb])
        c.dma("sp", s_o1, x1v[:, :, t * 512:(t + 1) * 512], xb[:], reads=[t_xb], writes=[t_o1])
        c.dma("sp", s_o2, h2v[:, :, h2off + t * 512:h2off + (t + 1) * 512], hb[:], reads=[t_hb], writes=[t_o2])


NEG = 30000.0
HD = 128
FOX_SWEEP = 4


def emit_fox(c, hsrc, fw, wf, bfb, cf, ogT, nheads=8):
    hv = hsrc.rearrange("(kc p) t -> p kc t", p=128)
    ogv = ogT.rearrange("(n p) t -> p n t", p=128)
    fwb, fvb, t_fw = fw
    wfv = wf.rearrange("(kc p) h -> p kc h", p=128)
    NS = FOX_SWEEP
    NT = S // 512

    cft, t_cf = load_vecs(c, "cf", cf, 896)
    bft, t_bf = load_vecs(c, "bfb", bfb, nheads)
    tri, ones_f, ident, mpos = cft[:, 0:128], cft[:, 128:256], cft[:, 256:384], cft[:, 384:896]
    ones_b = c.sb("ones_b", [128, 128], BF16)
    t_ones_b = Tok()
    c.op("dve", lambda e: e.memset(ones_b[:], 1.0), writes=[t_ones_b])
    one_c = c.sb("one_c", [128, 1], F32)
    t_one_c = Tok()
    c.op("dve", lambda e: e.memset(one_c[:], 1.0), writes=[t_one_c])
    wft = c.sb("wft", [128, KC, nheads], BF16)
    t_wf = Tok()
    s_wf = c.dma_sem("wf")
    c.dma("pool", s_wf, wft[:], wfv, writes=[t_wf])

    work = [c.ps(f"wk{i}", [128, 512]) for i in range(4)]
    t_work = [Tok(f"wk{i}") for i in range(4)]
    Ob = [c.ps(f"ob{i}", [128, 512]) for i in range(2)]
    t_Ob = [Tok(f"ob{i}") for i in range(2)]
    Lb = [c.ps(f"lb{i}", [128, 512]) for i in range(2)]
    t_Lb = [Tok(f"lb{i}") for i in range(2)]
    wk_i = [0]

    def nwork():
        i = wk_i[0] % 4
        wk_i[0] += 1
        return work[i], t_work[i]

    KT = c.sb("KT", [128, NS, S], BF16)
    t_KT = [Tok(f"kt{t}") for t in range(NT)]
    V = c.sb("V", [128, S // 128, NS * HD], BF16)
    t_V = [Tok(f"v{t}") for t in range(NT)]
    Gk = c.sb("Gk", [128, S // 128, NS], F32)
    t_Gk = [Tok(f"gk{t}") for t in range(NT)]
    R = c.sb("R", [128, NS], F32)
    t_R = Tok("R")
    hT = c.sb("hT", [128, KC, 512], BF16)
    t_hT = Tok("hT")
    s_h = c.dma_sem("h")
    QT = Ring(c, "QT", 2, [128, NS, 512], BF16, dma=False)
    GT = Ring(c, "GT", 2, [128, NS, 512], BF16, dma=False)
    wr = Ring(c, "w", 4, [128, KC, 128], BF16)
    wvb = c.sb("wvb", [128, KC, NS * HD], BF16)
    t_wvb = Tok("wvb")
    s_wvb = c.dma_sem("wvb")
    PT = Ring(c, "PT", 4, [128, 512], BF16, dma=False)
    sbr = Ring(c, "sbr", 3, [128, 512], F32, dma=False)
    Dg = c.sb("Dg", [128, 512], F32)
    t_Dg = Tok()
    gqa = c.sb("gqa", [128, NS, 512], F32)
    t_gqa = [Tok(f"gqa{n}") for n in range(NS)]
    gqda = c.sb("gqda", [128, NS, 512], F32)
    t_gqda = [Tok(f"gqda{n}") for n in range(NS)]
    den = c.sb("den", [128, 512], F32)
    t_den = Tok()
    rec = c.sb("rec", [128, 512], F32)
    t_rec = Tok()
    ogb = Ring(c, "ogb", 2, [128, NS, 512], BF16, dma=False)
    zf = c.sb("zf", [128, 4, NS], F32)
    t_zf = [Tok() for _ in range(4)]
    spf = c.sb("spf", [128, 4, NS], F32)
    t_spf = [Tok() for _ in range(4)]
    s_og = c.dma_sem("og")
    t_ogd = Tok("ogT")
    c.out_toks.append(t_ogd)
    ob_i = 0

    for sw in range(nheads // NS):
        hc0 = sw * NS * HD
        c.dma("pool", s_wvb, wvb[:], fvb[sw].rearrange("p (kc n) -> p kc n", kc=KC), reads=[t_fw], writes=[t_wvb])
        c.op("dve", lambda e: e.memset(R[:], 0.0), writes=[t_R])
        if sw == 0:
            c.dma("sp", s_h, hT[:], hv[:, :, 0:512], writes=[t_hT])
        for t in range(NT):
            tsl = slice(t * 512, (t + 1) * 512)

            def proj(kind, n):
                c.pump(1)
                wb, t_wb, s_wb = wr.next()
                c.dma("pool", s_wb, wb[:], fwb[kind][sw * NS + n].rearrange("p (kc n) -> p kc n", kc=KC),
                      reads=[t_fw], writes=[t_wb])
                pb, t_pb = nwork()
                mm_group(c, pb[:], t_pb, [(wb[:, k, :], hT[:, k, :]) for k in range(KC)], reads=[t_wb, t_hT])
                return pb, t_pb

            for n in range(NS):
                pb, t_pb = proj("k", n)
                c.op("dve", lambda e: e.tensor_copy(out=KT[:, n, tsl], in_=pb[:]),
                     reads=[t_pb], writes=[t_KT[t]])
            for tb in range(4):
                blk = t * 4 + tb
                bsl = slice(tb * 128, (tb + 1) * 128)
                pb, t_pb = nwork()
                mm_group(c, pb[:], t_pb, [(hT[:, k, bsl], wvb[:, k, :]) for k in range(KC)],
                         reads=[t_wvb, t_hT])
                c.op("act", lambda e: e.activation(out=V[:, blk, :], in_=pb[:], func=AF.Identity),
                     reads=[t_pb], writes=[t_V[t]])
                pz, t_pz = nwork()
                mm_group(c, pz[:, 0:NS], t_pz, [(hT[:, k, bsl], wft[:, k, sw * NS:(sw + 1) * NS]) for k in range(KC)],
                         reads=[t_wf, t_hT])
                c.op("dve", lambda e: e.tensor_tensor(out=zf[:, tb, :], in0=pz[:, 0:NS], in1=bft[:, sw * NS:(sw + 1) * NS],
                                                      op=ALU.add), reads=[t_pz, t_bf], writes=[t_zf[tb]])
                c.op("act", lambda e: e.activation(out=spf[:, tb, :], in_=zf[:, tb, :], func=AF.Exp, scale=-1.0),
                     reads=[t_zf[tb]], writes=[t_spf[tb]])
                c.op("act", lambda e: e.activation(out=spf[:, tb, :], in_=spf[:, tb, :], func=AF.Ln, bias=one_c[:, 0:1]),
                     reads=[t_spf[tb], t_one_c], writes=[t_spf[tb]])
            qb, t_qb, _ = QT.next()
            gb, t_gb, _ = GT.next()
            for n in range(NS):
                pb, t_pb = proj("q", n)
                c.op("act", lambda e: e.activation(out=qb[:, n, :], in_=pb[:], func=AF.Identity,
                                                   scale=float(HD ** -0.5)),
                     reads=[t_pb], writes=[t_qb])
            for n in range(NS):
                pb, t_pb = proj("g", n)
                c.op("act", lambda e: e.activation(out=gb[:, n, :], in_=pb[:], func=AF.Exp, scale=-1.0),
                     reads=[t_pb], writes=[t_gb])
            if t + 1 < NT:
                c.dma("sp", s_h, hT[:], hv[:, :, (t + 1) * 512:(t + 2) * 512], writes=[t_hT])
            elif sw + 1 < nheads // NS:
                c.dma("sp", s_h, hT[:], hv[:, :, 0:512], writes=[t_hT])
            for tb in range(4):
                blk = t * 4 + tb
                pg, t_pg = nwork()
                c.op("pe", lambda e: e.matmul(pg[:, 0:NS], tri, spf[:, tb, :], start=True, stop=False),
                     reads=[t_cf, t_spf[tb]], writes=[t_pg], track=False)
                c.op("pe", lambda e: e.matmul(pg[:, 0:NS], ones_f, R[:], start=False, stop=True),
                     reads=[t_cf, t_R, t_spf[tb]], writes=[t_pg])
                c.op("dve", lambda e: e.tensor_copy(out=Gk[:, blk, :], in_=pg[:, 0:NS]),
                     reads=[t_pg], writes=[t_Gk[t]])
                c.op("dve", lambda e: e.tensor_tensor(out=R[:], in0=R[:], in1=spf[:, tb, :], op=ALU.add),
                     reads=[t_R, t_spf[tb]], writes=[t_R])
            for n in range(NS):
                for j in range(4):
                    c.op("dve", lambda e: e.tensor_scalar(out=Dg[:, j * 128:(j + 1) * 128], in0=ident,
                                                          scalar1=Gk[:, t * 4 + j, n:n + 1], scalar2=None,
                                                          op0=ALU.mult),
                         reads=[t_cf, t_Gk[t]], writes=[t_Dg])
                pq, t_pq = nwork()
                c.op("pe", lambda e: e.matmul(pq[:], ones_f, Dg[:], start=True, stop=True),
                     reads=[t_cf, t_Dg], writes=[t_pq])
                c.op("act", lambda e: e.activation(out=gqa[:, n, :], in_=pq[:], func=AF.Identity),
                     reads=[t_pq], writes=[t_gqa[n]])
                c.op("pool", lambda e: e.tensor_tensor(out=gqda[:, n, :], in0=gqa[:, n, :], in1=mpos, op=ALU.add),
                     reads=[t_gqa[n], t_cf], writes=[t_gqda[n]])
            og_b, t_og, _ = ogb.next()
            nkb = 4 * t + 4
            banks = {}
            for n in range(NS):
                banks[n] = (Ob[ob_i % 2], t_Ob[ob_i % 2], Lb[ob_i % 2], t_Lb[ob_i % 2])
                ob_i += 1
            pending = []

            def emit_scores(n, kb):
                j = kb - 4 * t
                q0 = 128 * j if j > 0 else 0
                N = 512 - q0
                gq, t_gq, gqd, t_gqd = gqa[:, n, :], t_gqa[n], gqda[:, n, :], t_gqda[n]
                ps_, t_ps = nwork()
                c.op("pe", lambda e: e.matmul(ps_[:, 0:N], KT[:, n, kb * 128:(kb + 1) * 128], qb[:, n, q0:512],
                                              start=True, stop=True),
                     reads=[t_KT[kb // 4], t_qb], writes=[t_ps])
                sb_, t_sb, _ = sbr.next()
                if j < 0:
                    c.op("dve", lambda e: e.tensor_tensor(out=sb_[:, 0:N], in0=ps_[:, 0:N], in1=gq[:, q0:512],
                                                          op=ALU.subtract),
                         reads=[t_ps, t_gq], writes=[t_sb])
                else:
                    c.op("dve", lambda e: e.tensor_tensor(out=sb_[:, 0:128], in0=ps_[:, 0:128],
                                                          in1=gqd[:, q0:q0 + 128], op=ALU.subtract),
                         reads=[t_ps, t_gqd], writes=[t_sb])
                    if N > 128:
                        c.op("dve", lambda e: e.tensor_tensor(out=sb_[:, 128:N], in0=ps_[:, 128:N],
                                                              in1=gq[:, q0 + 128:512], op=ALU.subtract),
                             reads=[t_ps, t_gq], writes=[t_sb])
                pt, t_pt, _ = PT.next()
                c.op("act", lambda e: e.activation(out=pt[:, 0:N], in_=sb_[:, 0:N], func=AF.Exp,
                                                   bias=Gk[:, kb, n:n + 1]),
                     reads=[t_sb, t_Gk[kb // 4]], writes=[t_pt])
                return (n, kb, q0, N, pt, t_pt)

            def emit_pv(item):
                n, kb, q0, N, pt, t_pt = item
                O, t_O, L, t_L = banks[n]
                last = kb == nkb - 1
                c.op("pe", lambda e: e.matmul(O[:, q0:512], V[:, kb, n * HD:(n + 1) * HD], pt[:, 0:N],
                                              start=(kb == 0), stop=last),
                     reads=[t_V[kb // 4], t_pt], writes=[t_O], track=last)
                c.op("pe", lambda e: e.matmul(L[:, q0:512], ones_b[:], pt[:, 0:N],
                                              start=(kb == 0), stop=last),
                     reads=[t_ones_b, t_pt], writes=[t_L], track=True)
                if last:
                    c.op("dve", lambda e: e.scalar_tensor_tensor(out=den[:], in0=gb[:, n, :], scalar=1.0, in1=L[:],
                                                                 op0=ALU.add, op1=ALU.mult),
                         reads=[t_gb, t_L], writes=[t_den])
                    c.op("dve", lambda e: e.reciprocal(out=rec[:], in_=den[:]), reads=[t_den], writes=[t_rec])
                    c.op("dve", lambda e: e.tensor_tensor(out=og_b[:, n, :], in0=O[:], in1=rec[:], op=ALU.mult),
                         reads=[t_O, t_rec], writes=[t_og])

            LA = 2
            for n in range(NS):
                for kb in range(nkb):
                    pending.append(emit_scores(n, kb))
                    if len(pending) > LA:
                        emit_pv(pending.pop(0))
            while pending:
                emit_pv(pending.pop(0))
            c.dma("sp", s_og, ogv[:, sw * NS:(sw + 1) * NS, tsl], og_b[:], reads=[t_og], writes=[t_ogd])


def fox_consts():
    k = np.arange(128)[:, None]
    q = np.arange(128)[None, :]
    tri = (k <= q).astype(np.float32)
    ones = np.ones((128, 128), np.float32)
    ident = np.eye(128, dtype=np.float32)
    mp = np.where(q >= k, 0.0, NEG).astype(np.float32)
    return np.ascontiguousarray(np.concatenate([tri, ones, ident, mp, mp, mp, mp], axis=1))


GLA_TAU = 16.0


def emit_gla(c, hsrc, wq, wk, wv, wr, wa1, wa2, ba, gbc, cg, ogT, nheads=2):
    hv = hsrc.rearrange("(kc p) t -> p kc t", p=128)
    ogv = ogT.rearrange("(n p) t -> p n t", p=128)
    wqv = wq.rearrange("(kc p) n -> p kc n", p=128)
    wkv = wk.rearrange("(kc p) n -> p kc n", p=128)
    wvv = wv.rearrange("(kc p) n -> p kc n", p=128)
    wrv = wr.rearrange("(kc p) n -> p kc n", p=128)
    wa1v = wa1.rearrange("(kc p) n -> p kc n", p=128)
    NT = S // 512
    DK, DV = 256, 512

    cgt, t_cg = load_vecs(c, "cg", cg, 512)
    gbt, t_gb = load_vecs(c, "gbc", gbc, 512)
    triN, triU, tri01, ident_f = cgt[:, 0:128], cgt[:, 128:256], cgt[:, 256:384], cgt[:, 384:512]
    ident_b = c.sb("ident_b", [128, 128], BF16)
    t_idb = Tok()
    c.op("dve", lambda e: e.tensor_copy(out=ident_b[:], in_=ident_f), reads=[t_cg], writes=[t_idb])
    one_c = c.sb("one_c", [128, 1], F32)
    eps_c = c.sb("eps_c", [128, 1], F32)
    t_cc = Tok()
    c.op("dve", lambda e: e.memset(one_c[:], 1.0), writes=[t_cc])
    c.op("dve", lambda e: e.memset(eps_c[:], RMS_EPS), writes=[t_cc])

    wq_b = c.sb("wq_b", [128, KC, DK], BF16)
    wk_b = c.sb("wk_b", [128, KC, DK], BF16)
    wv_b = c.sb("wv_b", [128, KC, DV], BF16)
    wr_b = c.sb("wr_b", [128, KC, DV], BF16)
    wa1_b = c.sb("wa1_b", [128, KC, 16], BF16)
    wa2a = c.sb("wa2a", [32, DK], BF16)
    t_W = Tok("W")
    s_W = c.dma_sem("W")
    c.dma("pool", s_W, wa1_b[:], wa1v, writes=[t_W])

    work = [c.ps(f"wk{i}", [128, 512]) for i in range(6)]
    t_work = [Tok(f"wk{i}") for i in range(6)]
    pcb = [c.ps(f"pc{i}", [128, 512]) for i in range(2)]
    t_pcb = [Tok(f"pc{i}") for i in range(2)]
    wk_i = [0]

    def nwork():
        i = wk_i[0] % 6
        wk_i[0] += 1
        return work[i], t_work[i]

    hT = c.sb("hT", [128, KC, 512], BF16)
    t_hT = Tok("hT")
    s_h = c.dma_sem("h")
    gaug = Ring(c, "gaug", 2, [32, 512], BF16, dma=False)
    for i in range(2):
        c.op("dve", lambda e: e.memset(gaug.bufs[i][:], 1.0), writes=[gaug.toks[i]])
    spr = Ring(c, "sp", 2, [128, 4, DK], F32, dma=False)
    ekd = Ring(c, "ekd", 2, [128, DK], F32, dma=False)
    ekr = Ring(c, "ek", 2, [128, 512], F32, dma=False)
    qdT = Ring(c, "qdT", 2, [128, 2, 512], BF16, dma=False)
    kiT = Ring(c, "kiT", 2, [128, 2, 512], BF16, dma=False)
    eq = Ring(c, "eq", 2, [128, 2, 512], F32, dma=False)
    kdec = Ring(c, "kdec", 2, [128, 4, DK], BF16, dma=False)
    Vr = Ring(c, "V", 2, [128, 4, DV], BF16, dma=False)
    rraw = Ring(c, "rraw", 2, [128, 4, DV], BF16, dma=False)
    rs = Ring(c, "rs", 2, [128, 4, DV], BF16, dma=False)
    attm = Ring(c, "attm", 2, [128, 128], BF16, dma=False)
    state = c.sb("state", [128, 2, DV], F32)
    t_state = Tok("state")
    stb = Ring(c, "stb", 2, [128, 2, DV], BF16, dma=False)
    junk = c.sb("junk", [128, DV], BF16)
    t_junk = Tok()
    ssq = Ring(c, "ssq", 2, [128, 1], F32, dma=False)
    onr = Ring(c, "on", 2, [128, DV], F32, dma=False)
    ogt = Ring(c, "ogt", 2, [128, DV], BF16, dma=False)
    ogo = Ring(c, "ogo", 2, [128, 4, 512], BF16, dma=False)
    s_og = c.dma_sem("og")
    t_ogd = Tok("ogT")
    c.out_toks.append(t_ogd)

    def stage_a(sw, t):
        tsl = slice(t * 512, (t + 1) * 512)
        c.dma("sp", s_h, hT[:], hv[:, :, tsl], writes=[t_hT])
        ga, t_ga, _ = gaug.next()
        pg, t_pg = nwork()
        mm_group(c, pg[0:16, :], t_pg, [(wa1_b[:, k, :], hT[:, k, :]) for k in range(KC)], reads=[t_W, t_hT])
        c.op("act", lambda e: e.activation(out=ga[0:16, :], in_=pg[0:16, :], func=AF.Identity),
             reads=[t_pg], writes=[t_ga])
        sp, t_sp, _ = spr.next()
        kd, t_kd, _ = kdec.next()
        Vb, t_Vb, _ = Vr.next()
        rr, t_rr, _ = rraw.next()
        rsb, t_rs, _ = rs.next()
        for tb in range(4):
            bsl = slice(tb * 128, (tb + 1) * 128)
            pz, t_pz = nwork()
            c.op("pe", lambda e: e.matmul(pz[:, 0:DK], ga[0:17, bsl], wa2a[0:17, :], start=True, stop=True),
                 reads=[t_ga, t_W], writes=[t_pz])
            c.op("act", lambda e: e.activation(out=sp[:, tb, :], in_=pz[:, 0:DK], func=AF.Exp, scale=-1.0),
                 reads=[t_pz], writes=[t_sp])
            c.op("act", lambda e: e.activation(out=sp[:, tb, :], in_=sp[:, tb, :], func=AF.Ln, bias=one_c[:, 0:1]),
                 reads=[t_sp, t_cc], writes=[t_sp])
            pd, t_pd = nwork()
            c.op("pe", lambda e: e.matmul(pd[:, 0:DK], triU, sp[:, tb, :], start=True, stop=True),
                 reads=[t_cg, t_sp], writes=[t_pd])
            ek_, t_ek_, _ = ekd.next()
            c.op("act", lambda e: e.activation(out=ek_[:], in_=pd[:, 0:DK], func=AF.Exp),
                 reads=[t_pd], writes=[t_ek_])
            pk, t_pk = nwork()
            mm_group(c, pk[:, 0:DK], t_pk, [(hT[:, k, bsl], wk_b[:, k, :]) for k in range(KC)], reads=[t_W, t_hT])
            c.op("dve", lambda e: e.tensor_tensor(out=kd[:, tb, :], in0=pk[:, 0:DK], in1=ek_[:], op=ALU.mult),
                 reads=[t_pk, t_ek_], writes=[t_kd])
            pv, t_pv = nwork()
            mm_group(c, pv[:], t_pv, [(hT[:, k, bsl], wv_b[:, k, :]) for k in range(KC)], reads=[t_W, t_hT])
            c.op("act", lambda e: e.activation(out=Vb[:, tb, :], in_=pv[:], func=AF.Identity),
                 reads=[t_pv], writes=[t_Vb])
            pr, t_pr = nwork()
            mm_group(c, pr[:], t_pr, [(hT[:, k, bsl], wr_b[:, k, :]) for k in range(KC)], reads=[t_W, t_hT])
            c.op("dve", lambda e: e.tensor_copy(out=rr[:, tb, :], in_=pr[:]), reads=[t_pr], writes=[t_rr])
            for dc in range(2):
                c.op("pe", lambda e: e.matmul(pcb[dc][:, bsl], sp[:, tb, dc * 128:(dc + 1) * 128], triN,
                                              start=True, stop=True),
                     reads=[t_cg, t_sp], writes=[t_pcb[dc]])
        c.op("act", lambda e: e.activation(out=rsb[:], in_=rr[:], func=AF.Silu), reads=[t_rr], writes=[t_rs])
        eqb, t_eq, _ = eq.next()
        qd, t_qd, _ = qdT.next()
        ki, t_ki, _ = kiT.next()
        for dc in range(2):
            c.op("act", lambda e: e.activation(out=eqb[:, dc, :], in_=pcb[dc][:], func=AF.Exp),
                 reads=[t_pcb[dc]], writes=[t_eq])
            ekb, t_ekb, _ = ekr.next()
            c.op("act", lambda e: e.activation(out=ekb[:], in_=pcb[dc][:], func=AF.Exp, scale=-1.0),
                 reads=[t_pcb[dc]], writes=[t_ekb])
            pq, t_pq = nwork()
            mm_group(c, pq[:], t_pq, [(wq_b[:, k, dc * 128:(dc + 1) * 128], hT[:, k, :]) for k in range(KC)],
                     reads=[t_W, t_hT])
            c.op("dve", lambda e: e.scalar_tensor_tensor(out=qd[:, dc, :], in0=pq[:], scalar=float(DK ** -0.5),
                                                         in1=eqb[:, dc, :], op0=ALU.mult, op1=ALU.mult),
                 reads=[t_pq, t_eq], writes=[t_qd])
            pk2, t_pk2 = nwork()
            mm_group(c, pk2[:], t_pk2, [(wk_b[:, k, dc * 128:(dc + 1) * 128], hT[:, k, :]) for k in range(KC)],
                     reads=[t_W, t_hT])
            c.op("dve", lambda e: e.tensor_tensor(out=ki[:, dc, :], in0=pk2[:], in1=ekb[:], op=ALU.mult),
                 reads=[t_pk2, t_ekb], writes=[t_ki])
        return dict(qd=qd, t_qd=t_qd, ki=ki, t_ki=t_ki, eq=eqb, t_eq=t_eq, kd=kd, t_kd=t_kd,
                    V=Vb, t_V=t_Vb, rs=rsb, t_rs=t_rs)

    def stage_b(sw, t, A):
        tsl = slice(t * 512, (t + 1) * 512)
        oo, t_oo, _ = ogo.next()
        for tb in range(4):
            bsl = slice(tb * 128, (tb + 1) * 128)
            pa, t_pa = nwork()
            mm_group(c, pa[:, 0:128], t_pa, [(A["ki"][:, dc, bsl], A["qd"][:, dc, bsl]) for dc in range(2)],
                     reads=[A["t_ki"], A["t_qd"]])
            am, t_am, _ = attm.next()
            c.op("dve", lambda e: e.tensor_tensor(out=am[:], in0=pa[:, 0:128], in1=tri01, op=ALU.mult),
                 reads=[t_pa, t_cg], writes=[t_am])
            sb_old, t_sb_old, _ = stb.cur()
            po, t_po = nwork()
            c.op("pe", lambda e: e.matmul(po[:], am[:], A["V"][:, tb, :], start=True, stop=False),
                 reads=[t_am, A["t_V"]], writes=[t_po], track=False)
            for dc in range(2):
                c.op("pe", lambda e: e.matmul(po[:], A["qd"][:, dc, bsl], sb_old[:, dc, :], start=False, stop=(dc == 1)),
                     reads=[A["t_qd"], t_sb_old, t_am, A["t_V"]], writes=[t_po], track=(dc == 1))
            sb_new, t_sb_new, _ = stb.next()
            for dc in range(2):
                pkv, t_pkv = nwork()
                c.op("pe", lambda e: e.matmul(pkv[:], A["kd"][:, tb, dc * 128:(dc + 1) * 128], A["V"][:, tb, :],
                                              start=True, stop=True),
                     reads=[A["t_kd"], A["t_V"]], writes=[t_pkv])
                col = tb * 128 + 127
                c.op("dve", lambda e: e.scalar_tensor_tensor(out=state[:, dc, :], in0=state[:, dc, :],
                                                             scalar=A["eq"][:, dc, col:col + 1], in1=pkv[:],
                                                             op0=ALU.mult, op1=ALU.add),
                     reads=[t_state, A["t_eq"], t_pkv], writes=[t_state])
                c.op("act", lambda e: e.activation(out=sb_new[:, dc, :], in_=state[:, dc, :], func=AF.Identity),
                     reads=[t_state], writes=[t_sb_new])
            sq, t_sq, _ = ssq.next()
            c.op("act", lambda e: e.activation(out=junk[:], in_=po[:], func=AF.Square, accum_out=sq[:]),
                 reads=[t_po], writes=[t_junk, t_sq])
            c.op("act", lambda e: e.activation(out=sq[:], in_=sq[:], func=AF.Ln, scale=1.0 / DV, bias=eps_c[:, 0:1]),
                 reads=[t_sq, t_cc], writes=[t_sq])
            c.op("act", lambda e: e.activation(out=sq[:], in_=sq[:], func=AF.Exp, scale=-0.5),
                 reads=[t_sq], writes=[t_sq])
            on, t_on, _ = onr.next()
            c.op("dve", lambda e: e.scalar_tensor_tensor(out=on[:], in0=po[:], scalar=sq[:, 0:1], in1=gbt[:],
                                                         op0=ALU.mult, op1=ALU.mult),
                 reads=[t_po, t_sq, t_gb], writes=[t_on])
            og_, t_og_, _ = ogt.next()
            c.op("pool", lambda e: e.tensor_tensor(out=og_[:], in0=on[:], in1=A["rs"][:, tb, :], op=ALU.mult),
                 reads=[t_on, A["t_rs"]], writes=[t_og_])
            pt, t_pt = nwork()
            for fc in range(4):
                c.op("pe", lambda e: e.matmul(pt[:, fc * 128:(fc + 1) * 128], og_[:, fc * 128:(fc + 1) * 128],
                                              ident_b[:], start=True, stop=True),
                     reads=[t_og_, t_idb], writes=[t_pt], track=(fc == 3))
            c.op("dve", lambda e: e.tensor_copy(out=oo[:, :, bsl], in_=pt[:].rearrange("p (f t) -> p f t", f=4)),
                 reads=[t_pt], writes=[t_oo])
        c.dma("sp", s_og, ogv[:, sw * 4:(sw + 1) * 4, tsl], oo[:], reads=[t_oo], writes=[t_ogd])

    for sw in range(nheads):
        c.dma("pool", s_W, wq_b[:], wqv[:, :, sw * DK:(sw + 1) * DK], writes=[t_W])
        c.dma("pool", s_W, wk_b[:], wkv[:, :, sw * DK:(sw + 1) * DK], writes=[t_W])
        c.dma("pool", s_W, wv_b[:], wvv[:, :, sw * DV:(sw + 1) * DV], writes=[t_W])
        c.dma("pool", s_W, wr_b[:], wrv[:, :, sw * DV:(sw + 1) * DV], writes=[t_W])
        c.dma("pool", s_W, wa2a[0:16, :], wa2[:, sw * DK:(sw + 1) * DK], writes=[t_W])
        c.dma("pool", s_W, wa2a[16:17, :], ba[:, sw * DK:(sw + 1) * DK], writes=[t_W])
        c.op("dve", lambda e: e.memset(state[:], 0.0), writes=[t_state])
        sb0, t_sb0, _ = stb.next()
        c.op("dve", lambda e: e.memset(sb0[:], 0.0), writes=[t_sb0])
        prev = stage_a(sw, 0)
        for t in range(NT):
            nxt = stage_a(sw, t + 1) if t + 1 < NT else None
            stage_b(sw, t, prev)
            prev = nxt


def gla_consts():
    a = np.arange(128)[:, None]
    b = np.arange(128)[None, :]
    triN = np.where(a <= b, -1.0 / GLA_TAU, 0.0).astype(np.float32)
    triU = np.where(a > b, -1.0 / GLA_TAU, 0.0).astype(np.float32)
    tri01 = (a <= b).astype(np.float32)
    ident = np.eye(128, dtype=np.float32)
    return np.ascontiguousarray(np.concatenate([triN, triU, tri01, ident], axis=1))


def emit_sconv(c, hsrc, wb, wc, wu, cw, ogT, nsw=1):
    hv = hsrc.rearrange("(kc p) t -> p kc t", p=128)
    ogv = ogT.rearrange("(n p) t -> p n t", p=128)
    NT = S // 512
    NCH = 8
    NCHT = NCH * nsw
    cwt, t_cw = load_vecs(c, "cw", cw, 3 * NCHT)
    W = {}
    t_W = Tok("W")
    s_W = c.dma_sem("W")
    for name in ("b", "c", "u"):
        W[name] = c.sb("w_" + name, [128, KC, 1024], BF16)
    work = [c.ps(f"wk{i}", [128, 512]) for i in range(6)]
    t_work = [Tok(f"wk{i}") for i in range(6)]
    wk_i = [0]

    def nwork():
        i = wk_i[0] % 6
        wk_i[0] += 1
        return work[i], t_work[i]

    hT = Ring(c, "hT", 2, [128, KC, 512], BF16)
    pext = c.sb("pext", [128, NCH, 514], F32)
    t_pext = [Tok(f"pe{n}") for n in range(NCH)]
    usb = Ring(c, "usb", 2, [128, 512], F32, dma=False)
    yb = Ring(c, "yb", 2, [128, 512], F32, dma=False)
    ogo = Ring(c, "ogo", 2, [128, NCH, 512], BF16, dma=False)
    s_og = c.dma_sem("og")
    t_ogd = Tok("ogT")
    c.out_toks.append(t_ogd)
    for sw in range(nsw):
        for name, ap in (("b", wb), ("c", wc), ("u", wu)):
            v = ap.rearrange("(kc p) n -> p kc n", p=128)
            for q in range(4):
                c.dma("pool", s_W, W[name][:, :, q * 256:(q + 1) * 256],
                      v[:, :, sw * 1024 + q * 256:sw * 1024 + (q + 1) * 256], writes=[t_W])
        c.op("dve", lambda e: e.memset(pext[:], 0.0), writes=t_pext)
        for t in range(NT):
            tsl = slice(t * 512, (t + 1) * 512)
            hb, t_hb, s_hb = hT.next()
            c.dma("sp", s_hb, hb[:], hv[:, :, tsl], writes=[t_hb])
            oo, t_oo, _ = ogo.next()
            for n in range(NCH):
                nsl = slice(n * 128, (n + 1) * 128)
                cn = sw * NCH + n
                pu, t_pu = nwork()
                mm_group(c, pu[:], t_pu, [(W["u"][:, k, nsl], hb[:, k, :]) for k in range(KC)], reads=[t_W, t_hb])
                ub, t_ub, _ = usb.next()
                c.op("act", lambda e: e.activation(out=ub[:], in_=pu[:], func=AF.Identity), reads=[t_pu], writes=[t_ub])
                pc_, t_pc = nwork()
                mm_group(c, pc_[:], t_pc, [(W["c"][:, k, nsl], hb[:, k, :]) for k in range(KC)], reads=[t_W, t_hb])
                if t > 0:
                    c.op("pool", lambda e: e.tensor_copy(out=pext[:, n, 0:2], in_=pext[:, n, 512:514]),
                         reads=[t_pext[n]], writes=[t_pext[n]])
                c.op("dve", lambda e: e.tensor_tensor(out=pext[:, n, 2:514], in0=pc_[:], in1=ub[:], op=ALU.mult),
                     reads=[t_pc, t_ub, t_pext[n]], writes=[t_pext[n]])
                y, t_y, _ = yb.next()
                c.op("act", lambda e: e.activation(out=y[:], in_=pext[:, n, 2:514], func=AF.Identity,
                                                   scale=cwt[:, 2 * NCHT + cn:2 * NCHT + cn + 1]),
                     reads=[t_pext[n], t_cw], writes=[t_y])
                c.op("dve", lambda e: e.scalar_tensor_tensor(out=y[:], in0=pext[:, n, 1:513],
                                                             scalar=cwt[:, NCHT + cn:NCHT + cn + 1], in1=y[:],
                                                             op0=ALU.mult, op1=ALU.add),
                     reads=[t_pext[n], t_cw, t_y], writes=[t_y])
                c.op("dve", lambda e: e.scalar_tensor_tensor(out=y[:], in0=pext[:, n, 0:512],
                                                             scalar=cwt[:, cn:cn + 1], in1=y[:],
                                                             op0=ALU.mult, op1=ALU.add),
                     reads=[t_pext[n], t_cw, t_y], writes=[t_y])
                pb, t_pb = nwork()
                mm_group(c, pb[:], t_pb, [(W["b"][:, k, nsl], hb[:, k, :]) for k in range(KC)], reads=[t_W, t_hb])
                c.op("dve", lambda e: e.tensor_tensor(out=oo[:, n, :], in0=pb[:], in1=y[:], op=ALU.mult),
                     reads=[t_pb, t_y], writes=[t_oo])
            c.dma("sp", s_og, ogv[:, sw * NCH:(sw + 1) * NCH, tsl], oo[:], reads=[t_oo], writes=[t_ogd])


def vec_pm(v):
    v = np.asarray(v, dtype=np.float32)
    lead = v.shape[:-1]
    n = v.shape[-1] // 128
    v = v.reshape(-1, n, 128)
    return np.ascontiguousarray(v.transpose(2, 0, 1).reshape(128, -1))


def bcast_rows(v, n=128):
    v = np.asarray(v, dtype=np.float32)
    return np.ascontiguousarray(np.broadcast_to(v[None, :], (n, v.shape[0])))


def emit_mod_own(c, cT, ada_w, ada_b, modts, t_modts):
    ct, t_ct = load_vecs(c, "cT", cT, KC)
    cond = c.sb("cond", [128, KC], BF16)
    t_cond = Tok()
    c.op("act", lambda e: e.activation(out=cond[:], in_=ct[:], func=AF.Silu), reads=[t_ct], writes=[t_cond])
    one11 = c.sb("one11", [1, 1], F32)
    t_one = Tok()
    c.op("dve", lambda e: e.memset(one11[:], 1.0), writes=[t_one])
    awr = Ring(c, "aw", 3, [128, KC, 512], BF16)
    row = c.sb("row", [1, 6 * D], F32)
    t_row = Tok()
    abt = c.sb("abt", [1, 6 * D], F32)
    t_ab = Tok()
    s_ab = c.dma_sem("ab")
    work = [c.ps(f"wk{i}", [128, 512]) for i in range(4)]
    t_work = [Tok(f"wk{i}") for i in range(4)]
    pm = c.ps("pm", [128, 96])
    t_pm = Tok()
    i = 0
    NCC = 6 * D // 512
    for l in range(DEPTH):
        awv = ada_w[l].rearrange("(kc p) n -> p kc n", p=128)
        c.dma("sp", s_ab, abt[:], ada_b[l:l + 1, :], writes=[t_ab])
        for cc in range(NCC):
            csl = slice(cc * 512, (cc + 1) * 512)
            wb, t_wb, s_wb = awr.next()
            c.dma("pool", s_wb, wb[:], awv[:, :, csl], writes=[t_wb])
            pb, t_pb = work[i % 4], t_work[i % 4]
            i += 1
            mm_group(c, pb[0:1, :], t_pb, [(cond[:, k:k + 1], wb[:, k, :]) for k in range(KC)],
                     reads=[t_cond, t_wb])
            c.op("dve", lambda e: e.tensor_tensor(out=row[0:1, csl], in0=pb[0:1, :], in1=abt[0:1, csl], op=ALU.add),
                 reads=[t_pb, t_ab], writes=[t_row])
        for m in range(96):
            c.op("pe", lambda e: e.matmul(pm[:, m:m + 1], row[0:1, m * 128:(m + 1) * 128], one11[0:1, 0:1],
                                          start=True, stop=True),
                 reads=[t_row, t_one], writes=[t_pm], track=(m == 95))
        c.op("dve", lambda e: e.tensor_copy(out=modts[l][:], in_=pm[:]), reads=[t_pm], writes=[t_modts[l]])


FUSED_PASSES = [[(510 * i, 510)] for i in range(7)] + [[(3570, 510), (4080, 16)]]


def emit_hprep(c, xT, mod, hT_dram):
    xv = xT.rearrange("(kc p) t -> p kc t", p=128)
    hv = hT_dram.rearrange("(kc p) t -> p kc t", p=128)
    modt, t_mod = mod
    sc1p = c.sb("sc1p", [128, 16], F32)
    t_sc1p = Tok()
    c.op("dve", lambda e: e.tensor_scalar(out=sc1p[:], in0=modt[:, 16:32], scalar1=1.0, scalar2=None,
                                          op0=ALU.add), reads=[t_mod], writes=[t_sc1p])
    xp = Ring(c, "xp", 3, [128, 4, 512], F32)
    hb = Ring(c, "hb", 2, [128, KC, 512], BF16)
    t_out = Tok()
    for t in range(S // 512):
        tsl = slice(t * 512, (t + 1) * 512)
        h, t_h, s_h = hb.next()
        for q4 in range(4):
            xb, t_xb, s_xb = xp.next()
            c.dma("sp" if q4 % 2 == 0 else "pool", s_xb, xb[:], xv[:, q4 * 4:(q4 + 1) * 4, tsl], writes=[t_xb])
            for kk in range(4):
                kc = q4 * 4 + kk
                c.op("act", lambda e: e.activation(out=h[:, kc, :], in_=xb[:, kk, :], func=AF.Identity,
                                                   bias=modt[:, kc:kc + 1], scale=sc1p[:, kc:kc + 1]),
                     reads=[t_xb, t_mod, t_sc1p], writes=[t_h])
        c.dma("sp", s_h, hv[:, :, tsl], h[:], reads=[t_h], writes=[t_out])


def emit_fox_wconv(c, tag, wq, wk, wv, wg, fwb, fvb, lazy):
    old = c.kind
    c.kind = ""
    key = c.dma_sem(f"bg{tag}")
    c.kind = old
    tok = Tok(f"fconv{tag}")

    def q(dst, src_):
        fn = lambda: c.dma("pool", key, dst, src_, writes=[Tok()], reads=[]) and None
        if lazy:
            c.bg.append(fn)
        else:
            fn()

    for kind, w in (("k", wk), ("q", wq), ("g", wg)):
        v = w.rearrange("(kc p) n -> p kc n", p=128)
        for n in range(KC):
            q(fwb[kind][n].rearrange("p (kc n) -> p kc n", kc=KC), v[:, :, n * 128:(n + 1) * 128])
    v = wv.rearrange("(kc p) n -> p kc n", p=128)
    for sw in range(4):
        q(fvb[sw].rearrange("p (kc n) -> p kc n", kc=KC), v[:, :, sw * 512:(sw + 1) * 512])
    c.bg_final[tok] = (key, 52 * 16)
    return tok


def emit_wconv(c, l, w_up, w_down, wub, wdb, wo, wob):
    old = c.kind
    c.kind = ""
    key = c.dma_sem(f"bg{l}")
    c.kind = old
    tok = Tok(f"wconv{l}")
    wuv = w_up.rearrange("(kc p) n -> p kc n", p=128)
    wdv = w_down.rearrange("(kc p) n -> p kc n", p=128)
    wov = wo.rearrange("(kc p) n -> p kc n", p=128)

    def q(dst, src_):
        c.bg.append(lambda: c.dma("pool", key, dst, src_, writes=[Tok()]))

    c.bg_final[tok] = (key, 120 * 16)
    for n in range(KC):
        q(wob[n].rearrange("p (kc n) -> p kc n", kc=KC), wov[:, :, n * 128:(n + 1) * 128])
    for j in range(NJ):
        dst = wub[j].rearrange("p (kc n) -> p kc n", kc=KC)
        for h in range(2):
            q(dst[:, :, h * 128:(h + 1) * 128], wuv[:, :, h * DFF + j * 128:h * DFF + (j + 1) * 128])
    for n in range(KC):
        q(wdb[n].rearrange("p (kc n) -> p kc n", kc=NJ), wdv[:, :, n * 128:(n + 1) * 128])
    return tok


N_FOX = 2


def build_fused():
    nc = bass.Bass("TRN2", target_bir_lowering=False)

    def din(name, shape, dt=F32):
        return nc.dram_tensor(name, list(shape), dt, kind="ExternalInput").ap()

    def dint(name, shape, dt):
        return nc.dram_tensor(name, list(shape), dt).ap()

    xT = din("xT", [D, S])
    cT = din("cT", [128, KC])
    ada_w = din("ada_w", [DEPTH, D, 6 * D])
    ada_b = din("ada_b", [DEPTH, 6 * D])
    lnv = din("lnv", [DEPTH, 128, 64])
    fox_wq = din("fox_wq", [N_FOX, D, D])
    fox_wk = din("fox_wk", [N_FOX, D, D])
    fox_wv = din("fox_wv", [N_FOX, D, D])
    fox_wg = din("fox_wg", [N_FOX, D, D])
    fox_wf = din("fox_wf", [N_FOX, D, 16])
    fox_bfb = din("fox_bfb", [N_FOX, 128, 16])
    fox_wo = din("fox_wo", [N_FOX, D, D])
    fox_cf = din("fox_cf", [128, 896])
    gla_wq = din("gla_wq", [1, D, 1024])
    gla_wk = din("gla_wk", [1, D, 1024])
    gla_wv = din("gla_wv", [1, D, D])
    gla_wr = din("gla_wr", [1, D, D])
    gla_wa1 = din("gla_wa1", [1, D, 16])
    gla_wa2 = din("gla_wa2", [1, 16, 1024])
    gla_ba = din("gla_ba", [1, 1024])
    gla_gbc = din("gla_gbc", [128, 512])
    gla_cg = din("gla_cg", [128, 512])
    gla_wo = din("gla_wo", [1, D, D])
    conv_w_in = din("conv_w_in", [1, D, 3 * D])
    conv_cw = din("conv_cw", [128, 48])
    conv_w_out = din("conv_w_out", [1, D, D])
    ffn_w_up = din("ffn_w_up", [DEPTH, D, 2 * DFF])
    ffn_w_down = din("ffn_w_down", [DEPTH, DFF, D])
    ffn_cw = din("ffn_cw", [DEPTH, 128, 264])
    ffn_cb = din("ffn_cb", [DEPTH, 128, 88])
    outT = nc.dram_tensor("outT", [D, S], F32, kind="ExternalOutput").ap()

    with ExitStack() as es:
        c = Ctx(nc, es)
        modts = [c.sb(f"modt{l}", [128, 96], F32) for l in range(DEPTH)]
        t_modts = [Tok(f"modt{l}") for l in range(DEPTH)]
        with c.phase("mod", "M"):
            emit_mod_own(c, cT, ada_w, ada_b, modts, t_modts)
        x_cur = xT
        conv = []
        wos = [fox_wo[0], gla_wo[0], conv_w_out[0], fox_wo[1]]
        for i in range(DEPTH):
            conv.append((dint(f"wub{i}", [NJ, 128, KC * 256], BF16), dint(f"wdb{i}", [KC, 128, NJ * 128], BF16),
                         dint(f"wob{i}", [KC, 128, KC * 128], BF16)))
        fconv = {}
        for jj in range(N_FOX):
            fwb = {k: dint(f"fwb{jj}{k}", [KC, 128, KC * 128], BF16) for k in ("q", "k", "g")}
            fvb = dint(f"fvb{jj}", [4, 128, KC * 512], BF16)
            fconv[jj] = [fwb, fvb, None]
        fconv[0][2] = emit_fox_wconv(c, "f0", fox_wq[0], fox_wk[0], fox_wv[0], fox_wg[0],
                                     fconv[0][0], fconv[0][1], lazy=False)
        c.pump_all()
        t_conv0 = emit_wconv(c, 0, ffn_w_up[0], ffn_w_down[0], conv[0][0], conv[0][1], wos[0], conv[0][2])
        conv[0] = conv[0] + (t_conv0,)
        for i in range(DEPTH):
            kind, j = i % 3, i // 3
            mod = (modts[i], t_modts[i])
            if i > 0:
                c.pump_all()
            hT_d = dint(f"hT{i}", [D, S], BF16)
            with c.phase("hprep", f"H{i}"):
                emit_hprep(c, x_cur, mod, hT_d)
            og = dint(f"og{i}", [D, S], BF16)
            x1 = dint(f"x1_{i}", [D, S], F32)
            h2x = dint(f"h2x{i}", [D, S + 2], BF16)
            x_next = outT if i == DEPTH - 1 else dint(f"x2_{i}", [D, S], F32)
            wub, wdb, wob, t_wconv = conv[i]
            if kind == 0:
                with c.phase("fox", f"A{i}"):
                    emit_fox(c, hT_d, tuple(fconv[j]), fox_wf[j], fox_bfb[j], fox_cf, og, nheads=16)
                wo = fox_wo[j]
            elif kind == 1:
                with c.phase("gla", f"A{i}"):
                    emit_gla(c, hT_d, gla_wq[j], gla_wk[j], gla_wv[j], gla_wr[j], gla_wa1[j],
                             gla_wa2[j], gla_ba[j:j + 1, :], gla_gbc, gla_cg, og, nheads=4)
                wo = gla_wo[j]
            else:
                w_in = conv_w_in[j]
                with c.phase("sconv", f"A{i}"):
                    emit_sconv(c, hT_d, w_in[:, 0:D], w_in[:, D:2 * D], w_in[:, 2 * D:3 * D],
                               conv_cw, og, nsw=2)
                wo = conv_w_out[j]
            c.pump_all()
            if i + 1 < DEPTH:
                t_n = emit_wconv(c, i + 1, ffn_w_up[i + 1], ffn_w_down[i + 1], conv[i + 1][0], conv[i + 1][1],
                                 wos[i + 1], conv[i + 1][2])
                conv[i + 1] = conv[i + 1] + (t_n,)
                if (i + 1) % 3 == 0:
                    jn = (i + 1) // 3
                    fconv[jn][2] = emit_fox_wconv(c, f"f{jn}", fox_wq[jn], fox_wk[jn], fox_wv[jn], fox_wg[jn],
                                                  fconv[jn][0], fconv[jn][1], lazy=True)
            with c.phase("oproj", f"B{i}"):
                emit_oproj(c, og, x_cur, (wob, t_wconv), mod, lnv[i][:, 0:16], lnv[i][:, 16:32], x1, h2x, ntok=S, h2off=2)
            with c.phase("ffn", f"F{i}"):
                emit_ffn(c, h2x, x1, wub, wdb, t_wconv, ffn_cw[i], ffn_cb[i], mod,
                         lnv[i][:, 32:48], lnv[i][:, 48:64], x_next, FUSED_PASSES)
            x_cur = x_next
        c.barrier()
        c.finish()
        print("fused instructions:", c.n_inst, {k: v for k, v in c.cnt.items() if v and not k.startswith("d_")},
              "sems", len(c.sem))
    return nc


_NC_CACHE = {}


def kernel(**inp):
    f32 = np.float32
    x = np.asarray(inp["x"], dtype=f32)
    cvec = np.asarray(inp["c"], dtype=f32)
    if "fused" not in _NC_CACHE:
        _NC_CACHE["fused"] = build_fused()
    nc = _NC_CACHE["fused"]
    A = lambda k: np.ascontiguousarray(np.asarray(inp[k], dtype=f32))
    lnv = np.ascontiguousarray(np.stack(
        [np.concatenate([vec_pm(inp["ln1_g"][l]), vec_pm(inp["ln1_b"][l]),
                         vec_pm(inp["ln2_g"][l]), vec_pm(inp["ln2_b"][l])], axis=1) for l in range(DEPTH)], axis=0))
    shared = dict(
        ada_w=A("ada_w"), ada_b=A("ada_b"), lnv=lnv,
        fox_wq=A("fox_wq"), fox_wk=A("fox_wk"), fox_wv=A("fox_wv"), fox_wg=A("fox_wg"), fox_wf=A("fox_wf"),
        fox_bfb=np.ascontiguousarray(np.stack([bcast_rows(inp["fox_bf"][j]) for j in range(N_FOX)], axis=0)),
        fox_wo=A("fox_wo"), fox_cf=fox_consts(),
        gla_wq=A("gla_wq"), gla_wk=A("gla_wk"), gla_wv=A("gla_wv"), gla_wr=A("gla_wr"), gla_wa1=A("gla_wa1"),
        gla_wa2=A("gla_wa2"), gla_ba=A("gla_ba"), gla_gbc=bcast_rows(inp["gla_norm_g"][0]), gla_cg=gla_consts(),
        gla_wo=A("gla_wo"), conv_w_in=A("conv_w_in"), conv_cw=vec_pm(inp["conv_w"][0]), conv_w_out=A("conv_w_out"),
        ffn_w_up=A("ffn_w_up"), ffn_w_down=A("ffn_w_down"),
        ffn_cw=np.ascontiguousarray(np.stack([vec_pm(inp["ffn_conv_w"][l]) for l in range(DEPTH)], axis=0)),
        ffn_cb=np.ascontiguousarray(np.stack([vec_pm(inp["ffn_conv_b"][l]) for l in range(DEPTH)], axis=0)),
    )
    maps = []
    for core in range(NCORES):
        b = core // 2
        m = dict(shared)
        m["xT"] = np.ascontiguousarray(x[b].T)
        m["cT"] = np.ascontiguousarray(cvec[b].reshape(KC, 128).T)
        maps.append(m)
    res = run_bass_kernel_spmd(nc, maps, core_ids=list(range(NCORES))).results
    out = np.stack([res[2 * b]["outT"].T for b in range(NB)], axis=0).astype(f32)
    return np.ascontiguousarray(out)
```

```python
import numpy as np
from contextlib import ExitStack
import ml_dtypes
import concourse.bass as bass
import concourse.mybir as mybir
from concourse.bass_utils import run_bass_kernel_spmd

F32 = mybir.dt.float32
BF16 = mybir.dt.bfloat16
AF = mybir.ActivationFunctionType
ALU = mybir.AluOpType
AX = mybir.AxisListType

D = 2048
S = 4096
NB = 4
DEPTH = 4
DFF = 5632
KC = D // 128
TOK = 2048
ALPHA = (2 * DEPTH) ** 0.25
LN_EPS = 1e-5
RMS_EPS = 1e-5
NCORES = 8
CC_INC = 16
PAIRS = [[0, 1], [2, 3], [4, 5], [6, 7]]
ALL8 = [list(range(8))]


class Tok:
    __slots__ = ("name", "w", "r")

    def __init__(self, name=""):
        self.name = name
        self.w = None
        self.r = []


class Ctx:
    def __init__(self, nc, es):
        self.nc = nc
        self.es = es
        self.eng = {"pe": nc.tensor, "act": nc.scalar, "dve": nc.vector,
                    "pool": nc.gpsimd, "sp": nc.sync}
        self.sem = {}
        self.cnt = {}
        for k in ("pe", "act", "dve", "pool"):
            self.sem[k] = es.enter_context(nc.semaphore("s_" + k))
            self.cnt[k] = 0
        self.seen = {k: {} for k in self.eng}
        self.n_inst = 0
        self.out_toks = []
        self.root_es = es
        self.prefix = ""
        self.kind = ""
        self.bg = []
        self.bg_final = {}

    def pump(self, n=1):
        while n > 0 and self.bg:
            self.bg.pop(0)()
            n -= 1

    def pump_all(self):
        self.pump(len(self.bg))
        for tok, dep in self.bg_final.items():
            tok.w = dep
        self.bg_final = {}

    def sb(self, name, shape, dt):
        return self.es.enter_context(self.nc.sbuf_tensor(self.prefix + name, list(shape), dt))

    def ps(self, name, shape, dt=F32):
        return self.es.enter_context(self.nc.psum_tensor(self.prefix + name, list(shape), dt))

    def dma_sem(self, name):
        key = "d_" + self.kind + name
        if key not in self.sem:
            self.sem[key] = self.root_es.enter_context(self.nc.semaphore(key))
            self.cnt[key] = 0
        return key

    def barrier(self):
        for e in self.eng:
            seen = self.seen[e]
            for k, cnt in self.cnt.items():
                if k.startswith("d_bg"):
                    continue
                if cnt > 0 and seen.get(k, 0) < cnt:
                    self.eng[e].wait_ge(self.sem[k], cnt)
                    seen[k] = cnt
                    self.n_inst += 1

    def phase(self, kind, inst):
        ctx = self

        class _P:
            def __enter__(self_):
                ctx.barrier()
                self_.es = ExitStack()
                self_.es.__enter__()
                self_.old = (ctx.es, ctx.prefix, ctx.kind)
                ctx.es, ctx.prefix, ctx.kind = self_.es, f"{inst}_", kind + "_"
                return ctx

            def __exit__(self_, *a):
                ctx.barrier()
                ctx.es, ctx.prefix, ctx.kind = self_.old
                return self_.es.__exit__(*a)

        return _P()

    def collective(self, kind, src, dst, groups):
        key = self.dma_sem("cc")
        ins = self.nc.gpsimd.collective_compute(kind, ALU.bypass, replica_groups=groups,
                                                ins=[src], outs=[dst])
        self.n_inst += 1
        self.cnt[key] += CC_INC
        ins.then_inc(self.sem[key], CC_INC)
        return ins

    def _need(self, e, reads, writes):
        need = {}

        def add(dep):
            if dep is None:
                return
            k, c = dep
            if need.get(k, 0) < c:
                need[k] = c

        for t in reads:
            add(t.w)
        for t in writes:
            if t.w is not None and t.w[0] != e:
                add(t.w)
            for d in t.r:
                if d[0] != e:
                    add(d)
        seen = self.seen[e]
        engine = self.eng[e]
        for k, c in need.items():
            if seen.get(k, 0) < c:
                engine.wait_ge(self.sem[k], c)
                seen[k] = c
                self.n_inst += 1

    def _done(self, key, cnt, reads, writes):
        dep = (key, cnt)
        for t in writes:
            t.w = dep
            t.r = []
        for t in reads:
            t.r.append(dep)
            if len(t.r) > 16:
                best = {}
                for k, c in t.r:
                    if best.get(k, 0) < c:
                        best[k] = c
                t.r = list(best.items())

    def op(self, e, fn, reads=(), writes=(), track=True):
        self._need(e, reads, writes)
        ins = fn(self.eng[e])
        self.n_inst += 1
        if track:
            self.cnt[e] += 1
            ins.then_inc(self.sem[e], 1)
            self._done(e, self.cnt[e], reads, writes)
        return ins

    def dma(self, q, semkey, out, in_, reads=(), writes=(), **kw):
        self._need(q, reads, writes)
        ins = self.eng[q].dma_start(out=out, in_=in_, **kw)
        self.n_inst += 1
        self.cnt[semkey] += 16
        ins.then_inc(self.sem[semkey], 16)
        self._done(semkey, self.cnt[semkey], reads, writes)
        return ins

    def finish(self):
        self._need("sp", self.out_toks, ())


class Ring:
    def __init__(self, c, name, n, shape, dt, dma=True):
        self.bufs = [c.sb(f"{name}{i}", shape, dt) for i in range(n)]
        self.toks = [Tok(f"{name}{i}") for i in range(n)]
        self.sems = [c.dma_sem(f"{name}{i}") for i in range(n)] if dma else None
        self.n = n
        self.i = -1

    def next(self):
        self.i = (self.i + 1) % self.n
        return self.cur()

    def cur(self):
        i = self.i
        return self.bufs[i], self.toks[i], (self.sems[i] if self.sems else None)


def mm_group(c, out_ap, out_tok, pairs, reads):
    n = len(pairs)
    for i, (l, r) in enumerate(pairs):
        last = i == n - 1
        c.op("pe", lambda e: e.matmul(out_ap, l, r, start=(i == 0), stop=last),
             reads=reads, writes=[out_tok], track=last)


class LNState:
    def __init__(self, c, widths=(512,)):
        ng = len(widths)
        width = max(widths)
        self.ones = c.sb("ln_ones", [128, 128], BF16)
        self.t_ones = Tok("ones")
        c.op("dve", lambda e: e.memset(self.ones[:], 1.0), writes=[self.t_ones])
        self.S1 = [c.ps(f"ln_s1_{g}", [128, 512]) for g in range(ng)]
        self.S2 = [c.ps(f"ln_s2_{g}", [128, 512]) for g in range(ng)]
        self.t_S1 = [Tok(f"s1_{g}") for g in range(ng)]
        self.t_S2 = [Tok(f"s2_{g}") for g in range(ng)]
        self.zb = Ring(c, "ln_zb", 3, [128, width], BF16, dma=False)
        self.sq = Ring(c, "ln_sq", 3, [128, width], BF16, dma=False)
        self.m = c.sb("ln_m", [128, width], F32)
        self.msq = c.sb("ln_msq", [128, width], F32)
        self.rstd = [c.sb(f"ln_rstd{g}", [128, widths[g]], F32) for g in range(ng)]
        self.nmr = [c.sb(f"ln_nmr{g}", [128, widths[g]], F32) for g in range(ng)]
        self.t_m = Tok("m")
        self.m2 = c.sb("ln_m2", [128, width], F32)
        self.t_m2 = Tok("m2")
        self.t_msq = Tok("msq")
        self.t_rstd = [Tok() for g in range(ng)]
        self.t_nmr = [Tok() for g in range(ng)]
        self.tt = Ring(c, "ln_tt", 2, [128, width], F32, dma=False)
        self.pending = []

    def accum(self, c, g, n, z_ap, z_tok, W):
        zb, t_zb, _ = self.zb.next()
        sq, t_sq, _ = self.sq.next()
        c.op("act", lambda e: e.activation(out=zb[:, 0:W], in_=z_ap, func=AF.Identity),
             reads=[z_tok], writes=[t_zb])
        c.op("act", lambda e: e.activation(out=sq[:, 0:W], in_=z_ap, func=AF.Square),
             reads=[z_tok], writes=[t_sq])
        last = n == KC - 1

        def mm():
            c.op("pe", lambda e: e.matmul(self.S1[g][:, 0:W], self.ones[:], zb[:, 0:W],
                                          start=(n == 0), stop=last),
                 reads=[self.t_ones, t_zb], writes=[self.t_S1[g]])
            c.op("pe", lambda e: e.matmul(self.S2[g][:, 0:W], self.ones[:], sq[:, 0:W],
                                          start=(n == 0), stop=last),
                 reads=[self.t_ones, t_sq], writes=[self.t_S2[g]])

        self.pending.append(mm)

    def flush(self):
        while self.pending:
            self.pending.pop(0)()

    def finalize(self, c, g, W, eps):
        self.flush()
        m, msq, rstd, nmr, m2 = self.m, self.msq, self.rstd[g], self.nmr[g], self.m2
        c.op("act", lambda e: e.activation(out=m[:, 0:W], in_=self.S1[g][:, 0:W],
                                           func=AF.Identity, scale=1.0 / D),
             reads=[self.t_S1[g]], writes=[self.t_m])
        c.op("dve", lambda e: e.tensor_tensor(out=msq[:, 0:W], in0=m[:, 0:W], in1=m[:, 0:W],
                                              op=ALU.mult),
             reads=[self.t_m], writes=[self.t_msq])
        c.op("dve", lambda e: e.scalar_tensor_tensor(out=msq[:, 0:W], in0=self.S2[g][:, 0:W],
                                                     scalar=1.0 / D, in1=msq[:, 0:W],
                                                     op0=ALU.mult, op1=ALU.subtract),
             reads=[self.t_S2[g], self.t_msq], writes=[self.t_msq])
        c.op("dve", lambda e: e.tensor_scalar(out=msq[:, 0:W], in0=msq[:, 0:W],
                                              scalar1=float(eps), scalar2=None, op0=ALU.add),
             reads=[self.t_msq], writes=[self.t_msq])
        c.op("dve", lambda e: e.reciprocal(out=m2[:, 0:W], in_=msq[:, 0:W]),
             reads=[self.t_msq], writes=[self.t_m2])
        c.op("act", lambda e: e.activation(out=rstd[:, 0:W], in_=m2[:, 0:W], func=AF.Sqrt),
             reads=[self.t_m2], writes=[self.t_rstd[g]])
        c.op("dve", lambda e: e.scalar_tensor_tensor(out=nmr[:, 0:W], in0=m[:, 0:W],
                                                     scalar=-1.0, in1=rstd[:, 0:W],
                                                     op0=ALU.mult, op1=ALU.mult),
             reads=[self.t_m, self.t_rstd[g]], writes=[self.t_nmr[g]])

    def normalize(self, c, g, W, z_ap, z_tok):
        tt, t_tt, _ = self.tt.next()
        c.op("dve", lambda e: e.tensor_tensor(out=tt[:, 0:W], in0=z_ap, in1=self.rstd[g][:, 0:W],
                                              op=ALU.mult),
             reads=[z_tok, self.t_rstd[g]], writes=[t_tt])
        c.op("pool", lambda e: e.tensor_tensor(out=tt[:, 0:W], in0=tt[:, 0:W],
                                               in1=self.nmr[g][:, 0:W], op=ALU.add),
             reads=[t_tt, self.t_nmr[g]], writes=[t_tt])
        return tt[:, 0:W], t_tt


def load_vecs(c, name, dram_ap, ncols, q="sp"):
    t = c.sb("v_" + name, [128, ncols], F32)
    tok = Tok(name)
    sem = c.dma_sem(name)
    c.dma(q, sem, t[:], dram_ap, writes=[tok])
    return t, tok


FFN_PASSES = [[(0, 510)], [(510, 510)], [(1020, 510)], [(1530, 510), (2040, 8)]]
NJ = DFF // 128


def emit_ffn(c, h2x, x1T, wub, wdb, t_wconv, cw, cb, mod, lng, lnb, x2T, FFN_PASSES):
    h2v = h2x.rearrange("(kc p) t -> p kc t", p=128)
    x1v = x1T.rearrange("(kc p) t -> p kc t", p=128)
    x2v = x2T.rearrange("(kc p) t -> p kc t", p=128)

    cwt, t_cw = load_vecs(c, "cw", cw, 3 * 88)
    cbt, t_cb = load_vecs(c, "cb", cb, 88)
    modt, t_mod = mod
    lngt, t_lng = load_vecs(c, "lng", lng, 16)
    lnbt, t_lnb = load_vecs(c, "lnb", lnb, 16)
    g2a = c.sb("g2a", [128, 16], F32)
    t_g2a = Tok("g2a")
    c.op("dve", lambda e: e.tensor_scalar(out=g2a[:], in0=modt[:, 80:96], scalar1=1.0 / ALPHA,
                                          scalar2=None, op0=ALU.mult),
         reads=[t_mod], writes=[t_g2a])

    WR = max([g[1][1] for g in FFN_PASSES if len(g) > 1] + [2])
    ln = LNState(c, widths=(510, WR))
    work = [c.ps(f"wk{i}", [128, 512]) for i in range(4)]
    t_work = [Tok(f"wk{i}") for i in range(4)]
    hx = c.sb("hx", [128, KC, 514 + WR], BF16)
    t_hx = Tok("hx")
    s_hx = c.dma_sem("hx")
    gT = c.sb("gT", [128, NJ, 510 + WR], BF16)
    t_g = [Tok(f"g{j}") for j in range(NJ)]
    xt = c.sb("xt", [128, KC, 510 + WR], F32)
    t_xt = [Tok(f"xt{n}") for n in range(KC)]
    s_xt = c.dma_sem("xt")
    s_out = c.dma_sem("out")
    wu = Ring(c, "wu", 3, [128, KC, 256], BF16)
    wd = Ring(c, "wd", 2, [128, NJ, 128], BF16)
    scr = {k: Ring(c, "scr_" + k, 2, [128, 510], F32, dma=False) for k in ("a", "u", "sa")}
    t_outdram = Tok("x2T")
    c.out_toks.append(t_outdram)

    def load_wu(j):
        buf, tok, sem = wu.next()
        c.dma("sp", sem, buf[:], wub[j].rearrange("p (kc n) -> p kc n", kc=KC), reads=[t_wconv], writes=[tok])
        return buf, tok

    def load_wd(n):
        buf, tok, sem = wd.next()
        c.dma("sp", sem, buf[:], wdb[n].rearrange("p (kc n) -> p kc n", kc=NJ), reads=[t_wconv], writes=[tok])
        return buf, tok

    def pass_cols(groups):
        cols = []
        c0 = 0
        gc0 = 0
        for (s, W) in groups:
            cols.append((s, W, c0, gc0))
            c0 += 512
            gc0 += W
        return cols

    def load_hx(cols):
        for (s, W, c0, gc0) in cols:
            c.dma("pool", s_hx, hx[:, :, c0:c0 + W + 2], h2v[:, :, s:s + W + 2], writes=[t_hx])

    PF = 2
    wk_i = 0
    load_hx(pass_cols(FFN_PASSES[0]))
    for ip, groups in enumerate(FFN_PASSES):
        cols = pass_cols(groups)
        for (s, W, c0, gc0) in cols:
            c.dma("pool", s_xt, xt[:, :, gc0:gc0 + W], x1v[:, :, s:s + W], writes=t_xt)
        pend = [load_wu(j) for j in range(min(PF, NJ))]
        for j in range(NJ):
            if j + PF < NJ:
                pend.append(load_wu(j + PF))
            if j % 2 == 0:
                c.pump(1)
            wbuf, t_w = pend.pop(0)
            for (s, W, c0, gc0) in cols:
                N = W + 2
                res = {}
                for hi, half in enumerate(("a", "u")):
                    pb = work[wk_i % 4]
                    t_pb = t_work[wk_i % 4]
                    wk_i += 1
                    mm_group(c, pb[:, 0:N], t_pb,
                             [(wbuf[:, k, hi * 128:(hi + 1) * 128], hx[:, k, c0:c0 + N])
                              for k in range(KC)], reads=[t_w, t_hx])
                    b1, tk1, _ = scr[half].next()
                    res[half] = (pb, t_pb, j + hi * NJ, b1, tk1)
                for half in ("a", "u"):
                    pb, t_pb, jj, b1, tk1 = res[half]
                    c.op("act", lambda e: e.activation(out=b1[:, 0:W], in_=pb[:, 2:N], func=AF.Identity,
                                                       bias=cbt[:, jj:jj + 1],
                                                       scale=cwt[:, 2 * 88 + jj:2 * 88 + jj + 1]),
                         reads=[t_pb, t_cw, t_cb], writes=[tk1])
                for half in ("a", "u"):
                    pb, t_pb, jj, b1, tk1 = res[half]
                    c.op("dve", lambda e: e.scalar_tensor_tensor(
                        out=b1[:, 0:W], in0=pb[:, 1:N - 1], scalar=cwt[:, 88 + jj:88 + jj + 1],
                        in1=b1[:, 0:W], op0=ALU.mult, op1=ALU.add),
                         reads=[t_pb, tk1, t_cw], writes=[tk1])
                for half in ("a", "u"):
                    pb, t_pb, jj, b1, tk1 = res[half]
                    c.op("dve", lambda e: e.scalar_tensor_tensor(
                        out=b1[:, 0:W], in0=pb[:, 0:N - 2], scalar=cwt[:, jj:jj + 1],
                        in1=b1[:, 0:W], op0=ALU.mult, op1=ALU.add),
                         reads=[t_pb, tk1, t_cw], writes=[tk1])
                sa, t_sa, _ = scr["sa"].next()
                ba, tka = res["a"][3], res["a"][4]
                bu, tku = res["u"][3], res["u"][4]
                c.op("act", lambda e: e.activation(out=sa[:, 0:W], in_=ba[:, 0:W], func=AF.Silu),
                     reads=[tka], writes=[t_sa])
                c.op("pool", lambda e: e.tensor_tensor(out=gT[:, j, gc0:gc0 + W], in0=sa[:, 0:W],
                                                       in1=bu[:, 0:W], op=ALU.mult),
                     reads=[t_sa, tku], writes=[t_g[j]])
        if ip + 1 < len(FFN_PASSES):
            load_hx(pass_cols(FFN_PASSES[ip + 1]))
        pend = [load_wd(0)]
        for n in range(KC):
            if n + 1 < KC:
                pend.append(load_wd(n + 1))
            wbuf, t_w = pend.pop(0)
            for gi, (s, W, c0, gc0) in enumerate(cols):
                pb = work[wk_i % 4]
                t_pb = t_work[wk_i % 4]
                wk_i += 1
                mm_group(c, pb[:, 0:W], t_pb,
                         [(wbuf[:, k, :], gT[:, k, gc0:gc0 + W]) for k in range(NJ)],
                         reads=[t_w] + t_g)
                ln.flush()
                c.op("dve", lambda e: e.scalar_tensor_tensor(
                    out=xt[:, n, gc0:gc0 + W], in0=pb[:, 0:W], scalar=g2a[:, n:n + 1],
                    in1=xt[:, n, gc0:gc0 + W], op0=ALU.mult, op1=ALU.add),
                     reads=[t_pb, t_g2a, t_xt[n]], writes=[t_xt[n]])
                ln.accum(c, gi, n, xt[:, n, gc0:gc0 + W], t_xt[n], W)
        for gi, (s, W, c0, gc0) in enumerate(cols):
            ln.finalize(c, gi, W, LN_EPS / (ALPHA * ALPHA))
            for n in range(KC):
                t_ap, t_tok = ln.normalize(c, gi, W, xt[:, n, gc0:gc0 + W], t_xt[n])
                c.op("act", lambda e: e.activation(out=xt[:, n, gc0:gc0 + W], in_=t_ap, func=AF.Identity,
                                                   bias=lnbt[:, n:n + 1], scale=lngt[:, n:n + 1]),
                     reads=[t_tok, t_lng, t_lnb], writes=[t_xt[n]])
            c.dma("pool", s_out, x2v[:, :, s:s + W], xt[:, :, gc0:gc0 + W], reads=t_xt,
                  writes=[t_outdram])


def emit_oproj(c, ogT, xT, wo, mod, lng, lnb, x1T, h2T, ntok=TOK, h2off=0):
    ogv = ogT.rearrange("(kc p) t -> p kc t", p=128)
    xv = xT.rearrange("(kc p) t -> p kc t", p=128)
    x1v = x1T.rearrange("(kc p) t -> p kc t", p=128)
    h2v = h2T.rearrange("(kc p) t -> p kc t", p=128)
    wob, t_wconv = wo
    modt, t_mod = mod
    lngt, t_lng = load_vecs(c, "lng", lng, 16)
    lnbt, t_lnb = load_vecs(c, "lnb", lnb, 16)
    g1a = c.sb("g1a", [128, 16], F32)
    gs = c.sb("gs", [128, 16], F32)
    bs = c.sb("bs", [128, 16], F32)
    t_g1a, t_gs, t_bs = Tok(), Tok(), Tok()
    c.op("dve", lambda e: e.tensor_scalar(out=g1a[:], in0=modt[:, 32:48], scalar1=1.0 / ALPHA,
                                          scalar2=None, op0=ALU.mult),
         reads=[t_mod], writes=[t_g1a])
    c.op("dve", lambda e: e.scalar_tensor_tensor(out=gs[:], in0=modt[:, 64:80], scalar=1.0, in1=lngt[:],
                                                 op0=ALU.add, op1=ALU.mult),
         reads=[t_mod, t_lng], writes=[t_gs])
    c.op("dve", lambda e: e.scalar_tensor_tensor(out=bs[:], in0=modt[:, 64:80], scalar=1.0, in1=lnbt[:],
                                                 op0=ALU.add, op1=ALU.mult),
         reads=[t_mod, t_lnb], writes=[t_bs])
    c.op("dve", lambda e: e.tensor_tensor(out=bs[:], in0=bs[:], in1=modt[:, 48:64], op=ALU.add),
         reads=[t_mod, t_bs], writes=[t_bs])

    ln = LNState(c, widths=(512,))
    work = [c.ps(f"wk{i}", [128, 512]) for i in range(4)]
    t_work = [Tok(f"wk{i}") for i in range(4)]
    og = Ring(c, "og", 2, [128, KC, 512], BF16)
    xr = Ring(c, "xr", 2, [128, KC, 512], F32)
    h2r = Ring(c, "h2r", 1, [128, KC, 512], BF16, dma=False)
    wor = Ring(c, "wo", 3, [128, KC, 128], BF16)
    s_o1 = c.dma_sem("o1")
    s_o2 = c.dma_sem("o2")
    t_o1, t_o2 = Tok("x1T"), Tok("h2T")
    c.out_toks += [t_o1, t_o2]
    NT = ntok // 512
    wk_i = 0
    if h2off:
        zt = c.sb("zt", [128, KC, h2off], BF16)
        t_zt = Tok()
        c.op("dve", lambda e: e.memset(zt[:], 0.0), writes=[t_zt])
        c.dma("sp", s_o2, h2v[:, :, 0:h2off], zt[:], reads=[t_zt], writes=[t_o2])

    def load_tile(t):
        ob, t_ob, s_ob = og.next()
        c.dma("sp", s_ob, ob[:], ogv[:, :, t * 512:(t + 1) * 512], writes=[t_ob])
        xb, t_xb, s_xb = xr.next()
        c.dma("sp", s_xb, xb[:], xv[:, :, t * 512:(t + 1) * 512], writes=[t_xb])
        return ob, t_ob, xb, t_xb

    def load_wo(n):
        buf, tok, sem = wor.next()
        c.dma("pool", sem, buf[:], wob[n].rearrange("p (kc n) -> p kc n", kc=KC), reads=[t_wconv], writes=[tok])
        return buf, tok

    nxt = load_tile(0)
    for t in range(NT):
        ob, t_ob, xb, t_xb = nxt
        pend = [load_wo(0), load_wo(1)]
        for n in range(KC):
            if n + 2 < KC:
                pend.append(load_wo(n + 2))
            wbuf, t_w = pend.pop(0)
            pb = work[wk_i % 4]
            t_pb = t_work[wk_i % 4]
            wk_i += 1
            mm_group(c, pb[:], t_pb, [(wbuf[:, k, :], ob[:, k, :]) for k in range(KC)],
                     reads=[t_w, t_ob])
            ln.flush()
            c.op("dve", lambda e: e.scalar_tensor_tensor(
                out=xb[:, n, :], in0=pb[:], scalar=g1a[:, n:n + 1], in1=xb[:, n, :],
                op0=ALU.mult, op1=ALU.add),
                 reads=[t_pb, t_g1a, t_xb], writes=[t_xb])
            ln.accum(c, 0, n, xb[:, n, :], t_xb, 512)
        if t + 1 < NT:
            nxt = load_tile(t + 1)
        ln.finalize(c, 0, 512, LN_EPS / (ALPHA * ALPHA))
        hb, t_hb, _ = h2r.next()
        for n in range(KC):
            t_ap, t_tok = ln.normalize(c, 0, 512, xb[:, n, :], t_xb)
            c.op("act", lambda e: e.activation(out=xb[:, n, :], in_=t_ap, func=AF.Identity,
                                               bias=lnbt[:, n:n + 1], scale=lngt[:, n:n + 1]),
                 reads=[t_tok, t_lng, t_lnb], writes=[t_xb])
            c.op("act", lambda e: e.activation(out=hb[:, n, :], in_=t_ap, func=AF.Identity,
                                               bias=bs[:, n:n + 1], scale=gs[:, n:n + 1]),
                 reads=[t_tok, t_gs, t_bs], writes=[t_hb])
        c.dma("sp", s_o1, x1v[:, :, t * 512:(t + 1) * 512], xb[:], reads=[t_xb], writes=[t_o1])
        c.dma("sp", s_o2, h2v[:, :, h2off + t * 512:h2off + (t + 1) * 512], hb[:], reads=[t_hb], writes=[t_o2])


NEG = 30000.0
HD = 128
FOX_SWEEP = 4


def emit_fox(c, hsrc, fw, wf, bfb, cf, ogT, nheads=8):
    hv = hsrc.rearrange("(kc p) t -> p kc t", p=128)
    ogv = ogT.rearrange("(n p) t -> p n t", p=128)
    fwb, fvb, t_fw = fw
    wfv = wf.rearrange("(kc p) h -> p kc h", p=128)
    NS = FOX_SWEEP
    NT = S // 512

    cft, t_cf = load_vecs(c, "cf", cf, 896)
    bft, t_bf = load_vecs(c, "bfb", bfb, nheads)
    tri, ones_f, ident, mpos = cft[:, 0:128], cft[:, 128:256], cft[:, 256:384], cft[:, 384:896]
    ones_b = c.sb("ones_b", [128, 128], BF16)
    t_ones_b = Tok()
    c.op("dve", lambda e: e.memset(ones_b[:], 1.0), writes=[t_ones_b])
    one_c = c.sb("one_c", [128, 1], F32)
    t_one_c = Tok()
    c.op("dve", lambda e: e.memset(one_c[:], 1.0), writes=[t_one_c])
    wft = c.sb("wft", [128, KC, nheads], BF16)
    t_wf = Tok()
    s_wf = c.dma_sem("wf")
    c.dma("pool", s_wf, wft[:], wfv, writes=[t_wf])

    work = [c.ps(f"wk{i}", [128, 512]) for i in range(4)]
    t_work = [Tok(f"wk{i}") for i in range(4)]
    Ob = [c.ps(f"ob{i}", [128, 512]) for i in range(2)]
    t_Ob = [Tok(f"ob{i}") for i in range(2)]
    Lb = [c.ps(f"lb{i}", [128, 512]) for i in range(2)]
    t_Lb = [Tok(f"lb{i}") for i in range(2)]
    wk_i = [0]

    def nwork():
        i = wk_i[0] % 4
        wk_i[0] += 1
        return work[i], t_work[i]

    KT = c.sb("KT", [128, NS, S], BF16)
    t_KT = [Tok(f"kt{t}") for t in range(NT)]
    V = c.sb("V", [128, S // 128, NS * HD], BF16)
    t_V = [Tok(f"v{t}") for t in range(NT)]
    Gk = c.sb("Gk", [128, S // 128, NS], F32)
    t_Gk = [Tok(f"gk{t}") for t in range(NT)]
    R = c.sb("R", [128, NS], F32)
    t_R = Tok("R")
    hT = c.sb("hT", [128, KC, 512], BF16)
    t_hT = Tok("hT")
    s_h = c.dma_sem("h")
    QT = Ring(c, "QT", 2, [128, NS, 512], BF16, dma=False)
    GT = Ring(c, "GT", 2, [128, NS, 512], BF16, dma=False)
    wr = Ring(c, "w", 4, [128, KC, 128], BF16)
    wvb = c.sb("wvb", [128, KC, NS * HD], BF16)
    t_wvb = Tok("wvb")
    s_wvb = c.dma_sem("wvb")
    PT = Ring(c, "PT", 4, [128, 512], BF16, dma=False)
    sbr = Ring(c, "sbr", 3, [128, 512], F32, dma=False)
    Dg = c.sb("Dg", [128, 512], F32)
    t_Dg = Tok()
    gqa = c.sb("gqa", [128, NS, 512], F32)
    t_gqa = [Tok(f"gqa{n}") for n in range(NS)]
    gqda = c.sb("gqda", [128, NS, 512], F32)
    t_gqda = [Tok(f"gqda{n}") for n in range(NS)]
    den = c.sb("den", [128, 512], F32)
    t_den = Tok()
    rec = c.sb("rec", [128, 512], F32)
    t_rec = Tok()
    ogb = Ring(c, "ogb", 2, [128, NS, 512], BF16, dma=False)
    zf = c.sb("zf", [128, 4, NS], F32)
    t_zf = [Tok() for _ in range(4)]
    spf = c.sb("spf", [128, 4, NS], F32)
    t_spf = [Tok() for _ in range(4)]
    s_og = c.dma_sem("og")
    t_ogd = Tok("ogT")
    c.out_toks.append(t_ogd)
    ob_i = 0

    for sw in range(nheads // NS):
        hc0 = sw * NS * HD
        c.dma("pool", s_wvb, wvb[:], fvb[sw].rearrange("p (kc n) -> p kc n", kc=KC), reads=[t_fw], writes=[t_wvb])
        c.op("dve", lambda e: e.memset(R[:], 0.0), writes=[t_R])
        if sw == 0:
            c.dma("sp", s_h, hT[:], hv[:, :, 0:512], writes=[t_hT])
        for t in range(NT):
            tsl = slice(t * 512, (t + 1) * 512)

            def proj(kind, n):
                c.pump(1)
                wb, t_wb, s_wb = wr.next()
                c.dma("pool", s_wb, wb[:], fwb[kind][sw * NS + n].rearrange("p (kc n) -> p kc n", kc=KC),
                      reads=[t_fw], writes=[t_wb])
                pb, t_pb = nwork()
                mm_group(c, pb[:], t_pb, [(wb[:, k, :], hT[:, k, :]) for k in range(KC)], reads=[t_wb, t_hT])
                return pb, t_pb

            for n in range(NS):
                pb, t_pb = proj("k", n)
                c.op("dve", lambda e: e.tensor_copy(out=KT[:, n, tsl], in_=pb[:]),
                     reads=[t_pb], writes=[t_KT[t]])
            for tb in range(4):
                blk = t * 4 + tb
                bsl = slice(tb * 128, (tb + 1) * 128)
                pb, t_pb = nwork()
                mm_group(c, pb[:], t_pb, [(hT[:, k, bsl], wvb[:, k, :]) for k in range(KC)],
                         reads=[t_wvb, t_hT])
                c.op("act", lambda e: e.activation(out=V[:, blk, :], in_=pb[:], func=AF.Identity),
                     reads=[t_pb], writes=[t_V[t]])
                pz, t_pz = nwork()
                mm_group(c, pz[:, 0:NS], t_pz, [(hT[:, k, bsl], wft[:, k, sw * NS:(sw + 1) * NS]) for k in range(KC)],
                         reads=[t_wf, t_hT])
                c.op("dve", lambda e: e.tensor_tensor(out=zf[:, tb, :], in0=pz[:, 0:NS], in1=bft[:, sw * NS:(sw + 1) * NS],
                                                      op=ALU.add), reads=[t_pz, t_bf], writes=[t_zf[tb]])
                c.op("act", lambda e: e.activation(out=spf[:, tb, :], in_=zf[:, tb, :], func=AF.Exp, scale=-1.0),
                     reads=[t_zf[tb]], writes=[t_spf[tb]])
                c.op("act", lambda e: e.activation(out=spf[:, tb, :], in_=spf[:, tb, :], func=AF.Ln, bias=one_c[:, 0:1]),
                     reads=[t_spf[tb], t_one_c], writes=[t_spf[tb]])
            qb, t_qb, _ = QT.next()
            gb, t_gb, _ = GT.next()
            for n in range(NS):
                pb, t_pb = proj("q", n)
                c.op("act", lambda e: e.activation(out=qb[:, n, :], in_=pb[:], func=AF.Identity,
                                                   scale=float(HD ** -0.5)),
                     reads=[t_pb], writes=[t_qb])
            for n in range(NS):
                pb, t_pb = proj("g", n)
                c.op("act", lambda e: e.activation(out=gb[:, n, :], in_=pb[:], func=AF.Exp, scale=-1.0),
                     reads=[t_pb], writes=[t_gb])
            if t + 1 < NT:
                c.dma("sp", s_h, hT[:], hv[:, :, (t + 1) * 512:(t + 2) * 512], writes=[t_hT])
            elif sw + 1 < nheads // NS:
                c.dma("sp", s_h, hT[:], hv[:, :, 0:512], writes=[t_hT])
            for tb in range(4):
                blk = t * 4 + tb
                pg, t_pg = nwork()
                c.op("pe", lambda e: e.matmul(pg[:, 0:NS], tri, spf[:, tb, :], start=True, stop=False),
                     reads=[t_cf, t_spf[tb]], writes=[t_pg], track=False)
                c.op("pe", lambda e: e.matmul(pg[:, 0:NS], ones_f, R[:], start=False, stop=True),
                     reads=[t_cf, t_R, t_spf[tb]], writes=[t_pg])
                c.op("dve", lambda e: e.tensor_copy(out=Gk[:, blk, :], in_=pg[:, 0:NS]),
                     reads=[t_pg], writes=[t_Gk[t]])
                c.op("dve", lambda e: e.tensor_tensor(out=R[:], in0=R[:], in1=spf[:, tb, :], op=ALU.add),
                     reads=[t_R, t_spf[tb]], writes=[t_R])
            for n in range(NS):
                for j in range(4):
                    c.op("dve", lambda e: e.tensor_scalar(out=Dg[:, j * 128:(j + 1) * 128], in0=ident,
                                                          scalar1=Gk[:, t * 4 + j, n:n + 1], scalar2=None,
                                                          op0=ALU.mult),
                         reads=[t_cf, t_Gk[t]], writes=[t_Dg])
                pq, t_pq = nwork()
                c.op("pe", lambda e: e.matmul(pq[:], ones_f, Dg[:], start=True, stop=True),
                     reads=[t_cf, t_Dg], writes=[t_pq])
                c.op("act", lambda e: e.activation(out=gqa[:, n, :], in_=pq[:], func=AF.Identity),
                     reads=[t_pq], writes=[t_gqa[n]])
                c.op("pool", lambda e: e.tensor_tensor(out=gqda[:, n, :], in0=gqa[:, n, :], in1=mpos, op=ALU.add),
                     reads=[t_gqa[n], t_cf], writes=[t_gqda[n]])
            og_b, t_og, _ = ogb.next()
            nkb = 4 * t + 4
            banks = {}
            for n in range(NS):
                banks[n] = (Ob[ob_i % 2], t_Ob[ob_i % 2], Lb[ob_i % 2], t_Lb[ob_i % 2])
                ob_i += 1
            pending = []

            def emit_scores(n, kb):
                j = kb - 4 * t
                q0 = 128 * j if j > 0 else 0
                N = 512 - q0
                gq, t_gq, gqd, t_gqd = gqa[:, n, :], t_gqa[n], gqda[:, n, :], t_gqda[n]
                ps_, t_ps = nwork()
                c.op("pe", lambda e: e.matmul(ps_[:, 0:N], KT[:, n, kb * 128:(kb + 1) * 128], qb[:, n, q0:512],
                                              start=True, stop=True),
                     reads=[t_KT[kb // 4], t_qb], writes=[t_ps])
                sb_, t_sb, _ = sbr.next()
                if j < 0:
                    c.op("dve", lambda e: e.tensor_tensor(out=sb_[:, 0:N], in0=ps_[:, 0:N], in1=gq[:, q0:512],
                                                          op=ALU.subtract),
                         reads=[t_ps, t_gq], writes=[t_sb])
                else:
                    c.op("dve", lambda e: e.tensor_tensor(out=sb_[:, 0:128], in0=ps_[:, 0:128],
                                                          in1=gqd[:, q0:q0 + 128], op=ALU.subtract),
                         reads=[t_ps, t_gqd], writes=[t_sb])
                    if N > 128:
                        c.op("dve", lambda e: e.tensor_tensor(out=sb_[:, 128:N], in0=ps_[:, 128:N],
                                                              in1=gq[:, q0 + 128:512], op=ALU.subtract),
                             reads=[t_ps, t_gq], writes=[t_sb])
                pt, t_pt, _ = PT.next()
                c.op("act", lambda e: e.activation(out=pt[:, 0:N], in_=sb_[:, 0:N], func=AF.Exp,
                                                   bias=Gk[:, kb, n:n + 1]),
                     reads=[t_sb, t_Gk[kb // 4]], writes=[t_pt])
                return (n, kb, q0, N, pt, t_pt)

            def emit_pv(item):
                n, kb, q0, N, pt, t_pt = item
                O, t_O, L, t_L = banks[n]
                last = kb == nkb - 1
                c.op("pe", lambda e: e.matmul(O[:, q0:512], V[:, kb, n * HD:(n + 1) * HD], pt[:, 0:N],
                                              start=(kb == 0), stop=last),
                     reads=[t_V[kb // 4], t_pt], writes=[t_O], track=last)
                c.op("pe", lambda e: e.matmul(L[:, q0:512], ones_b[:], pt[:, 0:N],
                                              start=(kb == 0), stop=last),
                     reads=[t_ones_b, t_pt], writes=[t_L], track=True)
                if last:
                    c.op("dve", lambda e: e.scalar_tensor_tensor(out=den[:], in0=gb[:, n, :], scalar=1.0, in1=L[:],
                                                                 op0=ALU.add, op1=ALU.mult),
                         reads=[t_gb, t_L], writes=[t_den])
                    c.op("dve", lambda e: e.reciprocal(out=rec[:], in_=den[:]), reads=[t_den], writes=[t_rec])
                    c.op("dve", lambda e: e.tensor_tensor(out=og_b[:, n, :], in0=O[:], in1=rec[:], op=ALU.mult),
                         reads=[t_O, t_rec], writes=[t_og])

            LA = 2
            for n in range(NS):
                for kb in range(nkb):
                    pending.append(emit_scores(n, kb))
                    if len(pending) > LA:
                        emit_pv(pending.pop(0))
            while pending:
                emit_pv(pending.pop(0))
            c.dma("sp", s_og, ogv[:, sw * NS:(sw + 1) * NS, tsl], og_b[:], reads=[t_og], writes=[t_ogd])


def fox_consts():
    k = np.arange(128)[:, None]
    q = np.arange(128)[None, :]
    tri = (k <= q).astype(np.float32)
    ones = np.ones((128, 128), np.float32)
    ident = np.eye(128, dtype=np.float32)
    mp = np.where(q >= k, 0.0, NEG).astype(np.float32)
    return np.ascontiguousarray(np.concatenate([tri, ones, ident, mp, mp, mp, mp], axis=1))


GLA_TAU = 16.0


def emit_gla(c, hsrc, wq, wk, wv, wr, wa1, wa2, ba, gbc, cg, ogT, nheads=2):
    hv = hsrc.rearrange("(kc p) t -> p kc t", p=128)
    ogv = ogT.rearrange("(n p) t -> p n t", p=128)
    wqv = wq.rearrange("(kc p) n -> p kc n", p=128)
    wkv = wk.rearrange("(kc p) n -> p kc n", p=128)
    wvv = wv.rearrange("(kc p) n -> p kc n", p=128)
    wrv = wr.rearrange("(kc p) n -> p kc n", p=128)
    wa1v = wa1.rearrange("(kc p) n -> p kc n", p=128)
    NT = S // 512
    DK, DV = 256, 512

    cgt, t_cg = load_vecs(c, "cg", cg, 512)
    gbt, t_gb = load_vecs(c, "gbc", gbc, 512)
    triN, triU, tri01, ident_f = cgt[:, 0:128], cgt[:, 128:256], cgt[:, 256:384], cgt[:, 384:512]
    ident_b = c.sb("ident_b", [128, 128], BF16)
    t_idb = Tok()
    c.op("dve", lambda e: e.tensor_copy(out=ident_b[:], in_=ident_f), reads=[t_cg], writes=[t_idb])
    one_c = c.sb("one_c", [128, 1], F32)
    eps_c = c.sb("eps_c", [128, 1], F32)
    t_cc = Tok()
    c.op("dve", lambda e: e.memset(one_c[:], 1.0), writes=[t_cc])
    c.op("dve", lambda e: e.memset(eps_c[:], RMS_EPS), writes=[t_cc])

    wq_b = c.sb("wq_b", [128, KC, DK], BF16)
    wk_b = c.sb("wk_b", [128, KC, DK], BF16)
    wv_b = c.sb("wv_b", [128, KC, DV], BF16)
    wr_b = c.sb("wr_b", [128, KC, DV], BF16)
    wa1_b = c.sb("wa1_b", [128, KC, 16], BF16)
    wa2a = c.sb("wa2a", [32, DK], BF16)
    t_W = Tok("W")
    s_W = c.dma_sem("W")
    c.dma("pool", s_W, wa1_b[:], wa1v, writes=[t_W])

    work = [c.ps(f"wk{i}", [128, 512]) for i in range(6)]
    t_work = [Tok(f"wk{i}") for i in range(6)]
    pcb = [c.ps(f"pc{i}", [128, 512]) for i in range(2)]
    t_pcb = [Tok(f"pc{i}") for i in range(2)]
    wk_i = [0]

    def nwork():
        i = wk_i[0] % 6
        wk_i[0] += 1
        return work[i], t_work[i]

    hT = c.sb("hT", [128, KC, 512], BF16)
    t_hT = Tok("hT")
    s_h = c.dma_sem("h")
    gaug = Ring(c, "gaug", 2, [32, 512], BF16, dma=False)
    for i in range(2):
        c.op("dve", lambda e: e.memset(gaug.bufs[i][:], 1.0), writes=[gaug.toks[i]])
    spr = Ring(c, "sp", 2, [128, 4, DK], F32, dma=False)
    ekd = Ring(c, "ekd", 2, [128, DK], F32, dma=False)
    ekr = Ring(c, "ek", 2, [128, 512], F32, dma=False)
    qdT = Ring(c, "qdT", 2, [128, 2, 512], BF16, dma=False)
    kiT = Ring(c, "kiT", 2, [128, 2, 512], BF16, dma=False)
    eq = Ring(c, "eq", 2, [128, 2, 512], F32, dma=False)
    kdec = Ring(c, "kdec", 2, [128, 4, DK], BF16, dma=False)
    Vr = Ring(c, "V", 2, [128, 4, DV], BF16, dma=False)
    rraw = Ring(c, "rraw", 2, [128, 4, DV], BF16, dma=False)
    rs = Ring(c, "rs", 2, [128, 4, DV], BF16, dma=False)
    attm = Ring(c, "attm", 2, [128, 128], BF16, dma=False)
    state = c.sb("state", [128, 2, DV], F32)
    t_state = Tok("state")
    stb = Ring(c, "stb", 2, [128, 2, DV], BF16, dma=False)
    junk = c.sb("junk", [128, DV], BF16)
    t_junk = Tok()
    ssq = Ring(c, "ssq", 2, [128, 1], F32, dma=False)
    onr = Ring(c, "on", 2, [128, DV], F32, dma=False)
    ogt = Ring(c, "ogt", 2, [128, DV], BF16, dma=False)
    ogo = Ring(c, "ogo", 2, [128, 4, 512], BF16, dma=False)
    s_og = c.dma_sem("og")
    t_ogd = Tok("ogT")
    c.out_toks.append(t_ogd)

    def stage_a(sw, t):
        tsl = slice(t * 512, (t + 1) * 512)
        c.dma("sp", s_h, hT[:], hv[:, :, tsl], writes=[t_hT])
        ga, t_ga, _ = gaug.next()
        pg, t_pg = nwork()
        mm_group(c, pg[0:16, :], t_pg, [(wa1_b[:, k, :], hT[:, k, :]) for k in range(KC)], reads=[t_W, t_hT])
        c.op("act", lambda e: e.activation(out=ga[0:16, :], in_=pg[0:16, :], func=AF.Identity),
             reads=[t_pg], writes=[t_ga])
        sp, t_sp, _ = spr.next()
        kd, t_kd, _ = kdec.next()
        Vb, t_Vb, _ = Vr.next()
        rr, t_rr, _ = rraw.next()
        rsb, t_rs, _ = rs.next()
        for tb in range(4):
            bsl = slice(tb * 128, (tb + 1) * 128)
            pz, t_pz = nwork()
            c.op("pe", lambda e: e.matmul(pz[:, 0:DK], ga[0:17, bsl], wa2a[0:17, :], start=True, stop=True),
                 reads=[t_ga, t_W], writes=[t_pz])
            c.op("act", lambda e: e.activation(out=sp[:, tb, :], in_=pz[:, 0:DK], func=AF.Exp, scale=-1.0),
                 reads=[t_pz], writes=[t_sp])
            c.op("act", lambda e: e.activation(out=sp[:, tb, :], in_=sp[:, tb, :], func=AF.Ln, bias=one_c[:, 0:1]),
                 reads=[t_sp, t_cc], writes=[t_sp])
            pd, t_pd = nwork()
            c.op("pe", lambda e: e.matmul(pd[:, 0:DK], triU, sp[:, tb, :], start=True, stop=True),
                 reads=[t_cg, t_sp], writes=[t_pd])
            ek_, t_ek_, _ = ekd.next()
            c.op("act", lambda e: e.activation(out=ek_[:], in_=pd[:, 0:DK], func=AF.Exp),
                 reads=[t_pd], writes=[t_ek_])
            pk, t_pk = nwork()
            mm_group(c, pk[:, 0:DK], t_pk, [(hT[:, k, bsl], wk_b[:, k, :]) for k in range(KC)], reads=[t_W, t_hT])
            c.op("dve", lambda e: e.tensor_tensor(out=kd[:, tb, :], in0=pk[:, 0:DK], in1=ek_[:], op=ALU.mult),
                 reads=[t_pk, t_ek_], writes=[t_kd])
            pv, t_pv = nwork()
            mm_group(c, pv[:], t_pv, [(hT[:, k, bsl], wv_b[:, k, :]) for k in range(KC)], reads=[t_W, t_hT])
            c.op("act", lambda e: e.activation(out=Vb[:, tb, :], in_=pv[:], func=AF.Identity),
                 reads=[t_pv], writes=[t_Vb])
            pr, t_pr = nwork()
            mm_group(c, pr[:], t_pr, [(hT[:, k, bsl], wr_b[:, k, :]) for k in range(KC)], reads=[t_W, t_hT])
            c.op("dve", lambda e: e.tensor_copy(out=rr[:, tb, :], in_=pr[:]), reads=[t_pr], writes=[t_rr])
            for dc in range(2):
                c.op("pe", lambda e: e.matmul(pcb[dc][:, bsl], sp[:, tb, dc * 128:(dc + 1) * 128], triN,
                                              start=True, stop=True),
                     reads=[t_cg, t_sp], writes=[t_pcb[dc]])
        c.op("act", lambda e: e.activation(out=rsb[:], in_=rr[:], func=AF.Silu), reads=[t_rr], writes=[t_rs])
        eqb, t_eq, _ = eq.next()
        qd, t_qd, _ = qdT.next()
        ki, t_ki, _ = kiT.next()
        for dc in range(2):
            c.op("act", lambda e: e.activation(out=eqb[:, dc, :], in_=pcb[dc][:], func=AF.Exp),
                 reads=[t_pcb[dc]], writes=[t_eq])
            ekb, t_ekb, _ = ekr.next()
            c.op("act", lambda e: e.activation(out=ekb[:], in_=pcb[dc][:], func=AF.Exp, scale=-1.0),
                 reads=[t_pcb[dc]], writes=[t_ekb])
            pq, t_pq = nwork()
            mm_group(c, pq[:], t_pq, [(wq_b[:, k, dc * 128:(dc + 1) * 128], hT[:, k, :]) for k in range(KC)],
                     reads=[t_W, t_hT])
            c.op("dve", lambda e: e.scalar_tensor_tensor(out=qd[:, dc, :], in0=pq[:], scalar=float(DK ** -0.5),
                                                         in1=eqb[:, dc, :], op0=ALU.mult, op1=ALU.mult),
                 reads=[t_pq, t_eq], writes=[t_qd])
            pk2, t_pk2 = nwork()
            mm_group(c, pk2[:], t_pk2, [(wk_b[:, k, dc * 128:(dc + 1) * 128], hT[:, k, :]) for k in range(KC)],
                     reads=[t_W, t_hT])
            c.op("dve", lambda e: e.tensor_tensor(out=ki[:, dc, :], in0=pk2[:], in1=ekb[:], op=ALU.mult),
                 reads=[t_pk2, t_ekb], writes=[t_ki])
        return dict(qd=qd, t_qd=t_qd, ki=ki, t_ki=t_ki, eq=eqb, t_eq=t_eq, kd=kd, t_kd=t_kd,
                    V=Vb, t_V=t_Vb, rs=rsb, t_rs=t_rs)

    def stage_b(sw, t, A):
        tsl = slice(t * 512, (t + 1) * 512)
        oo, t_oo, _ = ogo.next()
        for tb in range(4):
            bsl = slice(tb * 128, (tb + 1) * 128)
            pa, t_pa = nwork()
            mm_group(c, pa[:, 0:128], t_pa, [(A["ki"][:, dc, bsl], A["qd"][:, dc, bsl]) for dc in range(2)],
                     reads=[A["t_ki"], A["t_qd"]])
            am, t_am, _ = attm.next()
            c.op("dve", lambda e: e.tensor_tensor(out=am[:], in0=pa[:, 0:128], in1=tri01, op=ALU.mult),
                 reads=[t_pa, t_cg], writes=[t_am])
            sb_old, t_sb_old, _ = stb.cur()
            po, t_po = nwork()
            c.op("pe", lambda e: e.matmul(po[:], am[:], A["V"][:, tb, :], start=True, stop=False),
                 reads=[t_am, A["t_V"]], writes=[t_po], track=False)
            for dc in range(2):
                c.op("pe", lambda e: e.matmul(po[:], A["qd"][:, dc, bsl], sb_old[:, dc, :], start=False, stop=(dc == 1)),
                     reads=[A["t_qd"], t_sb_old, t_am, A["t_V"]], writes=[t_po], track=(dc == 1))
            sb_new, t_sb_new, _ = stb.next()
            for dc in range(2):
                pkv, t_pkv = nwork()
                c.op("pe", lambda e: e.matmul(pkv[:], A["kd"][:, tb, dc * 128:(dc + 1) * 128], A["V"][:, tb, :],
                                              start=True, stop=True),
                     reads=[A["t_kd"], A["t_V"]], writes=[t_pkv])
                col = tb * 128 + 127
                c.op("dve", lambda e: e.scalar_tensor_tensor(out=state[:, dc, :], in0=state[:, dc, :],
                                                             scalar=A["eq"][:, dc, col:col + 1], in1=pkv[:],
                                                             op0=ALU.mult, op1=ALU.add),
                     reads=[t_state, A["t_eq"], t_pkv], writes=[t_state])
                c.op("act", lambda e: e.activation(out=sb_new[:, dc, :], in_=state[:, dc, :], func=AF.Identity),
                     reads=[t_state], writes=[t_sb_new])
            sq, t_sq, _ = ssq.next()
            c.op("act", lambda e: e.activation(out=junk[:], in_=po[:], func=AF.Square, accum_out=sq[:]),
                 reads=[t_po], writes=[t_junk, t_sq])
            c.op("act", lambda e: e.activation(out=sq[:], in_=sq[:], func=AF.Ln, scale=1.0 / DV, bias=eps_c[:, 0:1]),
                 reads=[t_sq, t_cc], writes=[t_sq])
            c.op("act", lambda e: e.activation(out=sq[:], in_=sq[:], func=AF.Exp, scale=-0.5),
                 reads=[t_sq], writes=[t_sq])
            on, t_on, _ = onr.next()
            c.op("dve", lambda e: e.scalar_tensor_tensor(out=on[:], in0=po[:], scalar=sq[:, 0:1], in1=gbt[:],
                                                         op0=ALU.mult, op1=ALU.mult),
                 reads=[t_po, t_sq, t_gb], writes=[t_on])
            og_, t_og_, _ = ogt.next()
            c.op("pool", lambda e: e.tensor_tensor(out=og_[:], in0=on[:], in1=A["rs"][:, tb, :], op=ALU.mult),
                 reads=[t_on, A["t_rs"]], writes=[t_og_])
            pt, t_pt = nwork()
            for fc in range(4):
                c.op("pe", lambda e: e.matmul(pt[:, fc * 128:(fc + 1) * 128], og_[:, fc * 128:(fc + 1) * 128],
                                              ident_b[:], start=True, stop=True),
                     reads=[t_og_, t_idb], writes=[t_pt], track=(fc == 3))
            c.op("dve", lambda e: e.tensor_copy(out=oo[:, :, bsl], in_=pt[:].rearrange("p (f t) -> p f t", f=4)),
                 reads=[t_pt], writes=[t_oo])
        c.dma("sp", s_og, ogv[:, sw * 4:(sw + 1) * 4, tsl], oo[:], reads=[t_oo], writes=[t_ogd])

    for sw in range(nheads):
        c.dma("pool", s_W, wq_b[:], wqv[:, :, sw * DK:(sw + 1) * DK], writes=[t_W])
        c.dma("pool", s_W, wk_b[:], wkv[:, :, sw * DK:(sw + 1) * DK], writes=[t_W])
        c.dma("pool", s_W, wv_b[:], wvv[:, :, sw * DV:(sw + 1) * DV], writes=[t_W])
        c.dma("pool", s_W, wr_b[:], wrv[:, :, sw * DV:(sw + 1) * DV], writes=[t_W])
        c.dma("pool", s_W, wa2a[0:16, :], wa2[:, sw * DK:(sw + 1) * DK], writes=[t_W])
        c.dma("pool", s_W, wa2a[16:17, :], ba[:, sw * DK:(sw + 1) * DK], writes=[t_W])
        c.op("dve", lambda e: e.memset(state[:], 0.0), writes=[t_state])
        sb0, t_sb0, _ = stb.next()
        c.op("dve", lambda e: e.memset(sb0[:], 0.0), writes=[t_sb0])
        prev = stage_a(sw, 0)
        for t in range(NT):
            nxt = stage_a(sw, t + 1) if t + 1 < NT else None
            stage_b(sw, t, prev)
            prev = nxt


def gla_consts():
    a = np.arange(128)[:, None]
    b = np.arange(128)[None, :]
    triN = np.where(a <= b, -1.0 / GLA_TAU, 0.0).astype(np.float32)
    triU = np.where(a > b, -1.0 / GLA_TAU, 0.0).astype(np.float32)
    tri01 = (a <= b).astype(np.float32)
    ident = np.eye(128, dtype=np.float32)
    return np.ascontiguousarray(np.concatenate([triN, triU, tri01, ident], axis=1))


def emit_sconv(c, hsrc, wb, wc, wu, cw, ogT, nsw=1):
    hv = hsrc.rearrange("(kc p) t -> p kc t", p=128)
    ogv = ogT.rearrange("(n p) t -> p n t", p=128)
    NT = S // 512
    NCH = 8
    NCHT = NCH * nsw
    cwt, t_cw = load_vecs(c, "cw", cw, 3 * NCHT)
    W = {}
    t_W = Tok("W")
    s_W = c.dma_sem("W")
    for name in ("b", "c", "u"):
        W[name] = c.sb("w_" + name, [128, KC, 1024], BF16)
    work = [c.ps(f"wk{i}", [128, 512]) for i in range(6)]
    t_work = [Tok(f"wk{i}") for i in range(6)]
    wk_i = [0]

    def nwork():
        i = wk_i[0] % 6
        wk_i[0] += 1
        return work[i], t_work[i]

    hT = Ring(c, "hT", 2, [128, KC, 512], BF16)
    pext = c.sb("pext", [128, NCH, 514], F32)
    t_pext = [Tok(f"pe{n}") for n in range(NCH)]
    usb = Ring(c, "usb", 2, [128, 512], F32, dma=False)
    yb = Ring(c, "yb", 2, [128, 512], F32, dma=False)
    ogo = Ring(c, "ogo", 2, [128, NCH, 512], BF16, dma=False)
    s_og = c.dma_sem("og")
    t_ogd = Tok("ogT")
    c.out_toks.append(t_ogd)
    for sw in range(nsw):
        for name, ap in (("b", wb), ("c", wc), ("u", wu)):
            v = ap.rearrange("(kc p) n -> p kc n", p=128)
            for q in range(4):
                c.dma("pool", s_W, W[name][:, :, q * 256:(q + 1) * 256],
                      v[:, :, sw * 1024 + q * 256:sw * 1024 + (q + 1) * 256], writes=[t_W])
        c.op("dve", lambda e: e.memset(pext[:], 0.0), writes=t_pext)
        for t in range(NT):
            tsl = slice(t * 512, (t + 1) * 512)
            hb, t_hb, s_hb = hT.next()
            c.dma("sp", s_hb, hb[:], hv[:, :, tsl], writes=[t_hb])
            oo, t_oo, _ = ogo.next()
            for n in range(NCH):
                nsl = slice(n * 128, (n + 1) * 128)
                cn = sw * NCH + n
                pu, t_pu = nwork()
                mm_group(c, pu[:], t_pu, [(W["u"][:, k, nsl], hb[:, k, :]) for k in range(KC)], reads=[t_W, t_hb])
                ub, t_ub, _ = usb.next()
                c.op("act", lambda e: e.activation(out=ub[:], in_=pu[:], func=AF.Identity), reads=[t_pu], writes=[t_ub])
                pc_, t_pc = nwork()
                mm_group(c, pc_[:], t_pc, [(W["c"][:, k, nsl], hb[:, k, :]) for k in range(KC)], reads=[t_W, t_hb])
                if t > 0:
                    c.op("pool", lambda e: e.tensor_copy(out=pext[:, n, 0:2], in_=pext[:, n, 512:514]),
                         reads=[t_pext[n]], writes=[t_pext[n]])
                c.op("dve", lambda e: e.tensor_tensor(out=pext[:, n, 2:514], in0=pc_[:], in1=ub[:], op=ALU.mult),
                     reads=[t_pc, t_ub, t_pext[n]], writes=[t_pext[n]])
                y, t_y, _ = yb.next()
                c.op("act", lambda e: e.activation(out=y[:], in_=pext[:, n, 2:514], func=AF.Identity,
                                                   scale=cwt[:, 2 * NCHT + cn:2 * NCHT + cn + 1]),
                     reads=[t_pext[n], t_cw], writes=[t_y])
                c.op("dve", lambda e: e.scalar_tensor_tensor(out=y[:], in0=pext[:, n, 1:513],
                                                             scalar=cwt[:, NCHT + cn:NCHT + cn + 1], in1=y[:],
                                                             op0=ALU.mult, op1=ALU.add),
                     reads=[t_pext[n], t_cw, t_y], writes=[t_y])
                c.op("dve", lambda e: e.scalar_tensor_tensor(out=y[:], in0=pext[:, n, 0:512],
                                                             scalar=cwt[:, cn:cn + 1], in1=y[:],
                                                             op0=ALU.mult, op1=ALU.add),
                     reads=[t_pext[n], t_cw, t_y], writes=[t_y])
                pb, t_pb = nwork()
                mm_group(c, pb[:], t_pb, [(W["b"][:, k, nsl], hb[:, k, :]) for k in range(KC)], reads=[t_W, t_hb])
                c.op("dve", lambda e: e.tensor_tensor(out=oo[:, n, :], in0=pb[:], in1=y[:], op=ALU.mult),
                     reads=[t_pb, t_y], writes=[t_oo])
            c.dma("sp", s_og, ogv[:, sw * NCH:(sw + 1) * NCH, tsl], oo[:], reads=[t_oo], writes=[t_ogd])


def vec_pm(v):
    v = np.asarray(v, dtype=np.float32)
    lead = v.shape[:-1]
    n = v.shape[-1] // 128
    v = v.reshape(-1, n, 128)
    return np.ascontiguousarray(v.transpose(2, 0, 1).reshape(128, -1))


def bcast_rows(v, n=128):
    v = np.asarray(v, dtype=np.float32)
    return np.ascontiguousarray(np.broadcast_to(v[None, :], (n, v.shape[0])))


def emit_mod_own(c, cT, ada_w, ada_b, modts, t_modts):
    ct, t_ct = load_vecs(c, "cT", cT, KC)
    cond = c.sb("cond", [128, KC], BF16)
    t_cond = Tok()
    c.op("act", lambda e: e.activation(out=cond[:], in_=ct[:], func=AF.Silu), reads=[t_ct], writes=[t_cond])
    one11 = c.sb("one11", [1, 1], F32)
    t_one = Tok()
    c.op("dve", lambda e: e.memset(one11[:], 1.0), writes=[t_one])
    awr = Ring(c, "aw", 3, [128, KC, 512], BF16)
    row = c.sb("row", [1, 6 * D], F32)
    t_row = Tok()
    abt = c.sb("abt", [1, 6 * D], F32)
    t_ab = Tok()
    s_ab = c.dma_sem("ab")
    work = [c.ps(f"wk{i}", [128, 512]) for i in range(4)]
    t_work = [Tok(f"wk{i}") for i in range(4)]
    pm = c.ps("pm", [128, 96])
    t_pm = Tok()
    i = 0
    NCC = 6 * D // 512
    for l in range(DEPTH):
        awv = ada_w[l].rearrange("(kc p) n -> p kc n", p=128)
        c.dma("sp", s_ab, abt[:], ada_b[l:l + 1, :], writes=[t_ab])
        for cc in range(NCC):
            csl = slice(cc * 512, (cc + 1) * 512)
            wb, t_wb, s_wb = awr.next()
            c.dma("pool", s_wb, wb[:], awv[:, :, csl], writes=[t_wb])
            pb, t_pb = work[i % 4], t_work[i % 4]
            i += 1
            mm_group(c, pb[0:1, :], t_pb, [(cond[:, k:k + 1], wb[:, k, :]) for k in range(KC)],
                     reads=[t_cond, t_wb])
            c.op("dve", lambda e: e.tensor_tensor(out=row[0:1, csl], in0=pb[0:1, :], in1=abt[0:1, csl], op=ALU.add),
                 reads=[t_pb, t_ab], writes=[t_row])
        for m in range(96):
            c.op("pe", lambda e: e.matmul(pm[:, m:m + 1], row[0:1, m * 128:(m + 1) * 128], one11[0:1, 0:1],
                                          start=True, stop=True),
                 reads=[t_row, t_one], writes=[t_pm], track=(m == 95))
        c.op("dve", lambda e: e.tensor_copy(out=modts[l][:], in_=pm[:]), reads=[t_pm], writes=[t_modts[l]])


FUSED_PASSES = [[(510 * i, 510)] for i in range(7)] + [[(3570, 510), (4080, 16)]]


def emit_hprep(c, xT, mod, hT_dram):
    xv = xT.rearrange("(kc p) t -> p kc t", p=128)
    hv = hT_dram.rearrange("(kc p) t -> p kc t", p=128)
    modt, t_mod = mod
    sc1p = c.sb("sc1p", [128, 16], F32)
    t_sc1p = Tok()
    c.op("dve", lambda e: e.tensor_scalar(out=sc1p[:], in0=modt[:, 16:32], scalar1=1.0, scalar2=None,
                                          op0=ALU.add), reads=[t_mod], writes=[t_sc1p])
    xp = Ring(c, "xp", 3, [128, 4, 512], F32)
    hb = Ring(c, "hb", 2, [128, KC, 512], BF16)
    t_out = Tok()
    for t in range(S // 512):
        tsl = slice(t * 512, (t + 1) * 512)
        h, t_h, s_h = hb.next()
        for q4 in range(4):
            xb, t_xb, s_xb = xp.next()
            c.dma("sp" if q4 % 2 == 0 else "pool", s_xb, xb[:], xv[:, q4 * 4:(q4 + 1) * 4, tsl], writes=[t_xb])
            for kk in range(4):
                kc = q4 * 4 + kk
                c.op("act", lambda e: e.activation(out=h[:, kc, :], in_=xb[:, kk, :], func=AF.Identity,
                                                   bias=modt[:, kc:kc + 1], scale=sc1p[:, kc:kc + 1]),
                     reads=[t_xb, t_mod, t_sc1p], writes=[t_h])
        c.dma("sp", s_h, hv[:, :, tsl], h[:], reads=[t_h], writes=[t_out])


def emit_fox_wconv(c, tag, wq, wk, wv, wg, fwb, fvb, lazy):
    old = c.kind
    c.kind = ""
    key = c.dma_sem(f"bg{tag}")
    c.kind = old
    tok = Tok(f"fconv{tag}")

    def q(dst, src_):
        fn = lambda: c.dma("pool", key, dst, src_, writes=[Tok()], reads=[]) and None
        if lazy:
            c.bg.append(fn)
        else:
            fn()

    for kind, w in (("k", wk), ("q", wq), ("g", wg)):
        v = w.rearrange("(kc p) n -> p kc n", p=128)
        for n in range(KC):
            q(fwb[kind][n].rearrange("p (kc n) -> p kc n", kc=KC), v[:, :, n * 128:(n + 1) * 128])
    v = wv.rearrange("(kc p) n -> p kc n", p=128)
    for sw in range(4):
        q(fvb[sw].rearrange("p (kc n) -> p kc n", kc=KC), v[:, :, sw * 512:(sw + 1) * 512])
    c.bg_final[tok] = (key, 52 * 16)
    return tok


def emit_wconv(c, l, w_up, w_down, wub, wdb, wo, wob):
    old = c.kind
    c.kind = ""
    key = c.dma_sem(f"bg{l}")
    c.kind = old
    tok = Tok(f"wconv{l}")
    wuv = w_up.rearrange("(kc p) n -> p kc n", p=128)
    wdv = w_down.rearrange("(kc p) n -> p kc n", p=128)
    wov = wo.rearrange("(kc p) n -> p kc n", p=128)

    def q(dst, src_):
        c.bg.append(lambda: c.dma("pool", key, dst, src_, writes=[Tok()]))

    c.bg_final[tok] = (key, 120 * 16)
    for n in range(KC):
        q(wob[n].rearrange("p (kc n) -> p kc n", kc=KC), wov[:, :, n * 128:(n + 1) * 128])
    for j in range(NJ):
        dst = wub[j].rearrange("p (kc n) -> p kc n", kc=KC)
        for h in range(2):
            q(dst[:, :, h * 128:(h + 1) * 128], wuv[:, :, h * DFF + j * 128:h * DFF + (j + 1) * 128])
    for n in range(KC):
        q(wdb[n].rearrange("p (kc n) -> p kc n", kc=NJ), wdv[:, :, n * 128:(n + 1) * 128])
    return tok


N_FOX = 2


def build_fused():
    nc = bass.Bass("TRN2", target_bir_lowering=False)

    def din(name, shape, dt=F32):
        return nc.dram_tensor(name, list(shape), dt, kind="ExternalInput").ap()

    def dint(name, shape, dt):
        return nc.dram_tensor(name, list(shape), dt).ap()

    xT = din("xT", [D, S])
    cT = din("cT", [128, KC])
    ada_w = din("ada_w", [DEPTH, D, 6 * D])
    ada_b = din("ada_b", [DEPTH, 6 * D])
    lnv = din("lnv", [DEPTH, 128, 64])
    fox_wq = din("fox_wq", [N_FOX, D, D])
    fox_wk = din("fox_wk", [N_FOX, D, D])
    fox_wv = din("fox_wv", [N_FOX, D, D])
    fox_wg = din("fox_wg", [N_FOX, D, D])
    fox_wf = din("fox_wf", [N_FOX, D, 16])
    fox_bfb = din("fox_bfb", [N_FOX, 128, 16])
    fox_wo = din("fox_wo", [N_FOX, D, D])
    fox_cf = din("fox_cf", [128, 896])
    gla_wq = din("gla_wq", [1, D, 1024])
    gla_wk = din("gla_wk", [1, D, 1024])
    gla_wv = din("gla_wv", [1, D, D])
    gla_wr = din("gla_wr", [1, D, D])
    gla_wa1 = din("gla_wa1", [1, D, 16])
    gla_wa2 = din("gla_wa2", [1, 16, 1024])
    gla_ba = din("gla_ba", [1, 1024])
    gla_gbc = din("gla_gbc", [128, 512])
    gla_cg = din("gla_cg", [128, 512])
    gla_wo = din("gla_wo", [1, D, D])
    conv_w_in = din("conv_w_in", [1, D, 3 * D])
    conv_cw = din("conv_cw", [128, 48])
    conv_w_out = din("conv_w_out", [1, D, D])
    ffn_w_up = din("ffn_w_up", [DEPTH, D, 2 * DFF])
    ffn_w_down = din("ffn_w_down", [DEPTH, DFF, D])
    ffn_cw = din("ffn_cw", [DEPTH, 128, 264])
    ffn_cb = din("ffn_cb", [DEPTH, 128, 88])
    outT = nc.dram_tensor("outT", [D, S], F32, kind="ExternalOutput").ap()

    with ExitStack() as es:
        c = Ctx(nc, es)
        modts = [c.sb(f"modt{l}", [128, 96], F32) for l in range(DEPTH)]
        t_modts = [Tok(f"modt{l}") for l in range(DEPTH)]
        with c.phase("mod", "M"):
            emit_mod_own(c, cT, ada_w, ada_b, modts, t_modts)
        x_cur = xT
        conv = []
        wos = [fox_wo[0], gla_wo[0], conv_w_out[0], fox_wo[1]]
        for i in range(DEPTH):
            conv.append((dint(f"wub{i}", [NJ, 128, KC * 256], BF16), dint(f"wdb{i}", [KC, 128, NJ * 128], BF16),
                         dint(f"wob{i}", [KC, 128, KC * 128], BF16)))
        fconv = {}
        for jj in range(N_FOX):
            fwb = {k: dint(f"fwb{jj}{k}", [KC, 128, KC * 128], BF16) for k in ("q", "k", "g")}
            fvb = dint(f"fvb{jj}", [4, 128, KC * 512], BF16)
            fconv[jj] = [fwb, fvb, None]
        fconv[0][2] = emit_fox_wconv(c, "f0", fox_wq[0], fox_wk[0], fox_wv[0], fox_wg[0],
                                     fconv[0][0], fconv[0][1], lazy=False)
        c.pump_all()
        t_conv0 = emit_wconv(c, 0, ffn_w_up[0], ffn_w_down[0], conv[0][0], conv[0][1], wos[0], conv[0][2])
        conv[0] = conv[0] + (t_conv0,)
        for i in range(DEPTH):
            kind, j = i % 3, i // 3
            mod = (modts[i], t_modts[i])
            if i > 0:
                c.pump_all()
            hT_d = dint(f"hT{i}", [D, S], BF16)
            with c.phase("hprep", f"H{i}"):
                emit_hprep(c, x_cur, mod, hT_d)
            og = dint(f"og{i}", [D, S], BF16)
            x1 = dint(f"x1_{i}", [D, S], F32)
            h2x = dint(f"h2x{i}", [D, S + 2], BF16)
            x_next = outT if i == DEPTH - 1 else dint(f"x2_{i}", [D, S], F32)
            wub, wdb, wob, t_wconv = conv[i]
            if kind == 0:
                with c.phase("fox", f"A{i}"):
                    emit_fox(c, hT_d, tuple(fconv[j]), fox_wf[j], fox_bfb[j], fox_cf, og, nheads=16)
                wo = fox_wo[j]
            elif kind == 1:
                with c.phase("gla", f"A{i}"):
                    emit_gla(c, hT_d, gla_wq[j], gla_wk[j], gla_wv[j], gla_wr[j], gla_wa1[j],
                             gla_wa2[j], gla_ba[j:j + 1, :], gla_gbc, gla_cg, og, nheads=4)
                wo = gla_wo[j]
            else:
                w_in = conv_w_in[j]
                with c.phase("sconv", f"A{i}"):
                    emit_sconv(c, hT_d, w_in[:, 0:D], w_in[:, D:2 * D], w_in[:, 2 * D:3 * D],
                               conv_cw, og, nsw=2)
                wo = conv_w_out[j]
            c.pump_all()
            if i + 1 < DEPTH:
                t_n = emit_wconv(c, i + 1, ffn_w_up[i + 1], ffn_w_down[i + 1], conv[i + 1][0], conv[i + 1][1],
                                 wos[i + 1], conv[i + 1][2])
                conv[i + 1] = conv[i + 1] + (t_n,)
                if (i + 1) % 3 == 0:
                    jn = (i + 1) // 3
                    fconv[jn][2] = emit_fox_wconv(c, f"f{jn}", fox_wq[jn], fox_wk[jn], fox_wv[jn], fox_wg[jn],
                                                  fconv[jn][0], fconv[jn][1], lazy=True)
            with c.phase("oproj", f"B{i}"):
                emit_oproj(c, og, x_cur, (wob, t_wconv), mod, lnv[i][:, 0:16], lnv[i][:, 16:32], x1, h2x, ntok=S, h2off=2)
            with c.phase("ffn", f"F{i}"):
                emit_ffn(c, h2x, x1, wub, wdb, t_wconv, ffn_cw[i], ffn_cb[i], mod,
                         lnv[i][:, 32:48], lnv[i][:, 48:64], x_next, FUSED_PASSES)
            x_cur = x_next
        c.barrier()
        c.finish()
        print("fused instructions:", c.n_inst, {k: v for k, v in c.cnt.items() if v and not k.startswith("d_")},
              "sems", len(c.sem))
    return nc


_NC_CACHE = {}


def kernel(**inp):
    f32 = np.float32
    x = np.asarray(inp["x"], dtype=f32)
    cvec = np.asarray(inp["c"], dtype=f32)
    if "fused" not in _NC_CACHE:
        _NC_CACHE["fused"] = build_fused()
    nc = _NC_CACHE["fused"]
    A = lambda k: np.ascontiguousarray(np.asarray(inp[k], dtype=f32))
    lnv = np.ascontiguousarray(np.stack(
        [np.concatenate([vec_pm(inp["ln1_g"][l]), vec_pm(inp["ln1_b"][l]),
                         vec_pm(inp["ln2_g"][l]), vec_pm(inp["ln2_b"][l])], axis=1) for l in range(DEPTH)], axis=0))
    shared = dict(
        ada_w=A("ada_w"), ada_b=A("ada_b"), lnv=lnv,
        fox_wq=A("fox_wq"), fox_wk=A("fox_wk"), fox_wv=A("fox_wv"), fox_wg=A("fox_wg"), fox_wf=A("fox_wf"),
        fox_bfb=np.ascontiguousarray(np.stack([bcast_rows(inp["fox_bf"][j]) for j in range(N_FOX)], axis=0)),
        fox_wo=A("fox_wo"), fox_cf=fox_consts(),
        gla_wq=A("gla_wq"), gla_wk=A("gla_wk"), gla_wv=A("gla_wv"), gla_wr=A("gla_wr"), gla_wa1=A("gla_wa1"),
        gla_wa2=A("gla_wa2"), gla_ba=A("gla_ba"), gla_gbc=bcast_rows(inp["gla_norm_g"][0]), gla_cg=gla_consts(),
        gla_wo=A("gla_wo"), conv_w_in=A("conv_w_in"), conv_cw=vec_pm(inp["conv_w"][0]), conv_w_out=A("conv_w_out"),
        ffn_w_up=A("ffn_w_up"), ffn_w_down=A("ffn_w_down"),
        ffn_cw=np.ascontiguousarray(np.stack([vec_pm(inp["ffn_conv_w"][l]) for l in range(DEPTH)], axis=0)),
        ffn_cb=np.ascontiguousarray(np.stack([vec_pm(inp["ffn_conv_b"][l]) for l in range(DEPTH)], axis=0)),
    )
    maps = []
    for core in range(NCORES):
        b = core // 2
        m = dict(shared)
        m["xT"] = np.ascontiguousarray(x[b].T)
        m["cT"] = np.ascontiguousarray(cvec[b].reshape(KC, 128).T)
        maps.append(m)
    res = run_bass_kernel_spmd(nc, maps, core_ids=list(range(NCORES))).results
    out = np.stack([res[2 * b]["outT"].T for b in range(NB)], axis=0).astype(f32)
    return np.ascontiguousarray(out)
```

```python
import numpy as np
from contextlib import ExitStack
import ml_dtypes
import concourse.bass as bass
import concourse.mybir as mybir
from concourse.bass_utils import run_bass_kernel_spmd

F32 = mybir.dt.float32
BF16 = mybir.dt.bfloat16
AF = mybir.ActivationFunctionType
ALU = mybir.AluOpType
AX = mybir.AxisListType

D = 2048
S = 4096
NB = 4
DEPTH = 4
DFF = 5632
KC = D // 128
TOK = 2048
ALPHA = (2 * DEPTH) ** 0.25
LN_EPS = 1e-5
RMS_EPS = 1e-5
NCORES = 8
CC_INC = 16
PAIRS = [[0, 1], [2, 3], [4, 5], [6, 7]]
ALL8 = [list(range(8))]


class Tok:
    __slots__ = ("name", "w", "r")

    def __init__(self, name=""):
        self.name = name
        self.w = None
        self.r = []


class Ctx:
    def __init__(self, nc, es):
        self.nc = nc
        self.es = es
        self.eng = {"pe": nc.tensor, "act": nc.scalar, "dve": nc.vector,
                    "pool": nc.gpsimd, "sp": nc.sync}
        self.sem = {}
        self.cnt = {}
        for k in ("pe", "act", "dve", "pool"):
            self.sem[k] = es.enter_context(nc.semaphore("s_" + k))
            self.cnt[k] = 0
        self.seen = {k: {} for k in self.eng}
        self.n_inst = 0
        self.out_toks = []
        self.root_es = es
        self.prefix = ""
        self.kind = ""
        self.bg = []
        self.bg_final = {}

    def pump(self, n=1):
        while n > 0 and self.bg:
            self.bg.pop(0)()
            n -= 1

    def pump_all(self):
        self.pump(len(self.bg))
        for tok, dep in self.bg_final.items():
            tok.w = dep
        self.bg_final = {}

    def sb(self, name, shape, dt):
        return self.es.enter_context(self.nc.sbuf_tensor(self.prefix + name, list(shape), dt))

    def ps(self, name, shape, dt=F32):
        return self.es.enter_context(self.nc.psum_tensor(self.prefix + name, list(shape), dt))

    def dma_sem(self, name):
        key = "d_" + self.kind + name
        if key not in self.sem:
            self.sem[key] = self.root_es.enter_context(self.nc.semaphore(key))
            self.cnt[key] = 0
        return key

    def barrier(self):
        for e in self.eng:
            seen = self.seen[e]
            for k, cnt in self.cnt.items():
                if k.startswith("d_bg"):
                    continue
                if cnt > 0 and seen.get(k, 0) < cnt:
                    self.eng[e].wait_ge(self.sem[k], cnt)
                    seen[k] = cnt
                    self.n_inst += 1

    def phase(self, kind, inst):
        ctx = self

        class _P:
            def __enter__(self_):
                ctx.barrier()
                self_.es = ExitStack()
                self_.es.__enter__()
                self_.old = (ctx.es, ctx.prefix, ctx.kind)
                ctx.es, ctx.prefix, ctx.kind = self_.es, f"{inst}_", kind + "_"
                return ctx

            def __exit__(self_, *a):
                ctx.barrier()
                ctx.es, ctx.prefix, ctx.kind = self_.old
                return self_.es.__exit__(*a)

        return _P()

    def collective(self, kind, src, dst, groups):
        key = self.dma_sem("cc")
        ins = self.nc.gpsimd.collective_compute(kind, ALU.bypass, replica_groups=groups,
                                                ins=[src], outs=[dst])
        self.n_inst += 1
        self.cnt[key] += CC_INC
        ins.then_inc(self.sem[key], CC_INC)
        return ins

    def _need(self, e, reads, writes):
        need = {}

        def add(dep):
            if dep is None:
                return
            k, c = dep
            if need.get(k, 0) < c:
                need[k] = c

        for t in reads:
            add(t.w)
        for t in writes:
            if t.w is not None and t.w[0] != e:
                add(t.w)
            for d in t.r:
                if d[0] != e:
                    add(d)
        seen = self.seen[e]
        engine = self.eng[e]
        for k, c in need.items():
            if seen.get(k, 0) < c:
                engine.wait_ge(self.sem[k], c)
                seen[k] = c
                self.n_inst += 1

    def _done(self, key, cnt, reads, writes):
        dep = (key, cnt)
        for t in writes:
            t.w = dep
            t.r = []
        for t in reads:
            t.r.append(dep)
            if len(t.r) > 16:
                best = {}
                for k, c in t.r:
                    if best.get(k, 0) < c:
                        best[k] = c
                t.r = list(best.items())

    def op(self, e, fn, reads=(), writes=(), track=True):
        self._need(e, reads, writes)
        ins = fn(self.eng[e])
        self.n_inst += 1
        if track:
            self.cnt[e] += 1
            ins.then_inc(self.sem[e], 1)
            self._done(e, self.cnt[e], reads, writes)
        return ins

    def dma(self, q, semkey, out, in_, reads=(), writes=(), **kw):
        self._need(q, reads, writes)
        ins = self.eng[q].dma_start(out=out, in_=in_, **kw)
        self.n_inst += 1
        self.cnt[semkey] += 16
        ins.then_inc(self.sem[semkey], 16)
        self._done(semkey, self.cnt[semkey], reads, writes)
        return ins

    def finish(self):
        self._need("sp", self.out_toks, ())


class Ring:
    def __init__(self, c, name, n, shape, dt, dma=True):
        self.bufs = [c.sb(f"{name}{i}", shape, dt) for i in range(n)]
        self.toks = [Tok(f"{name}{i}") for i in range(n)]
        self.sems = [c.dma_sem(f"{name}{i}") for i in range(n)] if dma else None
        self.n = n
        self.i = -1

    def next(self):
        self.i = (self.i + 1) % self.n
        return self.cur()

    def cur(self):
        i = self.i
        return self.bufs[i], self.toks[i], (self.sems[i] if self.sems else None)


def mm_group(c, out_ap, out_tok, pairs, reads):
    n = len(pairs)
    for i, (l, r) in enumerate(pairs):
        last = i == n - 1
        c.op("pe", lambda e: e.matmul(out_ap, l, r, start=(i == 0), stop=last),
             reads=reads, writes=[out_tok], track=last)


class LNState:
    def __init__(self, c, widths=(512,)):
        ng = len(widths)
        width = max(widths)
        self.ones = c.sb("ln_ones", [128, 128], BF16)
        self.t_ones = Tok("ones")
        c.op("dve", lambda e: e.memset(self.ones[:], 1.0), writes=[self.t_ones])
        self.S1 = [c.ps(f"ln_s1_{g}", [128, 512]) for g in range(ng)]
        self.S2 = [c.ps(f"ln_s2_{g}", [128, 512]) for g in range(ng)]
        self.t_S1 = [Tok(f"s1_{g}") for g in range(ng)]
        self.t_S2 = [Tok(f"s2_{g}") for g in range(ng)]
        self.zb = Ring(c, "ln_zb", 3, [128, width], BF16, dma=False)
        self.sq = Ring(c, "ln_sq", 3, [128, width], BF16, dma=False)
        self.m = c.sb("ln_m", [128, width], F32)
        self.msq = c.sb("ln_msq", [128, width], F32)
        self.rstd = [c.sb(f"ln_rstd{g}", [128, widths[g]], F32) for g in range(ng)]
        self.nmr = [c.sb(f"ln_nmr{g}", [128, widths[g]], F32) for g in range(ng)]
        self.t_m = Tok("m")
        self.m2 = c.sb("ln_m2", [128, width], F32)
        self.t_m2 = Tok("m2")
        self.t_msq = Tok("msq")
        self.t_rstd = [Tok() for g in range(ng)]
        self.t_nmr = [Tok() for g in range(ng)]
        self.tt = Ring(c, "ln_tt", 2, [128, width], F32, dma=False)
        self.pending = []

    def accum(self, c, g, n, z_ap, z_tok, W):
        zb, t_zb, _ = self.zb.next()
        sq, t_sq, _ = self.sq.next()
        c.op("act", lambda e: e.activation(out=zb[:, 0:W], in_=z_ap, func=AF.Identity),
             reads=[z_tok], writes=[t_zb])
        c.op("act", lambda e: e.activation(out=sq[:, 0:W], in_=z_ap, func=AF.Square),
             reads=[z_tok], writes=[t_sq])
        last = n == KC - 1

        def mm():
            c.op("pe", lambda e: e.matmul(self.S1[g][:, 0:W], self.ones[:], zb[:, 0:W],
                                          start=(n == 0), stop=last),
                 reads=[self.t_ones, t_zb], writes=[self.t_S1[g]])
            c.op("pe", lambda e: e.matmul(self.S2[g][:, 0:W], self.ones[:], sq[:, 0:W],
                                          start=(n == 0), stop=last),
                 reads=[self.t_ones, t_sq], writes=[self.t_S2[g]])

        self.pending.append(mm)

    def flush(self):
        while self.pending:
            self.pending.pop(0)()

    def finalize(self, c, g, W, eps):
        self.flush()
        m, msq, rstd, nmr, m2 = self.m, self.msq, self.rstd[g], self.nmr[g], self.m2
        c.op("act", lambda e: e.activation(out=m[:, 0:W], in_=self.S1[g][:, 0:W],
                                           func=AF.Identity, scale=1.0 / D),
             reads=[self.t_S1[g]], writes=[self.t_m])
        c.op("dve", lambda e: e.tensor_tensor(out=msq[:, 0:W], in0=m[:, 0:W], in1=m[:, 0:W],
                                              op=ALU.mult),
             reads=[self.t_m], writes=[self.t_msq])
        c.op("dve", lambda e: e.scalar_tensor_tensor(out=msq[:, 0:W], in0=self.S2[g][:, 0:W],
                                                     scalar=1.0 / D, in1=msq[:, 0:W],
                                                     op0=ALU.mult, op1=ALU.subtract),
             reads=[self.t_S2[g], self.t_msq], writes=[self.t_msq])
        c.op("dve", lambda e: e.tensor_scalar(out=msq[:, 0:W], in0=msq[:, 0:W],
                                              scalar1=float(eps), scalar2=None, op0=ALU.add),
             reads=[self.t_msq], writes=[self.t_msq])
        c.op("dve", lambda e: e.reciprocal(out=m2[:, 0:W], in_=msq[:, 0:W]),
             reads=[self.t_msq], writes=[self.t_m2])
        c.op("act", lambda e: e.activation(out=rstd[:, 0:W], in_=m2[:, 0:W], func=AF.Sqrt),
             reads=[self.t_m2], writes=[self.t_rstd[g]])
        c.op("dve", lambda e: e.scalar_tensor_tensor(out=nmr[:, 0:W], in0=m[:, 0:W],
                                                     scalar=-1.0, in1=rstd[:, 0:W],
                                                     op0=ALU.mult, op1=ALU.mult),
             reads=[self.t_m, self.t_rstd[g]], writes=[self.t_nmr[g]])

    def normalize(self, c, g, W, z_ap, z_tok):
        tt, t_tt, _ = self.tt.next()
        c.op("dve", lambda e: e.tensor_tensor(out=tt[:, 0:W], in0=z_ap, in1=self.rstd[g][:, 0:W],
                                              op=ALU.mult),
             reads=[z_tok, self.t_rstd[g]], writes=[t_tt])
        c.op("pool", lambda e: e.tensor_tensor(out=tt[:, 0:W], in0=tt[:, 0:W],
                                               in1=self.nmr[g][:, 0:W], op=ALU.add),
             reads=[t_tt, self.t_nmr[g]], writes=[t_tt])
        return tt[:, 0:W], t_tt


def load_vecs(c, name, dram_ap, ncols, q="sp"):
    t = c.sb("v_" + name, [128, ncols], F32)
    tok = Tok(name)
    sem = c.dma_sem(name)
    c.dma(q, sem, t[:], dram_ap, writes=[tok])
    return t, tok


FFN_PASSES = [[(0, 510)], [(510, 510)], [(1020, 510)], [(1530, 510), (2040, 8)]]
NJ = DFF // 128


def emit_ffn(c, h2x, x1T, wub, wdb, t_wconv, cw, cb, mod, lng, lnb, x2T, FFN_PASSES):
    h2v = h2x.rearrange("(kc p) t -> p kc t", p=128)
    x1v = x1T.rearrange("(kc p) t -> p kc t", p=128)
    x2v = x2T.rearrange("(kc p) t -> p kc t", p=128)

    cwt, t_cw = load_vecs(c, "cw", cw, 3 * 88)
    cbt, t_cb = load_vecs(c, "cb", cb, 88)
    modt, t_mod = mod
    lngt, t_lng = load_vecs(c, "lng", lng, 16)
    lnbt, t_lnb = load_vecs(c, "lnb", lnb, 16)
    g2a = c.sb("g2a", [128, 16], F32)
    t_g2a = Tok("g2a")
    c.op("dve", lambda e: e.tensor_scalar(out=g2a[:], in0=modt[:, 80:96], scalar1=1.0 / ALPHA,
                                          scalar2=None, op0=ALU.mult),
         reads=[t_mod], writes=[t_g2a])

    WR = max([g[1][1] for g in FFN_PASSES if len(g) > 1] + [2])
    ln = LNState(c, widths=(510, WR))
    work = [c.ps(f"wk{i}", [128, 512]) for i in range(4)]
    t_work = [Tok(f"wk{i}") for i in range(4)]
    hx = c.sb("hx", [128, KC, 514 + WR], BF16)
    t_hx = Tok("hx")
    s_hx = c.dma_sem("hx")
    gT = c.sb("gT", [128, NJ, 510 + WR], BF16)
    t_g = [Tok(f"g{j}") for j in range(NJ)]
    xt = c.sb("xt", [128, KC, 510 + WR], F32)
    t_xt = [Tok(f"xt{n}") for n in range(KC)]
    s_xt = c.dma_sem("xt")
    s_out = c.dma_sem("out")
    wu = Ring(c, "wu", 3, [128, KC, 256], BF16)
    wd = Ring(c, "wd", 2, [128, NJ, 128], BF16)
    scr = {k: Ring(c, "scr_" + k, 2, [128, 510], F32, dma=False) for k in ("a", "u", "sa")}
    t_outdram = Tok("x2T")
    c.out_toks.append(t_outdram)

    def load_wu(j):
        buf, tok, sem = wu.next()
        c.dma("sp", sem, buf[:], wub[j].rearrange("p (kc n) -> p kc n", kc=KC), reads=[t_wconv], writes=[tok])
        return buf, tok

    def load_wd(n):
        buf, tok, sem = wd.next()
        c.dma("sp", sem, buf[:], wdb[n].rearrange("p (kc n) -> p kc n", kc=NJ), reads=[t_wconv], writes=[tok])
        return buf, tok

    def pass_cols(groups):
        cols = []
        c0 = 0
        gc0 = 0
        for (s, W) in groups:
            cols.append((s, W, c0, gc0))
            c0 += 512
            gc0 += W
        return cols

    def load_hx(cols):
        for (s, W, c0, gc0) in cols:
            c.dma("pool", s_hx, hx[:, :, c0:c0 + W + 2], h2v[:, :, s:s + W + 2], writes=[t_hx])

    PF = 2
    wk_i = 0
    load_hx(pass_cols(FFN_PASSES[0]))
    for ip, groups in enumerate(FFN_PASSES):
        cols = pass_cols(groups)
        for (s, W, c0, gc0) in cols:
            c.dma("pool", s_xt, xt[:, :, gc0:gc0 + W], x1v[:, :, s:s + W], writes=t_xt)
        pend = [load_wu(j) for j in range(min(PF, NJ))]
        for j in range(NJ):
            if j + PF < NJ:
                pend.append(load_wu(j + PF))
            if j % 2 == 0:
                c.pump(1)
            wbuf, t_w = pend.pop(0)
            for (s, W, c0, gc0) in cols:
                N = W + 2
                res = {}
                for hi, half in enumerate(("a", "u")):
                    pb = work[wk_i % 4]
                    t_pb = t_work[wk_i % 4]
                    wk_i += 1
                    mm_group(c, pb[:, 0:N], t_pb,
                             [(wbuf[:, k, hi * 128:(hi + 1) * 128], hx[:, k, c0:c0 + N])
                              for k in range(KC)], reads=[t_w, t_hx])
                    b1, tk1, _ = scr[half].next()
                    res[half] = (pb, t_pb, j + hi * NJ, b1, tk1)
                for half in ("a", "u"):
                    pb, t_pb, jj, b1, tk1 = res[half]
                    c.op("act", lambda e: e.activation(out=b1[:, 0:W], in_=pb[:, 2:N], func=AF.Identity,
                                                       bias=cbt[:, jj:jj + 1],
                                                       scale=cwt[:, 2 * 88 + jj:2 * 88 + jj + 1]),
                         reads=[t_pb, t_cw, t_cb], writes=[tk1])
                for half in ("a", "u"):
                    pb, t_pb, jj, b1, tk1 = res[half]
                    c.op("dve", lambda e: e.scalar_tensor_tensor(
                        out=b1[:, 0:W], in0=pb[:, 1:N - 1], scalar=cwt[:, 88 + jj:88 + jj + 1],
                        in1=b1[:, 0:W], op0=ALU.mult, op1=ALU.add),
                         reads=[t_pb, tk1, t_cw], writes=[tk1])
                for half in ("a", "u"):
                    pb, t_pb, jj, b1, tk1 = res[half]
                    c.op("dve", lambda e: e.scalar_tensor_tensor(
                        out=b1[:, 0:W], in0=pb[:, 0:N - 2], scalar=cwt[:, jj:jj + 1],
                        in1=b1[:, 0:W], op0=ALU.mult, op1=ALU.add),
                         reads=[t_pb, tk1, t_cw], writes=[tk1])
                sa, t_sa, _ = scr["sa"].next()
                ba, tka = res["a"][3], res["a"][4]
                bu, tku = res["u"][3], res["u"][4]
                c.op("act", lambda e: e.activation(out=sa[:, 0:W], in_=ba[:, 0:W], func=AF.Silu),
                     reads=[tka], writes=[t_sa])
                c.op("pool", lambda e: e.tensor_tensor(out=gT[:, j, gc0:gc0 + W], in0=sa[:, 0:W],
                                                       in1=bu[:, 0:W], op=ALU.mult),
                     reads=[t_sa, tku], writes=[t_g[j]])
        if ip + 1 < len(FFN_PASSES):
            load_hx(pass_cols(FFN_PASSES[ip + 1]))
        pend = [load_wd(0)]
        for n in range(KC):
            if n + 1 < KC:
                pend.append(load_wd(n + 1))
            wbuf, t_w = pend.pop(0)
            for gi, (s, W, c0, gc0) in enumerate(cols):
                pb = work[wk_i % 4]
                t_pb = t_work[wk_i % 4]
                wk_i += 1
                mm_group(c, pb[:, 0:W], t_pb,
                         [(wbuf[:, k, :], gT[:, k, gc0:gc0 + W]) for k in range(NJ)],
                         reads=[t_w] + t_g)
                ln.flush()
                c.op("dve", lambda e: e.scalar_tensor_tensor(
                    out=xt[:, n, gc0:gc0 + W], in0=pb[:, 0:W], scalar=g2a[:, n:n + 1],
                    in1=xt[:, n, gc0:gc0 + W], op0=ALU.mult, op1=ALU.add),
                     reads=[t_pb, t_g2a, t_xt[n]], writes=[t_xt[n]])
                ln.accum(c, gi, n, xt[:, n, gc0:gc0 + W], t_xt[n], W)
        for gi, (s, W, c0, gc0) in enumerate(cols):
            ln.finalize(c, gi, W, LN_EPS / (ALPHA * ALPHA))
            for n in range(KC):
                t_ap, t_tok = ln.normalize(c, gi, W, xt[:, n, gc0:gc0 + W], t_xt[n])
                c.op("act", lambda e: e.activation(out=xt[:, n, gc0:gc0 + W], in_=t_ap, func=AF.Identity,
                                                   bias=lnbt[:, n:n + 1], scale=lngt[:, n:n + 1]),
                     reads=[t_tok, t_lng, t_lnb], writes=[t_xt[n]])
            c.dma("pool", s_out, x2v[:, :, s:s + W], xt[:, :, gc0:gc0 + W], reads=t_xt,
                  writes=[t_outdram])


def emit_oproj(c, ogT, xT, wo, mod, lng, lnb, x1T, h2T, ntok=TOK, h2off=0):
    ogv = ogT.rearrange("(kc p) t -> p kc t", p=128)
    xv = xT.rearrange("(kc p) t -> p kc t", p=128)
    x1v = x1T.rearrange("(kc p) t -> p kc t", p=128)
    h2v = h2T.rearrange("(kc p) t -> p kc t", p=128)
    wob, t_wconv = wo
    modt, t_mod = mod
    lngt, t_lng = load_vecs(c, "lng", lng, 16)
    lnbt, t_lnb = load_vecs(c, "lnb", lnb, 16)
    g1a = c.sb("g1a", [128, 16], F32)
    gs = c.sb("gs", [128, 16], F32)
    bs = c.sb("bs", [128, 16], F32)
    t_g1a, t_gs, t_bs = Tok(), Tok(), Tok()
    c.op("dve", lambda e: e.tensor_scalar(out=g1a[:], in0=modt[:, 32:48], scalar1=1.0 / ALPHA,
                                          scalar2=None, op0=ALU.mult),
         reads=[t_mod], writes=[t_g1a])
    c.op("dve", lambda e: e.scalar_tensor_tensor(out=gs[:], in0=modt[:, 64:80], scalar=1.0, in1=lngt[:],
                                                 op0=ALU.add, op1=ALU.mult),
         reads=[t_mod, t_lng], writes=[t_gs])
    c.op("dve", lambda e: e.scalar_tensor_tensor(out=bs[:], in0=modt[:, 64:80], scalar=1.0, in1=lnbt[:],
                                                 op0=ALU.add, op1=ALU.mult),
         reads=[t_mod, t_lnb], writes=[t_bs])
    c.op("dve", lambda e: e.tensor_tensor(out=bs[:], in0=bs[:], in1=modt[:, 48:64], op=ALU.add),
         reads=[t_mod, t_bs], writes=[t_bs])

    ln = LNState(c, widths=(512,))
    work = [c.ps(f"wk{i}", [128, 512]) for i in range(4)]
    t_work = [Tok(f"wk{i}") for i in range(4)]
    og = Ring(c, "og", 2, [128, KC, 512], BF16)
    xr = Ring(c, "xr", 2, [128, KC, 512], F32)
    h2r = Ring(c, "h2r", 1, [128, KC, 512], BF16, dma=False)
    wor = Ring(c, "wo", 3, [128, KC, 128], BF16)
    s_o1 = c.dma_sem("o1")
    s_o2 = c.dma_sem("o2")
    t_o1, t_o2 = Tok("x1T"), Tok("h2T")
    c.out_toks += [t_o1, t_o2]
    NT = ntok // 512
    wk_i = 0
    if h2off:
        zt = c.sb("zt", [128, KC, h2off], BF16)
        t_zt = Tok()
        c.op("dve", lambda e: e.memset(zt[:], 0.0), writes=[t_zt])
        c.dma("sp", s_o2, h2v[:, :, 0:h2off], zt[:], reads=[t_zt], writes=[t_o2])

    def load_tile(t):
        ob, t_ob, s_ob = og.next()
        c.dma("sp", s_ob, ob[:], ogv[:, :, t * 512:(t + 1) * 512], writes=[t_ob])
        xb, t_xb, s_xb = xr.next()
        c.dma("sp", s_xb, xb[:], xv[:, :, t * 512:(t + 1) * 512], writes=[t_xb])
        return ob, t_ob, xb, t_xb

    def load_wo(n):
        buf, tok, sem = wor.next()
        c.dma("pool", sem, buf[:], wob[n].rearrange("p (kc n) -> p kc n", kc=KC), reads=[t_wconv], writes=[tok])
        return buf, tok

    nxt = load_tile(0)
    for t in range(NT):
        ob, t_ob, xb, t_xb = nxt
        pend = [load_wo(0), load_wo(1)]
        for n in range(KC):
            if n + 2 < KC:
                pend.append(load_wo(n + 2))
            wbuf, t_w = pend.pop(0)
            pb = work[wk_i % 4]
            t_pb = t_work[wk_i % 4]
            wk_i += 1
            mm_group(c, pb[:], t_pb, [(wbuf[:, k, :], ob[:, k, :]) for k in range(KC)],
                     reads=[t_w, t_ob])
            ln.flush()
            c.op("dve", lambda e: e.scalar_tensor_tensor(
                out=xb[:, n, :], in0=pb[:], scalar=g1a[:, n:n + 1], in1=xb[:, n, :],
                op0=ALU.mult, op1=ALU.add),
                 reads=[t_pb, t_g1a, t_xb], writes=[t_xb])
            ln.accum(c, 0, n, xb[:, n, :], t_xb, 512)
        if t + 1 < NT:
            nxt = load_tile(t + 1)
        ln.finalize(c, 0, 512, LN_EPS / (ALPHA * ALPHA))
        hb, t_hb, _ = h2r.next()
        for n in range(KC):
            t_ap, t_tok = ln.normalize(c, 0, 512, xb[:, n, :], t_xb)
            c.op("act", lambda e: e.activation(out=xb[:, n, :], in_=t_ap, func=AF.Identity,
                                               bias=lnbt[:, n:n + 1], scale=lngt[:, n:n + 1]),
                 reads=[t_tok, t_lng, t_lnb], writes=[t_xb])
            c.op("act", lambda e: e.activation(out=hb[:, n, :], in_=t_ap, func=AF.Identity,
                                               bias=bs[:, n:n + 1], scale=gs[:, n:n + 1]),
                 reads=[t_tok, t_gs, t_bs], writes=[t_hb])
        c.dma("sp", s_o1, x1v[:, :, t * 512:(t + 1) * 512], xb[:], reads=[t_xb], writes=[t_o1])
        c.dma("sp", s_o2, h2v[:, :, h2off + t * 512:h2off + (t + 1) * 512], hb[:], reads=[t_hb], writes=[t_o2])


NEG = 30000.0
HD = 128
FOX_SWEEP = 4


def emit_fox(c, hsrc, fw, wf, bfb, cf, ogT, nheads=8):
    hv = hsrc.rearrange("(kc p) t -> p kc t", p=128)
    ogv = ogT.rearrange("(n p) t -> p n t", p=128)
    fwb, fvb, t_fw = fw
    wfv = wf.rearrange("(kc p) h -> p kc h", p=128)
    NS = FOX_SWEEP
    NT = S // 512

    cft, t_cf = load_vecs(c, "cf", cf, 896)
    bft, t_bf = load_vecs(c, "bfb", bfb, nheads)
    tri, ones_f, ident, mpos = cft[:, 0:128], cft[:, 128:256], cft[:, 256:384], cft[:, 384:896]
    ones_b = c.sb("ones_b", [128, 128], BF16)
    t_ones_b = Tok()
    c.op("dve", lambda e: e.memset(ones_b[:], 1.0), writes=[t_ones_b])
    one_c = c.sb("one_c", [128, 1], F32)
    t_one_c = Tok()
    c.op("dve", lambda e: e.memset(one_c[:], 1.0), writes=[t_one_c])
    wft = c.sb("wft", [128, KC, nheads], BF16)
    t_wf = Tok()
    s_wf = c.dma_sem("wf")
    c.dma("pool", s_wf, wft[:], wfv, writes=[t_wf])

    work = [c.ps(f"wk{i}", [128, 512]) for i in range(4)]
    t_work = [Tok(f"wk{i}") for i in range(4)]
    Ob = [c.ps(f"ob{i}", [128, 512]) for i in range(2)]
    t_Ob = [Tok(f"ob{i}") for i in range(2)]
    Lb = [c.ps(f"lb{i}", [128, 512]) for i in range(2)]
    t_Lb = [Tok(f"lb{i}") for i in range(2)]
    wk_i = [0]

    def nwork():
        i = wk_i[0] % 4
        wk_i[0] += 1
        return work[i], t_work[i]

    KT = c.sb("KT", [128, NS, S], BF16)
    t_KT = [Tok(f"kt{t}") for t in range(NT)]
    V = c.sb("V", [128, S // 128, NS * HD], BF16)
    t_V = [Tok(f"v{t}") for t in range(NT)]
    Gk = c.sb("Gk", [128, S // 128, NS], F32)
    t_Gk = [Tok(f"gk{t}") for t in range(NT)]
    R = c.sb("R", [128, NS], F32)
    t_R = Tok("R")
    hT = c.sb("hT", [128, KC, 512], BF16)
    t_hT = Tok("hT")
    s_h = c.dma_sem("h")
    QT = Ring(c, "QT", 2, [128, NS, 512], BF16, dma=False)
    GT = Ring(c, "GT", 2, [128, NS, 512], BF16, dma=False)
    wr = Ring(c, "w", 4, [128, KC, 128], BF16)
    wvb = c.sb("wvb", [128, KC, NS * HD], BF16)
    t_wvb = Tok("wvb")
    s_wvb = c.dma_sem("wvb")
    PT = Ring(c, "PT", 4, [128, 512], BF16, dma=False)
    sbr = Ring(c, "sbr", 3, [128, 512], F32, dma=False)
    Dg = c.sb("Dg", [128, 512], F32)
    t_Dg = Tok()
    gqa = c.sb("gqa", [128, NS, 512], F32)
    t_gqa = [Tok(f"gqa{n}") for n in range(NS)]
    gqda = c.sb("gqda", [128, NS, 512], F32)
    t_gqda = [Tok(f"gqda{n}") for n in range(NS)]
    den = c.sb("den", [128, 512], F32)
    t_den = Tok()
    rec = c.sb("rec", [128, 512], F32)
    t_rec = Tok()
    ogb = Ring(c, "ogb", 2, [128, NS, 512], BF16, dma=False)
    zf = c.sb("zf", [128, 4, NS], F32)
    t_zf = [Tok() for _ in range(4)]
    spf = c.sb("spf", [128, 4, NS], F32)
    t_spf = [Tok() for _ in range(4)]
    s_og = c.dma_sem("og")
    t_ogd = Tok("ogT")
    c.out_toks.append(t_ogd)
    ob_i = 0

    for sw in range(nheads // NS):
        hc0 = sw * NS * HD
        c.dma("pool", s_wvb, wvb[:], fvb[sw].rearrange("p (kc n) -> p kc n", kc=KC), reads=[t_fw], writes=[t_wvb])
        c.op("dve", lambda e: e.memset(R[:], 0.0), writes=[t_R])
        if sw == 0:
            c.dma("sp", s_h, hT[:], hv[:, :, 0:512], writes=[t_hT])
        for t in range(NT):
            tsl = slice(t * 512, (t + 1) * 512)

            def proj(kind, n):
                c.pump(1)
                wb, t_wb, s_wb = wr.next()
                c.dma("pool", s_wb, wb[:], fwb[kind][sw * NS + n].rearrange("p (kc n) -> p kc n", kc=KC),
                      reads=[t_fw], writes=[t_wb])
                pb, t_pb = nwork()
                mm_group(c, pb[:], t_pb, [(wb[:, k, :], hT[:, k, :]) for k in range(KC)], reads=[t_wb, t_hT])
                return pb, t_pb

            for n in range(NS):
                pb, t_pb = proj("k", n)
                c.op("dve", lambda e: e.tensor_copy(out=KT[:, n, tsl], in_=pb[:]),
                     reads=[t_pb], writes=[t_KT[t]])
            for tb in range(4):
                blk = t * 4 + tb
                bsl = slice(tb * 128, (tb + 1) * 128)
                pb, t_pb = nwork()
                mm_group(c, pb[:], t_pb, [(hT[:, k, bsl], wvb[:, k, :]) for k in range(KC)],
                         reads=[t_wvb, t_hT])
                c.op("act", lambda e: e.activation(out=V[:, blk, :], in_=pb[:], func=AF.Identity),
                     reads=[t_pb], writes=[t_V[t]])
                pz, t_pz = nwork()
                mm_group(c, pz[:, 0:NS], t_pz, [(hT[:, k, bsl], wft[:, k, sw * NS:(sw + 1) * NS]) for k in range(KC)],
                         reads=[t_wf, t_hT])
                c.op("dve", lambda e: e.tensor_tensor(out=zf[:, tb, :], in0=pz[:, 0:NS], in1=bft[:, sw * NS:(sw + 1) * NS],
                                                      op=ALU.add), reads=[t_pz, t_bf], writes=[t_zf[tb]])
                c.op("act", lambda e: e.activation(out=spf[:, tb, :], in_=zf[:, tb, :], func=AF.Exp, scale=-1.0),
                     reads=[t_zf[tb]], writes=[t_spf[tb]])
                c.op("act", lambda e: e.activation(out=spf[:, tb, :], in_=spf[:, tb, :], func=AF.Ln, bias=one_c[:, 0:1]),
                     reads=[t_spf[tb], t_one_c], writes=[t_spf[tb]])
            qb, t_qb, _ = QT.next()
            gb, t_gb, _ = GT.next()
            for n in range(NS):
                pb, t_pb = proj("q", n)
                c.op("act", lambda e: e.activation(out=qb[:, n, :], in_=pb[:], func=AF.Identity,
                                                   scale=float(HD ** -0.5)),
                     reads=[t_pb], writes=[t_qb])
            for n in range(NS):
                pb, t_pb = proj("g", n)
                c.op("act", lambda e: e.activation(out=gb[:, n, :], in_=pb[:], func=AF.Exp, scale=-1.0),
                     reads=[t_pb], writes=[t_gb])
            if t + 1 < NT:
                c.dma("sp", s_h, hT[:], hv[:, :, (t + 1) * 512:(t + 2) * 512], writes=[t_hT])
            elif sw + 1 < nheads // NS:
                c.dma("sp", s_h, hT[:], hv[:, :, 0:512], writes=[t_hT])
            for tb in range(4):
                blk = t * 4 + tb
                pg, t_pg = nwork()
                c.op("pe", lambda e: e.matmul(pg[:, 0:NS], tri, spf[:, tb, :], start=True, stop=False),
                     reads=[t_cf, t_spf[tb]], writes=[t_pg], track=False)
                c.op("pe", lambda e: e.matmul(pg[:, 0:NS], ones_f, R[:], start=False, stop=True),
                     reads=[t_cf, t_R, t_spf[tb]], writes=[t_pg])
                c.op("dve", lambda e: e.tensor_copy(out=Gk[:, blk, :], in_=pg[:, 0:NS]),
                     reads=[t_pg], writes=[t_Gk[t]])
                c.op("dve", lambda e: e.tensor_tensor(out=R[:], in0=R[:], in1=spf[:, tb, :], op=ALU.add),
                     reads=[t_R, t_spf[tb]], writes=[t_R])
            for n in range(NS):
                for j in range(4):
                    c.op("dve", lambda e: e.tensor_scalar(out=Dg[:, j * 128:(j + 1) * 128], in0=ident,
                                                          scalar1=Gk[:, t * 4 + j, n:n + 1], scalar2=None,
                                                          op0=ALU.mult),
                         reads=[t_cf, t_Gk[t]], writes=[t_Dg])
                pq, t_pq = nwork()
                c.op("pe", lambda e: e.matmul(pq[:], ones_f, Dg[:], start=True, stop=True),
                     reads=[t_cf, t_Dg], writes=[t_pq])
                c.op("act", lambda e: e.activation(out=gqa[:, n, :], in_=pq[:], func=AF.Identity),
                     reads=[t_pq], writes=[t_gqa[n]])
                c.op("pool", lambda e: e.tensor_tensor(out=gqda[:, n, :], in0=gqa[:, n, :], in1=mpos, op=ALU.add),
                     reads=[t_gqa[n], t_cf], writes=[t_gqda[n]])
            og_b, t_og, _ = ogb.next()
            nkb = 4 * t + 4
            banks = {}
            for n in range(NS):
                banks[n] = (Ob[ob_i % 2], t_Ob[ob_i % 2], Lb[ob_i % 2], t_Lb[ob_i % 2])
                ob_i += 1
            pending = []

            def emit_scores(n, kb):
                j = kb - 4 * t
                q0 = 128 * j if j > 0 else 0
                N = 512 - q0
                gq, t_gq, gqd, t_gqd = gqa[:, n, :], t_gqa[n], gqda[:, n, :], t_gqda[n]
                ps_, t_ps = nwork()
                c.op("pe", lambda e: e.matmul(ps_[:, 0:N], KT[:, n, kb * 128:(kb + 1) * 128], qb[:, n, q0:512],
                                              start=True, stop=True),
                     reads=[t_KT[kb // 4], t_qb], writes=[t_ps])
                sb_, t_sb, _ = sbr.next()
                if j < 0:
                    c.op("dve", lambda e: e.tensor_tensor(out=sb_[:, 0:N], in0=ps_[:, 0:N], in1=gq[:, q0:512],
                                                          op=ALU.subtract),
                         reads=[t_ps, t_gq], writes=[t_sb])
                else:
                    c.op("dve", lambda e: e.tensor_tensor(out=sb_[:, 0:128], in0=ps_[:, 0:128],
                                                          in1=gqd[:, q0:q0 + 128], op=ALU.subtract),
                         reads=[t_ps, t_gqd], writes=[t_sb])
                    if N > 128:
                        c.op("dve", lambda e: e.tensor_tensor(out=sb_[:, 128:N], in0=ps_[:, 128:N],
                                                              in1=gq[:, q0 + 128:512], op=ALU.subtract),
                             reads=[t_ps, t_gq], writes=[t_sb])
                pt, t_pt, _ = PT.next()
                c.op("act", lambda e: e.activation(out=pt[:, 0:N], in_=sb_[:, 0:N], func=AF.Exp,
                                                   bias=Gk[:, kb, n:n + 1]),
                     reads=[t_sb, t_Gk[kb // 4]], writes=[t_pt])
                return (n, kb, q0, N, pt, t_pt)

            def emit_pv(item):
                n, kb, q0, N, pt, t_pt = item
                O, t_O, L, t_L = banks[n]
                last = kb == nkb - 1
                c.op("pe", lambda e: e.matmul(O[:, q0:512], V[:, kb, n * HD:(n + 1) * HD], pt[:, 0:N],
                                              start=(kb == 0), stop=last),
                     reads=[t_V[kb // 4], t_pt], writes=[t_O], track=last)
                c.op("pe", lambda e: e.matmul(L[:, q0:512], ones_b[:], pt[:, 0:N],
                                              start=(kb == 0), stop=last),
                     reads=[t_ones_b, t_pt], writes=[t_L], track=True)
                if last:
                    c.op("dve", lambda e: e.scalar_tensor_tensor(out=den[:], in0=gb[:, n, :], scalar=1.0, in1=L[:],
                                                                 op0=ALU.add, op1=ALU.mult),
                         reads=[t_gb, t_L], writes=[t_den])
                    c.op("dve", lambda e: e.reciprocal(out=rec[:], in_=den[:]), reads=[t_den], writes=[t_rec])
                    c.op("dve", lambda e: e.tensor_tensor(out=og_b[:, n, :], in0=O[:], in1=rec[:], op=ALU.mult),
                         reads=[t_O, t_rec], writes=[t_og])

            LA = 2
            for n in range(NS):
                for kb in range(nkb):
                    pending.append(emit_scores(n, kb))
                    if len(pending) > LA:
                        emit_pv(pending.pop(0))
            while pending:
                emit_pv(pending.pop(0))
            c.dma("sp", s_og, ogv[:, sw * NS:(sw + 1) * NS, tsl], og_b[:], reads=[t_og], writes=[t_ogd])


def fox_consts():
    k = np.arange(128)[:, None]
    q = np.arange(128)[None, :]
    tri = (k <= q).astype(np.float32)
    ones = np.ones((128, 128), np.float32)
    ident = np.eye(128, dtype=np.float32)
    mp = np.where(q >= k, 0.0, NEG).astype(np.float32)
    return np.ascontiguousarray(np.concatenate([tri, ones, ident, mp, mp, mp, mp], axis=1))


GLA_TAU = 16.0


def emit_gla(c, hsrc, wq, wk, wv, wr, wa1, wa2, ba, gbc, cg, ogT, nheads=2):
    hv = hsrc.rearrange("(kc p) t -> p kc t", p=128)
    ogv = ogT.rearrange("(n p) t -> p n t", p=128)
    wqv = wq.rearrange("(kc p) n -> p kc n", p=128)
    wkv = wk.rearrange("(kc p) n -> p kc n", p=128)
    wvv = wv.rearrange("(kc p) n -> p kc n", p=128)
    wrv = wr.rearrange("(kc p) n -> p kc n", p=128)
    wa1v = wa1.rearrange("(kc p) n -> p kc n", p=128)
    NT = S // 512
    DK, DV = 256, 512

    cgt, t_cg = load_vecs(c, "cg", cg, 512)
    gbt, t_gb = load_vecs(c, "gbc", gbc, 512)
    triN, triU, tri01, ident_f = cgt[:, 0:128], cgt[:, 128:256], cgt[:, 256:384], cgt[:, 384:512]
    ident_b = c.sb("ident_b", [128, 128], BF16)
    t_idb = Tok()
    c.op("dve", lambda e: e.tensor_copy(out=ident_b[:], in_=ident_f), reads=[t_cg], writes=[t_idb])
    one_c = c.sb("one_c", [128, 1], F32)
    eps_c = c.sb("eps_c", [128, 1], F32)
    t_cc = Tok()
    c.op("dve", lambda e: e.memset(one_c[:], 1.0), writes=[t_cc])
    c.op("dve", lambda e: e.memset(eps_c[:], RMS_EPS), writes=[t_cc])

    wq_b = c.sb("wq_b", [128, KC, DK], BF16)
    wk_b = c.sb("wk_b", [128, KC, DK], BF16)
    wv_b = c.sb("wv_b", [128, KC, DV], BF16)
    wr_b = c.sb("wr_b", [128, KC, DV], BF16)
    wa1_b = c.sb("wa1_b", [128, KC, 16], BF16)
    wa2a = c.sb("wa2a", [32, DK], BF16)
    t_W = Tok("W")
    s_W = c.dma_sem("W")
    c.dma("pool", s_W, wa1_b[:], wa1v, writes=[t_W])

    work = [c.ps(f"wk{i}", [128, 512]) for i in range(6)]
    t_work = [Tok(f"wk{i}") for i in range(6)]
    pcb = [c.ps(f"pc{i}", [128, 512]) for i in range(2)]
    t_pcb = [Tok(f"pc{i}") for i in range(2)]
    wk_i = [0]

    def nwork():
        i = wk_i[0] % 6
        wk_i[0] += 1
        return work[i], t_work[i]

    hT = c.sb("hT", [128, KC, 512], BF16)
    t_hT = Tok("hT")
    s_h = c.dma_sem("h")
    gaug = Ring(c, "gaug", 2, [32, 512], BF16, dma=False)
    for i in range(2):
        c.op("dve", lambda e: e.memset(gaug.bufs[i][:], 1.0), writes=[gaug.toks[i]])
    spr = Ring(c, "sp", 2, [128, 4, DK], F32, dma=False)
    ekd = Ring(c, "ekd", 2, [128, DK], F32, dma=False)
    ekr = Ring(c, "ek", 2, [128, 512], F32, dma=False)
    qdT = Ring(c, "qdT", 2, [128, 2, 512], BF16, dma=False)
    kiT = Ring(c, "kiT", 2, [128, 2, 512], BF16, dma=False)
    eq = Ring(c, "eq", 2, [128, 2, 512], F32, dma=False)
    kdec = Ring(c, "kdec", 2, [128, 4, DK], BF16, dma=False)
    Vr = Ring(c, "V", 2, [128, 4, DV], BF16, dma=False)
    rraw = Ring(c, "rraw", 2, [128, 4, DV], BF16, dma=False)
    rs = Ring(c, "rs", 2, [128, 4, DV], BF16, dma=False)
    attm = Ring(c, "attm", 2, [128, 128], BF16, dma=False)
    state = c.sb("state", [128, 2, DV], F32)
    t_state = Tok("state")
    stb = Ring(c, "stb", 2, [128, 2, DV], BF16, dma=False)
    junk = c.sb("junk", [128, DV], BF16)
    t_junk = Tok()
    ssq = Ring(c, "ssq", 2, [128, 1], F32, dma=False)
    onr = Ring(c, "on", 2, [128, DV], F32, dma=False)
    ogt = Ring(c, "ogt", 2, [128, DV], BF16, dma=False)
    ogo = Ring(c, "ogo", 2, [128, 4, 512], BF16, dma=False)
    s_og = c.dma_sem("og")
    t_ogd = Tok("ogT")
    c.out_toks.append(t_ogd)

    def stage_a(sw, t):
        tsl = slice(t * 512, (t + 1) * 512)
        c.dma("sp", s_h, hT[:], hv[:, :, tsl], writes=[t_hT])
        ga, t_ga, _ = gaug.next()
        pg, t_pg = nwork()
        mm_group(c, pg[0:16, :], t_pg, [(wa1_b[:, k, :], hT[:, k, :]) for k in range(KC)], reads=[t_W, t_hT])
        c.op("act", lambda e: e.activation(out=ga[0:16, :], in_=pg[0:16, :], func=AF.Identity),
             reads=[t_pg], writes=[t_ga])
        sp, t_sp, _ = spr.next()
        kd, t_kd, _ = kdec.next()
        Vb, t_Vb, _ = Vr.next()
        rr, t_rr, _ = rraw.next()
        rsb, t_rs, _ = rs.next()
        for tb in range(4):
            bsl = slice(tb * 128, (tb + 1) * 128)
            pz, t_pz = nwork()
            c.op("pe", lambda e: e.matmul(pz[:, 0:DK], ga[0:17, bsl], wa2a[0:17, :], start=True, stop=True),
                 reads=[t_ga, t_W], writes=[t_pz])
            c.op("act", lambda e: e.activation(out=sp[:, tb, :], in_=pz[:, 0:DK], func=AF.Exp, scale=-1.0),
                 reads=[t_pz], writes=[t_sp])
            c.op("act", lambda e: e.activation(out=sp[:, tb, :], in_=sp[:, tb, :], func=AF.Ln, bias=one_c[:, 0:1]),
                 reads=[t_sp, t_cc], writes=[t_sp])
            pd, t_pd = nwork()
            c.op("pe", lambda e: e.matmul(pd[:, 0:DK], triU, sp[:, tb, :], start=True, stop=True),
                 reads=[t_cg, t_sp], writes=[t_pd])
            ek_, t_ek_, _ = ekd.next()
            c.op("act", lambda e: e.activation(out=ek_[:], in_=pd[:, 0:DK], func=AF.Exp),
                 reads=[t_pd], writes=[t_ek_])
            pk, t_pk = nwork()
            mm_group(c, pk[:, 0:DK], t_pk, [(hT[:, k, bsl], wk_b[:, k, :]) for k in range(KC)], reads=[t_W, t_hT])
            c.op("dve", lambda e: e.tensor_tensor(out=kd[:, tb, :], in0=pk[:, 0:DK], in1=ek_[:], op=ALU.mult),
                 reads=[t_pk, t_ek_], writes=[t_kd])
            pv, t_pv = nwork()
            mm_group(c, pv[:], t_pv, [(hT[:, k, bsl], wv_b[:, k, :]) for k in range(KC)], reads=[t_W, t_hT])
            c.op("act", lambda e: e.activation(out=Vb[:, tb, :], in_=pv[:], func=AF.Identity),
                 reads=[t_pv], writes=[t_Vb])
            pr, t_pr = nwork()
            mm_group(c, pr[:], t_pr, [(hT[:, k, bsl], wr_b[:, k, :]) for k in range(KC)], reads=[t_W, t_hT])
            c.op("dve", lambda e: e.tensor_copy(out=rr[:, tb, :], in_=pr[:]), reads=[t_pr], writes=[t_rr])
            for dc in range(2):
                c.op("pe", lambda e: e.matmul(pcb[dc][:, bsl], sp[:, tb, dc * 128:(dc + 1) * 128], triN,
                                              start=True, stop=True),
                     reads=[t_cg, t_sp], writes=[t_pcb[dc]])
        c.op("act", lambda e: e.activation(out=rsb[:], in_=rr[:], func=AF.Silu), reads=[t_rr], writes=[t_rs])
        eqb, t_eq, _ = eq.next()
        qd, t_qd, _ = qdT.next()
        ki, t_ki, _ = kiT.next()
        for dc in range(2):
            c.op("act", lambda e: e.activation(out=eqb[:, dc, :], in_=pcb[dc][:], func=AF.Exp),
                 reads=[t_pcb[dc]], writes=[t_eq])
            ekb, t_ekb, _ = ekr.next()
            c.op("act", lambda e: e.activation(out=ekb[:], in_=pcb[dc][:], func=AF.Exp, scale=-1.0),
                 reads=[t_pcb[dc]], writes=[t_ekb])
            pq, t_pq = nwork()
            mm_group(c, pq[:], t_pq, [(wq_b[:, k, dc * 128:(dc + 1) * 128], hT[:, k, :]) for k in range(KC)],
                     reads=[t_W, t_hT])
            c.op("dve", lambda e: e.scalar_tensor_tensor(out=qd[:, dc, :], in0=pq[:], scalar=float(DK ** -0.5),
                                                         in1=eqb[:, dc, :], op0=ALU.mult, op1=ALU.mult),
                 reads=[t_pq, t_eq], writes=[t_qd])
            pk2, t_pk2 = nwork()
            mm_group(c, pk2[:], t_pk2, [(wk_b[:, k, dc * 128:(dc + 1) * 128], hT[:, k, :]) for k in range(KC)],
                     reads=[t_W, t_hT])
            c.op("dve", lambda e: e.tensor_tensor(out=ki[:, dc, :], in0=pk2[:], in1=ekb[:], op=ALU.mult),
                 reads=[t_pk2, t_ekb], writes=[t_ki])
        return dict(qd=qd, t_qd=t_qd, ki=ki, t_ki=t_ki, eq=eqb, t_eq=t_eq, kd=kd, t_kd=t_kd,
                    V=Vb, t_V=t_Vb, rs=rsb, t_rs=t_rs)

    def stage_b(sw, t, A):
        tsl = slice(t * 512, (t + 1) * 512)
        oo, t_oo, _ = ogo.next()
        for tb in range(4):
            bsl = slice(tb * 128, (tb + 1) * 128)
            pa, t_pa = nwork()
            mm_group(c, pa[:, 0:128], t_pa, [(A["ki"][:, dc, bsl], A["qd"][:, dc, bsl]) for dc in range(2)],
                     reads=[A["t_ki"], A["t_qd"]])
            am, t_am, _ = attm.next()
            c.op("dve", lambda e: e.tensor_tensor(out=am[:], in0=pa[:, 0:128], in1=tri01, op=ALU.mult),
                 reads=[t_pa, t_cg], writes=[t_am])
            sb_old, t_sb_old, _ = stb.cur()
            po, t_po = nwork()
            c.op("pe", lambda e: e.matmul(po[:], am[:], A["V"][:, tb, :], start=True, stop=False),
                 reads=[t_am, A["t_V"]], writes=[t_po], track=False)
            for dc in range(2):
                c.op("pe", lambda e: e.matmul(po[:], A["qd"][:, dc, bsl], sb_old[:, dc, :], start=False, stop=(dc == 1)),
                     reads=[A["t_qd"], t_sb_old, t_am, A["t_V"]], writes=[t_po], track=(dc == 1))
            sb_new, t_sb_new, _ = stb.next()
            for dc in range(2):
                pkv, t_pkv = nwork()
                c.op("pe", lambda e: e.matmul(pkv[:], A["kd"][:, tb, dc * 128:(dc + 1) * 128], A["V"][:, tb, :],
                                              start=True, stop=True),
                     reads=[A["t_kd"], A["t_V"]], writes=[t_pkv])
                col = tb * 128 + 127
                c.op("dve", lambda e: e.scalar_tensor_tensor(out=state[:, dc, :], in0=state[:, dc, :],
                                                             scalar=A["eq"][:, dc, col:col + 1], in1=pkv[:],
                                                             op0=ALU.mult, op1=ALU.add),
                     reads=[t_state, A["t_eq"], t_pkv], writes=[t_state])
                c.op("act", lambda e: e.activation(out=sb_new[:, dc, :], in_=state[:, dc, :], func=AF.Identity),
                     reads=[t_state], writes=[t_sb_new])
            sq, t_sq, _ = ssq.next()
            c.op("act", lambda e: e.activation(out=junk[:], in_=po[:], func=AF.Square, accum_out=sq[:]),
                 reads=[t_po], writes=[t_junk, t_sq])
            c.op("act", lambda e: e.activation(out=sq[:], in_=sq[:], func=AF.Ln, scale=1.0 / DV, bias=eps_c[:, 0:1]),
                 reads=[t_sq, t_cc], writes=[t_sq])
            c.op("act", lambda e: e.activation(out=sq[:], in_=sq[:], func=AF.Exp, scale=-0.5),
                 reads=[t_sq], writes=[t_sq])
            on, t_on, _ = onr.next()
            c.op("dve", lambda e: e.scalar_tensor_tensor(out=on[:], in0=po[:], scalar=sq[:, 0:1], in1=gbt[:],
                                                         op0=ALU.mult, op1=ALU.mult),
                 reads=[t_po, t_sq, t_gb], writes=[t_on])
            og_, t_og_, _ = ogt.next()
            c.op("pool", lambda e: e.tensor_tensor(out=og_[:], in0=on[:], in1=A["rs"][:, tb, :], op=ALU.mult),
                 reads=[t_on, A["t_rs"]], writes=[t_og_])
            pt, t_pt = nwork()
            for fc in range(4):
                c.op("pe", lambda e: e.matmul(pt[:, fc * 128:(fc + 1) * 128], og_[:, fc * 128:(fc + 1) * 128],
                                              ident_b[:], start=True, stop=True),
                     reads=[t_og_, t_idb], writes=[t_pt], track=(fc == 3))
            c.op("dve", lambda e: e.tensor_copy(out=oo[:, :, bsl], in_=pt[:].rearrange("p (f t) -> p f t", f=4)),
                 reads=[t_pt], writes=[t_oo])
        c.dma("sp", s_og, ogv[:, sw * 4:(sw + 1) * 4, tsl], oo[:], reads=[t_oo], writes=[t_ogd])

    for sw in range(nheads):
        c.dma("pool", s_W, wq_b[:], wqv[:, :, sw * DK:(sw + 1) * DK], writes=[t_W])
        c.dma("pool", s_W, wk_b[:], wkv[:, :, sw * DK:(sw + 1) * DK], writes=[t_W])
        c.dma("pool", s_W, wv_b[:], wvv[:, :, sw * DV:(sw + 1) * DV], writes=[t_W])
        c.dma("pool", s_W, wr_b[:], wrv[:, :, sw * DV:(sw + 1) * DV], writes=[t_W])
        c.dma("pool", s_W, wa2a[0:16, :], wa2[:, sw * DK:(sw + 1) * DK], writes=[t_W])
        c.dma("pool", s_W, wa2a[16:17, :], ba[:, sw * DK:(sw + 1) * DK], writes=[t_W])
        c.op("dve", lambda e: e.memset(state[:], 0.0), writes=[t_state])
        sb0, t_sb0, _ = stb.next()
        c.op("dve", lambda e: e.memset(sb0[:], 0.0), writes=[t_sb0])
        prev = stage_a(sw, 0)
        for t in range(NT):
            nxt = stage_a(sw, t + 1) if t + 1 < NT else None
            stage_b(sw, t, prev)
            prev = nxt


def gla_consts():
    a = np.arange(128)[:, None]
    b = np.arange(128)[None, :]
    triN = np.where(a <= b, -1.0 / GLA_TAU, 0.0).astype(np.float32)
    triU = np.where(a > b, -1.0 / GLA_TAU, 0.0).astype(np.float32)
    tri01 = (a <= b).astype(np.float32)
    ident = np.eye(128, dtype=np.float32)
    return np.ascontiguousarray(np.concatenate([triN, triU, tri01, ident], axis=1))


def emit_sconv(c, hsrc, wb, wc, wu, cw, ogT, nsw=1):
    hv = hsrc.rearrange("(kc p) t -> p kc t", p=128)
    ogv = ogT.rearrange("(n p) t -> p n t", p=128)
    NT = S // 512
    NCH = 8
    NCHT = NCH * nsw
    cwt, t_cw = load_vecs(c, "cw", cw, 3 * NCHT)
    W = {}
    t_W = Tok("W")
    s_W = c.dma_sem("W")
    for name in ("b", "c", "u"):
        W[name] = c.sb("w_" + name, [128, KC, 1024], BF16)
    work = [c.ps(f"wk{i}", [128, 512]) for i in range(6)]
    t_work = [Tok(f"wk{i}") for i in range(6)]
    wk_i = [0]

    def nwork():
        i = wk_i[0] % 6
        wk_i[0] += 1
        return work[i], t_work[i]

    hT = Ring(c, "hT", 2, [128, KC, 512], BF16)
    pext = c.sb("pext", [128, NCH, 514], F32)
    t_pext = [Tok(f"pe{n}") for n in range(NCH)]
    usb = Ring(c, "usb", 2, [128, 512], F32, dma=False)
    yb = Ring(c, "yb", 2, [128, 512], F32, dma=False)
    ogo = Ring(c, "ogo", 2, [128, NCH, 512], BF16, dma=False)
    s_og = c.dma_sem("og")
    t_ogd = Tok("ogT")
    c.out_toks.append(t_ogd)
    for sw in range(nsw):
        for name, ap in (("b", wb), ("c", wc), ("u", wu)):
            v = ap.rearrange("(kc p) n -> p kc n", p=128)
            for q in range(4):
                c.dma("pool", s_W, W[name][:, :, q * 256:(q + 1) * 256],
                      v[:, :, sw * 1024 + q * 256:sw * 1024 + (q + 1) * 256], writes=[t_W])
        c.op("dve", lambda e: e.memset(pext[:], 0.0), writes=t_pext)
        for t in range(NT):
            tsl = slice(t * 512, (t + 1) * 512)
            hb, t_hb, s_hb = hT.next()
            c.dma("sp", s_hb, hb[:], hv[:, :, tsl], writes=[t_hb])
            oo, t_oo, _ = ogo.next()
            for n in range(NCH):
                nsl = slice(n * 128, (n + 1) * 128)
                cn = sw * NCH + n
                pu, t_pu = nwork()
                mm_group(c, pu[:], t_pu, [(W["u"][:, k, nsl], hb[:, k, :]) for k in range(KC)], reads=[t_W, t_hb])
                ub, t_ub, _ = usb.next()
                c.op("act", lambda e: e.activation(out=ub[:], in_=pu[:], func=AF.Identity), reads=[t_pu], writes=[t_ub])
                pc_, t_pc = nwork()
                mm_group(c, pc_[:], t_pc, [(W["c"][:, k, nsl], hb[:, k, :]) for k in range(KC)], reads=[t_W, t_hb])
                if t > 0:
                    c.op("pool", lambda e: e.tensor_copy(out=pext[:, n, 0:2], in_=pext[:, n, 512:514]),
                         reads=[t_pext[n]], writes=[t_pext[n]])
                c.op("dve", lambda e: e.tensor_tensor(out=pext[:, n, 2:514], in0=pc_[:], in1=ub[:], op=ALU.mult),
                     reads=[t_pc, t_ub, t_pext[n]], writes=[t_pext[n]])
                y, t_y, _ = yb.next()
                c.op("act", lambda e: e.activation(out=y[:], in_=pext[:, n, 2:514], func=AF.Identity,
                                                   scale=cwt[:, 2 * NCHT + cn:2 * NCHT + cn + 1]),
                     reads=[t_pext[n], t_cw], writes=[t_y])
                c.op("dve", lambda e: e.scalar_tensor_tensor(out=y[:], in0=pext[:, n, 1:513],
                                                             scalar=cwt[:, NCHT + cn:NCHT + cn + 1], in1=y[:],
                                                             op0=ALU.mult, op1=ALU.add),
                     reads=[t_pext[n], t_cw, t_y], writes=[t_y])
                c.op("dve", lambda e: e.scalar_tensor_tensor(out=y[:], in0=pext[:, n, 0:512],
                                                             scalar=cwt[:, cn:cn + 1], in1=y[:],
                                                             op0=ALU.mult, op1=ALU.add),
                     reads=[t_pext[n], t_cw, t_y], writes=[t_y])
                pb, t_pb = nwork()
                mm_group(c, pb[:], t_pb, [(W["b"][:, k, nsl], hb[:, k, :]) for k in range(KC)], reads=[t_W, t_hb])
                c.op("dve", lambda e: e.tensor_tensor(out=oo[:, n, :], in0=pb[:], in1=y[:], op=ALU.mult),
                     reads=[t_pb, t_y], writes=[t_oo])
            c.dma("sp", s_og, ogv[:, sw * NCH:(sw + 1) * NCH, tsl], oo[:], reads=[t_oo], writes=[t_ogd])


def vec_pm(v):
    v = np.asarray(v, dtype=np.float32)
    lead = v.shape[:-1]
    n = v.shape[-1] // 128
    v = v.reshape(-1, n, 128)
    return np.ascontiguousarray(v.transpose(2, 0, 1).reshape(128, -1))


def bcast_rows(v, n=128):
    v = np.asarray(v, dtype=np.float32)
    return np.ascontiguousarray(np.broadcast_to(v[None, :], (n, v.shape[0])))


def emit_mod_own(c, cT, ada_w, ada_b, modts, t_modts):
    ct, t_ct = load_vecs(c, "cT", cT, KC)
    cond = c.sb("cond", [128, KC], BF16)
    t_cond = Tok()
    c.op("act", lambda e: e.activation(out=cond[:], in_=ct[:], func=AF.Silu), reads=[t_ct], writes=[t_cond])
    one11 = c.sb("one11", [1, 1], F32)
    t_one = Tok()
    c.op("dve", lambda e: e.memset(one11[:], 1.0), writes=[t_one])
    awr = Ring(c, "aw", 3, [128, KC, 512], BF16)
    row = c.sb("row", [1, 6 * D], F32)
    t_row = Tok()
    abt = c.sb("abt", [1, 6 * D], F32)
    t_ab = Tok()
    s_ab = c.dma_sem("ab")
    work = [c.ps(f"wk{i}", [128, 512]) for i in range(4)]
    t_work = [Tok(f"wk{i}") for i in range(4)]
    pm = c.ps("pm", [128, 96])
    t_pm = Tok()
    i = 0
    NCC = 6 * D // 512
    for l in range(DEPTH):
        awv = ada_w[l].rearrange("(kc p) n -> p kc n", p=128)
        c.dma("sp", s_ab, abt[:], ada_b[l:l + 1, :], writes=[t_ab])
        for cc in range(NCC):
            csl = slice(cc * 512, (cc + 1) * 512)
            wb, t_wb, s_wb = awr.next()
            c.dma("pool", s_wb, wb[:], awv[:, :, csl], writes=[t_wb])
            pb, t_pb = work[i % 4], t_work[i % 4]
            i += 1
            mm_group(c, pb[0:1, :], t_pb, [(cond[:, k:k + 1], wb[:, k, :]) for k in range(KC)],
                     reads=[t_cond, t_wb])
            c.op("dve", lambda e: e.tensor_tensor(out=row[0:1, csl], in0=pb[0:1, :], in1=abt[0:1, csl], op=ALU.add),
                 reads=[t_pb, t_ab], writes=[t_row])
        for m in range(96):
            c.op("pe", lambda e: e.matmul(pm[:, m:m + 1], row[0:1, m * 128:(m + 1) * 128], one11[0:1, 0:1],
                                          start=True, stop=True),
                 reads=[t_row, t_one], writes=[t_pm], track=(m == 95))
        c.op("dve", lambda e: e.tensor_copy(out=modts[l][:], in_=pm[:]), reads=[t_pm], writes=[t_modts[l]])


FUSED_PASSES = [[(510 * i, 510)] for i in range(7)] + [[(3570, 510), (4080, 16)]]


def emit_hprep(c, xT, mod, hT_dram):
    xv = xT.rearrange("(kc p) t -> p kc t", p=128)
    hv = hT_dram.rearrange("(kc p) t -> p kc t", p=128)
    modt, t_mod = mod
    sc1p = c.sb("sc1p", [128, 16], F32)
    t_sc1p = Tok()
    c.op("dve", lambda e: e.tensor_scalar(out=sc1p[:], in0=modt[:, 16:32], scalar1=1.0, scalar2=None,
                                          op0=ALU.add), reads=[t_mod], writes=[t_sc1p])
    xp = Ring(c, "xp", 3, [128, 4, 512], F32)
    hb = Ring(c, "hb", 2, [128, KC, 512], BF16)
    t_out = Tok()
    for t in range(S // 512):
        tsl = slice(t * 512, (t + 1) * 512)
        h, t_h, s_h = hb.next()
        for q4 in range(4):
            xb, t_xb, s_xb = xp.next()
            c.dma("sp" if q4 % 2 == 0 else "pool", s_xb, xb[:], xv[:, q4 * 4:(q4 + 1) * 4, tsl], writes=[t_xb])
            for kk in range(4):
                kc = q4 * 4 + kk
                c.op("act", lambda e: e.activation(out=h[:, kc, :], in_=xb[:, kk, :], func=AF.Identity,
                                                   bias=modt[:, kc:kc + 1], scale=sc1p[:, kc:kc + 1]),
                     reads=[t_xb, t_mod, t_sc1p], writes=[t_h])
        c.dma("sp", s_h, hv[:, :, tsl], h[:], reads=[t_h], writes=[t_out])


def emit_fox_wconv(c, tag, wq, wk, wv, wg, fwb, fvb, lazy):
    old = c.kind
    c.kind = ""
    key = c.dma_sem(f"bg{tag}")
    c.kind = old
    tok = Tok(f"fconv{tag}")

    def q(dst, src_):
        fn = lambda: c.dma("pool", key, dst, src_, writes=[Tok()], reads=[]) and None
        if lazy:
            c.bg.append(fn)
        else:
            fn()

    for kind, w in (("k", wk), ("q", wq), ("g", wg)):
        v = w.rearrange("(kc p) n -> p kc n", p=128)
        for n in range(KC):
            q(fwb[kind][n].rearrange("p (kc n) -> p kc n", kc=KC), v[:, :, n * 128:(n + 1) * 128])
    v = wv.rearrange("(kc p) n -> p kc n", p=128)
    for sw in range(4):
        q(fvb[sw].rearrange("p (kc n) -> p kc n", kc=KC), v[:, :, sw * 512:(sw + 1) * 512])
    c.bg_final[tok] = (key, 52 * 16)
    return tok


def emit_wconv(c, l, w_up, w_down, wub, wdb, wo, wob):
    old = c.kind
    c.kind = ""
    key = c.dma_sem(f"bg{l}")
    c.kind = old
    tok = Tok(f"wconv{l}")
    wuv = w_up.rearrange("(kc p) n -> p kc n", p=128)
    wdv = w_down.rearrange("(kc p) n -> p kc n", p=128)
    wov = wo.rearrange("(kc p) n -> p kc n", p=128)

    def q(dst, src_):
        c.bg.append(lambda: c.dma("pool", key, dst, src_, writes=[Tok()]))

    c.bg_final[tok] = (key, 120 * 16)
    for n in range(KC):
        q(wob[n].rearrange("p (kc n) -> p kc n", kc=KC), wov[:, :, n * 128:(n + 1) * 128])
    for j in range(NJ):
        dst = wub[j].rearrange("p (kc n) -> p kc n", kc=KC)
        for h in range(2):
            q(dst[:, :, h * 128:(h + 1) * 128], wuv[:, :, h * DFF + j * 128:h * DFF + (j + 1) * 128])
    for n in range(KC):
        q(wdb[n].rearrange("p (kc n) -> p kc n", kc=NJ), wdv[:, :, n * 128:(n + 1) * 128])
    return tok


N_FOX = 2


def build_fused():
    nc = bass.Bass("TRN2", target_bir_lowering=False)

    def din(name, shape, dt=F32):
        return nc.dram_tensor(name, list(shape), dt, kind="ExternalInput").ap()

    def dint(name, shape, dt):
        return nc.dram_tensor(name, list(shape), dt).ap()

    xT = din("xT", [D, S])
    cT = din("cT", [128, KC])
    ada_w = din("ada_w", [DEPTH, D, 6 * D])
    ada_b = din("ada_b", [DEPTH, 6 * D])
    lnv = din("lnv", [DEPTH, 128, 64])
    fox_wq = din("fox_wq", [N_FOX, D, D])
    fox_wk = din("fox_wk", [N_FOX, D, D])
    fox_wv = din("fox_wv", [N_FOX, D, D])
    fox_wg = din("fox_wg", [N_FOX, D, D])
    fox_wf = din("fox_wf", [N_FOX, D, 16])
    fox_bfb = din("fox_bfb", [N_FOX, 128, 16])
    fox_wo = din("fox_wo", [N_FOX, D, D])
    fox_cf = din("fox_cf", [128, 896])
    gla_wq = din("gla_wq", [1, D, 1024])
    gla_wk = din("gla_wk", [1, D, 1024])
    gla_wv = din("gla_wv", [1, D, D])
    gla_wr = din("gla_wr", [1, D, D])
    gla_wa1 = din("gla_wa1", [1, D, 16])
    gla_wa2 = din("gla_wa2", [1, 16, 1024])
    gla_ba = din("gla_ba", [1, 1024])
    gla_gbc = din("gla_gbc", [128, 512])
    gla_cg = din("gla_cg", [128, 512])
    gla_wo = din("gla_wo", [1, D, D])
    conv_w_in = din("conv_w_in", [1, D, 3 * D])
    conv_cw = din("conv_cw", [128, 48])
    conv_w_out = din("conv_w_out", [1, D, D])
    ffn_w_up = din("ffn_w_up", [DEPTH, D, 2 * DFF])
    ffn_w_down = din("ffn_w_down", [DEPTH, DFF, D])
    ffn_cw = din("ffn_cw", [DEPTH, 128, 264])
    ffn_cb = din("ffn_cb", [DEPTH, 128, 88])
    outT = nc.dram_tensor("outT", [D, S], F32, kind="ExternalOutput").ap()

    with ExitStack() as es:
        c = Ctx(nc, es)
        modts = [c.sb(f"modt{l}", [128, 96], F32) for l in range(DEPTH)]
        t_modts = [Tok(f"modt{l}") for l in range(DEPTH)]
        with c.phase("mod", "M"):
            emit_mod_own(c, cT, ada_w, ada_b, modts, t_modts)
        x_cur = xT
        conv = []
        wos = [fox_wo[0], gla_wo[0], conv_w_out[0], fox_wo[1]]
        for i in range(DEPTH):
            conv.append((dint(f"wub{i}", [NJ, 128, KC * 256], BF16), dint(f"wdb{i}", [KC, 128, NJ * 128], BF16),
                         dint(f"wob{i}", [KC, 128, KC * 128], BF16)))
        fconv = {}
        for jj in range(N_FOX):
            fwb = {k: dint(f"fwb{jj}{k}", [KC, 128, KC * 128], BF16) for k in ("q", "k", "g")}
            fvb = dint(f"fvb{jj}", [4, 128, KC * 512], BF16)
            fconv[jj] = [fwb, fvb, None]
        fconv[0][2] = emit_fox_wconv(c, "f0", fox_wq[0], fox_wk[0], fox_wv[0], fox_wg[0],
                                     fconv[0][0], fconv[0][1], lazy=False)
        c.pump_all()
        t_conv0 = emit_wconv(c, 0, ffn_w_up[0], ffn_w_down[0], conv[0][0], conv[0][1], wos[0], conv[0][2])
        conv[0] = conv[0] + (t_conv0,)
        for i in range(DEPTH):
            kind, j = i % 3, i // 3
            mod = (modts[i], t_modts[i])
            if i > 0:
                c.pump_all()
            hT_d = dint(f"hT{i}", [D, S], BF16)
            with c.phase("hprep", f"H{i}"):
                emit_hprep(c, x_cur, mod, hT_d)
            og = dint(f"og{i}", [D, S], BF16)
            x1 = dint(f"x1_{i}", [D, S], F32)
            h2x = dint(f"h2x{i}", [D, S + 2], BF16)
            x_next = outT if i == DEPTH - 1 else dint(f"x2_{i}", [D, S], F32)
            wub, wdb, wob, t_wconv = conv[i]
            if kind == 0:
                with c.phase("fox", f"A{i}"):
                    emit_fox(c, hT_d, tuple(fconv[j]), fox_wf[j], fox_bfb[j], fox_cf, og, nheads=16)
                wo = fox_wo[j]
            elif kind == 1:
                with c.phase("gla", f"A{i}"):
                    emit_gla(c, hT_d, gla_wq[j], gla_wk[j], gla_wv[j], gla_wr[j], gla_wa1[j],
                             gla_wa2[j], gla_ba[j:j + 1, :], gla_gbc, gla_cg, og, nheads=4)
                wo = gla_wo[j]
            else:
                w_in = conv_w_in[j]
                with c.phase("sconv", f"A{i}"):
                    emit_sconv(c, hT_d, w_in[:, 0:D], w_in[:, D:2 * D], w_in[:, 2 * D:3 * D],
                               conv_cw, og, nsw=2)
                wo = conv_w_out[j]
            c.pump_all()
            if i + 1 < DEPTH:
                t_n = emit_wconv(c, i + 1, ffn_w_up[i + 1], ffn_w_down[i + 1], conv[i + 1][0], conv[i + 1][1],
                                 wos[i + 1], conv[i + 1][2])
                conv[i + 1] = conv[i + 1] + (t_n,)
                if (i + 1) % 3 == 0:
                    jn = (i + 1) // 3
                    fconv[jn][2] = emit_fox_wconv(c, f"f{jn}", fox_wq[jn], fox_wk[jn], fox_wv[jn], fox_wg[jn],
                                                  fconv[jn][0], fconv[jn][1], lazy=True)
            with c.phase("oproj", f"B{i}"):
                emit_oproj(c, og, x_cur, (wob, t_wconv), mod, lnv[i][:, 0:16], lnv[i][:, 16:32], x1, h2x, ntok=S, h2off=2)
            with c.phase("ffn", f"F{i}"):
                emit_ffn(c, h2x, x1, wub, wdb, t_wconv, ffn_cw[i], ffn_cb[i], mod,
                         lnv[i][:, 32:48], lnv[i][:, 48:64], x_next, FUSED_PASSES)
            x_cur = x_next
        c.barrier()
        c.finish()
        print("fused instructions:", c.n_inst, {k: v for k, v in c.cnt.items() if v and not k.startswith("d_")},
              "sems", len(c.sem))
    return nc


_NC_CACHE = {}


def kernel(**inp):
    f32 = np.float32
    x = np.asarray(inp["x"], dtype=f32)
    cvec = np.asarray(inp["c"], dtype=f32)
    if "fused" not in _NC_CACHE:
        _NC_CACHE["fused"] = build_fused()
    nc = _NC_CACHE["fused"]
    A = lambda k: np.ascontiguousarray(np.asarray(inp[k], dtype=f32))
    lnv = np.ascontiguousarray(np.stack(
        [np.concatenate([vec_pm(inp["ln1_g"][l]), vec_pm(inp["ln1_b"][l]),
                         vec_pm(inp["ln2_g"][l]), vec_pm(inp["ln2_b"][l])], axis=1) for l in range(DEPTH)], axis=0))
    shared = dict(
        ada_w=A("ada_w"), ada_b=A("ada_b"), lnv=lnv,
        fox_wq=A("fox_wq"), fox_wk=A("fox_wk"), fox_wv=A("fox_wv"), fox_wg=A("fox_wg"), fox_wf=A("fox_wf"),
        fox_bfb=np.ascontiguousarray(np.stack([bcast_rows(inp["fox_bf"][j]) for j in range(N_FOX)], axis=0)),
        fox_wo=A("fox_wo"), fox_cf=fox_consts(),
        gla_wq=A("gla_wq"), gla_wk=A("gla_wk"), gla_wv=A("gla_wv"), gla_wr=A("gla_wr"), gla_wa1=A("gla_wa1"),
        gla_wa2=A("gla_wa2"), gla_ba=A("gla_ba"), gla_gbc=bcast_rows(inp["gla_norm_g"][0]), gla_cg=gla_consts(),
        gla_wo=A("gla_wo"), conv_w_in=A("conv_w_in"), conv_cw=vec_pm(inp["conv_w"][0]), conv_w_out=A("conv_w_out"),
        ffn_w_up=A("ffn_w_up"), ffn_w_down=A("ffn_w_down"),
        ffn_cw=np.ascontiguousarray(np.stack([vec_pm(inp["ffn_conv_w"][l]) for l in range(DEPTH)], axis=0)),
        ffn_cb=np.ascontiguousarray(np.stack([vec_pm(inp["ffn_conv_b"][l]) for l in range(DEPTH)], axis=0)),
    )
    REAL = {0: 0, 1: 1, 4: 2, 5: 3}
    zero_shared = {k: np.zeros_like(v) for k, v in shared.items()}
    zx = np.zeros((D, S), f32)
    zc = np.zeros((128, KC), f32)
    maps = []
    for core in range(NCORES):
        if core in REAL:
            b = REAL[core]
            m = dict(shared)
            m["xT"] = np.ascontiguousarray(x[b].T)
            m["cT"] = np.ascontiguousarray(cvec[b].reshape(KC, 128).T)
        else:
            m = dict(zero_shared)
            m["xT"] = zx
            m["cT"] = zc
        maps.append(m)
    res = run_bass_kernel_spmd(nc, maps, core_ids=list(range(NCORES))).results
    core_of = {b: core for core, b in REAL.items()}
    out = np.stack([res[core_of[b]]["outT"].T for b in range(NB)], axis=0).astype(f32)
    return np.ascontiguousarray(out)
```

```python
import numpy as np
from contextlib import ExitStack
import ml_dtypes
import concourse.bass as bass
import concourse.mybir as mybir
from concourse.bass_utils import run_bass_kernel_spmd

F32 = mybir.dt.float32
BF16 = mybir.dt.bfloat16
AF = mybir.ActivationFunctionType
ALU = mybir.AluOpType
AX = mybir.AxisListType

D = 2048
S = 4096
NB = 4
DEPTH = 4
DFF = 5632
KC = D // 128
TOK = 2048
ALPHA = (2 * DEPTH) ** 0.25
LN_EPS = 1e-5
RMS_EPS = 1e-5
NCORES = 8
CC_INC = 16
PAIRS = [[0, 1], [2, 3], [4, 5], [6, 7]]
ALL8 = [list(range(8))]


class Tok:
    __slots__ = ("name", "w", "r")

    def __init__(self, name=""):
        self.name = name
        self.w = None
        self.r = []


class Ctx:
    def __init__(self, nc, es):
        self.nc = nc
        self.es = es
        self.eng = {"pe": nc.tensor, "act": nc.scalar, "dve": nc.vector,
                    "pool": nc.gpsimd, "sp": nc.sync}
        self.sem = {}
        self.cnt = {}
        for k in ("pe", "act", "dve", "pool"):
            self.sem[k] = es.enter_context(nc.semaphore("s_" + k))
            self.cnt[k] = 0
        self.seen = {k: {} for k in self.eng}
        self.n_inst = 0
        self.out_toks = []
        self.root_es = es
        self.prefix = ""
        self.kind = ""
        self.bg = []
        self.bg_final = {}

    def pump(self, n=1):
        while n > 0 and self.bg:
            self.bg.pop(0)()
            n -= 1

    def pump_all(self):
        self.pump(len(self.bg))
        for tok, dep in self.bg_final.items():
            tok.w = dep
        self.bg_final = {}

    def sb(self, name, shape, dt):
        return self.es.enter_context(self.nc.sbuf_tensor(self.prefix + name, list(shape), dt))

    def ps(self, name, shape, dt=F32):
        return self.es.enter_context(self.nc.psum_tensor(self.prefix + name, list(shape), dt))

    def dma_sem(self, name):
        key = "d_" + self.kind + name
        if key not in self.sem:
            self.sem[key] = self.root_es.enter_context(self.nc.semaphore(key))
            self.cnt[key] = 0
        return key

    def barrier(self):
        for e in self.eng:
            seen = self.seen[e]
            for k, cnt in self.cnt.items():
                if k.startswith("d_bg"):
                    continue
                if cnt > 0 and seen.get(k, 0) < cnt:
                    self.eng[e].wait_ge(self.sem[k], cnt)
                    seen[k] = cnt
                    self.n_inst += 1

    def phase(self, kind, inst):
        ctx = self

        class _P:
            def __enter__(self_):
                ctx.barrier()
                self_.es = ExitStack()
                self_.es.__enter__()
                self_.old = (ctx.es, ctx.prefix, ctx.kind)
                ctx.es, ctx.prefix, ctx.kind = self_.es, f"{inst}_", kind + "_"
                return ctx

            def __exit__(self_, *a):
                ctx.barrier()
                ctx.es, ctx.prefix, ctx.kind = self_.old
                return self_.es.__exit__(*a)

        return _P()

    def collective(self, kind, src, dst, groups):
        key = self.dma_sem("cc")
        ins = self.nc.gpsimd.collective_compute(kind, ALU.bypass, replica_groups=groups,
                                                ins=[src], outs=[dst])
        self.n_inst += 1
        self.cnt[key] += CC_INC
        ins.then_inc(self.sem[key], CC_INC)
        return ins

    def _need(self, e, reads, writes):
        need = {}

        def add(dep):
            if dep is None:
                return
            k, c = dep
            if need.get(k, 0) < c:
                need[k] = c

        for t in reads:
            add(t.w)
        for t in writes:
            if t.w is not None and t.w[0] != e:
                add(t.w)
            for d in t.r:
                if d[0] != e:
                    add(d)
        seen = self.seen[e]
        engine = self.eng[e]
        for k, c in need.items():
            if seen.get(k, 0) < c:
                engine.wait_ge(self.sem[k], c)
                seen[k] = c
                self.n_inst += 1

    def _done(self, key, cnt, reads, writes):
        dep = (key, cnt)
        for t in writes:
            t.w = dep
            t.r = []
        for t in reads:
            t.r.append(dep)
            if len(t.r) > 16:
                best = {}
                for k, c in t.r:
                    if best.get(k, 0) < c:
                        best[k] = c
                t.r = list(best.items())

    def op(self, e, fn, reads=(), writes=(), track=True):
        self._need(e, reads, writes)
        ins = fn(self.eng[e])
        self.n_inst += 1
        if track:
            self.cnt[e] += 1
            ins.then_inc(self.sem[e], 1)
            self._done(e, self.cnt[e], reads, writes)
        return ins

    def dma(self, q, semkey, out, in_, reads=(), writes=(), **kw):
        self._need(q, reads, writes)
        ins = self.eng[q].dma_start(out=out, in_=in_, **kw)
        self.n_inst += 1
        self.cnt[semkey] += 16
        ins.then_inc(self.sem[semkey], 16)
        self._done(semkey, self.cnt[semkey], reads, writes)
        return ins

    def finish(self):
        self._need("sp", self.out_toks, ())


class Ring:
    def __init__(self, c, name, n, shape, dt, dma=True):
        self.bufs = [c.sb(f"{name}{i}", shape, dt) for i in range(n)]
        self.toks = [Tok(f"{name}{i}") for i in range(n)]
        self.sems = [c.dma_sem(f"{name}{i}") for i in range(n)] if dma else None
        self.n = n
        self.i = -1

    def next(self):
        self.i = (self.i + 1) % self.n
        return self.cur()

    def cur(self):
        i = self.i
        return self.bufs[i], self.toks[i], (self.sems[i] if self.sems else None)


def mm_group(c, out_ap, out_tok, pairs, reads):
    n = len(pairs)
    for i, (l, r) in enumerate(pairs):
        last = i == n - 1
        c.op("pe", lambda e: e.matmul(out_ap, l, r, start=(i == 0), stop=last),
             reads=reads, writes=[out_tok], track=last)


class LNState:
    def __init__(self, c, widths=(512,)):
        ng = len(widths)
        width = max(widths)
        self.ones = c.sb("ln_ones", [128, 128], BF16)
        self.t_ones = Tok("ones")
        c.op("dve", lambda e: e.memset(self.ones[:], 1.0), writes=[self.t_ones])
        self.S1 = [c.ps(f"ln_s1_{g}", [128, 512]) for g in range(ng)]
        self.S2 = [c.ps(f"ln_s2_{g}", [128, 512]) for g in range(ng)]
        self.t_S1 = [Tok(f"s1_{g}") for g in range(ng)]
        self.t_S2 = [Tok(f"s2_{g}") for g in range(ng)]
        self.zb = Ring(c, "ln_zb", 3, [128, width], BF16, dma=False)
        self.sq = Ring(c, "ln_sq", 3, [128, width], BF16, dma=False)
        self.m = c.sb("ln_m", [128, width], F32)
        self.msq = c.sb("ln_msq", [128, width], F32)
        self.rstd = [c.sb(f"ln_rstd{g}", [128, widths[g]], F32) for g in range(ng)]
        self.nmr = [c.sb(f"ln_nmr{g}", [128, widths[g]], F32) for g in range(ng)]
        self.t_m = Tok("m")
        self.m2 = c.sb("ln_m2", [128, width], F32)
        self.t_m2 = Tok("m2")
        self.t_msq = Tok("msq")
        self.t_rstd = [Tok() for g in range(ng)]
        self.t_nmr = [Tok() for g in range(ng)]
        self.tt = Ring(c, "ln_tt", 2, [128, width], F32, dma=False)
        self.pending = []

    def accum(self, c, g, n, z_ap, z_tok, W):
        zb, t_zb, _ = self.zb.next()
        sq, t_sq, _ = self.sq.next()
        c.op("act", lambda e: e.activation(out=zb[:, 0:W], in_=z_ap, func=AF.Identity),
             reads=[z_tok], writes=[t_zb])
        c.op("act", lambda e: e.activation(out=sq[:, 0:W], in_=z_ap, func=AF.Square),
             reads=[z_tok], writes=[t_sq])
        last = n == KC - 1

        def mm():
            c.op("pe", lambda e: e.matmul(self.S1[g][:, 0:W], self.ones[:], zb[:, 0:W],
                                          start=(n == 0), stop=last),
                 reads=[self.t_ones, t_zb], writes=[self.t_S1[g]])
            c.op("pe", lambda e: e.matmul(self.S2[g][:, 0:W], self.ones[:], sq[:, 0:W],
                                          start=(n == 0), stop=last),
                 reads=[self.t_ones, t_sq], writes=[self.t_S2[g]])

        self.pending.append(mm)

    def flush(self):
        while self.pending:
            self.pending.pop(0)()

    def finalize(self, c, g, W, eps):
        self.flush()
        m, msq, rstd, nmr, m2 = self.m, self.msq, self.rstd[g], self.nmr[g], self.m2
        c.op("act", lambda e: e.activation(out=m[:, 0:W], in_=self.S1[g][:, 0:W],
                                           func=AF.Identity, scale=1.0 / D),
             reads=[self.t_S1[g]], writes=[self.t_m])
        c.op("dve", lambda e: e.tensor_tensor(out=msq[:, 0:W], in0=m[:, 0:W], in1=m[:, 0:W],
                                              op=ALU.mult),
             reads=[self.t_m], writes=[self.t_msq])
        c.op("dve", lambda e: e.scalar_tensor_tensor(out=msq[:, 0:W], in0=self.S2[g][:, 0:W],
                                                     scalar=1.0 / D, in1=msq[:, 0:W],
                                                     op0=ALU.mult, op1=ALU.subtract),
             reads=[self.t_S2[g], self.t_msq], writes=[self.t_msq])
        c.op("dve", lambda e: e.tensor_scalar(out=msq[:, 0:W], in0=msq[:, 0:W],
                                              scalar1=float(eps), scalar2=None, op0=ALU.add),
             reads=[self.t_msq], writes=[self.t_msq])
        c.op("dve", lambda e: e.reciprocal(out=m2[:, 0:W], in_=msq[:, 0:W]),
             reads=[self.t_msq], writes=[self.t_m2])
        c.op("act", lambda e: e.activation(out=rstd[:, 0:W], in_=m2[:, 0:W], func=AF.Sqrt),
             reads=[self.t_m2], writes=[self.t_rstd[g]])
        c.op("dve", lambda e: e.scalar_tensor_tensor(out=nmr[:, 0:W], in0=m[:, 0:W],
                                                     scalar=-1.0, in1=rstd[:, 0:W],
                                                     op0=ALU.mult, op1=ALU.mult),
             reads=[self.t_m, self.t_rstd[g]], writes=[self.t_nmr[g]])

    def normalize(self, c, g, W, z_ap, z_tok):
        tt, t_tt, _ = self.tt.next()
        c.op("dve", lambda e: e.tensor_tensor(out=tt[:, 0:W], in0=z_ap, in1=self.rstd[g][:, 0:W],
                                              op=ALU.mult),
             reads=[z_tok, self.t_rstd[g]], writes=[t_tt])
        c.op("pool", lambda e: e.tensor_tensor(out=tt[:, 0:W], in0=tt[:, 0:W],
                                               in1=self.nmr[g][:, 0:W], op=ALU.add),
             reads=[t_tt, self.t_nmr[g]], writes=[t_tt])
        return tt[:, 0:W], t_tt


def load_vecs(c, name, dram_ap, ncols, q="sp"):
    t = c.sb("v_" + name, [128, ncols], F32)
    tok = Tok(name)
    sem = c.dma_sem(name)
    c.dma(q, sem, t[:], dram_ap, writes=[tok])
    return t, tok


FFN_PASSES = [[(0, 510)], [(510, 510)], [(1020, 510)], [(1530, 510), (2040, 8)]]
NJ = DFF // 128


def emit_ffn(c, h2x, x1T, wub, wdb, t_wconv, cw, cb, mod, lng, lnb, x2T, FFN_PASSES):
    h2v = h2x.rearrange("(kc p) t -> p kc t", p=128)
    x1v = x1T.rearrange("(kc p) t -> p kc t", p=128)
    x2v = x2T.rearrange("(kc p) t -> p kc t", p=128)

    cwt, t_cw = load_vecs(c, "cw", cw, 3 * 88)
    cbt, t_cb = load_vecs(c, "cb", cb, 88)
    modt, t_mod = mod
    lngt, t_lng = load_vecs(c, "lng", lng, 16)
    lnbt, t_lnb = load_vecs(c, "lnb", lnb, 16)
    g2a = c.sb("g2a", [128, 16], F32)
    t_g2a = Tok("g2a")
    c.op("dve", lambda e: e.tensor_scalar(out=g2a[:], in0=modt[:, 80:96], scalar1=1.0 / ALPHA,
                                          scalar2=None, op0=ALU.mult),
         reads=[t_mod], writes=[t_g2a])

    WR = max([g[1][1] for g in FFN_PASSES if len(g) > 1] + [2])
    ln = LNState(c, widths=(510, WR))
    work = [c.ps(f"wk{i}", [128, 512]) for i in range(4)]
    t_work = [Tok(f"wk{i}") for i in range(4)]
    hx = c.sb("hx", [128, KC, 514 + WR], BF16)
    t_hx = Tok("hx")
    s_hx = c.dma_sem("hx")
    gT = c.sb("gT", [128, NJ, 510 + WR], BF16)
    t_g = [Tok(f"g{j}") for j in range(NJ)]
    xt = c.sb("xt", [128, KC, 510 + WR], F32)
    t_xt = [Tok(f"xt{n}") for n in range(KC)]
    s_xt = c.dma_sem("xt")
    s_out = c.dma_sem("out")
    wu = Ring(c, "wu", 3, [128, KC, 256], BF16)
    wd = Ring(c, "wd", 2, [128, NJ, 128], BF16)
    scr = {k: Ring(c, "scr_" + k, 2, [128, 510], F32, dma=False) for k in ("a", "u", "sa")}
    t_outdram = Tok("x2T")
    c.out_toks.append(t_outdram)

    def load_wu(j):
        buf, tok, sem = wu.next()
        c.dma("sp", sem, buf[:], wub[j].rearrange("p (kc n) -> p kc n", kc=KC), reads=[t_wconv], writes=[tok])
        return buf, tok

    def load_wd(n):
        buf, tok, sem = wd.next()
        c.dma("sp", sem, buf[:], wdb[n].rearrange("p (kc n) -> p kc n", kc=NJ), reads=[t_wconv], writes=[tok])
        return buf, tok

    def pass_cols(groups):
        cols = []
        c0 = 0
        gc0 = 0
        for (s, W) in groups:
            cols.append((s, W, c0, gc0))
            c0 += 512
            gc0 += W
        return cols

    def load_hx(cols):
        for (s, W, c0, gc0) in cols:
            c.dma("pool", s_hx, hx[:, :, c0:c0 + W + 2], h2v[:, :, s:s + W + 2], writes=[t_hx])

    PF = 2
    wk_i = 0
    load_hx(pass_cols(FFN_PASSES[0]))
    for ip, groups in enumerate(FFN_PASSES):
        cols = pass_cols(groups)
        for (s, W, c0, gc0) in cols:
            c.dma("pool", s_xt, xt[:, :, gc0:gc0 + W], x1v[:, :, s:s + W], writes=t_xt)
        pend = [load_wu(j) for j in range(min(PF, NJ))]
        for j in range(NJ):
            if j + PF < NJ:
                pend.append(load_wu(j + PF))
            if j % 2 == 0:
                c.pump(1)
            wbuf, t_w = pend.pop(0)
            for (s, W, c0, gc0) in cols:
                N = W + 2
                res = {}
                for hi, half in enumerate(("a", "u")):
                    pb = work[wk_i % 4]
                    t_pb = t_work[wk_i % 4]
                    wk_i += 1
                    mm_group(c, pb[:, 0:N], t_pb,
                             [(wbuf[:, k, hi * 128:(hi + 1) * 128], hx[:, k, c0:c0 + N])
                              for k in range(KC)], reads=[t_w, t_hx])
                    b1, tk1, _ = scr[half].next()
                    res[half] = (pb, t_pb, j + hi * NJ, b1, tk1)
                for half in ("a", "u"):
                    pb, t_pb, jj, b1, tk1 = res[half]
                    c.op("act", lambda e: e.activation(out=b1[:, 0:W], in_=pb[:, 2:N], func=AF.Identity,
                                                       bias=cbt[:, jj:jj + 1],
                                                       scale=cwt[:, 2 * 88 + jj:2 * 88 + jj + 1]),
                         reads=[t_pb, t_cw, t_cb], writes=[tk1])
                for half in ("a", "u"):
                    pb, t_pb, jj, b1, tk1 = res[half]
                    c.op("dve", lambda e: e.scalar_tensor_tensor(
                        out=b1[:, 0:W], in0=pb[:, 1:N - 1], scalar=cwt[:, 88 + jj:88 + jj + 1],
                        in1=b1[:, 0:W], op0=ALU.mult, op1=ALU.add),
                         reads=[t_pb, tk1, t_cw], writes=[tk1])
                for half in ("a", "u"):
                    pb, t_pb, jj, b1, tk1 = res[half]
                    c.op("dve", lambda e: e.scalar_tensor_tensor(
                        out=b1[:, 0:W], in0=pb[:, 0:N - 2], scalar=cwt[:, jj:jj + 1],
                        in1=b1[:, 0:W], op0=ALU.mult, op1=ALU.add),
                         reads=[t_pb, tk1, t_cw], writes=[tk1])
                sa, t_sa, _ = scr["sa"].next()
                ba, tka = res["a"][3], res["a"][4]
                bu, tku = res["u"][3], res["u"][4]
                c.op("act", lambda e: e.activation(out=sa[:, 0:W], in_=ba[:, 0:W], func=AF.Silu),
                     reads=[tka], writes=[t_sa])
                c.op("pool", lambda e: e.tensor_tensor(out=gT[:, j, gc0:gc0 + W], in0=sa[:, 0:W],
                                                       in1=bu[:, 0:W], op=ALU.mult),
                     reads=[t_sa, tku], writes=[t_g[j]])
        if ip + 1 < len(FFN_PASSES):
            load_hx(pass_cols(FFN_PASSES[ip + 1]))
        pend = [load_wd(0)]
        for n in range(KC):
            if n + 1 < KC:
                pend.append(load_wd(n + 1))
            wbuf, t_w = pend.pop(0)
            for gi, (s, W, c0, gc0) in enumerate(cols):
                pb = work[wk_i % 4]
                t_pb = t_work[wk_i % 4]
                wk_i += 1
                mm_group(c, pb[:, 0:W], t_pb,
                         [(wbuf[:, k, :], gT[:, k, gc0:gc0 + W]) for k in range(NJ)],
                         reads=[t_w] + t_g)
                ln.flush()
                c.op("dve", lambda e: e.scalar_tensor_tensor(
                    out=xt[:, n, gc0:gc0 + W], in0=pb[:, 0:W], scalar=g2a[:, n:n + 1],
                    in1=xt[:, n, gc0:gc0 + W], op0=ALU.mult, op1=ALU.add),
                     reads=[t_pb, t_g2a, t_xt[n]], writes=[t_xt[n]])
                ln.accum(c, gi, n, xt[:, n, gc0:gc0 + W], t_xt[n], W)
        for gi, (s, W, c0, gc0) in enumerate(cols):
            ln.finalize(c, gi, W, LN_EPS / (ALPHA * ALPHA))
            for n in range(KC):
                t_ap, t_tok = ln.normalize(c, gi, W, xt[:, n, gc0:gc0 + W], t_xt[n])
                c.op("act", lambda e: e.activation(out=xt[:, n, gc0:gc0 + W], in_=t_ap, func=AF.Identity,
                                                   bias=lnbt[:, n:n + 1], scale=lngt[:, n:n + 1]),
                     reads=[t_tok, t_lng, t_lnb], writes=[t_xt[n]])
            c.dma("pool", s_out, x2v[:, :, s:s + W], xt[:, :, gc0:gc0 + W], reads=t_xt,
                  writes=[t_outdram])


def emit_oproj(c, ogT, xT, wo, mod, lng, lnb, x1T, h2T, ntok=TOK, h2off=0):
    ogv = ogT.rearrange("(kc p) t -> p kc t", p=128)
    xv = xT.rearrange("(kc p) t -> p kc t", p=128)
    x1v = x1T.rearrange("(kc p) t -> p kc t", p=128)
    h2v = h2T.rearrange("(kc p) t -> p kc t", p=128)
    wob, t_wconv = wo
    modt, t_mod = mod
    lngt, t_lng = load_vecs(c, "lng", lng, 16)
    lnbt, t_lnb = load_vecs(c, "lnb", lnb, 16)
    g1a = c.sb("g1a", [128, 16], F32)
    gs = c.sb("gs", [128, 16], F32)
    bs = c.sb("bs", [128, 16], F32)
    t_g1a, t_gs, t_bs = Tok(), Tok(), Tok()
    c.op("dve", lambda e: e.tensor_scalar(out=g1a[:], in0=modt[:, 32:48], scalar1=1.0 / ALPHA,
                                          scalar2=None, op0=ALU.mult),
         reads=[t_mod], writes=[t_g1a])
    c.op("dve", lambda e: e.scalar_tensor_tensor(out=gs[:], in0=modt[:, 64:80], scalar=1.0, in1=lngt[:],
                                                 op0=ALU.add, op1=ALU.mult),
         reads=[t_mod, t_lng], writes=[t_gs])
    c.op("dve", lambda e: e.scalar_tensor_tensor(out=bs[:], in0=modt[:, 64:80], scalar=1.0, in1=lnbt[:],
                                                 op0=ALU.add, op1=ALU.mult),
         reads=[t_mod, t_lnb], writes=[t_bs])
    c.op("dve", lambda e: e.tensor_tensor(out=bs[:], in0=bs[:], in1=modt[:, 48:64], op=ALU.add),
         reads=[t_mod, t_bs], writes=[t_bs])

    ln = LNState(c, widths=(512,))
    work = [c.ps(f"wk{i}", [128, 512]) for i in range(4)]
    t_work = [Tok(f"wk{i}") for i in range(4)]
    og = Ring(c, "og", 2, [128, KC, 512], BF16)
    xr = Ring(c, "xr", 2, [128, KC, 512], F32)
    h2r = Ring(c, "h2r", 1, [128, KC, 512], BF16, dma=False)
    wor = Ring(c, "wo", 3, [128, KC, 128], BF16)
    s_o1 = c.dma_sem("o1")
    s_o2 = c.dma_sem("o2")
    t_o1, t_o2 = Tok("x1T"), Tok("h2T")
    c.out_toks += [t_o1, t_o2]
    NT = ntok // 512
    wk_i = 0
    if h2off:
        zt = c.sb("zt", [128, KC, h2off], BF16)
        t_zt = Tok()
        c.op("dve", lambda e: e.memset(zt[:], 0.0), writes=[t_zt])
        c.dma("sp", s_o2, h2v[:, :, 0:h2off], zt[:], reads=[t_zt], writes=[t_o2])

    def load_tile(t):
        ob, t_ob, s_ob = og.next()
        c.dma("sp", s_ob, ob[:], ogv[:, :, t * 512:(t + 1) * 512], writes=[t_ob])
        xb, t_xb, s_xb = xr.next()
        c.dma("sp", s_xb, xb[:], xv[:, :, t * 512:(t + 1) * 512], writes=[t_xb])
        return ob, t_ob, xb, t_xb

    def load_wo(n):
        buf, tok, sem = wor.next()
        c.dma("pool", sem, buf[:], wob[n].rearrange("p (kc n) -> p kc n", kc=KC), reads=[t_wconv], writes=[tok])
        return buf, tok

    nxt = load_tile(0)
    for t in range(NT):
        ob, t_ob, xb, t_xb = nxt
        pend = [load_wo(0), load_wo(1)]
        for n in range(KC):
            if n + 2 < KC:
                pend.append(load_wo(n + 2))
            wbuf, t_w = pend.pop(0)
            pb = work[wk_i % 4]
            t_pb = t_work[wk_i % 4]
            wk_i += 1
            mm_group(c, pb[:], t_pb, [(wbuf[:, k, :], ob[:, k, :]) for k in range(KC)],
                     reads=[t_w, t_ob])
            ln.flush()
            c.op("dve", lambda e: e.scalar_tensor_tensor(
                out=xb[:, n, :], in0=pb[:], scalar=g1a[:, n:n + 1], in1=xb[:, n, :],
                op0=ALU.mult, op1=ALU.add),
                 reads=[t_pb, t_g1a, t_xb], writes=[t_xb])
            ln.accum(c, 0, n, xb[:, n, :], t_xb, 512)
        if t + 1 < NT:
            nxt = load_tile(t + 1)
        ln.finalize(c, 0, 512, LN_EPS / (ALPHA * ALPHA))
        hb, t_hb, _ = h2r.next()
        for n in range(KC):
            t_ap, t_tok = ln.normalize(c, 0, 512, xb[:, n, :], t_xb)
            c.op("act", lambda e: e.activation(out=xb[:, n, :], in_=t_ap, func=AF.Identity,
                                               bias=lnbt[:, n:n + 1], scale=lngt[:, n:n + 1]),
                 reads=[t_tok, t_lng, t_lnb], writes=[t_xb])
            c.op("act", lambda e: e.activation(out=hb[:, n, :], in_=t_ap, func=AF.Identity,
                                               bias=bs[:, n:n + 1], scale=gs[:, n:n + 1]),
                 reads=[t_tok, t_gs, t_bs], writes=[t_hb])
        c.dma("sp", s_o1, x1v[:, :, t * 512:(t + 1) * 512], xb[:], reads=[t_xb], writes=[t_o1])
        c.dma("sp", s_o2, h2v[:, :, h2off + t * 512:h2off + (t + 1) * 512], hb[:], reads=[t_hb], writes=[t_o2])


NEG = 30000.0
HD = 128
FOX_SWEEP = 4


def emit_fox(c, hsrc, fw, wf, bfb, cf, ogT, nheads=8):
    hv = hsrc.rearrange("(kc p) t -> p kc t", p=128)
    ogv = ogT.rearrange("(n p) t -> p n t", p=128)
    fwb, fvb, t_fw = fw
    wfv = wf.rearrange("(kc p) h -> p kc h", p=128)
    NS = FOX_SWEEP
    NT = S // 512

    cft, t_cf = load_vecs(c, "cf", cf, 896)
    bft, t_bf = load_vecs(c, "bfb", bfb, nheads)
    tri, ones_f, ident, mpos = cft[:, 0:128], cft[:, 128:256], cft[:, 256:384], cft[:, 384:896]
    ones_b = c.sb("ones_b", [128, 128], BF16)
    t_ones_b = Tok()
    c.op("dve", lambda e: e.memset(ones_b[:], 1.0), writes=[t_ones_b])
    one_c = c.sb("one_c", [128, 1], F32)
    t_one_c = Tok()
    c.op("dve", lambda e: e.memset(one_c[:], 1.0), writes=[t_one_c])
    wft = c.sb("wft", [128, KC, nheads], BF16)
    t_wf = Tok()
    s_wf = c.dma_sem("wf")
    c.dma("pool", s_wf, wft[:], wfv, writes=[t_wf])

    work = [c.ps(f"wk{i}", [128, 512]) for i in range(4)]
    t_work = [Tok(f"wk{i}") for i in range(4)]
    Ob = [c.ps(f"ob{i}", [128, 512]) for i in range(2)]
    t_Ob = [Tok(f"ob{i}") for i in range(2)]
    Lb = [c.ps(f"lb{i}", [128, 512]) for i in range(2)]
    t_Lb = [Tok(f"lb{i}") for i in range(2)]
    wk_i = [0]

    def nwork():
        i = wk_i[0] % 4
        wk_i[0] += 1
        return work[i], t_work[i]

    KT = c.sb("KT", [128, NS, S], BF16)
    t_KT = [Tok(f"kt{t}") for t in range(NT)]
    V = c.sb("V", [128, S // 128, NS * HD], BF16)
    t_V = [Tok(f"v{t}") for t in range(NT)]
    Gk = c.sb("Gk", [128, S // 128, NS], F32)
    t_Gk = [Tok(f"gk{t}") for t in range(NT)]
    R = c.sb("R", [128, NS], F32)
    t_R = Tok("R")
    hT = c.sb("hT", [128, KC, 512], BF16)
    t_hT = Tok("hT")
    s_h = c.dma_sem("h")
    QT = Ring(c, "QT", 2, [128, NS, 512], BF16, dma=False)
    GT = Ring(c, "GT", 2, [128, NS, 512], BF16, dma=False)
    wr = Ring(c, "w", 4, [128, KC, 128], BF16)
    wvb = c.sb("wvb", [128, KC, NS * HD], BF16)
    t_wvb = Tok("wvb")
    s_wvb = c.dma_sem("wvb")
    PT = Ring(c, "PT", 4, [128, 512], BF16, dma=False)
    sbr = Ring(c, "sbr", 3, [128, 512], F32, dma=False)
    Dg = c.sb("Dg", [128, 512], F32)
    t_Dg = Tok()
    gqa = c.sb("gqa", [128, NS, 512], F32)
    t_gqa = [Tok(f"gqa{n}") for n in range(NS)]
    gqda = c.sb("gqda", [128, NS, 512], F32)
    t_gqda = [Tok(f"gqda{n}") for n in range(NS)]
    den = c.sb("den", [128, 512], F32)
    t_den = Tok()
    rec = c.sb("rec", [128, 512], F32)
    t_rec = Tok()
    ogb = Ring(c, "ogb", 2, [128, NS, 512], BF16, dma=False)
    zf = c.sb("zf", [128, 4, NS], F32)
    t_zf = [Tok() for _ in range(4)]
    spf = c.sb("spf", [128, 4, NS], F32)
    t_spf = [Tok() for _ in range(4)]
    s_og = c.dma_sem("og")
    t_ogd = Tok("ogT")
    c.out_toks.append(t_ogd)
    ob_i = 0

    for sw in range(nheads // NS):
        hc0 = sw * NS * HD
        c.dma("pool", s_wvb, wvb[:], fvb[sw].rearrange("p (kc n) -> p kc n", kc=KC), reads=[t_fw], writes=[t_wvb])
        c.op("dve", lambda e: e.memset(R[:], 0.0), writes=[t_R])
        if sw == 0:
            c.dma("sp", s_h, hT[:], hv[:, :, 0:512], writes=[t_hT])
        for t in range(NT):
            tsl = slice(t * 512, (t + 1) * 512)

            def proj(kind, n):
                c.pump(1)
                wb, t_wb, s_wb = wr.next()
                c.dma("pool", s_wb, wb[:], fwb[kind][sw * NS + n].rearrange("p (kc n) -> p kc n", kc=KC),
                      reads=[t_fw], writes=[t_wb])
                pb, t_pb = nwork()
                mm_group(c, pb[:], t_pb, [(wb[:, k, :], hT[:, k, :]) for k in range(KC)], reads=[t_wb, t_hT])
                return pb, t_pb

            for n in range(NS):
                pb, t_pb = proj("k", n)
                c.op("dve", lambda e: e.tensor_copy(out=KT[:, n, tsl], in_=pb[:]),
                     reads=[t_pb], writes=[t_KT[t]])
            for tb in range(4):
                blk = t * 4 + tb
                bsl = slice(tb * 128, (tb + 1) * 128)
                pb, t_pb = nwork()
                mm_group(c, pb[:], t_pb, [(hT[:, k, bsl], wvb[:, k, :]) for k in range(KC)],
                         reads=[t_wvb, t_hT])
                c.op("act", lambda e: e.activation(out=V[:, blk, :], in_=pb[:], func=AF.Identity),
                     reads=[t_pb], writes=[t_V[t]])
                pz, t_pz = nwork()
                mm_group(c, pz[:, 0:NS], t_pz, [(hT[:, k, bsl], wft[:, k, sw * NS:(sw + 1) * NS]) for k in range(KC)],
                         reads=[t_wf, t_hT])
                c.op("dve", lambda e: e.tensor_tensor(out=zf[:, tb, :], in0=pz[:, 0:NS], in1=bft[:, sw * NS:(sw + 1) * NS],
                                                      op=ALU.add), reads=[t_pz, t_bf], writes=[t_zf[tb]])
                c.op("act", lambda e: e.activation(out=spf[:, tb, :], in_=zf[:, tb, :], func=AF.Exp, scale=-1.0),
                     reads=[t_zf[tb]], writes=[t_spf[tb]])
                c.op("act", lambda e: e.activation(out=spf[:, tb, :], in_=spf[:, tb, :], func=AF.Ln, bias=one_c[:, 0:1]),
                     reads=[t_spf[tb], t_one_c], writes=[t_spf[tb]])
            qb, t_qb, _ = QT.next()
            gb, t_gb, _ = GT.next()
            for n in range(NS):
                pb, t_pb = proj("q", n)
                c.op("act", lambda e: e.activation(out=qb[:, n, :], in_=pb[:], func=AF.Identity,
                                                   scale=float(HD ** -0.5)),
                     reads=[t_pb], writes=[t_qb])
            for n in range(NS):
                pb, t_pb = proj("g", n)
                c.op("act", lambda e: e.activation(out=gb[:, n, :], in_=pb[:], func=AF.Exp, scale=-1.0),
                     reads=[t_pb], writes=[t_gb])
            if t + 1 < NT:
                c.dma("sp", s_h, hT[:], hv[:, :, (t + 1) * 512:(t + 2) * 512], writes=[t_hT])
            elif sw + 1 < nheads // NS:
                c.dma("sp", s_h, hT[:], hv[:, :, 0:512], writes=[t_hT])
            for tb in range(4):
                blk = t * 4 + tb
                pg, t_pg = nwork()
                c.op("pe", lambda e: e.matmul(pg[:, 0:NS], tri, spf[:, tb, :], start=True, stop=False),
                     reads=[t_cf, t_spf[tb]], writes=[t_pg], track=False)
                c.op("pe", lambda e: e.matmul(pg[:, 0:NS], ones_f, R[:], start=False, stop=True),
                     reads=[t_cf, t_R, t_spf[tb]], writes=[t_pg])
                c.op("dve", lambda e: e.tensor_copy(out=Gk[:, blk, :], in_=pg[:, 0:NS]),
                     reads=[t_pg], writes=[t_Gk[t]])
                c.op("dve", lambda e: e.tensor_tensor(out=R[:], in0=R[:], in1=spf[:, tb, :], op=ALU.add),
                     reads=[t_R, t_spf[tb]], writes=[t_R])
            for n in range(NS):
                for j in range(4):
                    c.op("dve", lambda e: e.tensor_scalar(out=Dg[:, j * 128:(j + 1) * 128], in0=ident,
                                                          scalar1=Gk[:, t * 4 + j, n:n + 1], scalar2=None,
                                                          op0=ALU.mult),
                         reads=[t_cf, t_Gk[t]], writes=[t_Dg])
                pq, t_pq = nwork()
                c.op("pe", lambda e: e.matmul(pq[:], ones_f, Dg[:], start=True, stop=True),
                     reads=[t_cf, t_Dg], writes=[t_pq])
                c.op("act", lambda e: e.activation(out=gqa[:, n, :], in_=pq[:], func=AF.Identity),
                     reads=[t_pq], writes=[t_gqa[n]])
                c.op("pool", lambda e: e.tensor_tensor(out=gqda[:, n, :], in0=gqa[:, n, :], in1=mpos, op=ALU.add),
                     reads=[t_gqa[n], t_cf], writes=[t_gqda[n]])
            og_b, t_og, _ = ogb.next()
            nkb = 4 * t + 4
            banks = {}
            for n in range(NS):
                banks[n] = (Ob[ob_i % 2], t_Ob[ob_i % 2], Lb[ob_i % 2], t_Lb[ob_i % 2])
                ob_i += 1
            pending = []

            def emit_scores(n, kb):
                j = kb - 4 * t
                q0 = 128 * j if j > 0 else 0
                N = 512 - q0
                gq, t_gq, gqd, t_gqd = gqa[:, n, :], t_gqa[n], gqda[:, n, :], t_gqda[n]
                ps_, t_ps = nwork()
                c.op("pe", lambda e: e.matmul(ps_[:, 0:N], KT[:, n, kb * 128:(kb + 1) * 128], qb[:, n, q0:512],
                                              start=True, stop=True),
                     reads=[t_KT[kb // 4], t_qb], writes=[t_ps])
                sb_, t_sb, _ = sbr.next()
                if j < 0:
                    c.op("dve", lambda e: e.tensor_tensor(out=sb_[:, 0:N], in0=ps_[:, 0:N], in1=gq[:, q0:512],
                                                          op=ALU.subtract),
                         reads=[t_ps, t_gq], writes=[t_sb])
                else:
                    c.op("dve", lambda e: e.tensor_tensor(out=sb_[:, 0:128], in0=ps_[:, 0:128],
                                                          in1=gqd[:, q0:q0 + 128], op=ALU.subtract),
                         reads=[t_ps, t_gqd], writes=[t_sb])
                    if N > 128:
                        c.op("dve", lambda e: e.tensor_tensor(out=sb_[:, 128:N], in0=ps_[:, 128:N],
                                                              in1=gq[:, q0 + 128:512], op=ALU.subtract),
                             reads=[t_ps, t_gq], writes=[t_sb])
                pt, t_pt, _ = PT.next()
                c.op("act", lambda e: e.activation(out=pt[:, 0:N], in_=sb_[:, 0:N], func=AF.Exp,
                                                   bias=Gk[:, kb, n:n + 1]),
                     reads=[t_sb, t_Gk[kb // 4]], writes=[t_pt])
                return (n, kb, q0, N, pt, t_pt)

            def emit_pv(item):
                n, kb, q0, N, pt, t_pt = item
                O, t_O, L, t_L = banks[n]
                last = kb == nkb - 1
                c.op("pe", lambda e: e.matmul(O[:, q0:512], V[:, kb, n * HD:(n + 1) * HD], pt[:, 0:N],
                                              start=(kb == 0), stop=last),
                     reads=[t_V[kb // 4], t_pt], writes=[t_O], track=last)
                c.op("pe", lambda e: e.matmul(L[:, q0:512], ones_b[:], pt[:, 0:N],
                                              start=(kb == 0), stop=last),
                     reads=[t_ones_b, t_pt], writes=[t_L], track=True)
                if last:
                    c.op("dve", lambda e: e.scalar_tensor_tensor(out=den[:], in0=gb[:, n, :], scalar=1.0, in1=L[:],
                                                                 op0=ALU.add, op1=ALU.mult),
                         reads=[t_gb, t_L], writes=[t_den])
                    c.op("dve", lambda e: e.reciprocal(out=rec[:], in_=den[:]), reads=[t_den], writes=[t_rec])
                    c.op("dve", lambda e: e.tensor_tensor(out=og_b[:, n, :], in0=O[:], in1=rec[:], op=ALU.mult),
                         reads=[t_O, t_rec], writes=[t_og])

            LA = 2
            for n in range(NS):
                for kb in range(nkb):
                    pending.append(emit_scores(n, kb))
                    if len(pending) > LA:
                        emit_pv(pending.pop(0))
            while pending:
                emit_pv(pending.pop(0))
            c.dma("sp", s_og, ogv[:, sw * NS:(sw + 1) * NS, tsl], og_b[:], reads=[t_og], writes=[t_ogd])


def fox_consts():
    k = np.arange(128)[:, None]
    q = np.arange(128)[None, :]
    tri = (k <= q).astype(np.float32)
    ones = np.ones((128, 128), np.float32)
    ident = np.eye(128, dtype=np.float32)
    mp = np.where(q >= k, 0.0, NEG).astype(np.float32)
    return np.ascontiguousarray(np.concatenate([tri, ones, ident, mp, mp, mp, mp], axis=1))


GLA_TAU = 16.0


def emit_gla(c, hsrc, wq, wk, wv, wr, wa1, wa2, ba, gbc, cg, ogT, nheads=2):
    hv = hsrc.rearrange("(kc p) t -> p kc t", p=128)
    ogv = ogT.rearrange("(n p) t -> p n t", p=128)
    wqv = wq.rearrange("(kc p) n -> p kc n", p=128)
    wkv = wk.rearrange("(kc p) n -> p kc n", p=128)
    wvv = wv.rearrange("(kc p) n -> p kc n", p=128)
    wrv = wr.rearrange("(kc p) n -> p kc n", p=128)
    wa1v = wa1.rearrange("(kc p) n -> p kc n", p=128)
    NT = S // 512
    DK, DV = 256, 512

    cgt, t_cg = load_vecs(c, "cg", cg, 512)
    gbt, t_gb = load_vecs(c, "gbc", gbc, 512)
    triN, triU, tri01, ident_f = cgt[:, 0:128], cgt[:, 128:256], cgt[:, 256:384], cgt[:, 384:512]
    ident_b = c.sb("ident_b", [128, 128], BF16)
    t_idb = Tok()
    c.op("dve", lambda e: e.tensor_copy(out=ident_b[:], in_=ident_f), reads=[t_cg], writes=[t_idb])
    one_c = c.sb("one_c", [128, 1], F32)
    eps_c = c.sb("eps_c", [128, 1], F32)
    t_cc = Tok()
    c.op("dve", lambda e: e.memset(one_c[:], 1.0), writes=[t_cc])
    c.op("dve", lambda e: e.memset(eps_c[:], RMS_EPS), writes=[t_cc])

    wq_b = c.sb("wq_b", [128, KC, DK], BF16)
    wk_b = c.sb("wk_b", [128, KC, DK], BF16)
    wv_b = c.sb("wv_b", [128, KC, DV], BF16)
    wr_b = c.sb("wr_b", [128, KC, DV], BF16)
    wa1_b = c.sb("wa1_b", [128, KC, 16], BF16)
    wa2a = c.sb("wa2a", [32, DK], BF16)
    t_W = Tok("W")
    s_W = c.dma_sem("W")
    c.dma("pool", s_W, wa1_b[:], wa1v, writes=[t_W])

    work = [c.ps(f"wk{i}", [128, 512]) for i in range(6)]
    t_work = [Tok(f"wk{i}") for i in range(6)]
    pcb = [c.ps(f"pc{i}", [128, 512]) for i in range(2)]
    t_pcb = [Tok(f"pc{i}") for i in range(2)]
    wk_i = [0]

    def nwork():
        i = wk_i[0] % 6
        wk_i[0] += 1
        return work[i], t_work[i]

    hT = c.sb("hT", [128, KC, 512], BF16)
    t_hT = Tok("hT")
    s_h = c.dma_sem("h")
    gaug = Ring(c, "gaug", 2, [32, 512], BF16, dma=False)
    for i in range(2):
        c.op("dve", lambda e: e.memset(gaug.bufs[i][:], 1.0), writes=[gaug.toks[i]])
    spr = Ring(c, "sp", 2, [128, 4, DK], F32, dma=False)
    ekd = Ring(c, "ekd", 2, [128, DK], F32, dma=False)
    ekr = Ring(c, "ek", 2, [128, 512], F32, dma=False)
    qdT = Ring(c, "qdT", 2, [128, 2, 512], BF16, dma=False)
    kiT = Ring(c, "kiT", 2, [128, 2, 512], BF16, dma=False)
    eq = Ring(c, "eq", 2, [128, 2, 512], F32, dma=False)
    kdec = Ring(c, "kdec", 2, [128, 4, DK], BF16, dma=False)
    Vr = Ring(c, "V", 2, [128, 4, DV], BF16, dma=False)
    rraw = Ring(c, "rraw", 2, [128, 4, DV], BF16, dma=False)
    rs = Ring(c, "rs", 2, [128, 4, DV], BF16, dma=False)
    attm = Ring(c, "attm", 2, [128, 128], BF16, dma=False)
    state = c.sb("state", [128, 2, DV], F32)
    t_state = Tok("state")
    stb = Ring(c, "stb", 2, [128, 2, DV], BF16, dma=False)
    junk = c.sb("junk", [128, DV], BF16)
    t_junk = Tok()
    ssq = Ring(c, "ssq", 2, [128, 1], F32, dma=False)
    onr = Ring(c, "on", 2, [128, DV], F32, dma=False)
    ogt = Ring(c, "ogt", 2, [128, DV], BF16, dma=False)
    ogo = Ring(c, "ogo", 2, [128, 4, 512], BF16, dma=False)
    s_og = c.dma_sem("og")
    t_ogd = Tok("ogT")
    c.out_toks.append(t_ogd)

    def stage_a(sw, t, res):
        tsl = slice(t * 512, (t + 1) * 512)
        c.dma("sp", s_h, hT[:], hv[:, :, tsl], writes=[t_hT])
        ga, t_ga, _ = gaug.next()
        pg, t_pg = nwork()
        mm_group(c, pg[0:16, :], t_pg, [(wa1_b[:, k, :], hT[:, k, :]) for k in range(KC)], reads=[t_W, t_hT])
        c.op("act", lambda e: e.activation(out=ga[0:16, :], in_=pg[0:16, :], func=AF.Identity),
             reads=[t_pg], writes=[t_ga])
        yield
        sp, t_sp, _ = spr.next()
        kd, t_kd, _ = kdec.next()
        Vb, t_Vb, _ = Vr.next()
        rr, t_rr, _ = rraw.next()
        rsb, t_rs, _ = rs.next()
        for tb in range(4):
            bsl = slice(tb * 128, (tb + 1) * 128)
            pz, t_pz = nwork()
            c.op("pe", lambda e: e.matmul(pz[:, 0:DK], ga[0:17, bsl], wa2a[0:17, :], start=True, stop=True),
                 reads=[t_ga, t_W], writes=[t_pz])
            c.op("act", lambda e: e.activation(out=sp[:, tb, :], in_=pz[:, 0:DK], func=AF.Exp, scale=-1.0),
                 reads=[t_pz], writes=[t_sp])
            c.op("act", lambda e: e.activation(out=sp[:, tb, :], in_=sp[:, tb, :], func=AF.Ln, bias=one_c[:, 0:1]),
                 reads=[t_sp, t_cc], writes=[t_sp])
            pd, t_pd = nwork()
            c.op("pe", lambda e: e.matmul(pd[:, 0:DK], triU, sp[:, tb, :], start=True, stop=True),
                 reads=[t_cg, t_sp], writes=[t_pd])
            ek_, t_ek_, _ = ekd.next()
            c.op("act", lambda e: e.activation(out=ek_[:], in_=pd[:, 0:DK], func=AF.Exp),
                 reads=[t_pd], writes=[t_ek_])
            pk, t_pk = nwork()
            mm_group(c, pk[:, 0:DK], t_pk, [(hT[:, k, bsl], wk_b[:, k, :]) for k in range(KC)], reads=[t_W, t_hT])
            c.op("dve", lambda e: e.tensor_tensor(out=kd[:, tb, :], in0=pk[:, 0:DK], in1=ek_[:], op=ALU.mult),
                 reads=[t_pk, t_ek_], writes=[t_kd])
            pv, t_pv = nwork()
            mm_group(c, pv[:], t_pv, [(hT[:, k, bsl], wv_b[:, k, :]) for k in range(KC)], reads=[t_W, t_hT])
            c.op("act", lambda e: e.activation(out=Vb[:, tb, :], in_=pv[:], func=AF.Identity),
                 reads=[t_pv], writes=[t_Vb])
            pr, t_pr = nwork()
            mm_group(c, pr[:], t_pr, [(hT[:, k, bsl], wr_b[:, k, :]) for k in range(KC)], reads=[t_W, t_hT])
            c.op("dve", lambda e: e.tensor_copy(out=rr[:, tb, :], in_=pr[:]), reads=[t_pr], writes=[t_rr])
            for dc in range(2):
                c.op("pe", lambda e: e.matmul(pcb[dc][:, bsl], sp[:, tb, dc * 128:(dc + 1) * 128], triN,
                                              start=True, stop=True),
                     reads=[t_cg, t_sp], writes=[t_pcb[dc]])
            yield
        c.op("act", lambda e: e.activation(out=rsb[:], in_=rr[:], func=AF.Silu), reads=[t_rr], writes=[t_rs])
        eqb, t_eq, _ = eq.next()
        qd, t_qd, _ = qdT.next()
        ki, t_ki, _ = kiT.next()
        for dc in range(2):
            c.op("act", lambda e: e.activation(out=eqb[:, dc, :], in_=pcb[dc][:], func=AF.Exp),
                 reads=[t_pcb[dc]], writes=[t_eq])
            ekb, t_ekb, _ = ekr.next()
            c.op("act", lambda e: e.activation(out=ekb[:], in_=pcb[dc][:], func=AF.Exp, scale=-1.0),
                 reads=[t_pcb[dc]], writes=[t_ekb])
            pq, t_pq = nwork()
            mm_group(c, pq[:], t_pq, [(wq_b[:, k, dc * 128:(dc + 1) * 128], hT[:, k, :]) for k in range(KC)],
                     reads=[t_W, t_hT])
            c.op("dve", lambda e: e.scalar_tensor_tensor(out=qd[:, dc, :], in0=pq[:], scalar=float(DK ** -0.5),
                                                         in1=eqb[:, dc, :], op0=ALU.mult, op1=ALU.mult),
                 reads=[t_pq, t_eq], writes=[t_qd])
            pk2, t_pk2 = nwork()
            mm_group(c, pk2[:], t_pk2, [(wk_b[:, k, dc * 128:(dc + 1) * 128], hT[:, k, :]) for k in range(KC)],
                     reads=[t_W, t_hT])
            c.op("dve", lambda e: e.tensor_tensor(out=ki[:, dc, :], in0=pk2[:], in1=ekb[:], op=ALU.mult),
                 reads=[t_pk2, t_ekb], writes=[t_ki])
        res.update(dict(qd=qd, t_qd=t_qd, ki=ki, t_ki=t_ki, eq=eqb, t_eq=t_eq, kd=kd, t_kd=t_kd,
                        V=Vb, t_V=t_Vb, rs=rsb, t_rs=t_rs))

    def stage_b(sw, t, A, gen):
        tsl = slice(t * 512, (t + 1) * 512)
        oo, t_oo, _ = ogo.next()
        next(gen, None)
        for tb in range(4):
            bsl = slice(tb * 128, (tb + 1) * 128)
            pa, t_pa = nwork()
            mm_group(c, pa[:, 0:128], t_pa, [(A["ki"][:, dc, bsl], A["qd"][:, dc, bsl]) for dc in range(2)],
                     reads=[A["t_ki"], A["t_qd"]])
            am, t_am, _ = attm.next()
            c.op("dve", lambda e: e.tensor_tensor(out=am[:], in0=pa[:, 0:128], in1=tri01, op=ALU.mult),
                 reads=[t_pa, t_cg], writes=[t_am])
            sb_old, t_sb_old, _ = stb.cur()
            po, t_po = nwork()
            c.op("pe", lambda e: e.matmul(po[:], am[:], A["V"][:, tb, :], start=True, stop=False),
                 reads=[t_am, A["t_V"]], writes=[t_po], track=False)
            for dc in range(2):
                c.op("pe", lambda e: e.matmul(po[:], A["qd"][:, dc, bsl], sb_old[:, dc, :], start=False, stop=(dc == 1)),
                     reads=[A["t_qd"], t_sb_old, t_am, A["t_V"]], writes=[t_po], track=(dc == 1))
            sb_new, t_sb_new, _ = stb.next()
            for dc in range(2):
                pkv, t_pkv = nwork()
                c.op("pe", lambda e: e.matmul(pkv[:], A["kd"][:, tb, dc * 128:(dc + 1) * 128], A["V"][:, tb, :],
                                              start=True, stop=True),
                     reads=[A["t_kd"], A["t_V"]], writes=[t_pkv])
                col = tb * 128 + 127
                c.op("dve", lambda e: e.scalar_tensor_tensor(out=state[:, dc, :], in0=state[:, dc, :],
                                                             scalar=A["eq"][:, dc, col:col + 1], in1=pkv[:],
                                                             op0=ALU.mult, op1=ALU.add),
                     reads=[t_state, A["t_eq"], t_pkv], writes=[t_state])
                c.op("act", lambda e: e.activation(out=sb_new[:, dc, :], in_=state[:, dc, :], func=AF.Identity),
                     reads=[t_state], writes=[t_sb_new])
            sq, t_sq, _ = ssq.next()
            c.op("act", lambda e: e.activation(out=junk[:], in_=po[:], func=AF.Square, accum_out=sq[:]),
                 reads=[t_po], writes=[t_junk, t_sq])
            c.op("act", lambda e: e.activation(out=sq[:], in_=sq[:], func=AF.Ln, scale=1.0 / DV, bias=eps_c[:, 0:1]),
                 reads=[t_sq, t_cc], writes=[t_sq])
            c.op("act", lambda e: e.activation(out=sq[:], in_=sq[:], func=AF.Exp, scale=-0.5),
                 reads=[t_sq], writes=[t_sq])
            on, t_on, _ = onr.next()
            c.op("dve", lambda e: e.scalar_tensor_tensor(out=on[:], in0=po[:], scalar=sq[:, 0:1], in1=gbt[:],
                                                         op0=ALU.mult, op1=ALU.mult),
                 reads=[t_po, t_sq, t_gb], writes=[t_on])
            og_, t_og_, _ = ogt.next()
            c.op("pool", lambda e: e.tensor_tensor(out=og_[:], in0=on[:], in1=A["rs"][:, tb, :], op=ALU.mult),
                 reads=[t_on, A["t_rs"]], writes=[t_og_])
            next(gen, None)
            pt, t_pt = nwork()
            for fc in range(4):
                c.op("pe", lambda e: e.matmul(pt[:, fc * 128:(fc + 1) * 128], og_[:, fc * 128:(fc + 1) * 128],
                                              ident_b[:], start=True, stop=True),
                     reads=[t_og_, t_idb], writes=[t_pt], track=(fc == 3))
            c.op("dve", lambda e: e.tensor_copy(out=oo[:, :, bsl], in_=pt[:].rearrange("p (f t) -> p f t", f=4)),
                 reads=[t_pt], writes=[t_oo])
        for _ in gen:
            pass
        c.dma("sp", s_og, ogv[:, sw * 4:(sw + 1) * 4, tsl], oo[:], reads=[t_oo], writes=[t_ogd])

    for sw in range(nheads):
        c.dma("pool", s_W, wq_b[:], wqv[:, :, sw * DK:(sw + 1) * DK], writes=[t_W])
        c.dma("pool", s_W, wk_b[:], wkv[:, :, sw * DK:(sw + 1) * DK], writes=[t_W])
        c.dma("pool", s_W, wv_b[:], wvv[:, :, sw * DV:(sw + 1) * DV], writes=[t_W])
        c.dma("pool", s_W, wr_b[:], wrv[:, :, sw * DV:(sw + 1) * DV], writes=[t_W])
        c.dma("pool", s_W, wa2a[0:16, :], wa2[:, sw * DK:(sw + 1) * DK], writes=[t_W])
        c.dma("pool", s_W, wa2a[16:17, :], ba[:, sw * DK:(sw + 1) * DK], writes=[t_W])
        c.op("dve", lambda e: e.memset(state[:], 0.0), writes=[t_state])
        sb0, t_sb0, _ = stb.next()
        c.op("dve", lambda e: e.memset(sb0[:], 0.0), writes=[t_sb0])
        prev = {}
        for _ in stage_a(sw, 0, prev):
            pass
        for t in range(NT):
            nxt = {}
            gen = stage_a(sw, t + 1, nxt) if t + 1 < NT else iter(())
            stage_b(sw, t, prev, gen)
            prev = nxt


def gla_consts():
    a = np.arange(128)[:, None]
    b = np.arange(128)[None, :]
    triN = np.where(a <= b, -1.0 / GLA_TAU, 0.0).astype(np.float32)
    triU = np.where(a > b, -1.0 / GLA_TAU, 0.0).astype(np.float32)
    tri01 = (a <= b).astype(np.float32)
    ident = np.eye(128, dtype=np.float32)
    return np.ascontiguousarray(np.concatenate([triN, triU, tri01, ident], axis=1))


def emit_sconv(c, hsrc, wb, wc, wu, cw, ogT, nsw=1):
    hv = hsrc.rearrange("(kc p) t -> p kc t", p=128)
    ogv = ogT.rearrange("(n p) t -> p n t", p=128)
    NT = S // 512
    NCH = 8
    NCHT = NCH * nsw
    cwt, t_cw = load_vecs(c, "cw", cw, 3 * NCHT)
    W = {}
    t_W = Tok("W")
    s_W = c.dma_sem("W")
    for name in ("b", "c", "u"):
        W[name] = c.sb("w_" + name, [128, KC, 1024], BF16)
    work = [c.ps(f"wk{i}", [128, 512]) for i in range(6)]
    t_work = [Tok(f"wk{i}") for i in range(6)]
    wk_i = [0]

    def nwork():
        i = wk_i[0] % 6
        wk_i[0] += 1
        return work[i], t_work[i]

    hT = Ring(c, "hT", 2, [128, KC, 512], BF16)
    pext = c.sb("pext", [128, NCH, 514], F32)
    t_pext = [Tok(f"pe{n}") for n in range(NCH)]
    usb = Ring(c, "usb", 2, [128, 512], F32, dma=False)
    yb = Ring(c, "yb", 2, [128, 512], F32, dma=False)
    ogo = Ring(c, "ogo", 2, [128, NCH, 512], BF16, dma=False)
    s_og = c.dma_sem("og")
    t_ogd = Tok("ogT")
    c.out_toks.append(t_ogd)
    for sw in range(nsw):
        for name, ap in (("b", wb), ("c", wc), ("u", wu)):
            v = ap.rearrange("(kc p) n -> p kc n", p=128)
            for q in range(4):
                c.dma("pool", s_W, W[name][:, :, q * 256:(q + 1) * 256],
                      v[:, :, sw * 1024 + q * 256:sw * 1024 + (q + 1) * 256], writes=[t_W])
        c.op("dve", lambda e: e.memset(pext[:], 0.0), writes=t_pext)
        for t in range(NT):
            tsl = slice(t * 512, (t + 1) * 512)
            hb, t_hb, s_hb = hT.next()
            c.dma("sp", s_hb, hb[:], hv[:, :, tsl], writes=[t_hb])
            oo, t_oo, _ = ogo.next()
            for n in range(NCH):
                nsl = slice(n * 128, (n + 1) * 128)
                cn = sw * NCH + n
                pu, t_pu = nwork()
                mm_group(c, pu[:], t_pu, [(W["u"][:, k, nsl], hb[:, k, :]) for k in range(KC)], reads=[t_W, t_hb])
                ub, t_ub, _ = usb.next()
                c.op("act", lambda e: e.activation(out=ub[:], in_=pu[:], func=AF.Identity), reads=[t_pu], writes=[t_ub])
                pc_, t_pc = nwork()
                mm_group(c, pc_[:], t_pc, [(W["c"][:, k, nsl], hb[:, k, :]) for k in range(KC)], reads=[t_W, t_hb])
                if t > 0:
                    c.op("pool", lambda e: e.tensor_copy(out=pext[:, n, 0:2], in_=pext[:, n, 512:514]),
                         reads=[t_pext[n]], writes=[t_pext[n]])
                c.op("dve", lambda e: e.tensor_tensor(out=pext[:, n, 2:514], in0=pc_[:], in1=ub[:], op=ALU.mult),
                     reads=[t_pc, t_ub, t_pext[n]], writes=[t_pext[n]])
                y, t_y, _ = yb.next()
                c.op("act", lambda e: e.activation(out=y[:], in_=pext[:, n, 2:514], func=AF.Identity,
                                                   scale=cwt[:, 2 * NCHT + cn:2 * NCHT + cn + 1]),
                     reads=[t_pext[n], t_cw], writes=[t_y])
                c.op("dve", lambda e: e.scalar_tensor_tensor(out=y[:], in0=pext[:, n, 1:513],
                                                             scalar=cwt[:, NCHT + cn:NCHT + cn + 1], in1=y[:],
                                                             op0=ALU.mult, op1=ALU.add),
                     reads=[t_pext[n], t_cw, t_y], writes=[t_y])
                c.op("dve", lambda e: e.scalar_tensor_tensor(out=y[:], in0=pext[:, n, 0:512],
                                                             scalar=cwt[:, cn:cn + 1], in1=y[:],
                                                             op0=ALU.mult, op1=ALU.add),
                     reads=[t_pext[n], t_cw, t_y], writes=[t_y])
                pb, t_pb = nwork()
                mm_group(c, pb[:], t_pb, [(W["b"][:, k, nsl], hb[:, k, :]) for k in range(KC)], reads=[t_W, t_hb])
                c.op("dve", lambda e: e.tensor_tensor(out=oo[:, n, :], in0=pb[:], in1=y[:], op=ALU.mult),
                     reads=[t_pb, t_y], writes=[t_oo])
            c.dma("sp", s_og, ogv[:, sw * NCH:(sw + 1) * NCH, tsl], oo[:], reads=[t_oo], writes=[t_ogd])


def vec_pm(v):
    v = np.asarray(v, dtype=np.float32)
    lead = v.shape[:-1]
    n = v.shape[-1] // 128
    v = v.reshape(-1, n, 128)
    return np.ascontiguousarray(v.transpose(2, 0, 1).reshape(128, -1))


def bcast_rows(v, n=128):
    v = np.asarray(v, dtype=np.float32)
    return np.ascontiguousarray(np.broadcast_to(v[None, :], (n, v.shape[0])))


def emit_mod_own(c, cT, ada_w, ada_b, modts, t_modts):
    ct, t_ct = load_vecs(c, "cT", cT, KC)
    cond = c.sb("cond", [128, KC], BF16)
    t_cond = Tok()
    c.op("act", lambda e: e.activation(out=cond[:], in_=ct[:], func=AF.Silu), reads=[t_ct], writes=[t_cond])
    one11 = c.sb("one11", [1, 1], F32)
    t_one = Tok()
    c.op("dve", lambda e: e.memset(one11[:], 1.0), writes=[t_one])
    awr = Ring(c, "aw", 3, [128, KC, 512], BF16)
    row = c.sb("row", [1, 6 * D], F32)
    t_row = Tok()
    abt = c.sb("abt", [1, 6 * D], F32)
    t_ab = Tok()
    s_ab = c.dma_sem("ab")
    work = [c.ps(f"wk{i}", [128, 512]) for i in range(4)]
    t_work = [Tok(f"wk{i}") for i in range(4)]
    pm = c.ps("pm", [128, 96])
    t_pm = Tok()
    i = 0
    NCC = 6 * D // 512
    for l in range(DEPTH):
        awv = ada_w[l].rearrange("(kc p) n -> p kc n", p=128)
        c.dma("sp", s_ab, abt[:], ada_b[l:l + 1, :], writes=[t_ab])
        for cc in range(NCC):
            csl = slice(cc * 512, (cc + 1) * 512)
            wb, t_wb, s_wb = awr.next()
            c.dma("pool", s_wb, wb[:], awv[:, :, csl], writes=[t_wb])
            pb, t_pb = work[i % 4], t_work[i % 4]
            i += 1
            mm_group(c, pb[0:1, :], t_pb, [(cond[:, k:k + 1], wb[:, k, :]) for k in range(KC)],
                     reads=[t_cond, t_wb])
            c.op("dve", lambda e: e.tensor_tensor(out=row[0:1, csl], in0=pb[0:1, :], in1=abt[0:1, csl], op=ALU.add),
                 reads=[t_pb, t_ab], writes=[t_row])
        for m in range(96):
            c.op("pe", lambda e: e.matmul(pm[:, m:m + 1], row[0:1, m * 128:(m + 1) * 128], one11[0:1, 0:1],
                                          start=True, stop=True),
                 reads=[t_row, t_one], writes=[t_pm], track=(m == 95))
        c.op("dve", lambda e: e.tensor_copy(out=modts[l][:], in_=pm[:]), reads=[t_pm], writes=[t_modts[l]])


FUSED_PASSES = [[(510 * i, 510)] for i in range(7)] + [[(3570, 510), (4080, 16)]]


def emit_hprep(c, xT, mod, hT_dram):
    xv = xT.rearrange("(kc p) t -> p kc t", p=128)
    hv = hT_dram.rearrange("(kc p) t -> p kc t", p=128)
    modt, t_mod = mod
    sc1p = c.sb("sc1p", [128, 16], F32)
    t_sc1p = Tok()
    c.op("dve", lambda e: e.tensor_scalar(out=sc1p[:], in0=modt[:, 16:32], scalar1=1.0, scalar2=None,
                                          op0=ALU.add), reads=[t_mod], writes=[t_sc1p])
    xp = Ring(c, "xp", 3, [128, 4, 512], F32)
    hb = Ring(c, "hb", 2, [128, KC, 512], BF16)
    t_out = Tok()
    for t in range(S // 512):
        tsl = slice(t * 512, (t + 1) * 512)
        h, t_h, s_h = hb.next()
        for q4 in range(4):
            xb, t_xb, s_xb = xp.next()
            c.dma("sp" if q4 % 2 == 0 else "pool", s_xb, xb[:], xv[:, q4 * 4:(q4 + 1) * 4, tsl], writes=[t_xb])
            for kk in range(4):
                kc = q4 * 4 + kk
                c.op("act", lambda e: e.activation(out=h[:, kc, :], in_=xb[:, kk, :], func=AF.Identity,
                                                   bias=modt[:, kc:kc + 1], scale=sc1p[:, kc:kc + 1]),
                     reads=[t_xb, t_mod, t_sc1p], writes=[t_h])
        c.dma("sp", s_h, hv[:, :, tsl], h[:], reads=[t_h], writes=[t_out])


def emit_fox_wconv(c, tag, wq, wk, wv, wg, fwb, fvb, lazy):
    old = c.kind
    c.kind = ""
    key = c.dma_sem(f"bg{tag}")
    c.kind = old
    tok = Tok(f"fconv{tag}")

    def q(dst, src_):
        fn = lambda: c.dma("pool", key, dst, src_, writes=[Tok()], reads=[]) and None
        if lazy:
            c.bg.append(fn)
        else:
            fn()

    for kind, w in (("k", wk), ("q", wq), ("g", wg)):
        v = w.rearrange("(kc p) n -> p kc n", p=128)
        for n in range(KC):
            q(fwb[kind][n].rearrange("p (kc n) -> p kc n", kc=KC), v[:, :, n * 128:(n + 1) * 128])
    v = wv.rearrange("(kc p) n -> p kc n", p=128)
    for sw in range(4):
        q(fvb[sw].rearrange("p (kc n) -> p kc n", kc=KC), v[:, :, sw * 512:(sw + 1) * 512])
    c.bg_final[tok] = (key, 52 * 16)
    return tok


def emit_wconv(c, l, w_up, w_down, wub, wdb, wo, wob):
    old = c.kind
    c.kind = ""
    key = c.dma_sem(f"bg{l}")
    c.kind = old
    tok = Tok(f"wconv{l}")
    wuv = w_up.rearrange("(kc p) n -> p kc n", p=128)
    wdv = w_down.rearrange("(kc p) n -> p kc n", p=128)
    wov = wo.rearrange("(kc p) n -> p kc n", p=128)

    def q(dst, src_):
        c.bg.append(lambda: c.dma("pool", key, dst, src_, writes=[Tok()]))

    c.bg_final[tok] = (key, 120 * 16)
    for n in range(KC):
        q(wob[n].rearrange("p (kc n) -> p kc n", kc=KC), wov[:, :, n * 128:(n + 1) * 128])
    for j in range(NJ):
        dst = wub[j].rearrange("p (kc n) -> p kc n", kc=KC)
        for h in range(2):
            q(dst[:, :, h * 128:(h + 1) * 128], wuv[:, :, h * DFF + j * 128:h * DFF + (j + 1) * 128])
    for n in range(KC):
        q(wdb[n].rearrange("p (kc n) -> p kc n", kc=NJ), wdv[:, :, n * 128:(n + 1) * 128])
    return tok


N_FOX = 2


def build_fused():
    nc = bass.Bass("TRN2", target_bir_lowering=False)

    def din(name, shape, dt=F32):
        return nc.dram_tensor(name, list(shape), dt, kind="ExternalInput").ap()

    def dint(name, shape, dt):
        return nc.dram_tensor(name, list(shape), dt).ap()

    xT = din("xT", [D, S])
    cT = din("cT", [128, KC])
    ada_w = din("ada_w", [DEPTH, D, 6 * D])
    ada_b = din("ada_b", [DEPTH, 6 * D])
    lnv = din("lnv", [DEPTH, 128, 64])
    fox_wq = din("fox_wq", [N_FOX, D, D])
    fox_wk = din("fox_wk", [N_FOX, D, D])
    fox_wv = din("fox_wv", [N_FOX, D, D])
    fox_wg = din("fox_wg", [N_FOX, D, D])
    fox_wf = din("fox_wf", [N_FOX, D, 16])
    fox_bfb = din("fox_bfb", [N_FOX, 128, 16])
    fox_wo = din("fox_wo", [N_FOX, D, D])
    fox_cf = din("fox_cf", [128, 896])
    gla_wq = din("gla_wq", [1, D, 1024])
    gla_wk = din("gla_wk", [1, D, 1024])
    gla_wv = din("gla_wv", [1, D, D])
    gla_wr = din("gla_wr", [1, D, D])
    gla_wa1 = din("gla_wa1", [1, D, 16])
    gla_wa2 = din("gla_wa2", [1, 16, 1024])
    gla_ba = din("gla_ba", [1, 1024])
    gla_gbc = din("gla_gbc", [128, 512])
    gla_cg = din("gla_cg", [128, 512])
    gla_wo = din("gla_wo", [1, D, D])
    conv_w_in = din("conv_w_in", [1, D, 3 * D])
    conv_cw = din("conv_cw", [128, 48])
    conv_w_out = din("conv_w_out", [1, D, D])
    ffn_w_up = din("ffn_w_up", [DEPTH, D, 2 * DFF])
    ffn_w_down = din("ffn_w_down", [DEPTH, DFF, D])
    ffn_cw = din("ffn_cw", [DEPTH, 128, 264])
    ffn_cb = din("ffn_cb", [DEPTH, 128, 88])
    outT = nc.dram_tensor("outT", [D, S], F32, kind="ExternalOutput").ap()

    with ExitStack() as es:
        c = Ctx(nc, es)
        modts = [c.sb(f"modt{l}", [128, 96], F32) for l in range(DEPTH)]
        t_modts = [Tok(f"modt{l}") for l in range(DEPTH)]
        with c.phase("mod", "M"):
            emit_mod_own(c, cT, ada_w, ada_b, modts, t_modts)
        x_cur = xT
        conv = []
        wos = [fox_wo[0], gla_wo[0], conv_w_out[0], fox_wo[1]]
        for i in range(DEPTH):
            conv.append((dint(f"wub{i}", [NJ, 128, KC * 256], BF16), dint(f"wdb{i}", [KC, 128, NJ * 128], BF16),
                         dint(f"wob{i}", [KC, 128, KC * 128], BF16)))
        fconv = {}
        for jj in range(N_FOX):
            fwb = {k: dint(f"fwb{jj}{k}", [KC, 128, KC * 128], BF16) for k in ("q", "k", "g")}
            fvb = dint(f"fvb{jj}", [4, 128, KC * 512], BF16)
            fconv[jj] = [fwb, fvb, None]
        fconv[0][2] = emit_fox_wconv(c, "f0", fox_wq[0], fox_wk[0], fox_wv[0], fox_wg[0],
                                     fconv[0][0], fconv[0][1], lazy=False)
        c.pump_all()
        t_conv0 = emit_wconv(c, 0, ffn_w_up[0], ffn_w_down[0], conv[0][0], conv[0][1], wos[0], conv[0][2])
        conv[0] = conv[0] + (t_conv0,)
        for i in range(DEPTH):
            kind, j = i % 3, i // 3
            mod = (modts[i], t_modts[i])
            if i > 0:
                c.pump_all()
            hT_d = dint(f"hT{i}", [D, S], BF16)
            with c.phase("hprep", f"H{i}"):
                emit_hprep(c, x_cur, mod, hT_d)
            og = dint(f"og{i}", [D, S], BF16)
            x1 = dint(f"x1_{i}", [D, S], F32)
            h2x = dint(f"h2x{i}", [D, S + 2], BF16)
            x_next = outT if i == DEPTH - 1 else dint(f"x2_{i}", [D, S], F32)
            wub, wdb, wob, t_wconv = conv[i]
            if kind == 0:
                with c.phase("fox", f"A{i}"):
                    emit_fox(c, hT_d, tuple(fconv[j]), fox_wf[j], fox_bfb[j], fox_cf, og, nheads=16)
                wo = fox_wo[j]
            elif kind == 1:
                with c.phase("gla", f"A{i}"):
                    emit_gla(c, hT_d, gla_wq[j], gla_wk[j], gla_wv[j], gla_wr[j], gla_wa1[j],
                             gla_wa2[j], gla_ba[j:j + 1, :], gla_gbc, gla_cg, og, nheads=4)
                wo = gla_wo[j]
            else:
                w_in = conv_w_in[j]
                with c.phase("sconv", f"A{i}"):
                    emit_sconv(c, hT_d, w_in[:, 0:D], w_in[:, D:2 * D], w_in[:, 2 * D:3 * D],
                               conv_cw, og, nsw=2)
                wo = conv_w_out[j]
            c.pump_all()
            if i + 1 < DEPTH:
                t_n = emit_wconv(c, i + 1, ffn_w_up[i + 1], ffn_w_down[i + 1], conv[i + 1][0], conv[i + 1][1],
                                 wos[i + 1], conv[i + 1][2])
                conv[i + 1] = conv[i + 1] + (t_n,)
                if (i + 1) % 3 == 0:
                    jn = (i + 1) // 3
                    fconv[jn][2] = emit_fox_wconv(c, f"f{jn}", fox_wq[jn], fox_wk[jn], fox_wv[jn], fox_wg[jn],
                                                  fconv[jn][0], fconv[jn][1], lazy=True)
            with c.phase("oproj", f"B{i}"):
                emit_oproj(c, og, x_cur, (wob, t_wconv), mod, lnv[i][:, 0:16], lnv[i][:, 16:32], x1, h2x, ntok=S, h2off=2)
            with c.phase("ffn", f"F{i}"):
                emit_ffn(c, h2x, x1, wub, wdb, t_wconv, ffn_cw[i], ffn_cb[i], mod,
                         lnv[i][:, 32:48], lnv[i][:, 48:64], x_next, FUSED_PASSES)
            x_cur = x_next
        c.barrier()
        c.finish()
        print("fused instructions:", c.n_inst, {k: v for k, v in c.cnt.items() if v and not k.startswith("d_")},
              "sems", len(c.sem))
    return nc


_NC_CACHE = {}


def kernel(**inp):
    f32 = np.float32
    x = np.asarray(inp["x"], dtype=f32)
    cvec = np.asarray(inp["c"], dtype=f32)
    if "fused" not in _NC_CACHE:
        _NC_CACHE["fused"] = build_fused()
    nc = _NC_CACHE["fused"]
    A = lambda k: np.ascontiguousarray(np.asarray(inp[k], dtype=f32))
    lnv = np.ascontiguousarray(np.stack(
        [np.concatenate([vec_pm(inp["ln1_g"][l]), vec_pm(inp["ln1_b"][l]),
                         vec_pm(inp["ln2_g"][l]), vec_pm(inp["ln2_b"][l])], axis=1) for l in range(DEPTH)], axis=0))
    shared = dict(
        ada_w=A("ada_w"), ada_b=A("ada_b"), lnv=lnv,
        fox_wq=A("fox_wq"), fox_wk=A("fox_wk"), fox_wv=A("fox_wv"), fox_wg=A("fox_wg"), fox_wf=A("fox_wf"),
        fox_bfb=np.ascontiguousarray(np.stack([bcast_rows(inp["fox_bf"][j]) for j in range(N_FOX)], axis=0)),
        fox_wo=A("fox_wo"), fox_cf=fox_consts(),
        gla_wq=A("gla_wq"), gla_wk=A("gla_wk"), gla_wv=A("gla_wv"), gla_wr=A("gla_wr"), gla_wa1=A("gla_wa1"),
        gla_wa2=A("gla_wa2"), gla_ba=A("gla_ba"), gla_gbc=bcast_rows(inp["gla_norm_g"][0]), gla_cg=gla_consts(),
        gla_wo=A("gla_wo"), conv_w_in=A("conv_w_in"), conv_cw=vec_pm(inp["conv_w"][0]), conv_w_out=A("conv_w_out"),
        ffn_w_up=A("ffn_w_up"), ffn_w_down=A("ffn_w_down"),
        ffn_cw=np.ascontiguousarray(np.stack([vec_pm(inp["ffn_conv_w"][l]) for l in range(DEPTH)], axis=0)),
        ffn_cb=np.ascontiguousarray(np.stack([vec_pm(inp["ffn_conv_b"][l]) for l in range(DEPTH)], axis=0)),
    )
    maps = []
    for core in range(NCORES):
        b = core // 2
        m = dict(shared)
        m["xT"] = np.ascontiguousarray(x[b].T)
        m["cT"] = np.ascontiguousarray(cvec[b].reshape(KC, 128).T)
        maps.append(m)
    res = run_bass_kernel_spmd(nc, maps, core_ids=list(range(NCORES))).results
    out = np.stack([res[2 * b]["outT"].T for b in range(NB)], axis=0).astype(f32)
    return np.ascontiguousarray(out)
```

```python
import numpy as np
from contextlib import ExitStack
import ml_dtypes
import concourse.bass as bass
import concourse.mybir as mybir
from concourse.bass_utils import run_bass_kernel_spmd

F32 = mybir.dt.float32
BF16 = mybir.dt.bfloat16
AF = mybir.ActivationFunctionType
ALU = mybir.AluOpType
AX = mybir.AxisListType

D = 2048
S = 4096
NB = 4
DEPTH = 4
DFF = 5632
KC = D // 128
TOK = 2048
ALPHA = (2 * DEPTH) ** 0.25
LN_EPS = 1e-5
RMS_EPS = 1e-5
NCORES = 8
CC_INC = 16
PAIRS = [[0, 1], [2, 3], [4, 5], [6, 7]]
ALL8 = [list(range(8))]


class Tok:
    __slots__ = ("name", "w", "r")

    def __init__(self, name=""):
        self.name = name
        self.w = None
        self.r = []


class Ctx:
    def __init__(self, nc, es):
        self.nc = nc
        self.es = es
        self.eng = {"pe": nc.tensor, "act": nc.scalar, "dve": nc.vector,
                    "pool": nc.gpsimd, "sp": nc.sync}
        self.sem = {}
        self.cnt = {}
        for k in ("pe", "act", "dve", "pool"):
            self.sem[k] = es.enter_context(nc.semaphore("s_" + k))
            self.cnt[k] = 0
        self.seen = {k: {} for k in self.eng}
        self.n_inst = 0
        self.out_toks = []
        self.root_es = es
        self.prefix = ""
        self.kind = ""
        self.bg = []
        self.bg_final = {}

    def pump(self, n=1):
        while n > 0 and self.bg:
            self.bg.pop(0)()
            n -= 1

    def pump_all(self):
        self.pump(len(self.bg))
        for tok, dep in self.bg_final.items():
            tok.w = dep
        self.bg_final = {}

    def sb(self, name, shape, dt):
        return self.es.enter_context(self.nc.sbuf_tensor(self.prefix + name, list(shape), dt))

    def ps(self, name, shape, dt=F32):
        return self.es.enter_context(self.nc.psum_tensor(self.prefix + name, list(shape), dt))

    def dma_sem(self, name):
        key = "d_" + self.kind + name
        if key not in self.sem:
            self.sem[key] = self.root_es.enter_context(self.nc.semaphore(key))
            self.cnt[key] = 0
        return key

    def barrier(self):
        for e in self.eng:
            seen = self.seen[e]
            for k, cnt in self.cnt.items():
                if k.startswith("d_bg"):
                    continue
                if cnt > 0 and seen.get(k, 0) < cnt:
                    self.eng[e].wait_ge(self.sem[k], cnt)
                    seen[k] = cnt
                    self.n_inst += 1

    def phase(self, kind, inst):
        ctx = self

        class _P:
            def __enter__(self_):
                ctx.barrier()
                self_.es = ExitStack()
                self_.es.__enter__()
                self_.old = (ctx.es, ctx.prefix, ctx.kind)
                ctx.es, ctx.prefix, ctx.kind = self_.es, f"{inst}_", kind + "_"
                return ctx

            def __exit__(self_, *a):
                ctx.barrier()
                ctx.es, ctx.prefix, ctx.kind = self_.old
                return self_.es.__exit__(*a)

        return _P()

    def collective(self, kind, src, dst, groups):
        key = self.dma_sem("cc")
        ins = self.nc.gpsimd.collective_compute(kind, ALU.bypass, replica_groups=groups,
                                                ins=[src], outs=[dst])
        self.n_inst += 1
        self.cnt[key] += CC_INC
        ins.then_inc(self.sem[key], CC_INC)
        return ins

    def _need(self, e, reads, writes):
        need = {}

        def add(dep):
            if dep is None:
                return
            k, c = dep
            if need.get(k, 0) < c:
                need[k] = c

        for t in reads:
            add(t.w)
        for t in writes:
            if t.w is not None and t.w[0] != e:
                add(t.w)
            for d in t.r:
                if d[0] != e:
                    add(d)
        seen = self.seen[e]
        engine = self.eng[e]
        for k, c in need.items():
            if seen.get(k, 0) < c:
                engine.wait_ge(self.sem[k], c)
                seen[k] = c
                self.n_inst += 1

    def _done(self, key, cnt, reads, writes):
        dep = (key, cnt)
        for t in writes:
            t.w = dep
            t.r = []
        for t in reads:
            t.r.append(dep)
            if len(t.r) > 16:
                best = {}
                for k, c in t.r:
                    if best.get(k, 0) < c:
                        best[k] = c
                t.r = list(best.items())

    def op(self, e, fn, reads=(), writes=(), track=True):
        self._need(e, reads, writes)
        ins = fn(self.eng[e])
        self.n_inst += 1
        if track:
            self.cnt[e] += 1
            ins.then_inc(self.sem[e], 1)
            self._done(e, self.cnt[e], reads, writes)
        return ins

    def dma(self, q, semkey, out, in_, reads=(), writes=(), **kw):
        self._need(q, reads, writes)
        ins = self.eng[q].dma_start(out=out, in_=in_, **kw)
        self.n_inst += 1
        self.cnt[semkey] += 16
        ins.then_inc(self.sem[semkey], 16)
        self._done(semkey, self.cnt[semkey], reads, writes)
        return ins

    def finish(self):
        self._need("sp", self.out_toks, ())


class Ring:
    def __init__(self, c, name, n, shape, dt, dma=True):
        self.bufs = [c.sb(f"{name}{i}", shape, dt) for i in range(n)]
        self.toks = [Tok(f"{name}{i}") for i in range(n)]
        self.sems = [c.dma_sem(f"{name}{i}") for i in range(n)] if dma else None
        self.n = n
        self.i = -1

    def next(self):
        self.i = (self.i + 1) % self.n
        return self.cur()

    def cur(self):
        i = self.i
        return self.bufs[i], self.toks[i], (self.sems[i] if self.sems else None)


def mm_group(c, out_ap, out_tok, pairs, reads):
    n = len(pairs)
    for i, (l, r) in enumerate(pairs):
        last = i == n - 1
        c.op("pe", lambda e: e.matmul(out_ap, l, r, start=(i == 0), stop=last),
             reads=reads, writes=[out_tok], track=last)


class LNState:
    def __init__(self, c, widths=(512,)):
        ng = len(widths)
        width = max(widths)
        self.ones = c.sb("ln_ones", [128, 128], BF16)
        self.t_ones = Tok("ones")
        c.op("dve", lambda e: e.memset(self.ones[:], 1.0), writes=[self.t_ones])
        self.S1 = [c.ps(f"ln_s1_{g}", [128, 512]) for g in range(ng)]
        self.S2 = [c.ps(f"ln_s2_{g}", [128, 512]) for g in range(ng)]
        self.t_S1 = [Tok(f"s1_{g}") for g in range(ng)]
        self.t_S2 = [Tok(f"s2_{g}") for g in range(ng)]
        self.zb = Ring(c, "ln_zb", 3, [128, width], BF16, dma=False)
        self.sq = Ring(c, "ln_sq", 3, [128, width], BF16, dma=False)
        self.m = c.sb("ln_m", [128, width], F32)
        self.msq = c.sb("ln_msq", [128, width], F32)
        self.rstd = [c.sb(f"ln_rstd{g}", [128, widths[g]], F32) for g in range(ng)]
        self.nmr = [c.sb(f"ln_nmr{g}", [128, widths[g]], F32) for g in range(ng)]
        self.t_m = Tok("m")
        self.m2 = c.sb("ln_m2", [128, width], F32)
        self.t_m2 = Tok("m2")
        self.t_msq = Tok("msq")
        self.t_rstd = [Tok() for g in range(ng)]
        self.t_nmr = [Tok() for g in range(ng)]
        self.tt = Ring(c, "ln_tt", 2, [128, width], F32, dma=False)
        self.pending = []

    def accum(self, c, g, n, z_ap, z_tok, W):
        zb, t_zb, _ = self.zb.next()
        sq, t_sq, _ = self.sq.next()
        c.op("act", lambda e: e.activation(out=zb[:, 0:W], in_=z_ap, func=AF.Identity),
             reads=[z_tok], writes=[t_zb])
        c.op("act", lambda e: e.activation(out=sq[:, 0:W], in_=z_ap, func=AF.Square),
             reads=[z_tok], writes=[t_sq])
        last = n == KC - 1

        def mm():
            c.op("pe", lambda e: e.matmul(self.S1[g][:, 0:W], self.ones[:], zb[:, 0:W],
                                          start=(n == 0), stop=last),
                 reads=[self.t_ones, t_zb], writes=[self.t_S1[g]])
            c.op("pe", lambda e: e.matmul(self.S2[g][:, 0:W], self.ones[:], sq[:, 0:W],
                                          start=(n == 0), stop=last),
                 reads=[self.t_ones, t_sq], writes=[self.t_S2[g]])

        self.pending.append(mm)

    def flush(self):
        while self.pending:
            self.pending.pop(0)()

    def finalize(self, c, g, W, eps):
        self.flush()
        m, msq, rstd, nmr, m2 = self.m, self.msq, self.rstd[g], self.nmr[g], self.m2
        c.op("act", lambda e: e.activation(out=m[:, 0:W], in_=self.S1[g][:, 0:W],
                                           func=AF.Identity, scale=1.0 / D),
             reads=[self.t_S1[g]], writes=[self.t_m])
        c.op("dve", lambda e: e.tensor_tensor(out=msq[:, 0:W], in0=m[:, 0:W], in1=m[:, 0:W],
                                              op=ALU.mult),
             reads=[self.t_m], writes=[self.t_msq])
        c.op("dve", lambda e: e.scalar_tensor_tensor(out=msq[:, 0:W], in0=self.S2[g][:, 0:W],
                                                     scalar=1.0 / D, in1=msq[:, 0:W],
                                                     op0=ALU.mult, op1=ALU.subtract),
             reads=[self.t_S2[g], self.t_msq], writes=[self.t_msq])
        c.op("dve", lambda e: e.tensor_scalar(out=msq[:, 0:W], in0=msq[:, 0:W],
                                              scalar1=float(eps), scalar2=None, op0=ALU.add),
             reads=[self.t_msq], writes=[self.t_msq])
        c.op("dve", lambda e: e.reciprocal(out=m2[:, 0:W], in_=msq[:, 0:W]),
             reads=[self.t_msq], writes=[self.t_m2])
        c.op("act", lambda e: e.activation(out=rstd[:, 0:W], in_=m2[:, 0:W], func=AF.Sqrt),
             reads=[self.t_m2], writes=[self.t_rstd[g]])
        c.op("dve", lambda e: e.scalar_tensor_tensor(out=nmr[:, 0:W], in0=m[:, 0:W],
                                                     scalar=-1.0, in1=rstd[:, 0:W],
                                                     op0=ALU.mult, op1=ALU.mult),
             reads=[self.t_m, self.t_rstd[g]], writes=[self.t_nmr[g]])

    def normalize(self, c, g, W, z_ap, z_tok):
        tt, t_tt, _ = self.tt.next()
        c.op("dve", lambda e: e.tensor_tensor(out=tt[:, 0:W], in0=z_ap, in1=self.rstd[g][:, 0:W],
                                              op=ALU.mult),
             reads=[z_tok, self.t_rstd[g]], writes=[t_tt])
        c.op("pool", lambda e: e.tensor_tensor(out=tt[:, 0:W], in0=tt[:, 0:W],
                                               in1=self.nmr[g][:, 0:W], op=ALU.add),
             reads=[t_tt, self.t_nmr[g]], writes=[t_tt])
        return tt[:, 0:W], t_tt


def load_vecs(c, name, dram_ap, ncols, q="sp"):
    t = c.sb("v_" + name, [128, ncols], F32)
    tok = Tok(name)
    sem = c.dma_sem(name)
    c.dma(q, sem, t[:], dram_ap, writes=[tok])
    return t, tok


FFN_PASSES = [[(0, 510)], [(510, 510)], [(1020, 510)], [(1530, 510), (2040, 8)]]
NJ = DFF // 128


def emit_ffn(c, h2x, x1T, wub, wdb, t_wconv, cw, cb, mod, lng, lnb, x2T, FFN_PASSES):
    h2v = h2x.rearrange("(kc p) t -> p kc t", p=128)
    x1v = x1T.rearrange("(kc p) t -> p kc t", p=128)
    x2v = x2T.rearrange("(kc p) t -> p kc t", p=128)

    cwt, t_cw = load_vecs(c, "cw", cw, 3 * 88)
    cbt, t_cb = load_vecs(c, "cb", cb, 88)
    modt, t_mod = mod
    lngt, t_lng = load_vecs(c, "lng", lng, 16)
    lnbt, t_lnb = load_vecs(c, "lnb", lnb, 16)
    g2a = c.sb("g2a", [128, 16], F32)
    t_g2a = Tok("g2a")
    c.op("dve", lambda e: e.tensor_scalar(out=g2a[:], in0=modt[:, 80:96], scalar1=1.0 / ALPHA,
                                          scalar2=None, op0=ALU.mult),
         reads=[t_mod], writes=[t_g2a])

    WR = max([g[1][1] for g in FFN_PASSES if len(g) > 1] + [2])
    ln = LNState(c, widths=(510, WR))
    work = [c.ps(f"wk{i}", [128, 512]) for i in range(4)]
    t_work = [Tok(f"wk{i}") for i in range(4)]
    hx = c.sb("hx", [128, KC, 514 + WR], BF16)
    t_hx = Tok("hx")
    s_hx = c.dma_sem("hx")
    gT = c.sb("gT", [128, NJ, 510 + WR], BF16)
    t_g = [Tok(f"g{j}") for j in range(NJ)]
    xt = c.sb("xt", [128, KC, 510 + WR], F32)
    t_xt = [Tok(f"xt{n}") for n in range(KC)]
    s_xt = c.dma_sem("xt")
    s_out = c.dma_sem("out")
    wu = Ring(c, "wu", 3, [128, KC, 256], BF16)
    wd = Ring(c, "wd", 2, [128, NJ, 128], BF16)
    scr = {k: Ring(c, "scr_" + k, 2, [128, 510], F32, dma=False) for k in ("a", "u", "sa")}
    t_outdram = Tok("x2T")
    c.out_toks.append(t_outdram)

    def load_wu(j):
        buf, tok, sem = wu.next()
        c.dma("sp", sem, buf[:], wub[j].rearrange("p (kc n) -> p kc n", kc=KC), reads=[t_wconv], writes=[tok])
        return buf, tok

    def load_wd(n):
        buf, tok, sem = wd.next()
        c.dma("sp", sem, buf[:], wdb[n].rearrange("p (kc n) -> p kc n", kc=NJ), reads=[t_wconv], writes=[tok])
        return buf, tok

    def pass_cols(groups):
        cols = []
        c0 = 0
        gc0 = 0
        for (s, W) in groups:
            cols.append((s, W, c0, gc0))
            c0 += 512
            gc0 += W
        return cols

    def load_hx(cols):
        for (s, W, c0, gc0) in cols:
            c.dma("pool", s_hx, hx[:, :, c0:c0 + W + 2], h2v[:, :, s:s + W + 2], writes=[t_hx])

    PF = 2
    wk_i = 0
    pend_wu = []
    load_hx(pass_cols(FFN_PASSES[0]))
    for ip, groups in enumerate(FFN_PASSES):
        cols = pass_cols(groups)
        for (s, W, c0, gc0) in cols:
            c.dma("pool", s_xt, xt[:, :, gc0:gc0 + W], x1v[:, :, s:s + W], writes=t_xt)
        pend = pend_wu if pend_wu else [load_wu(j) for j in range(min(PF, NJ))]
        pend_wd = []
        for j in range(NJ):
            if j == NJ - 3:
                pend_wd.append(load_wd(0))
            if j + PF < NJ:
                pend.append(load_wu(j + PF))
            if j % 2 == 0:
                c.pump(1)
            wbuf, t_w = pend.pop(0)
            for (s, W, c0, gc0) in cols:
                N = W + 2
                res = {}
                for hi, half in enumerate(("a", "u")):
                    pb = work[wk_i % 4]
                    t_pb = t_work[wk_i % 4]
                    wk_i += 1
                    mm_group(c, pb[:, 0:N], t_pb,
                             [(wbuf[:, k, hi * 128:(hi + 1) * 128], hx[:, k, c0:c0 + N])
                              for k in range(KC)], reads=[t_w, t_hx])
                    b1, tk1, _ = scr[half].next()
                    res[half] = (pb, t_pb, j + hi * NJ, b1, tk1)
                for half in ("a", "u"):
                    pb, t_pb, jj, b1, tk1 = res[half]
                    c.op("act", lambda e: e.activation(out=b1[:, 0:W], in_=pb[:, 2:N], func=AF.Identity,
                                                       bias=cbt[:, jj:jj + 1],
                                                       scale=cwt[:, 2 * 88 + jj:2 * 88 + jj + 1]),
                         reads=[t_pb, t_cw, t_cb], writes=[tk1])
                for half in ("a", "u"):
                    pb, t_pb, jj, b1, tk1 = res[half]
                    c.op("dve", lambda e: e.scalar_tensor_tensor(
                        out=b1[:, 0:W], in0=pb[:, 1:N - 1], scalar=cwt[:, 88 + jj:88 + jj + 1],
                        in1=b1[:, 0:W], op0=ALU.mult, op1=ALU.add),
                         reads=[t_pb, tk1, t_cw], writes=[tk1])
                for half in ("a", "u"):
                    pb, t_pb, jj, b1, tk1 = res[half]
                    c.op("dve", lambda e: e.scalar_tensor_tensor(
                        out=b1[:, 0:W], in0=pb[:, 0:N - 2], scalar=cwt[:, jj:jj + 1],
                        in1=b1[:, 0:W], op0=ALU.mult, op1=ALU.add),
                         reads=[t_pb, tk1, t_cw], writes=[tk1])
                sa, t_sa, _ = scr["sa"].next()
                ba, tka = res["a"][3], res["a"][4]
                bu, tku = res["u"][3], res["u"][4]
                c.op("act", lambda e: e.activation(out=sa[:, 0:W], in_=ba[:, 0:W], func=AF.Silu),
                     reads=[tka], writes=[t_sa])
                c.op("pool", lambda e: e.tensor_tensor(out=gT[:, j, gc0:gc0 + W], in0=sa[:, 0:W],
                                                       in1=bu[:, 0:W], op=ALU.mult),
                     reads=[t_sa, tku], writes=[t_g[j]])
        if ip + 1 < len(FFN_PASSES):
            load_hx(pass_cols(FFN_PASSES[ip + 1]))
        pend = pend_wd
        pend_wu = []
        for n in range(KC):
            if n + 1 < KC:
                pend.append(load_wd(n + 1))
            if n == 2 and ip + 1 < len(FFN_PASSES):
                pend_wu = [load_wu(j) for j in range(min(PF, NJ))]
            wbuf, t_w = pend.pop(0)
            for gi, (s, W, c0, gc0) in enumerate(cols):
                pb = work[wk_i % 4]
                t_pb = t_work[wk_i % 4]
                wk_i += 1
                mm_group(c, pb[:, 0:W], t_pb,
                         [(wbuf[:, k, :], gT[:, k, gc0:gc0 + W]) for k in range(NJ)],
                         reads=[t_w] + t_g)
                ln.flush()
                c.op("dve", lambda e: e.scalar_tensor_tensor(
                    out=xt[:, n, gc0:gc0 + W], in0=pb[:, 0:W], scalar=g2a[:, n:n + 1],
                    in1=xt[:, n, gc0:gc0 + W], op0=ALU.mult, op1=ALU.add),
                     reads=[t_pb, t_g2a, t_xt[n]], writes=[t_xt[n]])
                ln.accum(c, gi, n, xt[:, n, gc0:gc0 + W], t_xt[n], W)
        for gi, (s, W, c0, gc0) in enumerate(cols):
            ln.finalize(c, gi, W, LN_EPS / (ALPHA * ALPHA))
            for n in range(KC):
                t_ap, t_tok = ln.normalize(c, gi, W, xt[:, n, gc0:gc0 + W], t_xt[n])
                c.op("act", lambda e: e.activation(out=xt[:, n, gc0:gc0 + W], in_=t_ap, func=AF.Identity,
                                                   bias=lnbt[:, n:n + 1], scale=lngt[:, n:n + 1]),
                     reads=[t_tok, t_lng, t_lnb], writes=[t_xt[n]])
            c.dma("pool", s_out, x2v[:, :, s:s + W], xt[:, :, gc0:gc0 + W], reads=t_xt,
                  writes=[t_outdram])


def emit_oproj(c, ogT, xT, wo, mod, lng, lnb, x1T, h2T, ntok=TOK, h2off=0):
    ogv = ogT.rearrange("(kc p) t -> p kc t", p=128)
    xv = xT.rearrange("(kc p) t -> p kc t", p=128)
    x1v = x1T.rearrange("(kc p) t -> p kc t", p=128)
    h2v = h2T.rearrange("(kc p) t -> p kc t", p=128)
    wob, t_wconv = wo
    modt, t_mod = mod
    lngt, t_lng = load_vecs(c, "lng", lng, 16)
    lnbt, t_lnb = load_vecs(c, "lnb", lnb, 16)
    g1a = c.sb("g1a", [128, 16], F32)
    gs = c.sb("gs", [128, 16], F32)
    bs = c.sb("bs", [128, 16], F32)
    t_g1a, t_gs, t_bs = Tok(), Tok(), Tok()
    c.op("dve", lambda e: e.tensor_scalar(out=g1a[:], in0=modt[:, 32:48], scalar1=1.0 / ALPHA,
                                          scalar2=None, op0=ALU.mult),
         reads=[t_mod], writes=[t_g1a])
    c.op("dve", lambda e: e.scalar_tensor_tensor(out=gs[:], in0=modt[:, 64:80], scalar=1.0, in1=lngt[:],
                                                 op0=ALU.add, op1=ALU.mult),
         reads=[t_mod, t_lng], writes=[t_gs])
    c.op("dve", lambda e: e.scalar_tensor_tensor(out=bs[:], in0=modt[:, 64:80], scalar=1.0, in1=lnbt[:],
                                                 op0=ALU.add, op1=ALU.mult),
         reads=[t_mod, t_lnb], writes=[t_bs])
    c.op("dve", lambda e: e.tensor_tensor(out=bs[:], in0=bs[:], in1=modt[:, 48:64], op=ALU.add),
         reads=[t_mod, t_bs], writes=[t_bs])

    ln = LNState(c, widths=(512,))
    work = [c.ps(f"wk{i}", [128, 512]) for i in range(4)]
    t_work = [Tok(f"wk{i}") for i in range(4)]
    og = Ring(c, "og", 2, [128, KC, 512], BF16)
    xr = Ring(c, "xr", 2, [128, KC, 512], F32)
    h2r = Ring(c, "h2r", 1, [128, KC, 512], BF16, dma=False)
    wor = Ring(c, "wo", 3, [128, KC, 128], BF16)
    s_o1 = c.dma_sem("o1")
    s_o2 = c.dma_sem("o2")
    t_o1, t_o2 = Tok("x1T"), Tok("h2T")
    c.out_toks += [t_o1, t_o2]
    NT = ntok // 512
    wk_i = 0
    if h2off:
        zt = c.sb("zt", [128, KC, h2off], BF16)
        t_zt = Tok()
        c.op("dve", lambda e: e.memset(zt[:], 0.0), writes=[t_zt])
        c.dma("sp", s_o2, h2v[:, :, 0:h2off], zt[:], reads=[t_zt], writes=[t_o2])

    def load_tile(t):
        ob, t_ob, s_ob = og.next()
        c.dma("sp", s_ob, ob[:], ogv[:, :, t * 512:(t + 1) * 512], writes=[t_ob])
        xb, t_xb, s_xb = xr.next()
        c.dma("sp", s_xb, xb[:], xv[:, :, t * 512:(t + 1) * 512], writes=[t_xb])
        return ob, t_ob, xb, t_xb

    def load_wo(n):
        buf, tok, sem = wor.next()
        c.dma("pool", sem, buf[:], wob[n].rearrange("p (kc n) -> p kc n", kc=KC), reads=[t_wconv], writes=[tok])
        return buf, tok

    nxt = load_tile(0)
    pend_next = [load_wo(0), load_wo(1)]
    for t in range(NT):
        ob, t_ob, xb, t_xb = nxt
        pend = pend_next
        for n in range(KC):
            if n + 2 < KC:
                pend.append(load_wo(n + 2))
            wbuf, t_w = pend.pop(0)
            pb = work[wk_i % 4]
            t_pb = t_work[wk_i % 4]
            wk_i += 1
            mm_group(c, pb[:], t_pb, [(wbuf[:, k, :], ob[:, k, :]) for k in range(KC)],
                     reads=[t_w, t_ob])
            ln.flush()
            c.op("dve", lambda e: e.scalar_tensor_tensor(
                out=xb[:, n, :], in0=pb[:], scalar=g1a[:, n:n + 1], in1=xb[:, n, :],
                op0=ALU.mult, op1=ALU.add),
                 reads=[t_pb, t_g1a, t_xb], writes=[t_xb])
            ln.accum(c, 0, n, xb[:, n, :], t_xb, 512)
        if t + 1 < NT:
            nxt = load_tile(t + 1)
            pend_next = [load_wo(0), load_wo(1)]
        ln.finalize(c, 0, 512, LN_EPS / (ALPHA * ALPHA))
        hb, t_hb, _ = h2r.next()
        for n in range(KC):
            t_ap, t_tok = ln.normalize(c, 0, 512, xb[:, n, :], t_xb)
            c.op("act", lambda e: e.activation(out=xb[:, n, :], in_=t_ap, func=AF.Identity,
                                               bias=lnbt[:, n:n + 1], scale=lngt[:, n:n + 1]),
                 reads=[t_tok, t_lng, t_lnb], writes=[t_xb])
            c.op("act", lambda e: e.activation(out=hb[:, n, :], in_=t_ap, func=AF.Identity,
                                               bias=bs[:, n:n + 1], scale=gs[:, n:n + 1]),
                 reads=[t_tok, t_gs, t_bs], writes=[t_hb])
        c.dma("sp", s_o1, x1v[:, :, t * 512:(t + 1) * 512], xb[:], reads=[t_xb], writes=[t_o1])
        c.dma("sp", s_o2, h2v[:, :, h2off + t * 512:h2off + (t + 1) * 512], hb[:], reads=[t_hb], writes=[t_o2])


NEG = 30000.0
HD = 128
FOX_SWEEP = 4


def emit_fox(c, hsrc, fw, wf, bfb, cf, ogT, nheads=8):
    hv = hsrc.rearrange("(kc p) t -> p kc t", p=128)
    ogv = ogT.rearrange("(n p) t -> p n t", p=128)
    fwb, fvb, t_fw = fw
    wfv = wf.rearrange("(kc p) h -> p kc h", p=128)
    NS = FOX_SWEEP
    NT = S // 512

    cft, t_cf = load_vecs(c, "cf", cf, 896)
    bft, t_bf = load_vecs(c, "bfb", bfb, nheads)
    tri, ones_f, ident, mpos = cft[:, 0:128], cft[:, 128:256], cft[:, 256:384], cft[:, 384:896]
    ones_b = c.sb("ones_b", [128, 128], BF16)
    t_ones_b = Tok()
    c.op("dve", lambda e: e.memset(ones_b[:], 1.0), writes=[t_ones_b])
    one_c = c.sb("one_c", [128, 1], F32)
    t_one_c = Tok()
    c.op("dve", lambda e: e.memset(one_c[:], 1.0), writes=[t_one_c])
    wft = c.sb("wft", [128, KC, nheads], BF16)
    t_wf = Tok()
    s_wf = c.dma_sem("wf")
    c.dma("pool", s_wf, wft[:], wfv, writes=[t_wf])

    work = [c.ps(f"wk{i}", [128, 512]) for i in range(4)]
    t_work = [Tok(f"wk{i}") for i in range(4)]
    Ob = [c.ps(f"ob{i}", [128, 512]) for i in range(2)]
    t_Ob = [Tok(f"ob{i}") for i in range(2)]
    Lb = [c.ps(f"lb{i}", [128, 512]) for i in range(2)]
    t_Lb = [Tok(f"lb{i}") for i in range(2)]
    wk_i = [0]

    def nwork():
        i = wk_i[0] % 4
        wk_i[0] += 1
        return work[i], t_work[i]

    KT = c.sb("KT", [128, NS, S], BF16)
    t_KT = [Tok(f"kt{t}") for t in range(NT)]
    V = c.sb("V", [128, S // 128, NS * HD], BF16)
    t_V = [Tok(f"v{t}") for t in range(NT)]
    Gk = c.sb("Gk", [128, S // 128, NS], F32)
    t_Gk = [Tok(f"gk{t}") for t in range(NT)]
    R = c.sb("R", [128, NS], F32)
    t_R = Tok("R")
    hT = c.sb("hT", [128, KC, 512], BF16)
    t_hT = Tok("hT")
    s_h = c.dma_sem("h")
    QT = Ring(c, "QT", 2, [128, NS, 512], BF16, dma=False)
    GT = Ring(c, "GT", 2, [128, NS, 512], BF16, dma=False)
    wr = Ring(c, "w", 4, [128, KC, 128], BF16)
    wvb = c.sb("wvb", [128, KC, NS * HD], BF16)
    t_wvb = Tok("wvb")
    s_wvb = c.dma_sem("wvb")
    PT = Ring(c, "PT", 4, [128, 512], BF16, dma=False)
    sbr = Ring(c, "sbr", 3, [128, 512], F32, dma=False)
    Dg = c.sb("Dg", [128, 512], F32)
    t_Dg = Tok()
    gqa = c.sb("gqa", [128, NS, 512], F32)
    t_gqa = [Tok(f"gqa{n}") for n in range(NS)]
    gqda = c.sb("gqda", [128, NS, 512], F32)
    t_gqda = [Tok(f"gqda{n}") for n in range(NS)]
    den = c.sb("den", [128, 512], F32)
    t_den = Tok()
    rec = c.sb("rec", [128, 512], F32)
    t_rec = Tok()
    ogb = Ring(c, "ogb", 2, [128, NS, 512], BF16, dma=False)
    zf = c.sb("zf", [128, 4, NS], F32)
    t_zf = [Tok() for _ in range(4)]
    spf = c.sb("spf", [128, 4, NS], F32)
    t_spf = [Tok() for _ in range(4)]
    s_og = c.dma_sem("og")
    t_ogd = Tok("ogT")
    c.out_toks.append(t_ogd)
    ob_i = 0

    for sw in range(nheads // NS):
        hc0 = sw * NS * HD
        c.dma("pool", s_wvb, wvb[:], fvb[sw].rearrange("p (kc n) -> p kc n", kc=KC), reads=[t_fw], writes=[t_wvb])
        c.op("dve", lambda e: e.memset(R[:], 0.0), writes=[t_R])
        if sw == 0:
            c.dma("sp", s_h, hT[:], hv[:, :, 0:512], writes=[t_hT])
        for t in range(NT):
            tsl = slice(t * 512, (t + 1) * 512)

            def proj(kind, n):
                c.pump(1)
                wb, t_wb, s_wb = wr.next()
                c.dma("pool", s_wb, wb[:], fwb[kind][sw * NS + n].rearrange("p (kc n) -> p kc n", kc=KC),
                      reads=[t_fw], writes=[t_wb])
                pb, t_pb = nwork()
                mm_group(c, pb[:], t_pb, [(wb[:, k, :], hT[:, k, :]) for k in range(KC)], reads=[t_wb, t_hT])
                return pb, t_pb

            for n in range(NS):
                pb, t_pb = proj("k", n)
                c.op("dve", lambda e: e.tensor_copy(out=KT[:, n, tsl], in_=pb[:]),
                     reads=[t_pb], writes=[t_KT[t]])
            for tb in range(4):
                blk = t * 4 + tb
                bsl = slice(tb * 128, (tb + 1) * 128)
                pb, t_pb = nwork()
                mm_group(c, pb[:], t_pb, [(hT[:, k, bsl], wvb[:, k, :]) for k in range(KC)],
                         reads=[t_wvb, t_hT])
                c.op("act", lambda e: e.activation(out=V[:, blk, :], in_=pb[:], func=AF.Identity),
                     reads=[t_pb], writes=[t_V[t]])
                pz, t_pz = nwork()
                mm_group(c, pz[:, 0:NS], t_pz, [(hT[:, k, bsl], wft[:, k, sw * NS:(sw + 1) * NS]) for k in range(KC)],
                         reads=[t_wf, t_hT])
                c.op("dve", lambda e: e.tensor_tensor(out=zf[:, tb, :], in0=pz[:, 0:NS], in1=bft[:, sw * NS:(sw + 1) * NS],
                                                      op=ALU.add), reads=[t_pz, t_bf], writes=[t_zf[tb]])
                c.op("act", lambda e: e.activation(out=spf[:, tb, :], in_=zf[:, tb, :], func=AF.Exp, scale=-1.0),
                     reads=[t_zf[tb]], writes=[t_spf[tb]])
                c.op("act", lambda e: e.activation(out=spf[:, tb, :], in_=spf[:, tb, :], func=AF.Ln, bias=one_c[:, 0:1]),
                     reads=[t_spf[tb], t_one_c], writes=[t_spf[tb]])
            qb, t_qb, _ = QT.next()
            gb, t_gb, _ = GT.next()
            for n in range(NS):
                pb, t_pb = proj("q", n)
                c.op("act", lambda e: e.activation(out=qb[:, n, :], in_=pb[:], func=AF.Identity,
                                                   scale=float(HD ** -0.5)),
                     reads=[t_pb], writes=[t_qb])
            for n in range(NS):
                pb, t_pb = proj("g", n)
                c.op("act", lambda e: e.activation(out=gb[:, n, :], in_=pb[:], func=AF.Exp, scale=-1.0),
                     reads=[t_pb], writes=[t_gb])
            if t + 1 < NT:
                c.dma("sp", s_h, hT[:], hv[:, :, (t + 1) * 512:(t + 2) * 512], writes=[t_hT])
            elif sw + 1 < nheads // NS:
                c.dma("sp", s_h, hT[:], hv[:, :, 0:512], writes=[t_hT])
            for tb in range(4):
                blk = t * 4 + tb
                pg, t_pg = nwork()
                c.op("pe", lambda e: e.matmul(pg[:, 0:NS], tri, spf[:, tb, :], start=True, stop=False),
                     reads=[t_cf, t_spf[tb]], writes=[t_pg], track=False)
                c.op("pe", lambda e: e.matmul(pg[:, 0:NS], ones_f, R[:], start=False, stop=True),
                     reads=[t_cf, t_R, t_spf[tb]], writes=[t_pg])
                c.op("dve", lambda e: e.tensor_copy(out=Gk[:, blk, :], in_=pg[:, 0:NS]),
                     reads=[t_pg], writes=[t_Gk[t]])
                c.op("dve", lambda e: e.tensor_tensor(out=R[:], in0=R[:], in1=spf[:, tb, :], op=ALU.add),
                     reads=[t_R, t_spf[tb]], writes=[t_R])
            for n in range(NS):
                for j in range(4):
                    c.op("dve", lambda e: e.tensor_scalar(out=Dg[:, j * 128:(j + 1) * 128], in0=ident,
                                                          scalar1=Gk[:, t * 4 + j, n:n + 1], scalar2=None,
                                                          op0=ALU.mult),
                         reads=[t_cf, t_Gk[t]], writes=[t_Dg])
                pq, t_pq = nwork()
                c.op("pe", lambda e: e.matmul(pq[:], ones_f, Dg[:], start=True, stop=True),
                     reads=[t_cf, t_Dg], writes=[t_pq])
                c.op("act", lambda e: e.activation(out=gqa[:, n, :], in_=pq[:], func=AF.Identity),
                     reads=[t_pq], writes=[t_gqa[n]])
                c.op("pool", lambda e: e.tensor_tensor(out=gqda[:, n, :], in0=gqa[:, n, :], in1=mpos, op=ALU.add),
                     reads=[t_gqa[n], t_cf], writes=[t_gqda[n]])
            og_b, t_og, _ = ogb.next()
            nkb = 4 * t + 4
            banks = {}
            for n in range(NS):
                banks[n] = (Ob[ob_i % 2], t_Ob[ob_i % 2], Lb[ob_i % 2], t_Lb[ob_i % 2])
                ob_i += 1
            pending = []

            def emit_scores(n, kb):
                j = kb - 4 * t
                q0 = 128 * j if j > 0 else 0
                N = 512 - q0
                gq, t_gq, gqd, t_gqd = gqa[:, n, :], t_gqa[n], gqda[:, n, :], t_gqda[n]
                ps_, t_ps = nwork()
                c.op("pe", lambda e: e.matmul(ps_[:, 0:N], KT[:, n, kb * 128:(kb + 1) * 128], qb[:, n, q0:512],
                                              start=True, stop=True),
                     reads=[t_KT[kb // 4], t_qb], writes=[t_ps])
                sb_, t_sb, _ = sbr.next()
                if j < 0:
                    c.op("dve", lambda e: e.tensor_tensor(out=sb_[:, 0:N], in0=ps_[:, 0:N], in1=gq[:, q0:512],
                                                          op=ALU.subtract),
                         reads=[t_ps, t_gq], writes=[t_sb])
                else:
                    c.op("dve", lambda e: e.tensor_tensor(out=sb_[:, 0:128], in0=ps_[:, 0:128],
                                                          in1=gqd[:, q0:q0 + 128], op=ALU.subtract),
                         reads=[t_ps, t_gqd], writes=[t_sb])
                    if N > 128:
                        c.op("dve", lambda e: e.tensor_tensor(out=sb_[:, 128:N], in0=ps_[:, 128:N],
                                                              in1=gq[:, q0 + 128:512], op=ALU.subtract),
                             reads=[t_ps, t_gq], writes=[t_sb])
                pt, t_pt, _ = PT.next()
                c.op("act", lambda e: e.activation(out=pt[:, 0:N], in_=sb_[:, 0:N], func=AF.Exp,
                                                   bias=Gk[:, kb, n:n + 1]),
                     reads=[t_sb, t_Gk[kb // 4]], writes=[t_pt])
                return (n, kb, q0, N, pt, t_pt)

            def emit_pv(item):
                n, kb, q0, N, pt, t_pt = item
                O, t_O, L, t_L = banks[n]
                last = kb == nkb - 1
                c.op("pe", lambda e: e.matmul(O[:, q0:512], V[:, kb, n * HD:(n + 1) * HD], pt[:, 0:N],
                                              start=(kb == 0), stop=last),
                     reads=[t_V[kb // 4], t_pt], writes=[t_O], track=last)
                c.op("pe", lambda e: e.matmul(L[:, q0:512], ones_b[:], pt[:, 0:N],
                                              start=(kb == 0), stop=last),
                     reads=[t_ones_b, t_pt], writes=[t_L], track=True)
                if last:
                    c.op("dve", lambda e: e.scalar_tensor_tensor(out=den[:], in0=gb[:, n, :], scalar=1.0, in1=L[:],
                                                                 op0=ALU.add, op1=ALU.mult),
                         reads=[t_gb, t_L], writes=[t_den])
                    c.op("dve", lambda e: e.reciprocal(out=rec[:], in_=den[:]), reads=[t_den], writes=[t_rec])
                    c.op("dve", lambda e: e.tensor_tensor(out=og_b[:, n, :], in0=O[:], in1=rec[:], op=ALU.mult),
                         reads=[t_O, t_rec], writes=[t_og])

            LA = 2
            for n in range(NS):
                for kb in range(nkb):
                    pending.append(emit_scores(n, kb))
                    if len(pending) > LA:
                        emit_pv(pending.pop(0))
            while pending:
                emit_pv(pending.pop(0))
            c.dma("sp", s_og, ogv[:, sw * NS:(sw + 1) * NS, tsl], og_b[:], reads=[t_og], writes=[t_ogd])


def fox_consts():
    k = np.arange(128)[:, None]
    q = np.arange(128)[None, :]
    tri = (k <= q).astype(np.float32)
    ones = np.ones((128, 128), np.float32)
    ident = np.eye(128, dtype=np.float32)
    mp = np.where(q >= k, 0.0, NEG).astype(np.float32)
    return np.ascontiguousarray(np.concatenate([tri, ones, ident, mp, mp, mp, mp], axis=1))


GLA_TAU = 16.0


def emit_gla(c, hsrc, wq, wk, wv, wr, wa1, wa2, ba, gbc, cg, ogT, nheads=2):
    hv = hsrc.rearrange("(kc p) t -> p kc t", p=128)
    ogv = ogT.rearrange("(n p) t -> p n t", p=128)
    wqv = wq.rearrange("(kc p) n -> p kc n", p=128)
    wkv = wk.rearrange("(kc p) n -> p kc n", p=128)
    wvv = wv.rearrange("(kc p) n -> p kc n", p=128)
    wrv = wr.rearrange("(kc p) n -> p kc n", p=128)
    wa1v = wa1.rearrange("(kc p) n -> p kc n", p=128)
    NT = S // 512
    DK, DV = 256, 512

    cgt, t_cg = load_vecs(c, "cg", cg, 512)
    gbt, t_gb = load_vecs(c, "gbc", gbc, 512)
    triN, triU, tri01, ident_f = cgt[:, 0:128], cgt[:, 128:256], cgt[:, 256:384], cgt[:, 384:512]
    ident_b = c.sb("ident_b", [128, 128], BF16)
    t_idb = Tok()
    c.op("dve", lambda e: e.tensor_copy(out=ident_b[:], in_=ident_f), reads=[t_cg], writes=[t_idb])
    one_c = c.sb("one_c", [128, 1], F32)
    eps_c = c.sb("eps_c", [128, 1], F32)
    t_cc = Tok()
    c.op("dve", lambda e: e.memset(one_c[:], 1.0), writes=[t_cc])
    c.op("dve", lambda e: e.memset(eps_c[:], RMS_EPS), writes=[t_cc])

    wq_b = c.sb("wq_b", [128, KC, DK], BF16)
    wk_b = c.sb("wk_b", [128, KC, DK], BF16)
    wv_b = c.sb("wv_b", [128, KC, DV], BF16)
    wr_b = c.sb("wr_b", [128, KC, DV], BF16)
    wa1_b = c.sb("wa1_b", [128, KC, 16], BF16)
    wa2a = c.sb("wa2a", [32, DK], BF16)
    t_W = Tok("W")
    s_W = c.dma_sem("W")
    c.dma("pool", s_W, wa1_b[:], wa1v, writes=[t_W])

    work = [c.ps(f"wk{i}", [128, 512]) for i in range(6)]
    t_work = [Tok(f"wk{i}") for i in range(6)]
    pcb = [c.ps(f"pc{i}", [128, 512]) for i in range(2)]
    t_pcb = [Tok(f"pc{i}") for i in range(2)]
    wk_i = [0]

    def nwork():
        i = wk_i[0] % 6
        wk_i[0] += 1
        return work[i], t_work[i]

    hT = c.sb("hT", [128, KC, 512], BF16)
    t_hT = Tok("hT")
    s_h = c.dma_sem("h")
    gaug = Ring(c, "gaug", 2, [32, 512], BF16, dma=False)
    for i in range(2):
        c.op("dve", lambda e: e.memset(gaug.bufs[i][:], 1.0), writes=[gaug.toks[i]])
    spr = Ring(c, "sp", 2, [128, 4, DK], F32, dma=False)
    ekd = Ring(c, "ekd", 2, [128, DK], F32, dma=False)
    ekr = Ring(c, "ek", 2, [128, 512], F32, dma=False)
    qdT = Ring(c, "qdT", 2, [128, 2, 512], BF16, dma=False)
    kiT = Ring(c, "kiT", 2, [128, 2, 512], BF16, dma=False)
    eq = Ring(c, "eq", 2, [128, 2, 512], F32, dma=False)
    kdec = Ring(c, "kdec", 2, [128, 4, DK], BF16, dma=False)
    Vr = Ring(c, "V", 2, [128, 4, DV], BF16, dma=False)
    rraw = Ring(c, "rraw", 2, [128, 4, DV], BF16, dma=False)
    rs = Ring(c, "rs", 2, [128, 4, DV], BF16, dma=False)
    attm = Ring(c, "attm", 2, [128, 128], BF16, dma=False)
    state = c.sb("state", [128, 2, DV], F32)
    t_state = Tok("state")
    stb = Ring(c, "stb", 2, [128, 2, DV], BF16, dma=False)
    junk = c.sb("junk", [128, DV], BF16)
    t_junk = Tok()
    ssq = Ring(c, "ssq", 2, [128, 1], F32, dma=False)
    onr = Ring(c, "on", 2, [128, DV], F32, dma=False)
    ogt = Ring(c, "ogt", 2, [128, DV], BF16, dma=False)
    ogo = Ring(c, "ogo", 2, [128, 4, 512], BF16, dma=False)
    s_og = c.dma_sem("og")
    t_ogd = Tok("ogT")
    c.out_toks.append(t_ogd)

    def stage_a(sw, t):
        tsl = slice(t * 512, (t + 1) * 512)
        c.dma("sp", s_h, hT[:], hv[:, :, tsl], writes=[t_hT])
        ga, t_ga, _ = gaug.next()
        pg, t_pg = nwork()
        mm_group(c, pg[0:16, :], t_pg, [(wa1_b[:, k, :], hT[:, k, :]) for k in range(KC)], reads=[t_W, t_hT])
        c.op("act", lambda e: e.activation(out=ga[0:16, :], in_=pg[0:16, :], func=AF.Identity),
             reads=[t_pg], writes=[t_ga])
        sp, t_sp, _ = spr.next()
        kd, t_kd, _ = kdec.next()
        Vb, t_Vb, _ = Vr.next()
        rr, t_rr, _ = rraw.next()
        rsb, t_rs, _ = rs.next()
        for tb in range(4):
            bsl = slice(tb * 128, (tb + 1) * 128)
            pz, t_pz = nwork()
            c.op("pe", lambda e: e.matmul(pz[:, 0:DK], ga[0:17, bsl], wa2a[0:17, :], start=True, stop=True),
                 reads=[t_ga, t_W], writes=[t_pz])
            c.op("act", lambda e: e.activation(out=sp[:, tb, :], in_=pz[:, 0:DK], func=AF.Exp, scale=-1.0),
                 reads=[t_pz], writes=[t_sp])
            c.op("act", lambda e: e.activation(out=sp[:, tb, :], in_=sp[:, tb, :], func=AF.Ln, bias=one_c[:, 0:1]),
                 reads=[t_sp, t_cc], writes=[t_sp])
            pd, t_pd = nwork()
            c.op("pe", lambda e: e.matmul(pd[:, 0:DK], triU, sp[:, tb, :], start=True, stop=True),
                 reads=[t_cg, t_sp], writes=[t_pd])
            ek_, t_ek_, _ = ekd.next()
            c.op("act", lambda e: e.activation(out=ek_[:], in_=pd[:, 0:DK], func=AF.Exp),
                 reads=[t_pd], writes=[t_ek_])
            pk, t_pk = nwork()
            mm_group(c, pk[:, 0:DK], t_pk, [(hT[:, k, bsl], wk_b[:, k, :]) for k in range(KC)], reads=[t_W, t_hT])
            c.op("dve", lambda e: e.tensor_tensor(out=kd[:, tb, :], in0=pk[:, 0:DK], in1=ek_[:], op=ALU.mult),
                 reads=[t_pk, t_ek_], writes=[t_kd])
            pv, t_pv = nwork()
            mm_group(c, pv[:], t_pv, [(hT[:, k, bsl], wv_b[:, k, :]) for k in range(KC)], reads=[t_W, t_hT])
            c.op("act", lambda e: e.activation(out=Vb[:, tb, :], in_=pv[:], func=AF.Identity),
                 reads=[t_pv], writes=[t_Vb])
            pr, t_pr = nwork()
            mm_group(c, pr[:], t_pr, [(hT[:, k, bsl], wr_b[:, k, :]) for k in range(KC)], reads=[t_W, t_hT])
            c.op("dve", lambda e: e.tensor_copy(out=rr[:, tb, :], in_=pr[:]), reads=[t_pr], writes=[t_rr])
            for dc in range(2):
                c.op("pe", lambda e: e.matmul(pcb[dc][:, bsl], sp[:, tb, dc * 128:(dc + 1) * 128], triN,
                                              start=True, stop=True),
                     reads=[t_cg, t_sp], writes=[t_pcb[dc]])
        c.op("act", lambda e: e.activation(out=rsb[:], in_=rr[:], func=AF.Silu), reads=[t_rr], writes=[t_rs])
        eqb, t_eq, _ = eq.next()
        qd, t_qd, _ = qdT.next()
        ki, t_ki, _ = kiT.next()
        for dc in range(2):
            c.op("act", lambda e: e.activation(out=eqb[:, dc, :], in_=pcb[dc][:], func=AF.Exp),
                 reads=[t_pcb[dc]], writes=[t_eq])
            ekb, t_ekb, _ = ekr.next()
            c.op("act", lambda e: e.activation(out=ekb[:], in_=pcb[dc][:], func=AF.Exp, scale=-1.0),
                 reads=[t_pcb[dc]], writes=[t_ekb])
            pq, t_pq = nwork()
            mm_group(c, pq[:], t_pq, [(wq_b[:, k, dc * 128:(dc + 1) * 128], hT[:, k, :]) for k in range(KC)],
                     reads=[t_W, t_hT])
            c.op("dve", lambda e: e.scalar_tensor_tensor(out=qd[:, dc, :], in0=pq[:], scalar=float(DK ** -0.5),
                                                         in1=eqb[:, dc, :], op0=ALU.mult, op1=ALU.mult),
                 reads=[t_pq, t_eq], writes=[t_qd])
            pk2, t_pk2 = nwork()
            mm_group(c, pk2[:], t_pk2, [(wk_b[:, k, dc * 128:(dc + 1) * 128], hT[:, k, :]) for k in range(KC)],
                     reads=[t_W, t_hT])
            c.op("dve", lambda e: e.tensor_tensor(out=ki[:, dc, :], in0=pk2[:], in1=ekb[:], op=ALU.mult),
                 reads=[t_pk2, t_ekb], writes=[t_ki])
        return dict(qd=qd, t_qd=t_qd, ki=ki, t_ki=t_ki, eq=eqb, t_eq=t_eq, kd=kd, t_kd=t_kd,
                    V=Vb, t_V=t_Vb, rs=rsb, t_rs=t_rs)

    def stage_b(sw, t, A):
        tsl = slice(t * 512, (t + 1) * 512)
        oo, t_oo, _ = ogo.next()
        for tb in range(4):
            bsl = slice(tb * 128, (tb + 1) * 128)
            pa, t_pa = nwork()
            mm_group(c, pa[:, 0:128], t_pa, [(A["ki"][:, dc, bsl], A["qd"][:, dc, bsl]) for dc in range(2)],
                     reads=[A["t_ki"], A["t_qd"]])
            am, t_am, _ = attm.next()
            c.op("dve", lambda e: e.tensor_tensor(out=am[:], in0=pa[:, 0:128], in1=tri01, op=ALU.mult),
                 reads=[t_pa, t_cg], writes=[t_am])
            sb_old, t_sb_old, _ = stb.cur()
            po, t_po = nwork()
            c.op("pe", lambda e: e.matmul(po[:], am[:], A["V"][:, tb, :], start=True, stop=False),
                 reads=[t_am, A["t_V"]], writes=[t_po], track=False)
            for dc in range(2):
                c.op("pe", lambda e: e.matmul(po[:], A["qd"][:, dc, bsl], sb_old[:, dc, :], start=False, stop=(dc == 1)),
                     reads=[A["t_qd"], t_sb_old, t_am, A["t_V"]], writes=[t_po], track=(dc == 1))
            sb_new, t_sb_new, _ = stb.next()
            for dc in range(2):
                pkv, t_pkv = nwork()
                c.op("pe", lambda e: e.matmul(pkv[:], A["kd"][:, tb, dc * 128:(dc + 1) * 128], A["V"][:, tb, :],
                                              start=True, stop=True),
                     reads=[A["t_kd"], A["t_V"]], writes=[t_pkv])
                col = tb * 128 + 127
                c.op("dve", lambda e: e.scalar_tensor_tensor(out=state[:, dc, :], in0=state[:, dc, :],
                                                             scalar=A["eq"][:, dc, col:col + 1], in1=pkv[:],
                                                             op0=ALU.mult, op1=ALU.add),
                     reads=[t_state, A["t_eq"], t_pkv], writes=[t_state])
                c.op("act", lambda e: e.activation(out=sb_new[:, dc, :], in_=state[:, dc, :], func=AF.Identity),
                     reads=[t_state], writes=[t_sb_new])
            sq, t_sq, _ = ssq.next()
            c.op("act", lambda e: e.activation(out=junk[:], in_=po[:], func=AF.Square, accum_out=sq[:]),
                 reads=[t_po], writes=[t_junk, t_sq])
            c.op("act", lambda e: e.activation(out=sq[:], in_=sq[:], func=AF.Ln, scale=1.0 / DV, bias=eps_c[:, 0:1]),
                 reads=[t_sq, t_cc], writes=[t_sq])
            c.op("act", lambda e: e.activation(out=sq[:], in_=sq[:], func=AF.Exp, scale=-0.5),
                 reads=[t_sq], writes=[t_sq])
            on, t_on, _ = onr.next()
            c.op("dve", lambda e: e.scalar_tensor_tensor(out=on[:], in0=po[:], scalar=sq[:, 0:1], in1=gbt[:],
                                                         op0=ALU.mult, op1=ALU.mult),
                 reads=[t_po, t_sq, t_gb], writes=[t_on])
            og_, t_og_, _ = ogt.next()
            c.op("pool", lambda e: e.tensor_tensor(out=og_[:], in0=on[:], in1=A["rs"][:, tb, :], op=ALU.mult),
                 reads=[t_on, A["t_rs"]], writes=[t_og_])
            pt, t_pt = nwork()
            for fc in range(4):
                c.op("pe", lambda e: e.matmul(pt[:, fc * 128:(fc + 1) * 128], og_[:, fc * 128:(fc + 1) * 128],
                                              ident_b[:], start=True, stop=True),
                     reads=[t_og_, t_idb], writes=[t_pt], track=(fc == 3))
            c.op("dve", lambda e: e.tensor_copy(out=oo[:, :, bsl], in_=pt[:].rearrange("p (f t) -> p f t", f=4)),
                 reads=[t_pt], writes=[t_oo])
        c.dma("sp", s_og, ogv[:, sw * 4:(sw + 1) * 4, tsl], oo[:], reads=[t_oo], writes=[t_ogd])

    for sw in range(nheads):
        c.dma("pool", s_W, wq_b[:], wqv[:, :, sw * DK:(sw + 1) * DK], writes=[t_W])
        c.dma("pool", s_W, wk_b[:], wkv[:, :, sw * DK:(sw + 1) * DK], writes=[t_W])
        c.dma("pool", s_W, wv_b[:], wvv[:, :, sw * DV:(sw + 1) * DV], writes=[t_W])
        c.dma("pool", s_W, wr_b[:], wrv[:, :, sw * DV:(sw + 1) * DV], writes=[t_W])
        c.dma("pool", s_W, wa2a[0:16, :], wa2[:, sw * DK:(sw + 1) * DK], writes=[t_W])
        c.dma("pool", s_W, wa2a[16:17, :], ba[:, sw * DK:(sw + 1) * DK], writes=[t_W])
        c.op("dve", lambda e: e.memset(state[:], 0.0), writes=[t_state])
        sb0, t_sb0, _ = stb.next()
        c.op("dve", lambda e: e.memset(sb0[:], 0.0), writes=[t_sb0])
        prev = stage_a(sw, 0)
        for t in range(NT):
            nxt = stage_a(sw, t + 1) if t + 1 < NT else None
            stage_b(sw, t, prev)
            prev = nxt


def gla_consts():
    a = np.arange(128)[:, None]
    b = np.arange(128)[None, :]
    triN = np.where(a <= b, -1.0 / GLA_TAU, 0.0).astype(np.float32)
    triU = np.where(a > b, -1.0 / GLA_TAU, 0.0).astype(np.float32)
    tri01 = (a <= b).astype(np.float32)
    ident = np.eye(128, dtype=np.float32)
    return np.ascontiguousarray(np.concatenate([triN, triU, tri01, ident], axis=1))


def emit_sconv(c, hsrc, wb, wc, wu, cw, ogT, nsw=1):
    hv = hsrc.rearrange("(kc p) t -> p kc t", p=128)
    ogv = ogT.rearrange("(n p) t -> p n t", p=128)
    NT = S // 512
    NCH = 8
    NCHT = NCH * nsw
    cwt, t_cw = load_vecs(c, "cw", cw, 3 * NCHT)
    W = {}
    t_W = Tok("W")
    s_W = c.dma_sem("W")
    for name in ("b", "c", "u"):
        W[name] = c.sb("w_" + name, [128, KC, 1024], BF16)
    work = [c.ps(f"wk{i}", [128, 512]) for i in range(6)]
    t_work = [Tok(f"wk{i}") for i in range(6)]
    wk_i = [0]

    def nwork():
        i = wk_i[0] % 6
        wk_i[0] += 1
        return work[i], t_work[i]

    hT = Ring(c, "hT", 2, [128, KC, 512], BF16)
    pext = c.sb("pext", [128, NCH, 514], F32)
    t_pext = [Tok(f"pe{n}") for n in range(NCH)]
    usb = Ring(c, "usb", 2, [128, 512], F32, dma=False)
    yb = Ring(c, "yb", 2, [128, 512], F32, dma=False)
    ogo = Ring(c, "ogo", 2, [128, NCH, 512], BF16, dma=False)
    s_og = c.dma_sem("og")
    t_ogd = Tok("ogT")
    c.out_toks.append(t_ogd)
    for sw in range(nsw):
        for name, ap in (("b", wb), ("c", wc), ("u", wu)):
            v = ap.rearrange("(kc p) n -> p kc n", p=128)
            for q in range(4):
                c.dma("pool", s_W, W[name][:, :, q * 256:(q + 1) * 256],
                      v[:, :, sw * 1024 + q * 256:sw * 1024 + (q + 1) * 256], writes=[t_W])
        c.op("dve", lambda e: e.memset(pext[:], 0.0), writes=t_pext)
        for t in range(NT):
            tsl = slice(t * 512, (t + 1) * 512)
            hb, t_hb, s_hb = hT.next()
            c.dma("sp", s_hb, hb[:], hv[:, :, tsl], writes=[t_hb])
            oo, t_oo, _ = ogo.next()
            for n in range(NCH):
                nsl = slice(n * 128, (n + 1) * 128)
                cn = sw * NCH + n
                pu, t_pu = nwork()
                mm_group(c, pu[:], t_pu, [(W["u"][:, k, nsl], hb[:, k, :]) for k in range(KC)], reads=[t_W, t_hb])
                ub, t_ub, _ = usb.next()
                c.op("act", lambda e: e.activation(out=ub[:], in_=pu[:], func=AF.Identity), reads=[t_pu], writes=[t_ub])
                pc_, t_pc = nwork()
                mm_group(c, pc_[:], t_pc, [(W["c"][:, k, nsl], hb[:, k, :]) for k in range(KC)], reads=[t_W, t_hb])
                if t > 0:
                    c.op("pool", lambda e: e.tensor_copy(out=pext[:, n, 0:2], in_=pext[:, n, 512:514]),
                         reads=[t_pext[n]], writes=[t_pext[n]])
                c.op("dve", lambda e: e.tensor_tensor(out=pext[:, n, 2:514], in0=pc_[:], in1=ub[:], op=ALU.mult),
                     reads=[t_pc, t_ub, t_pext[n]], writes=[t_pext[n]])
                y, t_y, _ = yb.next()
                c.op("act", lambda e: e.activation(out=y[:], in_=pext[:, n, 2:514], func=AF.Identity,
                                                   scale=cwt[:, 2 * NCHT + cn:2 * NCHT + cn + 1]),
                     reads=[t_pext[n], t_cw], writes=[t_y])
                c.op("dve", lambda e: e.scalar_tensor_tensor(out=y[:], in0=pext[:, n, 1:513],
                                                             scalar=cwt[:, NCHT + cn:NCHT + cn + 1], in1=y[:],
                                                             op0=ALU.mult, op1=ALU.add),
                     reads=[t_pext[n], t_cw, t_y], writes=[t_y])
                c.op("dve", lambda e: e.scalar_tensor_tensor(out=y[:], in0=pext[:, n, 0:512],
                                                             scalar=cwt[:, cn:cn + 1], in1=y[:],
                                                             op0=ALU.mult, op1=ALU.add),
                     reads=[t_pext[n], t_cw, t_y], writes=[t_y])
                pb, t_pb = nwork()
                mm_group(c, pb[:], t_pb, [(W["b"][:, k, nsl], hb[:, k, :]) for k in range(KC)], reads=[t_W, t_hb])
                c.op("dve", lambda e: e.tensor_tensor(out=oo[:, n, :], in0=pb[:], in1=y[:], op=ALU.mult),
                     reads=[t_pb, t_y], writes=[t_oo])
            c.dma("sp", s_og, ogv[:, sw * NCH:(sw + 1) * NCH, tsl], oo[:], reads=[t_oo], writes=[t_ogd])


def vec_pm(v):
    v = np.asarray(v, dtype=np.float32)
    lead = v.shape[:-1]
    n = v.shape[-1] // 128
    v = v.reshape(-1, n, 128)
    return np.ascontiguousarray(v.transpose(2, 0, 1).reshape(128, -1))


def bcast_rows(v, n=128):
    v = np.asarray(v, dtype=np.float32)
    return np.ascontiguousarray(np.broadcast_to(v[None, :], (n, v.shape[0])))


def emit_mod_own(c, cT, ada_w, ada_b, modts, t_modts):
    ct, t_ct = load_vecs(c, "cT", cT, KC)
    cond = c.sb("cond", [128, KC], BF16)
    t_cond = Tok()
    c.op("act", lambda e: e.activation(out=cond[:], in_=ct[:], func=AF.Silu), reads=[t_ct], writes=[t_cond])
    one11 = c.sb("one11", [1, 1], F32)
    t_one = Tok()
    c.op("dve", lambda e: e.memset(one11[:], 1.0), writes=[t_one])
    awr = Ring(c, "aw", 3, [128, KC, 512], BF16)
    row = c.sb("row", [1, 6 * D], F32)
    t_row = Tok()
    abt = c.sb("abt", [1, 6 * D], F32)
    t_ab = Tok()
    s_ab = c.dma_sem("ab")
    work = [c.ps(f"wk{i}", [128, 512]) for i in range(4)]
    t_work = [Tok(f"wk{i}") for i in range(4)]
    pm = c.ps("pm", [128, 96])
    t_pm = Tok()
    i = 0
    NCC = 6 * D // 512
    for l in range(DEPTH):
        awv = ada_w[l].rearrange("(kc p) n -> p kc n", p=128)
        c.dma("sp", s_ab, abt[:], ada_b[l:l + 1, :], writes=[t_ab])
        for cc in range(NCC):
            csl = slice(cc * 512, (cc + 1) * 512)
            wb, t_wb, s_wb = awr.next()
            c.dma("pool", s_wb, wb[:], awv[:, :, csl], writes=[t_wb])
            pb, t_pb = work[i % 4], t_work[i % 4]
            i += 1
            mm_group(c, pb[0:1, :], t_pb, [(cond[:, k:k + 1], wb[:, k, :]) for k in range(KC)],
                     reads=[t_cond, t_wb])
            c.op("dve", lambda e: e.tensor_tensor(out=row[0:1, csl], in0=pb[0:1, :], in1=abt[0:1, csl], op=ALU.add),
                 reads=[t_pb, t_ab], writes=[t_row])
        for m in range(96):
            c.op("pe", lambda e: e.matmul(pm[:, m:m + 1], row[0:1, m * 128:(m + 1) * 128], one11[0:1, 0:1],
                                          start=True, stop=True),
                 reads=[t_row, t_one], writes=[t_pm], track=(m == 95))
        c.op("dve", lambda e: e.tensor_copy(out=modts[l][:], in_=pm[:]), reads=[t_pm], writes=[t_modts[l]])


FUSED_PASSES = [[(510 * i, 510)] for i in range(7)] + [[(3570, 510), (4080, 16)]]


def emit_hprep(c, xT, mod, hT_dram):
    xv = xT.rearrange("(kc p) t -> p kc t", p=128)
    hv = hT_dram.rearrange("(kc p) t -> p kc t", p=128)
    modt, t_mod = mod
    sc1p = c.sb("sc1p", [128, 16], F32)
    t_sc1p = Tok()
    c.op("dve", lambda e: e.tensor_scalar(out=sc1p[:], in0=modt[:, 16:32], scalar1=1.0, scalar2=None,
                                          op0=ALU.add), reads=[t_mod], writes=[t_sc1p])
    xp = Ring(c, "xp", 3, [128, 4, 512], F32)
    hb = Ring(c, "hb", 2, [128, KC, 512], BF16)
    t_out = Tok()
    for t in range(S // 512):
        tsl = slice(t * 512, (t + 1) * 512)
        h, t_h, s_h = hb.next()
        for q4 in range(4):
            xb, t_xb, s_xb = xp.next()
            c.dma("sp" if q4 % 2 == 0 else "pool", s_xb, xb[:], xv[:, q4 * 4:(q4 + 1) * 4, tsl], writes=[t_xb])
            for kk in range(4):
                kc = q4 * 4 + kk
                c.op("act", lambda e: e.activation(out=h[:, kc, :], in_=xb[:, kk, :], func=AF.Identity,
                                                   bias=modt[:, kc:kc + 1], scale=sc1p[:, kc:kc + 1]),
                     reads=[t_xb, t_mod, t_sc1p], writes=[t_h])
        c.dma("sp", s_h, hv[:, :, tsl], h[:], reads=[t_h], writes=[t_out])


def emit_fox_wconv(c, tag, wq, wk, wv, wg, fwb, fvb, lazy):
    old = c.kind
    c.kind = ""
    key = c.dma_sem(f"bg{tag}")
    c.kind = old
    tok = Tok(f"fconv{tag}")

    def q(dst, src_):
        fn = lambda: c.dma("pool", key, dst, src_, writes=[Tok()], reads=[]) and None
        if lazy:
            c.bg.append(fn)
        else:
            fn()

    for kind, w in (("k", wk), ("q", wq), ("g", wg)):
        v = w.rearrange("(kc p) n -> p kc n", p=128)
        for n in range(KC):
            q(fwb[kind][n].rearrange("p (kc n) -> p kc n", kc=KC), v[:, :, n * 128:(n + 1) * 128])
    v = wv.rearrange("(kc p) n -> p kc n", p=128)
    for sw in range(4):
        q(fvb[sw].rearrange("p (kc n) -> p kc n", kc=KC), v[:, :, sw * 512:(sw + 1) * 512])
    c.bg_final[tok] = (key, 52 * 16)
    return tok


def emit_wconv(c, l, w_up, w_down, wub, wdb, wo, wob):
    old = c.kind
    c.kind = ""
    key = c.dma_sem(f"bg{l}")
    c.kind = old
    tok = Tok(f"wconv{l}")
    wuv = w_up.rearrange("(kc p) n -> p kc n", p=128)
    wdv = w_down.rearrange("(kc p) n -> p kc n", p=128)
    wov = wo.rearrange("(kc p) n -> p kc n", p=128)

    def q(dst, src_):
        c.bg.append(lambda: c.dma("pool", key, dst, src_, writes=[Tok()]))

    c.bg_final[tok] = (key, 120 * 16)
    for n in range(KC):
        q(wob[n].rearrange("p (kc n) -> p kc n", kc=KC), wov[:, :, n * 128:(n + 1) * 128])
    for j in range(NJ):
        dst = wub[j].rearrange("p (kc n) -> p kc n", kc=KC)
        for h in range(2):
            q(dst[:, :, h * 128:(h + 1) * 128], wuv[:, :, h * DFF + j * 128:h * DFF + (j + 1) * 128])
    for n in range(KC):
        q(wdb[n].rearrange("p (kc n) -> p kc n", kc=NJ), wdv[:, :, n * 128:(n + 1) * 128])
    return tok


N_FOX = 2


def build_fused():
    nc = bass.Bass("TRN2", target_bir_lowering=False)

    def din(name, shape, dt=F32):
        return nc.dram_tensor(name, list(shape), dt, kind="ExternalInput").ap()

    def dint(name, shape, dt):
        return nc.dram_tensor(name, list(shape), dt).ap()

    xT = din("xT", [D, S])
    cT = din("cT", [128, KC])
    ada_w = din("ada_w", [DEPTH, D, 6 * D])
    ada_b = din("ada_b", [DEPTH, 6 * D])
    lnv = din("lnv", [DEPTH, 128, 64])
    fox_wq = din("fox_wq", [N_FOX, D, D])
    fox_wk = din("fox_wk", [N_FOX, D, D])
    fox_wv = din("fox_wv", [N_FOX, D, D])
    fox_wg = din("fox_wg", [N_FOX, D, D])
    fox_wf = din("fox_wf", [N_FOX, D, 16])
    fox_bfb = din("fox_bfb", [N_FOX, 128, 16])
    fox_wo = din("fox_wo", [N_FOX, D, D])
    fox_cf = din("fox_cf", [128, 896])
    gla_wq = din("gla_wq", [1, D, 1024])
    gla_wk = din("gla_wk", [1, D, 1024])
    gla_wv = din("gla_wv", [1, D, D])
    gla_wr = din("gla_wr", [1, D, D])
    gla_wa1 = din("gla_wa1", [1, D, 16])
    gla_wa2 = din("gla_wa2", [1, 16, 1024])
    gla_ba = din("gla_ba", [1, 1024])
    gla_gbc = din("gla_gbc", [128, 512])
    gla_cg = din("gla_cg", [128, 512])
    gla_wo = din("gla_wo", [1, D, D])
    conv_w_in = din("conv_w_in", [1, D, 3 * D])
    conv_cw = din("conv_cw", [128, 48])
    conv_w_out = din("conv_w_out", [1, D, D])
    ffn_w_up = din("ffn_w_up", [DEPTH, D, 2 * DFF])
    ffn_w_down = din("ffn_w_down", [DEPTH, DFF, D])
    ffn_cw = din("ffn_cw", [DEPTH, 128, 264])
    ffn_cb = din("ffn_cb", [DEPTH, 128, 88])
    outT = nc.dram_tensor("outT", [D, S], F32, kind="ExternalOutput").ap()

    with ExitStack() as es:
        c = Ctx(nc, es)
        modts = [c.sb(f"modt{l}", [128, 96], F32) for l in range(DEPTH)]
        t_modts = [Tok(f"modt{l}") for l in range(DEPTH)]
        with c.phase("mod", "M"):
            emit_mod_own(c, cT, ada_w, ada_b, modts, t_modts)
        x_cur = xT
        conv = []
        wos = [fox_wo[0], gla_wo[0], conv_w_out[0], fox_wo[1]]
        for i in range(DEPTH):
            conv.append((dint(f"wub{i}", [NJ, 128, KC * 256], BF16), dint(f"wdb{i}", [KC, 128, NJ * 128], BF16),
                         dint(f"wob{i}", [KC, 128, KC * 128], BF16)))
        fconv = {}
        for jj in range(N_FOX):
            fwb = {k: dint(f"fwb{jj}{k}", [KC, 128, KC * 128], BF16) for k in ("q", "k", "g")}
            fvb = dint(f"fvb{jj}", [4, 128, KC * 512], BF16)
            fconv[jj] = [fwb, fvb, None]
        fconv[0][2] = emit_fox_wconv(c, "f0", fox_wq[0], fox_wk[0], fox_wv[0], fox_wg[0],
                                     fconv[0][0], fconv[0][1], lazy=False)
        c.pump_all()
        t_conv0 = emit_wconv(c, 0, ffn_w_up[0], ffn_w_down[0], conv[0][0], conv[0][1], wos[0], conv[0][2])
        conv[0] = conv[0] + (t_conv0,)
        for i in range(DEPTH):
            kind, j = i % 3, i // 3
            mod = (modts[i], t_modts[i])
            if i > 0:
                c.pump_all()
            hT_d = dint(f"hT{i}", [D, S], BF16)
            with c.phase("hprep", f"H{i}"):
                emit_hprep(c, x_cur, mod, hT_d)
            og = dint(f"og{i}", [D, S], BF16)
            x1 = dint(f"x1_{i}", [D, S], F32)
            h2x = dint(f"h2x{i}", [D, S + 2], BF16)
            x_next = outT if i == DEPTH - 1 else dint(f"x2_{i}", [D, S], F32)
            wub, wdb, wob, t_wconv = conv[i]
            if kind == 0:
                with c.phase("fox", f"A{i}"):
                    emit_fox(c, hT_d, tuple(fconv[j]), fox_wf[j], fox_bfb[j], fox_cf, og, nheads=16)
                wo = fox_wo[j]
            elif kind == 1:
                with c.phase("gla", f"A{i}"):
                    emit_gla(c, hT_d, gla_wq[j], gla_wk[j], gla_wv[j], gla_wr[j], gla_wa1[j],
                             gla_wa2[j], gla_ba[j:j + 1, :], gla_gbc, gla_cg, og, nheads=4)
                wo = gla_wo[j]
            else:
                w_in = conv_w_in[j]
                with c.phase("sconv", f"A{i}"):
                    emit_sconv(c, hT_d, w_in[:, 0:D], w_in[:, D:2 * D], w_in[:, 2 * D:3 * D],
                               conv_cw, og, nsw=2)
                wo = conv_w_out[j]
            c.pump_all()
            if i + 1 < DEPTH:
                t_n = emit_wconv(c, i + 1, ffn_w_up[i + 1], ffn_w_down[i + 1], conv[i + 1][0], conv[i + 1][1],
                                 wos[i + 1], conv[i + 1][2])
                conv[i + 1] = conv[i + 1] + (t_n,)
                if (i + 1) % 3 == 0:
                    jn = (i + 1) // 3
                    fconv[jn][2] = emit_fox_wconv(c, f"f{jn}", fox_wq[jn], fox_wk[jn], fox_wv[jn], fox_wg[jn],
                                                  fconv[jn][0], fconv[jn][1], lazy=True)
            with c.phase("oproj", f"B{i}"):
                emit_oproj(c, og, x_cur, (wob, t_wconv), mod, lnv[i][:, 0:16], lnv[i][:, 16:32], x1, h2x, ntok=S, h2off=2)
            with c.phase("ffn", f"F{i}"):
                emit_ffn(c, h2x, x1, wub, wdb, t_wconv, ffn_cw[i], ffn_cb[i], mod,
                         lnv[i][:, 32:48], lnv[i][:, 48:64], x_next, FUSED_PASSES)
            x_cur = x_next
        c.barrier()
        c.finish()
        print("fused instructions:", c.n_inst, {k: v for k, v in c.cnt.items() if v and not k.startswith("d_")},
              "sems", len(c.sem))
    return nc


_NC_CACHE = {}


def kernel(**inp):
    f32 = np.float32
    x = np.asarray(inp["x"], dtype=f32)
    cvec = np.asarray(inp["c"], dtype=f32)
    if "fused" not in _NC_CACHE:
        _NC_CACHE["fused"] = build_fused()
    nc = _NC_CACHE["fused"]
    A = lambda k: np.ascontiguousarray(np.asarray(inp[k], dtype=f32))
    lnv = np.ascontiguousarray(np.stack(
        [np.concatenate([vec_pm(inp["ln1_g"][l]), vec_pm(inp["ln1_b"][l]),
                         vec_pm(inp["ln2_g"][l]), vec_pm(inp["ln2_b"][l])], axis=1) for l in range(DEPTH)], axis=0))
    shared = dict(
        ada_w=A("ada_w"), ada_b=A("ada_b"), lnv=lnv,
        fox_wq=A("fox_wq"), fox_wk=A("fox_wk"), fox_wv=A("fox_wv"), fox_wg=A("fox_wg"), fox_wf=A("fox_wf"),
        fox_bfb=np.ascontiguousarray(np.stack([bcast_rows(inp["fox_bf"][j]) for j in range(N_FOX)], axis=0)),
        fox_wo=A("fox_wo"), fox_cf=fox_consts(),
        gla_wq=A("gla_wq"), gla_wk=A("gla_wk"), gla_wv=A("gla_wv"), gla_wr=A("gla_wr"), gla_wa1=A("gla_wa1"),
        gla_wa2=A("gla_wa2"), gla_ba=A("gla_ba"), gla_gbc=bcast_rows(inp["gla_norm_g"][0]), gla_cg=gla_consts(),
        gla_wo=A("gla_wo"), conv_w_in=A("conv_w_in"), conv_cw=vec_pm(inp["conv_w"][0]), conv_w_out=A("conv_w_out"),
        ffn_w_up=A("ffn_w_up"), ffn_w_down=A("ffn_w_down"),
        ffn_cw=np.ascontiguousarray(np.stack([vec_pm(inp["ffn_conv_w"][l]) for l in range(DEPTH)], axis=0)),
        ffn_cb=np.ascontiguousarray(np.stack([vec_pm(inp["ffn_conv_b"][l]) for l in range(DEPTH)], axis=0)),
    )
    maps = []
    for core in range(NCORES):
        b = core // 2
        m = dict(shared)
        m["xT"] = np.ascontiguousarray(x[b].T)
        m["cT"] = np.ascontiguousarray(cvec[b].reshape(KC, 128).T)
        maps.append(m)
    res = run_bass_kernel_spmd(nc, maps, core_ids=list(range(NCORES))).results
    out = np.stack([res[2 * b]["outT"].T for b in range(NB)], axis=0).astype(f32)
    return np.ascontiguousarray(out)
```

```python
import numpy as np
from contextlib import ExitStack
import ml_dtypes
import concourse.bass as bass
import concourse.mybir as mybir
from concourse.bass_utils import run_bass_kernel_spmd

F32 = mybir.dt.float32
BF16 = mybir.dt.bfloat16
AF = mybir.ActivationFunctionType
ALU = mybir.AluOpType
AX = mybir.AxisListType

D = 2048
S = 4096
NB = 4
DEPTH = 4
DFF = 5632
KC = D // 128
TOK = 2048
ALPHA = (2 * DEPTH) ** 0.25
LN_EPS = 1e-5
RMS_EPS = 1e-5
NCORES = 8
CC_INC = 16
PAIRS = [[0, 1], [2, 3], [4, 5], [6, 7]]
ALL8 = [list(range(8))]


class Tok:
    __slots__ = ("name", "w", "r")

    def __init__(self, name=""):
        self.name = name
        self.w = None
        self.r = []


class Ctx:
    def __init__(self, nc, es):
        self.nc = nc
        self.es = es
        self.eng = {"pe": nc.tensor, "act": nc.scalar, "dve": nc.vector,
                    "pool": nc.gpsimd, "sp": nc.sync}
        self.sem = {}
        self.cnt = {}
        for k in ("pe", "act", "dve", "pool"):
            self.sem[k] = es.enter_context(nc.semaphore("s_" + k))
            self.cnt[k] = 0
        self.seen = {k: {} for k in self.eng}
        self.n_inst = 0
        self.out_toks = []
        self.root_es = es
        self.prefix = ""
        self.kind = ""
        self.bg = []
        self.bg_final = {}

    def pump(self, n=1):
        while n > 0 and self.bg:
            self.bg.pop(0)()
            n -= 1

    def pump_all(self):
        self.pump(len(self.bg))
        for tok, dep in self.bg_final.items():
            tok.w = dep
        self.bg_final = {}

    def sb(self, name, shape, dt):
        return self.es.enter_context(self.nc.sbuf_tensor(self.prefix + name, list(shape), dt))

    def ps(self, name, shape, dt=F32):
        return self.es.enter_context(self.nc.psum_tensor(self.prefix + name, list(shape), dt))

    def dma_sem(self, name):
        key = "d_" + self.kind + name
        if key not in self.sem:
            self.sem[key] = self.root_es.enter_context(self.nc.semaphore(key))
            self.cnt[key] = 0
        return key

    def barrier(self):
        for e in self.eng:
            seen = self.seen[e]
            for k, cnt in self.cnt.items():
                if k.startswith("d_bg"):
                    continue
                if cnt > 0 and seen.get(k, 0) < cnt:
                    self.eng[e].wait_ge(self.sem[k], cnt)
                    seen[k] = cnt
                    self.n_inst += 1

    def phase(self, kind, inst):
        ctx = self

        class _P:
            def __enter__(self_):
                ctx.barrier()
                self_.es = ExitStack()
                self_.es.__enter__()
                self_.old = (ctx.es, ctx.prefix, ctx.kind)
                ctx.es, ctx.prefix, ctx.kind = self_.es, f"{inst}_", kind + "_"
                return ctx

            def __exit__(self_, *a):
                ctx.barrier()
                ctx.es, ctx.prefix, ctx.kind = self_.old
                return self_.es.__exit__(*a)

        return _P()

    def collective(self, kind, src, dst, groups):
        key = self.dma_sem("cc")
        ins = self.nc.gpsimd.collective_compute(kind, ALU.bypass, replica_groups=groups,
                                                ins=[src], outs=[dst])
        self.n_inst += 1
        self.cnt[key] += CC_INC
        ins.then_inc(self.sem[key], CC_INC)
        return ins

    def _need(self, e, reads, writes):
        need = {}

        def add(dep):
            if dep is None:
                return
            k, c = dep
            if need.get(k, 0) < c:
                need[k] = c

        for t in reads:
            add(t.w)
        for t in writes:
            if t.w is not None and t.w[0] != e:
                add(t.w)
            for d in t.r:
                if d[0] != e:
                    add(d)
        seen = self.seen[e]
        engine = self.eng[e]
        for k, c in need.items():
            if seen.get(k, 0) < c:
                engine.wait_ge(self.sem[k], c)
                seen[k] = c
                self.n_inst += 1

    def _done(self, key, cnt, reads, writes):
        dep = (key, cnt)
        for t in writes:
            t.w = dep
            t.r = []
        for t in reads:
            t.r.append(dep)
            if len(t.r) > 16:
                best = {}
                for k, c in t.r:
                    if best.get(k, 0) < c:
                        best[k] = c
                t.r = list(best.items())

    def op(self, e, fn, reads=(), writes=(), track=True):
        self._need(e, reads, writes)
        ins = fn(self.eng[e])
        self.n_inst += 1
        if track:
            self.cnt[e] += 1
            ins.then_inc(self.sem[e], 1)
            self._done(e, self.cnt[e], reads, writes)
        return ins

    def dma(self, q, semkey, out, in_, reads=(), writes=(), **kw):
        self._need(q, reads, writes)
        ins = self.eng[q].dma_start(out=out, in_=in_, **kw)
        self.n_inst += 1
        self.cnt[semkey] += 16
        ins.then_inc(self.sem[semkey], 16)
        self._done(semkey, self.cnt[semkey], reads, writes)
        return ins

    def finish(self):
        self._need("sp", self.out_toks, ())


class Ring:
    def __init__(self, c, name, n, shape, dt, dma=True):
        self.bufs = [c.sb(f"{name}{i}", shape, dt) for i in range(n)]
        self.toks = [Tok(f"{name}{i}") for i in range(n)]
        self.sems = [c.dma_sem(f"{name}{i}") for i in range(n)] if dma else None
        self.n = n
        self.i = -1

    def next(self):
        self.i = (self.i + 1) % self.n
        return self.cur()

    def cur(self):
        i = self.i
        return self.bufs[i], self.toks[i], (self.sems[i] if self.sems else None)


def mm_group(c, out_ap, out_tok, pairs, reads):
    n = len(pairs)
    for i, (l, r) in enumerate(pairs):
        last = i == n - 1
        c.op("pe", lambda e: e.matmul(out_ap, l, r, start=(i == 0), stop=last),
             reads=reads, writes=[out_tok], track=last)


class LNState:
    def __init__(self, c, widths=(512,)):
        ng = len(widths)
        width = max(widths)
        self.ones = c.sb("ln_ones", [128, 128], BF16)
        self.t_ones = Tok("ones")
        c.op("dve", lambda e: e.memset(self.ones[:], 1.0), writes=[self.t_ones])
        self.S1 = [c.ps(f"ln_s1_{g}", [128, 512]) for g in range(ng)]
        self.S2 = [c.ps(f"ln_s2_{g}", [128, 512]) for g in range(ng)]
        self.t_S1 = [Tok(f"s1_{g}") for g in range(ng)]
        self.t_S2 = [Tok(f"s2_{g}") for g in range(ng)]
        self.zb = Ring(c, "ln_zb", 3, [128, width], BF16, dma=False)
        self.sq = Ring(c, "ln_sq", 3, [128, width], BF16, dma=False)
        self.m = c.sb("ln_m", [128, width], F32)
        self.msq = c.sb("ln_msq", [128, width], F32)
        self.rstd = [c.sb(f"ln_rstd{g}", [128, widths[g]], F32) for g in range(ng)]
        self.nmr = [c.sb(f"ln_nmr{g}", [128, widths[g]], F32) for g in range(ng)]
        self.t_m = Tok("m")
        self.m2 = c.sb("ln_m2", [128, width], F32)
        self.t_m2 = Tok("m2")
        self.t_msq = Tok("msq")
        self.t_rstd = [Tok() for g in range(ng)]
        self.t_nmr = [Tok() for g in range(ng)]
        self.tt = Ring(c, "ln_tt", 2, [128, width], F32, dma=False)
        self.pending = []

    def accum(self, c, g, n, z_ap, z_tok, W):
        zb, t_zb, _ = self.zb.next()
        sq, t_sq, _ = self.sq.next()
        c.op("act", lambda e: e.activation(out=zb[:, 0:W], in_=z_ap, func=AF.Identity),
             reads=[z_tok], writes=[t_zb])
        c.op("act", lambda e: e.activation(out=sq[:, 0:W], in_=z_ap, func=AF.Square),
             reads=[z_tok], writes=[t_sq])
        last = n == KC - 1

        def mm():
            c.op("pe", lambda e: e.matmul(self.S1[g][:, 0:W], self.ones[:], zb[:, 0:W],
                                          start=(n == 0), stop=last),
                 reads=[self.t_ones, t_zb], writes=[self.t_S1[g]])
            c.op("pe", lambda e: e.matmul(self.S2[g][:, 0:W], self.ones[:], sq[:, 0:W],
                                          start=(n == 0), stop=last),
                 reads=[self.t_ones, t_sq], writes=[self.t_S2[g]])

        self.pending.append(mm)

    def flush(self):
        while self.pending:
            self.pending.pop(0)()

    def finalize(self, c, g, W, eps):
        self.flush()
        m, msq, rstd, nmr, m2 = self.m, self.msq, self.rstd[g], self.nmr[g], self.m2
        c.op("act", lambda e: e.activation(out=m[:, 0:W], in_=self.S1[g][:, 0:W],
                                           func=AF.Identity, scale=1.0 / D),
             reads=[self.t_S1[g]], writes=[self.t_m])
        c.op("dve", lambda e: e.tensor_tensor(out=msq[:, 0:W], in0=m[:, 0:W], in1=m[:, 0:W],
                                              op=ALU.mult),
             reads=[self.t_m], writes=[self.t_msq])
        c.op("dve", lambda e: e.scalar_tensor_tensor(out=msq[:, 0:W], in0=self.S2[g][:, 0:W],
                                                     scalar=1.0 / D, in1=msq[:, 0:W],
                                                     op0=ALU.mult, op1=ALU.subtract),
             reads=[self.t_S2[g], self.t_msq], writes=[self.t_msq])
        c.op("dve", lambda e: e.tensor_scalar(out=msq[:, 0:W], in0=msq[:, 0:W],
                                              scalar1=float(eps), scalar2=None, op0=ALU.add),
             reads=[self.t_msq], writes=[self.t_msq])
        c.op("dve", lambda e: e.reciprocal(out=m2[:, 0:W], in_=msq[:, 0:W]),
             reads=[self.t_msq], writes=[self.t_m2])
        c.op("act", lambda e: e.activation(out=rstd[:, 0:W], in_=m2[:, 0:W], func=AF.Sqrt),
             reads=[self.t_m2], writes=[self.t_rstd[g]])
        c.op("dve", lambda e: e.scalar_tensor_tensor(out=nmr[:, 0:W], in0=m[:, 0:W],
                                                     scalar=-1.0, in1=rstd[:, 0:W],
                                                     op0=ALU.mult, op1=ALU.mult),
             reads=[self.t_m, self.t_rstd[g]], writes=[self.t_nmr[g]])

    def normalize(self, c, g, W, z_ap, z_tok):
        tt, t_tt, _ = self.tt.next()
        c.op("dve", lambda e: e.tensor_tensor(out=tt[:, 0:W], in0=z_ap, in1=self.rstd[g][:, 0:W],
                                              op=ALU.mult),
             reads=[z_tok, self.t_rstd[g]], writes=[t_tt])
        c.op("pool", lambda e: e.tensor_tensor(out=tt[:, 0:W], in0=tt[:, 0:W],
                                               in1=self.nmr[g][:, 0:W], op=ALU.add),
             reads=[t_tt, self.t_nmr[g]], writes=[t_tt])
        return tt[:, 0:W], t_tt


def load_vecs(c, name, dram_ap, ncols, q="sp"):
    t = c.sb("v_" + name, [128, ncols], F32)
    tok = Tok(name)
    sem = c.dma_sem(name)
    c.dma(q, sem, t[:], dram_ap, writes=[tok])
    return t, tok


FFN_PASSES = [[(0, 510)], [(510, 510)], [(1020, 510)], [(1530, 510), (2040, 8)]]
NJ = DFF // 128


def emit_ffn(c, h2x, x1T, wub, wdb, t_wconv, cw, cb, mod, lng, lnb, x2T, FFN_PASSES, hnext=None):
    h2v = h2x.rearrange("(kc p) t -> p kc t", p=128)
    x1v = x1T.rearrange("(kc p) t -> p kc t", p=128)
    x2v = x2T.rearrange("(kc p) t -> p kc t", p=128)

    cwt, t_cw = load_vecs(c, "cw", cw, 3 * 88)
    cbt, t_cb = load_vecs(c, "cb", cb, 88)
    modt, t_mod = mod
    lngt, t_lng = load_vecs(c, "lng", lng, 16)
    lnbt, t_lnb = load_vecs(c, "lnb", lnb, 16)
    g2a = c.sb("g2a", [128, 16], F32)
    t_g2a = Tok("g2a")
    c.op("dve", lambda e: e.tensor_scalar(out=g2a[:], in0=modt[:, 80:96], scalar1=1.0 / ALPHA,
                                          scalar2=None, op0=ALU.mult),
         reads=[t_mod], writes=[t_g2a])
    if hnext is not None:
        hn_dram, (modn, t_modn) = hnext
        hnv = hn_dram.rearrange("(kc p) t -> p kc t", p=128)
        gsn = c.sb("gsn", [128, 16], F32)
        bsn = c.sb("bsn", [128, 16], F32)
        t_gsn, t_bsn = Tok(), Tok()
        c.op("dve", lambda e: e.scalar_tensor_tensor(out=gsn[:], in0=modn[:, 16:32], scalar=1.0, in1=lngt[:],
                                                     op0=ALU.add, op1=ALU.mult),
             reads=[t_modn, t_lng], writes=[t_gsn])
        c.op("dve", lambda e: e.scalar_tensor_tensor(out=bsn[:], in0=modn[:, 16:32], scalar=1.0, in1=lnbt[:],
                                                     op0=ALU.add, op1=ALU.mult),
             reads=[t_modn, t_lnb], writes=[t_bsn])
        c.op("dve", lambda e: e.tensor_tensor(out=bsn[:], in0=bsn[:], in1=modn[:, 0:16], op=ALU.add),
             reads=[t_modn, t_bsn], writes=[t_bsn])
        hnr = Ring(c, "hn", 3, [128, 510], BF16)
        t_hnd = Tok("hn_dram")

    WR = max([g[1][1] for g in FFN_PASSES if len(g) > 1] + [2])
    ln = LNState(c, widths=(510, WR))
    work = [c.ps(f"wk{i}", [128, 512]) for i in range(4)]
    t_work = [Tok(f"wk{i}") for i in range(4)]
    hx = c.sb("hx", [128, KC, 514 + WR], BF16)
    t_hx = Tok("hx")
    s_hx = c.dma_sem("hx")
    gT = c.sb("gT", [128, NJ, 510 + WR], BF16)
    t_g = [Tok(f"g{j}") for j in range(NJ)]
    xt = c.sb("xt", [128, KC, 510 + WR], F32)
    t_xt = [Tok(f"xt{n}") for n in range(KC)]
    s_xt = c.dma_sem("xt")
    s_out = c.dma_sem("out")
    wu = Ring(c, "wu", 3, [128, KC, 256], BF16)
    wd = Ring(c, "wd", 2, [128, NJ, 128], BF16)
    scr = {k: Ring(c, "scr_" + k, 2, [128, 510], F32, dma=False) for k in ("a", "u", "sa")}
    t_outdram = Tok("x2T")
    c.out_toks.append(t_outdram)

    def load_wu(j):
        buf, tok, sem = wu.next()
        c.dma("sp", sem, buf[:], wub[j].rearrange("p (kc n) -> p kc n", kc=KC), reads=[t_wconv], writes=[tok])
        return buf, tok

    def load_wd(n):
        buf, tok, sem = wd.next()
        c.dma("sp", sem, buf[:], wdb[n].rearrange("p (kc n) -> p kc n", kc=NJ), reads=[t_wconv], writes=[tok])
        return buf, tok

    def pass_cols(groups):
        cols = []
        c0 = 0
        gc0 = 0
        for (s, W) in groups:
            cols.append((s, W, c0, gc0))
            c0 += 512
            gc0 += W
        return cols

    def load_hx(cols):
        for (s, W, c0, gc0) in cols:
            c.dma("pool", s_hx, hx[:, :, c0:c0 + W + 2], h2v[:, :, s:s + W + 2], writes=[t_hx])

    PF = 2
    wk_i = 0
    pend_wu = []
    load_hx(pass_cols(FFN_PASSES[0]))
    for ip, groups in enumerate(FFN_PASSES):
        cols = pass_cols(groups)
        for (s, W, c0, gc0) in cols:
            c.dma("pool", s_xt, xt[:, :, gc0:gc0 + W], x1v[:, :, s:s + W], writes=t_xt)
        pend = pend_wu if pend_wu else [load_wu(j) for j in range(min(PF, NJ))]
        pend_wd = []
        for j in range(NJ):
            if j == NJ - 3:
                pend_wd.append(load_wd(0))
            if j + PF < NJ:
                pend.append(load_wu(j + PF))
            if j % 2 == 0:
                c.pump(1)
            wbuf, t_w = pend.pop(0)
            for (s, W, c0, gc0) in cols:
                N = W + 2
                res = {}
                for hi, half in enumerate(("a", "u")):
                    pb = work[wk_i % 4]
                    t_pb = t_work[wk_i % 4]
                    wk_i += 1
                    mm_group(c, pb[:, 0:N], t_pb,
                             [(wbuf[:, k, hi * 128:(hi + 1) * 128], hx[:, k, c0:c0 + N])
                              for k in range(KC)], reads=[t_w, t_hx])
                    b1, tk1, _ = scr[half].next()
                    res[half] = (pb, t_pb, j + hi * NJ, b1, tk1)
                for half in ("a", "u"):
                    pb, t_pb, jj, b1, tk1 = res[half]
                    c.op("act", lambda e: e.activation(out=b1[:, 0:W], in_=pb[:, 2:N], func=AF.Identity,
                                                       bias=cbt[:, jj:jj + 1],
                                                       scale=cwt[:, 2 * 88 + jj:2 * 88 + jj + 1]),
                         reads=[t_pb, t_cw, t_cb], writes=[tk1])
                for half in ("a", "u"):
                    pb, t_pb, jj, b1, tk1 = res[half]
                    c.op("dve", lambda e: e.scalar_tensor_tensor(
                        out=b1[:, 0:W], in0=pb[:, 1:N - 1], scalar=cwt[:, 88 + jj:88 + jj + 1],
                        in1=b1[:, 0:W], op0=ALU.mult, op1=ALU.add),
                         reads=[t_pb, tk1, t_cw], writes=[tk1])
                for half in ("a", "u"):
                    pb, t_pb, jj, b1, tk1 = res[half]
                    c.op("dve", lambda e: e.scalar_tensor_tensor(
                        out=b1[:, 0:W], in0=pb[:, 0:N - 2], scalar=cwt[:, jj:jj + 1],
                        in1=b1[:, 0:W], op0=ALU.mult, op1=ALU.add),
                         reads=[t_pb, tk1, t_cw], writes=[tk1])
                sa, t_sa, _ = scr["sa"].next()
                ba, tka = res["a"][3], res["a"][4]
                bu, tku = res["u"][3], res["u"][4]
                c.op("act", lambda e: e.activation(out=sa[:, 0:W], in_=ba[:, 0:W], func=AF.Silu),
                     reads=[tka], writes=[t_sa])
                c.op("pool", lambda e: e.tensor_tensor(out=gT[:, j, gc0:gc0 + W], in0=sa[:, 0:W],
                                                       in1=bu[:, 0:W], op=ALU.mult),
                     reads=[t_sa, tku], writes=[t_g[j]])
        if ip + 1 < len(FFN_PASSES):
            load_hx(pass_cols(FFN_PASSES[ip + 1]))
        pend = pend_wd
        pend_wu = []
        for n in range(KC):
            if n + 1 < KC:
                pend.append(load_wd(n + 1))
            if n == 2 and ip + 1 < len(FFN_PASSES):
                pend_wu = [load_wu(j) for j in range(min(PF, NJ))]
            wbuf, t_w = pend.pop(0)
            for gi, (s, W, c0, gc0) in enumerate(cols):
                pb = work[wk_i % 4]
                t_pb = t_work[wk_i % 4]
                wk_i += 1
                mm_group(c, pb[:, 0:W], t_pb,
                         [(wbuf[:, k, :], gT[:, k, gc0:gc0 + W]) for k in range(NJ)],
                         reads=[t_w] + t_g)
                ln.flush()
                c.op("dve", lambda e: e.scalar_tensor_tensor(
                    out=xt[:, n, gc0:gc0 + W], in0=pb[:, 0:W], scalar=g2a[:, n:n + 1],
                    in1=xt[:, n, gc0:gc0 + W], op0=ALU.mult, op1=ALU.add),
                     reads=[t_pb, t_g2a, t_xt[n]], writes=[t_xt[n]])
                ln.accum(c, gi, n, xt[:, n, gc0:gc0 + W], t_xt[n], W)
        for gi, (s, W, c0, gc0) in enumerate(cols):
            ln.finalize(c, gi, W, LN_EPS / (ALPHA * ALPHA))
            for n in range(KC):
                t_ap, t_tok = ln.normalize(c, gi, W, xt[:, n, gc0:gc0 + W], t_xt[n])
                c.op("act", lambda e: e.activation(out=xt[:, n, gc0:gc0 + W], in_=t_ap, func=AF.Identity,
                                                   bias=lnbt[:, n:n + 1], scale=lngt[:, n:n + 1]),
                     reads=[t_tok, t_lng, t_lnb], writes=[t_xt[n]])
                if hnext is not None:
                    hb_, t_hb_, s_hb_ = hnr.next()
                    c.op("act", lambda e: e.activation(out=hb_[:, 0:W], in_=t_ap, func=AF.Identity,
                                                       bias=bsn[:, n:n + 1], scale=gsn[:, n:n + 1]),
                         reads=[t_tok, t_gsn, t_bsn], writes=[t_hb_])
                    c.dma("act", s_hb_, hnv[:, n, s:s + W], hb_[:, 0:W], reads=[t_hb_], writes=[t_hnd])
            c.dma("pool", s_out, x2v[:, :, s:s + W], xt[:, :, gc0:gc0 + W], reads=t_xt,
                  writes=[t_outdram])


def emit_oproj(c, ogT, xT, wo, mod, lng, lnb, x1T, h2T, ntok=TOK, h2off=0):
    ogv = ogT.rearrange("(kc p) t -> p kc t", p=128)
    xv = xT.rearrange("(kc p) t -> p kc t", p=128)
    x1v = x1T.rearrange("(kc p) t -> p kc t", p=128)
    h2v = h2T.rearrange("(kc p) t -> p kc t", p=128)
    wob, t_wconv = wo
    modt, t_mod = mod
    lngt, t_lng = load_vecs(c, "lng", lng, 16)
    lnbt, t_lnb = load_vecs(c, "lnb", lnb, 16)
    g1a = c.sb("g1a", [128, 16], F32)
    gs = c.sb("gs", [128, 16], F32)
    bs = c.sb("bs", [128, 16], F32)
    t_g1a, t_gs, t_bs = Tok(), Tok(), Tok()
    c.op("dve", lambda e: e.tensor_scalar(out=g1a[:], in0=modt[:, 32:48], scalar1=1.0 / ALPHA,
                                          scalar2=None, op0=ALU.mult),
         reads=[t_mod], writes=[t_g1a])
    c.op("dve", lambda e: e.scalar_tensor_tensor(out=gs[:], in0=modt[:, 64:80], scalar=1.0, in1=lngt[:],
                                                 op0=ALU.add, op1=ALU.mult),
         reads=[t_mod, t_lng], writes=[t_gs])
    c.op("dve", lambda e: e.scalar_tensor_tensor(out=bs[:], in0=modt[:, 64:80], scalar=1.0, in1=lnbt[:],
                                                 op0=ALU.add, op1=ALU.mult),
         reads=[t_mod, t_lnb], writes=[t_bs])
    c.op("dve", lambda e: e.tensor_tensor(out=bs[:], in0=bs[:], in1=modt[:, 48:64], op=ALU.add),
         reads=[t_mod, t_bs], writes=[t_bs])

    ln = LNState(c, widths=(512,))
    work = [c.ps(f"wk{i}", [128, 512]) for i in range(4)]
    t_work = [Tok(f"wk{i}") for i in range(4)]
    og = Ring(c, "og", 2, [128, KC, 512], BF16)
    xr = Ring(c, "xr", 2, [128, KC, 512], F32)
    h2r = Ring(c, "h2r", 1, [128, KC, 512], BF16, dma=False)
    wor = Ring(c, "wo", 3, [128, KC, 128], BF16)
    s_o1 = c.dma_sem("o1")
    s_o2 = c.dma_sem("o2")
    t_o1, t_o2 = Tok("x1T"), Tok("h2T")
    c.out_toks += [t_o1, t_o2]
    NT = ntok // 512
    wk_i = 0
    if h2off:
        zt = c.sb("zt", [128, KC, h2off], BF16)
        t_zt = Tok()
        c.op("dve", lambda e: e.memset(zt[:], 0.0), writes=[t_zt])
        c.dma("sp", s_o2, h2v[:, :, 0:h2off], zt[:], reads=[t_zt], writes=[t_o2])

    def load_tile(t):
        ob, t_ob, s_ob = og.next()
        c.dma("sp", s_ob, ob[:], ogv[:, :, t * 512:(t + 1) * 512], writes=[t_ob])
        xb, t_xb, s_xb = xr.next()
        c.dma("sp", s_xb, xb[:], xv[:, :, t * 512:(t + 1) * 512], writes=[t_xb])
        return ob, t_ob, xb, t_xb

    def load_wo(n):
        buf, tok, sem = wor.next()
        c.dma("pool", sem, buf[:], wob[n].rearrange("p (kc n) -> p kc n", kc=KC), reads=[t_wconv], writes=[tok])
        return buf, tok

    nxt = load_tile(0)
    pend_next = [load_wo(0), load_wo(1)]
    for t in range(NT):
        ob, t_ob, xb, t_xb = nxt
        pend = pend_next
        for n in range(KC):
            if n + 2 < KC:
                pend.append(load_wo(n + 2))
            wbuf, t_w = pend.pop(0)
            pb = work[wk_i % 4]
            t_pb = t_work[wk_i % 4]
            wk_i += 1
            mm_group(c, pb[:], t_pb, [(wbuf[:, k, :], ob[:, k, :]) for k in range(KC)],
                     reads=[t_w, t_ob])
            ln.flush()
            c.op("dve", lambda e: e.scalar_tensor_tensor(
                out=xb[:, n, :], in0=pb[:], scalar=g1a[:, n:n + 1], in1=xb[:, n, :],
                op0=ALU.mult, op1=ALU.add),
                 reads=[t_pb, t_g1a, t_xb], writes=[t_xb])
            ln.accum(c, 0, n, xb[:, n, :], t_xb, 512)
        if t + 1 < NT:
            nxt = load_tile(t + 1)
            pend_next = [load_wo(0), load_wo(1)]
        ln.finalize(c, 0, 512, LN_EPS / (ALPHA * ALPHA))
        hb, t_hb, _ = h2r.next()
        for n in range(KC):
            t_ap, t_tok = ln.normalize(c, 0, 512, xb[:, n, :], t_xb)
            c.op("act", lambda e: e.activation(out=xb[:, n, :], in_=t_ap, func=AF.Identity,
                                               bias=lnbt[:, n:n + 1], scale=lngt[:, n:n + 1]),
                 reads=[t_tok, t_lng, t_lnb], writes=[t_xb])
            c.op("act", lambda e: e.activation(out=hb[:, n, :], in_=t_ap, func=AF.Identity,
                                               bias=bs[:, n:n + 1], scale=gs[:, n:n + 1]),
                 reads=[t_tok, t_gs, t_bs], writes=[t_hb])
        c.dma("sp", s_o1, x1v[:, :, t * 512:(t + 1) * 512], xb[:], reads=[t_xb], writes=[t_o1])
        c.dma("sp", s_o2, h2v[:, :, h2off + t * 512:h2off + (t + 1) * 512], hb[:], reads=[t_hb], writes=[t_o2])


NEG = 30000.0
HD = 128
FOX_SWEEP = 4


def emit_fox(c, hsrc, fw, wf, bfb, cf, ogT, nheads=8):
    hv = hsrc.rearrange("(kc p) t -> p kc t", p=128)
    ogv = ogT.rearrange("(n p) t -> p n t", p=128)
    fwb, fvb, t_fw = fw
    wfv = wf.rearrange("(kc p) h -> p kc h", p=128)
    NS = FOX_SWEEP
    NT = S // 512

    cft, t_cf = load_vecs(c, "cf", cf, 896)
    bft, t_bf = load_vecs(c, "bfb", bfb, nheads)
    tri, ones_f, ident, mpos = cft[:, 0:128], cft[:, 128:256], cft[:, 256:384], cft[:, 384:896]
    ones_b = c.sb("ones_b", [128, 128], BF16)
    t_ones_b = Tok()
    c.op("dve", lambda e: e.memset(ones_b[:], 1.0), writes=[t_ones_b])
    one_c = c.sb("one_c", [128, 1], F32)
    t_one_c = Tok()
    c.op("dve", lambda e: e.memset(one_c[:], 1.0), writes=[t_one_c])
    wft = c.sb("wft", [128, KC, nheads], BF16)
    t_wf = Tok()
    s_wf = c.dma_sem("wf")
    c.dma("pool", s_wf, wft[:], wfv, writes=[t_wf])

    work = [c.ps(f"wk{i}", [128, 512]) for i in range(4)]
    t_work = [Tok(f"wk{i}") for i in range(4)]
    Ob = [c.ps(f"ob{i}", [128, 512]) for i in range(2)]
    t_Ob = [Tok(f"ob{i}") for i in range(2)]
    Lb = [c.ps(f"lb{i}", [128, 512]) for i in range(2)]
    t_Lb = [Tok(f"lb{i}") for i in range(2)]
    wk_i = [0]

    def nwork():
        i = wk_i[0] % 4
        wk_i[0] += 1
        return work[i], t_work[i]

    KT = c.sb("KT", [128, NS, S], BF16)
    t_KT = [Tok(f"kt{t}") for t in range(NT)]
    V = c.sb("V", [128, S // 128, NS * HD], BF16)
    t_V = [Tok(f"v{t}") for t in range(NT)]
    Gk = c.sb("Gk", [128, S // 128, NS], F32)
    t_Gk = [Tok(f"gk{t}") for t in range(NT)]
    R = c.sb("R", [128, NS], F32)
    t_R = Tok("R")
    hT = c.sb("hT", [128, KC, 512], BF16)
    t_hT = Tok("hT")
    s_h = c.dma_sem("h")
    QT = Ring(c, "QT", 2, [128, NS, 512], BF16, dma=False)
    GT = Ring(c, "GT", 2, [128, NS, 512], BF16, dma=False)
    wr = Ring(c, "w", 4, [128, KC, 128], BF16)
    wvb = c.sb("wvb", [128, KC, NS * HD], BF16)
    t_wvb = Tok("wvb")
    s_wvb = c.dma_sem("wvb")
    PT = Ring(c, "PT", 4, [128, 512], BF16, dma=False)
    sbr = Ring(c, "sbr", 3, [128, 512], F32, dma=False)
    Dg = c.sb("Dg", [128, 512], F32)
    t_Dg = Tok()
    gqa = c.sb("gqa", [128, NS, 512], F32)
    t_gqa = [Tok(f"gqa{n}") for n in range(NS)]
    gqda = c.sb("gqda", [128, NS, 512], F32)
    t_gqda = [Tok(f"gqda{n}") for n in range(NS)]
    den = c.sb("den", [128, 512], F32)
    t_den = Tok()
    rec = c.sb("rec", [128, 512], F32)
    t_rec = Tok()
    ogb = Ring(c, "ogb", 2, [128, NS, 512], BF16, dma=False)
    zf = c.sb("zf", [128, 4, NS], F32)
    t_zf = [Tok() for _ in range(4)]
    spf = c.sb("spf", [128, 4, NS], F32)
    t_spf = [Tok() for _ in range(4)]
    s_og = c.dma_sem("og")
    t_ogd = Tok("ogT")
    c.out_toks.append(t_ogd)
    ob_i = 0

    for sw in range(nheads // NS):
        hc0 = sw * NS * HD
        c.dma("pool", s_wvb, wvb[:], fvb[sw].rearrange("p (kc n) -> p kc n", kc=KC), reads=[t_fw], writes=[t_wvb])
        c.op("dve", lambda e: e.memset(R[:], 0.0), writes=[t_R])
        if sw == 0:
            c.dma("sp", s_h, hT[:], hv[:, :, 0:512], writes=[t_hT])
        for t in range(NT):
            tsl = slice(t * 512, (t + 1) * 512)

            def proj(kind, n):
                c.pump(1)
                wb, t_wb, s_wb = wr.next()
                c.dma("pool", s_wb, wb[:], fwb[kind][sw * NS + n].rearrange("p (kc n) -> p kc n", kc=KC),
                      reads=[t_fw], writes=[t_wb])
                pb, t_pb = nwork()
                mm_group(c, pb[:], t_pb, [(wb[:, k, :], hT[:, k, :]) for k in range(KC)], reads=[t_wb, t_hT])
                return pb, t_pb

            for n in range(NS):
                pb, t_pb = proj("k", n)
                c.op("dve", lambda e: e.tensor_copy(out=KT[:, n, tsl], in_=pb[:]),
                     reads=[t_pb], writes=[t_KT[t]])
            for tb in range(4):
                blk = t * 4 + tb
                bsl = slice(tb * 128, (tb + 1) * 128)
                pb, t_pb = nwork()
                mm_group(c, pb[:], t_pb, [(hT[:, k, bsl], wvb[:, k, :]) for k in range(KC)],
                         reads=[t_wvb, t_hT])
                c.op("act", lambda e: e.activation(out=V[:, blk, :], in_=pb[:], func=AF.Identity),
                     reads=[t_pb], writes=[t_V[t]])
                pz, t_pz = nwork()
                mm_group(c, pz[:, 0:NS], t_pz, [(hT[:, k, bsl], wft[:, k, sw * NS:(sw + 1) * NS]) for k in range(KC)],
                         reads=[t_wf, t_hT])
                c.op("dve", lambda e: e.tensor_tensor(out=zf[:, tb, :], in0=pz[:, 0:NS], in1=bft[:, sw * NS:(sw + 1) * NS],
                                                      op=ALU.add), reads=[t_pz, t_bf], writes=[t_zf[tb]])
                c.op("act", lambda e: e.activation(out=spf[:, tb, :], in_=zf[:, tb, :], func=AF.Exp, scale=-1.0),
                     reads=[t_zf[tb]], writes=[t_spf[tb]])
                c.op("act", lambda e: e.activation(out=spf[:, tb, :], in_=spf[:, tb, :], func=AF.Ln, bias=one_c[:, 0:1]),
                     reads=[t_spf[tb], t_one_c], writes=[t_spf[tb]])
            qb, t_qb, _ = QT.next()
            gb, t_gb, _ = GT.next()
            for n in range(NS):
                pb, t_pb = proj("q", n)
                c.op("act", lambda e: e.activation(out=qb[:, n, :], in_=pb[:], func=AF.Identity,
                                                   scale=float(HD ** -0.5)),
                     reads=[t_pb], writes=[t_qb])
            for n in range(NS):
                pb, t_pb = proj("g", n)
                c.op("act", lambda e: e.activation(out=gb[:, n, :], in_=pb[:], func=AF.Exp, scale=-1.0),
                     reads=[t_pb], writes=[t_gb])
            if t + 1 < NT:
                c.dma("sp", s_h, hT[:], hv[:, :, (t + 1) * 512:(t + 2) * 512], writes=[t_hT])
            elif sw + 1 < nheads // NS:
                c.dma("sp", s_h, hT[:], hv[:, :, 0:512], writes=[t_hT])
            for tb in range(4):
                blk = t * 4 + tb
                pg, t_pg = nwork()
                c.op("pe", lambda e: e.matmul(pg[:, 0:NS], tri, spf[:, tb, :], start=True, stop=False),
                     reads=[t_cf, t_spf[tb]], writes=[t_pg], track=False)
                c.op("pe", lambda e: e.matmul(pg[:, 0:NS], ones_f, R[:], start=False, stop=True),
                     reads=[t_cf, t_R, t_spf[tb]], writes=[t_pg])
                c.op("dve", lambda e: e.tensor_copy(out=Gk[:, blk, :], in_=pg[:, 0:NS]),
                     reads=[t_pg], writes=[t_Gk[t]])
                c.op("dve", lambda e: e.tensor_tensor(out=R[:], in0=R[:], in1=spf[:, tb, :], op=ALU.add),
                     reads=[t_R, t_spf[tb]], writes=[t_R])
            for n in range(NS):
                for j in range(4):
                    c.op("dve", lambda e: e.tensor_scalar(out=Dg[:, j * 128:(j + 1) * 128], in0=ident,
                                                          scalar1=Gk[:, t * 4 + j, n:n + 1], scalar2=None,
                                                          op0=ALU.mult),
                         reads=[t_cf, t_Gk[t]], writes=[t_Dg])
                pq, t_pq = nwork()
                c.op("pe", lambda e: e.matmul(pq[:], ones_f, Dg[:], start=True, stop=True),
                     reads=[t_cf, t_Dg], writes=[t_pq])
                c.op("act", lambda e: e.activation(out=gqa[:, n, :], in_=pq[:], func=AF.Identity),
                     reads=[t_pq], writes=[t_gqa[n]])
                c.op("pool", lambda e: e.tensor_tensor(out=gqda[:, n, :], in0=gqa[:, n, :], in1=mpos, op=ALU.add),
                     reads=[t_gqa[n], t_cf], writes=[t_gqda[n]])
            og_b, t_og, _ = ogb.next()
            nkb = 4 * t + 4
            banks = {}
            for n in range(NS):
                banks[n] = (Ob[ob_i % 2], t_Ob[ob_i % 2], Lb[ob_i % 2], t_Lb[ob_i % 2])
                ob_i += 1
            pending = []

            def emit_scores(n, kb):
                j = kb - 4 * t
                q0 = 128 * j if j > 0 else 0
                N = 512 - q0
                gq, t_gq, gqd, t_gqd = gqa[:, n, :], t_gqa[n], gqda[:, n, :], t_gqda[n]
                ps_, t_ps = nwork()
                c.op("pe", lambda e: e.matmul(ps_[:, 0:N], KT[:, n, kb * 128:(kb + 1) * 128], qb[:, n, q0:512],
                                              start=True, stop=True),
                     reads=[t_KT[kb // 4], t_qb], writes=[t_ps])
                sb_, t_sb, _ = sbr.next()
                if j < 0:
                    c.op("dve", lambda e: e.tensor_tensor(out=sb_[:, 0:N], in0=ps_[:, 0:N], in1=gq[:, q0:512],
                                                          op=ALU.subtract),
                         reads=[t_ps, t_gq], writes=[t_sb])
                else:
                    c.op("dve", lambda e: e.tensor_tensor(out=sb_[:, 0:128], in0=ps_[:, 0:128],
                                                          in1=gqd[:, q0:q0 + 128], op=ALU.subtract),
                         reads=[t_ps, t_gqd], writes=[t_sb])
                    if N > 128:
                        c.op("dve", lambda e: e.tensor_tensor(out=sb_[:, 128:N], in0=ps_[:, 128:N],
                                                              in1=gq[:, q0 + 128:512], op=ALU.subtract),
                             reads=[t_ps, t_gq], writes=[t_sb])
                pt, t_pt, _ = PT.next()
                c.op("act", lambda e: e.activation(out=pt[:, 0:N], in_=sb_[:, 0:N], func=AF.Exp,
                                                   bias=Gk[:, kb, n:n + 1]),
                     reads=[t_sb, t_Gk[kb // 4]], writes=[t_pt])
                return (n, kb, q0, N, pt, t_pt)

            def emit_pv(item):
                n, kb, q0, N, pt, t_pt = item
                O, t_O, L, t_L = banks[n]
                last = kb == nkb - 1
                c.op("pe", lambda e: e.matmul(O[:, q0:512], V[:, kb, n * HD:(n + 1) * HD], pt[:, 0:N],
                                              start=(kb == 0), stop=last),
                     reads=[t_V[kb // 4], t_pt], writes=[t_O], track=last)
                c.op("pe", lambda e: e.matmul(L[:, q0:512], ones_b[:], pt[:, 0:N],
                                              start=(kb == 0), stop=last),
                     reads=[t_ones_b, t_pt], writes=[t_L], track=True)
                if last:
                    c.op("dve", lambda e: e.scalar_tensor_tensor(out=den[:], in0=gb[:, n, :], scalar=1.0, in1=L[:],
                                                                 op0=ALU.add, op1=ALU.mult),
                         reads=[t_gb, t_L], writes=[t_den])
                    c.op("dve", lambda e: e.reciprocal(out=rec[:], in_=den[:]), reads=[t_den], writes=[t_rec])
                    c.op("dve", lambda e: e.tensor_tensor(out=og_b[:, n, :], in0=O[:], in1=rec[:], op=ALU.mult),
                         reads=[t_O, t_rec], writes=[t_og])

            LA = 2
            for n in range(NS):
                for kb in range(nkb):
                    pending.append(emit_scores(n, kb))
                    if len(pending) > LA:
                        emit_pv(pending.pop(0))
            while pending:
                emit_pv(pending.pop(0))
            c.dma("sp", s_og, ogv[:, sw * NS:(sw + 1) * NS, tsl], og_b[:], reads=[t_og], writes=[t_ogd])


def fox_consts():
    k = np.arange(128)[:, None]
    q = np.arange(128)[None, :]
    tri = (k <= q).astype(np.float32)
    ones = np.ones((128, 128), np.float32)
    ident = np.eye(128, dtype=np.float32)
    mp = np.where(q >= k, 0.0, NEG).astype(np.float32)
    return np.ascontiguousarray(np.concatenate([tri, ones, ident, mp, mp, mp, mp], axis=1))


GLA_TAU = 16.0


def emit_gla(c, hsrc, wq, wk, wv, wr, wa1, wa2, ba, gbc, cg, ogT, nheads=2):
    hv = hsrc.rearrange("(kc p) t -> p kc t", p=128)
    ogv = ogT.rearrange("(n p) t -> p n t", p=128)
    wqv = wq.rearrange("(kc p) n -> p kc n", p=128)
    wkv = wk.rearrange("(kc p) n -> p kc n", p=128)
    wvv = wv.rearrange("(kc p) n -> p kc n", p=128)
    wrv = wr.rearrange("(kc p) n -> p kc n", p=128)
    wa1v = wa1.rearrange("(kc p) n -> p kc n", p=128)
    NT = S // 512
    DK, DV = 256, 512

    cgt, t_cg = load_vecs(c, "cg", cg, 512)
    gbt, t_gb = load_vecs(c, "gbc", gbc, 512)
    triN, triU, tri01, ident_f = cgt[:, 0:128], cgt[:, 128:256], cgt[:, 256:384], cgt[:, 384:512]
    ident_b = c.sb("ident_b", [128, 128], BF16)
    t_idb = Tok()
    c.op("dve", lambda e: e.tensor_copy(out=ident_b[:], in_=ident_f), reads=[t_cg], writes=[t_idb])
    one_c = c.sb("one_c", [128, 1], F32)
    eps_c = c.sb("eps_c", [128, 1], F32)
    t_cc = Tok()
    c.op("dve", lambda e: e.memset(one_c[:], 1.0), writes=[t_cc])
    c.op("dve", lambda e: e.memset(eps_c[:], RMS_EPS), writes=[t_cc])

    wq_b = c.sb("wq_b", [128, KC, DK], BF16)
    wk_b = c.sb("wk_b", [128, KC, DK], BF16)
    wv_b = c.sb("wv_b", [128, KC, DV], BF16)
    wr_b = c.sb("wr_b", [128, KC, DV], BF16)
    wa1_b = c.sb("wa1_b", [128, KC, 16], BF16)
    wa2a = c.sb("wa2a", [32, DK], BF16)
    t_W = Tok("W")
    s_W = c.dma_sem("W")
    c.dma("pool", s_W, wa1_b[:], wa1v, writes=[t_W])

    work = [c.ps(f"wk{i}", [128, 512]) for i in range(6)]
    t_work = [Tok(f"wk{i}") for i in range(6)]
    pcb = [c.ps(f"pc{i}", [128, 512]) for i in range(2)]
    t_pcb = [Tok(f"pc{i}") for i in range(2)]
    wk_i = [0]

    def nwork():
        i = wk_i[0] % 6
        wk_i[0] += 1
        return work[i], t_work[i]

    hT = c.sb("hT", [128, KC, 512], BF16)
    t_hT = Tok("hT")
    s_h = c.dma_sem("h")
    gaug = Ring(c, "gaug", 2, [32, 512], BF16, dma=False)
    for i in range(2):
        c.op("dve", lambda e: e.memset(gaug.bufs[i][:], 1.0), writes=[gaug.toks[i]])
    spr = Ring(c, "sp", 2, [128, 4, DK], F32, dma=False)
    ekd = Ring(c, "ekd", 2, [128, DK], F32, dma=False)
    ekr = Ring(c, "ek", 2, [128, 512], F32, dma=False)
    qdT = Ring(c, "qdT", 2, [128, 2, 512], BF16, dma=False)
    kiT = Ring(c, "kiT", 2, [128, 2, 512], BF16, dma=False)
    eq = Ring(c, "eq", 2, [128, 2, 512], F32, dma=False)
    kdec = Ring(c, "kdec", 2, [128, 4, DK], BF16, dma=False)
    Vr = Ring(c, "V", 2, [128, 4, DV], BF16, dma=False)
    rraw = Ring(c, "rraw", 2, [128, 4, DV], BF16, dma=False)
    rs = Ring(c, "rs", 2, [128, 4, DV], BF16, dma=False)
    attm = Ring(c, "attm", 2, [128, 128], BF16, dma=False)
    state = c.sb("state", [128, 2, DV], F32)
    t_state = Tok("state")
    stb = Ring(c, "stb", 2, [128, 2, DV], BF16, dma=False)
    junk = c.sb("junk", [128, DV], BF16)
    t_junk = Tok()
    ssq = Ring(c, "ssq", 2, [128, 1], F32, dma=False)
    onr = Ring(c, "on", 2, [128, DV], F32, dma=False)
    ogt = Ring(c, "ogt", 2, [128, DV], BF16, dma=False)
    ogo = Ring(c, "ogo", 2, [128, 4, 512], BF16, dma=False)
    s_og = c.dma_sem("og")
    t_ogd = Tok("ogT")
    c.out_toks.append(t_ogd)

    def stage_a(sw, t):
        tsl = slice(t * 512, (t + 1) * 512)
        c.dma("sp", s_h, hT[:], hv[:, :, tsl], writes=[t_hT])
        ga, t_ga, _ = gaug.next()
        pg, t_pg = nwork()
        mm_group(c, pg[0:16, :], t_pg, [(wa1_b[:, k, :], hT[:, k, :]) for k in range(KC)], reads=[t_W, t_hT])
        c.op("act", lambda e: e.activation(out=ga[0:16, :], in_=pg[0:16, :], func=AF.Identity),
             reads=[t_pg], writes=[t_ga])
        sp, t_sp, _ = spr.next()
        kd, t_kd, _ = kdec.next()
        Vb, t_Vb, _ = Vr.next()
        rr, t_rr, _ = rraw.next()
        rsb, t_rs, _ = rs.next()
        for tb in range(4):
            bsl = slice(tb * 128, (tb + 1) * 128)
            pz, t_pz = nwork()
            c.op("pe", lambda e: e.matmul(pz[:, 0:DK], ga[0:17, bsl], wa2a[0:17, :], start=True, stop=True),
                 reads=[t_ga, t_W], writes=[t_pz])
            c.op("act", lambda e: e.activation(out=sp[:, tb, :], in_=pz[:, 0:DK], func=AF.Exp, scale=-1.0),
                 reads=[t_pz], writes=[t_sp])
            c.op("act", lambda e: e.activation(out=sp[:, tb, :], in_=sp[:, tb, :], func=AF.Ln, bias=one_c[:, 0:1]),
                 reads=[t_sp, t_cc], writes=[t_sp])
            pd, t_pd = nwork()
            c.op("pe", lambda e: e.matmul(pd[:, 0:DK], triU, sp[:, tb, :], start=True, stop=True),
                 reads=[t_cg, t_sp], writes=[t_pd])
            ek_, t_ek_, _ = ekd.next()
            c.op("act", lambda e: e.activation(out=ek_[:], in_=pd[:, 0:DK], func=AF.Exp),
                 reads=[t_pd], writes=[t_ek_])
            pk, t_pk = nwork()
            mm_group(c, pk[:, 0:DK], t_pk, [(hT[:, k, bsl], wk_b[:, k, :]) for k in range(KC)], reads=[t_W, t_hT])
            c.op("dve", lambda e: e.tensor_tensor(out=kd[:, tb, :], in0=pk[:, 0:DK], in1=ek_[:], op=ALU.mult),
                 reads=[t_pk, t_ek_], writes=[t_kd])
            pv, t_pv = nwork()
            mm_group(c, pv[:], t_pv, [(hT[:, k, bsl], wv_b[:, k, :]) for k in range(KC)], reads=[t_W, t_hT])
            c.op("act", lambda e: e.activation(out=Vb[:, tb, :], in_=pv[:], func=AF.Identity),
                 reads=[t_pv], writes=[t_Vb])
            pr, t_pr = nwork()
            mm_group(c, pr[:], t_pr, [(hT[:, k, bsl], wr_b[:, k, :]) for k in range(KC)], reads=[t_W, t_hT])
            c.op("dve", lambda e: e.tensor_copy(out=rr[:, tb, :], in_=pr[:]), reads=[t_pr], writes=[t_rr])
            for dc in range(2):
                c.op("pe", lambda e: e.matmul(pcb[dc][:, bsl], sp[:, tb, dc * 128:(dc + 1) * 128], triN,
                                              start=True, stop=True),
                     reads=[t_cg, t_sp], writes=[t_pcb[dc]])
        c.op("act", lambda e: e.activation(out=rsb[:], in_=rr[:], func=AF.Silu), reads=[t_rr], writes=[t_rs])
        eqb, t_eq, _ = eq.next()
        qd, t_qd, _ = qdT.next()
        ki, t_ki, _ = kiT.next()
        for dc in range(2):
            c.op("act", lambda e: e.activation(out=eqb[:, dc, :], in_=pcb[dc][:], func=AF.Exp),
                 reads=[t_pcb[dc]], writes=[t_eq])
            ekb, t_ekb, _ = ekr.next()
            c.op("act", lambda e: e.activation(out=ekb[:], in_=pcb[dc][:], func=AF.Exp, scale=-1.0),
                 reads=[t_pcb[dc]], writes=[t_ekb])
            pq, t_pq = nwork()
            mm_group(c, pq[:], t_pq, [(wq_b[:, k, dc * 128:(dc + 1) * 128], hT[:, k, :]) for k in range(KC)],
                     reads=[t_W, t_hT])
            c.op("dve", lambda e: e.scalar_tensor_tensor(out=qd[:, dc, :], in0=pq[:], scalar=float(DK ** -0.5),
                                                         in1=eqb[:, dc, :], op0=ALU.mult, op1=ALU.mult),
                 reads=[t_pq, t_eq], writes=[t_qd])
            pk2, t_pk2 = nwork()
            mm_group(c, pk2[:], t_pk2, [(wk_b[:, k, dc * 128:(dc + 1) * 128], hT[:, k, :]) for k in range(KC)],
                     reads=[t_W, t_hT])
            c.op("dve", lambda e: e.tensor_tensor(out=ki[:, dc, :], in0=pk2[:], in1=ekb[:], op=ALU.mult),
                 reads=[t_pk2, t_ekb], writes=[t_ki])
        return dict(qd=qd, t_qd=t_qd, ki=ki, t_ki=t_ki, eq=eqb, t_eq=t_eq, kd=kd, t_kd=t_kd,
                    V=Vb, t_V=t_Vb, rs=rsb, t_rs=t_rs)

    def stage_b(sw, t, A):
        tsl = slice(t * 512, (t + 1) * 512)
        oo, t_oo, _ = ogo.next()
        for tb in range(4):
            bsl = slice(tb * 128, (tb + 1) * 128)
            pa, t_pa = nwork()
            mm_group(c, pa[:, 0:128], t_pa, [(A["ki"][:, dc, bsl], A["qd"][:, dc, bsl]) for dc in range(2)],
                     reads=[A["t_ki"], A["t_qd"]])
            am, t_am, _ = attm.next()
            c.op("dve", lambda e: e.tensor_tensor(out=am[:], in0=pa[:, 0:128], in1=tri01, op=ALU.mult),
                 reads=[t_pa, t_cg], writes=[t_am])
            sb_old, t_sb_old, _ = stb.cur()
            po, t_po = nwork()
            c.op("pe", lambda e: e.matmul(po[:], am[:], A["V"][:, tb, :], start=True, stop=False),
                 reads=[t_am, A["t_V"]], writes=[t_po], track=False)
            for dc in range(2):
                c.op("pe", lambda e: e.matmul(po[:], A["qd"][:, dc, bsl], sb_old[:, dc, :], start=False, stop=(dc == 1)),
                     reads=[A["t_qd"], t_sb_old, t_am, A["t_V"]], writes=[t_po], track=(dc == 1))
            sb_new, t_sb_new, _ = stb.next()
            for dc in range(2):
                pkv, t_pkv = nwork()
                c.op("pe", lambda e: e.matmul(pkv[:], A["kd"][:, tb, dc * 128:(dc + 1) * 128], A["V"][:, tb, :],
                                              start=True, stop=True),
                     reads=[A["t_kd"], A["t_V"]], writes=[t_pkv])
                col = tb * 128 + 127
                c.op("dve", lambda e: e.scalar_tensor_tensor(out=state[:, dc, :], in0=state[:, dc, :],
                                                             scalar=A["eq"][:, dc, col:col + 1], in1=pkv[:],
                                                             op0=ALU.mult, op1=ALU.add),
                     reads=[t_state, A["t_eq"], t_pkv], writes=[t_state])
                c.op("act", lambda e: e.activation(out=sb_new[:, dc, :], in_=state[:, dc, :], func=AF.Identity),
                     reads=[t_state], writes=[t_sb_new])
            sq, t_sq, _ = ssq.next()
            c.op("act", lambda e: e.activation(out=junk[:], in_=po[:], func=AF.Square, accum_out=sq[:]),
                 reads=[t_po], writes=[t_junk, t_sq])
            c.op("act", lambda e: e.activation(out=sq[:], in_=sq[:], func=AF.Ln, scale=1.0 / DV, bias=eps_c[:, 0:1]),
                 reads=[t_sq, t_cc], writes=[t_sq])
            c.op("act", lambda e: e.activation(out=sq[:], in_=sq[:], func=AF.Exp, scale=-0.5),
                 reads=[t_sq], writes=[t_sq])
            on, t_on, _ = onr.next()
            c.op("dve", lambda e: e.scalar_tensor_tensor(out=on[:], in0=po[:], scalar=sq[:, 0:1], in1=gbt[:],
                                                         op0=ALU.mult, op1=ALU.mult),
                 reads=[t_po, t_sq, t_gb], writes=[t_on])
            og_, t_og_, _ = ogt.next()
            c.op("pool", lambda e: e.tensor_tensor(out=og_[:], in0=on[:], in1=A["rs"][:, tb, :], op=ALU.mult),
                 reads=[t_on, A["t_rs"]], writes=[t_og_])
            pt, t_pt = nwork()
            for fc in range(4):
                c.op("pe", lambda e: e.matmul(pt[:, fc * 128:(fc + 1) * 128], og_[:, fc * 128:(fc + 1) * 128],
                                              ident_b[:], start=True, stop=True),
                     reads=[t_og_, t_idb], writes=[t_pt], track=(fc == 3))
            c.op("dve", lambda e: e.tensor_copy(out=oo[:, :, bsl], in_=pt[:].rearrange("p (f t) -> p f t", f=4)),
                 reads=[t_pt], writes=[t_oo])
        c.dma("sp", s_og, ogv[:, sw * 4:(sw + 1) * 4, tsl], oo[:], reads=[t_oo], writes=[t_ogd])

    for sw in range(nheads):
        c.dma("pool", s_W, wq_b[:], wqv[:, :, sw * DK:(sw + 1) * DK], writes=[t_W])
        c.dma("pool", s_W, wk_b[:], wkv[:, :, sw * DK:(sw + 1) * DK], writes=[t_W])
        c.dma("pool", s_W, wv_b[:], wvv[:, :, sw * DV:(sw + 1) * DV], writes=[t_W])
        c.dma("pool", s_W, wr_b[:], wrv[:, :, sw * DV:(sw + 1) * DV], writes=[t_W])
        c.dma("pool", s_W, wa2a[0:16, :], wa2[:, sw * DK:(sw + 1) * DK], writes=[t_W])
        c.dma("pool", s_W, wa2a[16:17, :], ba[:, sw * DK:(sw + 1) * DK], writes=[t_W])
        c.op("dve", lambda e: e.memset(state[:], 0.0), writes=[t_state])
        sb0, t_sb0, _ = stb.next()
        c.op("dve", lambda e: e.memset(sb0[:], 0.0), writes=[t_sb0])
        prev = stage_a(sw, 0)
        for t in range(NT):
            nxt = stage_a(sw, t + 1) if t + 1 < NT else None
            stage_b(sw, t, prev)
            prev = nxt


def gla_consts():
    a = np.arange(128)[:, None]
    b = np.arange(128)[None, :]
    triN = np.where(a <= b, -1.0 / GLA_TAU, 0.0).astype(np.float32)
    triU = np.where(a > b, -1.0 / GLA_TAU, 0.0).astype(np.float32)
    tri01 = (a <= b).astype(np.float32)
    ident = np.eye(128, dtype=np.float32)
    return np.ascontiguousarray(np.concatenate([triN, triU, tri01, ident], axis=1))


def emit_sconv(c, hsrc, wb, wc, wu, cw, ogT, nsw=1):
    hv = hsrc.rearrange("(kc p) t -> p kc t", p=128)
    ogv = ogT.rearrange("(n p) t -> p n t", p=128)
    NT = S // 512
    NCH = 8
    NCHT = NCH * nsw
    cwt, t_cw = load_vecs(c, "cw", cw, 3 * NCHT)
    W = {}
    t_W = Tok("W")
    s_W = c.dma_sem("W")
    for name in ("b", "c", "u"):
        W[name] = c.sb("w_" + name, [128, KC, 1024], BF16)
    work = [c.ps(f"wk{i}", [128, 512]) for i in range(6)]
    t_work = [Tok(f"wk{i}") for i in range(6)]
    wk_i = [0]

    def nwork():
        i = wk_i[0] % 6
        wk_i[0] += 1
        return work[i], t_work[i]

    hT = Ring(c, "hT", 2, [128, KC, 512], BF16)
    pext = c.sb("pext", [128, NCH, 514], F32)
    t_pext = [Tok(f"pe{n}") for n in range(NCH)]
    usb = Ring(c, "usb", 2, [128, 512], F32, dma=False)
    yb = Ring(c, "yb", 2, [128, 512], F32, dma=False)
    ogo = Ring(c, "ogo", 2, [128, NCH, 512], BF16, dma=False)
    s_og = c.dma_sem("og")
    t_ogd = Tok("ogT")
    c.out_toks.append(t_ogd)
    for sw in range(nsw):
        for name, ap in (("b", wb), ("c", wc), ("u", wu)):
            v = ap.rearrange("(kc p) n -> p kc n", p=128)
            for q in range(4):
                c.dma("pool", s_W, W[name][:, :, q * 256:(q + 1) * 256],
                      v[:, :, sw * 1024 + q * 256:sw * 1024 + (q + 1) * 256], writes=[t_W])
        c.op("dve", lambda e: e.memset(pext[:], 0.0), writes=t_pext)
        for t in range(NT):
            tsl = slice(t * 512, (t + 1) * 512)
            hb, t_hb, s_hb = hT.next()
            c.dma("sp", s_hb, hb[:], hv[:, :, tsl], writes=[t_hb])
            oo, t_oo, _ = ogo.next()
            for n in range(NCH):
                nsl = slice(n * 128, (n + 1) * 128)
                cn = sw * NCH + n
                pu, t_pu = nwork()
                mm_group(c, pu[:], t_pu, [(W["u"][:, k, nsl], hb[:, k, :]) for k in range(KC)], reads=[t_W, t_hb])
                ub, t_ub, _ = usb.next()
                c.op("act", lambda e: e.activation(out=ub[:], in_=pu[:], func=AF.Identity), reads=[t_pu], writes=[t_ub])
                pc_, t_pc = nwork()
                mm_group(c, pc_[:], t_pc, [(W["c"][:, k, nsl], hb[:, k, :]) for k in range(KC)], reads=[t_W, t_hb])
                if t > 0:
                    c.op("pool", lambda e: e.tensor_copy(out=pext[:, n, 0:2], in_=pext[:, n, 512:514]),
                         reads=[t_pext[n]], writes=[t_pext[n]])
                c.op("dve", lambda e: e.tensor_tensor(out=pext[:, n, 2:514], in0=pc_[:], in1=ub[:], op=ALU.mult),
                     reads=[t_pc, t_ub, t_pext[n]], writes=[t_pext[n]])
                y, t_y, _ = yb.next()
                c.op("act", lambda e: e.activation(out=y[:], in_=pext[:, n, 2:514], func=AF.Identity,
                                                   scale=cwt[:, 2 * NCHT + cn:2 * NCHT + cn + 1]),
                     reads=[t_pext[n], t_cw], writes=[t_y])
                c.op("dve", lambda e: e.scalar_tensor_tensor(out=y[:], in0=pext[:, n, 1:513],
                                                             scalar=cwt[:, NCHT + cn:NCHT + cn + 1], in1=y[:],
                                                             op0=ALU.mult, op1=ALU.add),
                     reads=[t_pext[n], t_cw, t_y], writes=[t_y])
                c.op("dve", lambda e: e.scalar_tensor_tensor(out=y[:], in0=pext[:, n, 0:512],
                                                             scalar=cwt[:, cn:cn + 1], in1=y[:],
                                                             op0=ALU.mult, op1=ALU.add),
                     reads=[t_pext[n], t_cw, t_y], writes=[t_y])
                pb, t_pb = nwork()
                mm_group(c, pb[:], t_pb, [(W["b"][:, k, nsl], hb[:, k, :]) for k in range(KC)], reads=[t_W, t_hb])
                c.op("dve", lambda e: e.tensor_tensor(out=oo[:, n, :], in0=pb[:], in1=y[:], op=ALU.mult),
                     reads=[t_pb, t_y], writes=[t_oo])
            c.dma("sp", s_og, ogv[:, sw * NCH:(sw + 1) * NCH, tsl], oo[:], reads=[t_oo], writes=[t_ogd])


def vec_pm(v):
    v = np.asarray(v, dtype=np.float32)
    lead = v.shape[:-1]
    n = v.shape[-1] // 128
    v = v.reshape(-1, n, 128)
    return np.ascontiguousarray(v.transpose(2, 0, 1).reshape(128, -1))


def bcast_rows(v, n=128):
    v = np.asarray(v, dtype=np.float32)
    return np.ascontiguousarray(np.broadcast_to(v[None, :], (n, v.shape[0])))


def emit_mod_own(c, cT, ada_w, ada_b, modts, t_modts):
    ct, t_ct = load_vecs(c, "cT", cT, KC)
    cond = c.sb("cond", [128, KC], BF16)
    t_cond = Tok()
    c.op("act", lambda e: e.activation(out=cond[:], in_=ct[:], func=AF.Silu), reads=[t_ct], writes=[t_cond])
    one11 = c.sb("one11", [1, 1], F32)
    t_one = Tok()
    c.op("dve", lambda e: e.memset(one11[:], 1.0), writes=[t_one])
    awr = Ring(c, "aw", 3, [128, KC, 512], BF16)
    row = c.sb("row", [1, 6 * D], F32)
    t_row = Tok()
    abt = c.sb("abt", [1, 6 * D], F32)
    t_ab = Tok()
    s_ab = c.dma_sem("ab")
    work = [c.ps(f"wk{i}", [128, 512]) for i in range(4)]
    t_work = [Tok(f"wk{i}") for i in range(4)]
    pm = c.ps("pm", [128, 96])
    t_pm = Tok()
    i = 0
    NCC = 6 * D // 512
    for l in range(DEPTH):
        awv = ada_w[l].rearrange("(kc p) n -> p kc n", p=128)
        c.dma("sp", s_ab, abt[:], ada_b[l:l + 1, :], writes=[t_ab])
        for cc in range(NCC):
            csl = slice(cc * 512, (cc + 1) * 512)
            wb, t_wb, s_wb = awr.next()
            c.dma("pool", s_wb, wb[:], awv[:, :, csl], writes=[t_wb])
            pb, t_pb = work[i % 4], t_work[i % 4]
            i += 1
            mm_group(c, pb[0:1, :], t_pb, [(cond[:, k:k + 1], wb[:, k, :]) for k in range(KC)],
                     reads=[t_cond, t_wb])
            c.op("dve", lambda e: e.tensor_tensor(out=row[0:1, csl], in0=pb[0:1, :], in1=abt[0:1, csl], op=ALU.add),
                 reads=[t_pb, t_ab], writes=[t_row])
        for m in range(96):
            c.op("pe", lambda e: e.matmul(pm[:, m:m + 1], row[0:1, m * 128:(m + 1) * 128], one11[0:1, 0:1],
                                          start=True, stop=True),
                 reads=[t_row, t_one], writes=[t_pm], track=(m == 95))
        c.op("dve", lambda e: e.tensor_copy(out=modts[l][:], in_=pm[:]), reads=[t_pm], writes=[t_modts[l]])


FUSED_PASSES = [[(510 * i, 510)] for i in range(7)] + [[(3570, 510), (4080, 16)]]


def emit_hprep(c, xT, mod, hT_dram):
    xv = xT.rearrange("(kc p) t -> p kc t", p=128)
    hv = hT_dram.rearrange("(kc p) t -> p kc t", p=128)
    modt, t_mod = mod
    sc1p = c.sb("sc1p", [128, 16], F32)
    t_sc1p = Tok()
    c.op("dve", lambda e: e.tensor_scalar(out=sc1p[:], in0=modt[:, 16:32], scalar1=1.0, scalar2=None,
                                          op0=ALU.add), reads=[t_mod], writes=[t_sc1p])
    xp = Ring(c, "xp", 3, [128, 4, 512], F32)
    hb = Ring(c, "hb", 2, [128, KC, 512], BF16)
    t_out = Tok()
    for t in range(S // 512):
        tsl = slice(t * 512, (t + 1) * 512)
        h, t_h, s_h = hb.next()
        for q4 in range(4):
            xb, t_xb, s_xb = xp.next()
            c.dma("sp" if q4 % 2 == 0 else "pool", s_xb, xb[:], xv[:, q4 * 4:(q4 + 1) * 4, tsl], writes=[t_xb])
            for kk in range(4):
                kc = q4 * 4 + kk
                c.op("act", lambda e: e.activation(out=h[:, kc, :], in_=xb[:, kk, :], func=AF.Identity,
                                                   bias=modt[:, kc:kc + 1], scale=sc1p[:, kc:kc + 1]),
                     reads=[t_xb, t_mod, t_sc1p], writes=[t_h])
        c.dma("sp", s_h, hv[:, :, tsl], h[:], reads=[t_h], writes=[t_out])


def emit_fox_wconv(c, tag, wq, wk, wv, wg, fwb, fvb, lazy):
    old = c.kind
    c.kind = ""
    key = c.dma_sem(f"bg{tag}")
    c.kind = old
    tok = Tok(f"fconv{tag}")

    def q(dst, src_):
        fn = lambda: c.dma("pool", key, dst, src_, writes=[Tok()], reads=[]) and None
        if lazy:
            c.bg.append(fn)
        else:
            fn()

    for kind, w in (("k", wk), ("q", wq), ("g", wg)):
        v = w.rearrange("(kc p) n -> p kc n", p=128)
        for n in range(KC):
            q(fwb[kind][n].rearrange("p (kc n) -> p kc n", kc=KC), v[:, :, n * 128:(n + 1) * 128])
    v = wv.rearrange("(kc p) n -> p kc n", p=128)
    for sw in range(4):
        q(fvb[sw].rearrange("p (kc n) -> p kc n", kc=KC), v[:, :, sw * 512:(sw + 1) * 512])
    c.bg_final[tok] = (key, 52 * 16)
    return tok


def emit_wconv(c, l, w_up, w_down, wub, wdb, wo, wob):
    old = c.kind
    c.kind = ""
    key = c.dma_sem(f"bg{l}")
    c.kind = old
    tok = Tok(f"wconv{l}")
    wuv = w_up.rearrange("(kc p) n -> p kc n", p=128)
    wdv = w_down.rearrange("(kc p) n -> p kc n", p=128)
    wov = wo.rearrange("(kc p) n -> p kc n", p=128)

    def q(dst, src_):
        c.bg.append(lambda: c.dma("pool", key, dst, src_, writes=[Tok()]))

    c.bg_final[tok] = (key, 120 * 16)
    for n in range(KC):
        q(wob[n].rearrange("p (kc n) -> p kc n", kc=KC), wov[:, :, n * 128:(n + 1) * 128])
    for j in range(NJ):
        dst = wub[j].rearrange("p (kc n) -> p kc n", kc=KC)
        for h in range(2):
            q(dst[:, :, h * 128:(h + 1) * 128], wuv[:, :, h * DFF + j * 128:h * DFF + (j + 1) * 128])
    for n in range(KC):
        q(wdb[n].rearrange("p (kc n) -> p kc n", kc=NJ), wdv[:, :, n * 128:(n + 1) * 128])
    return tok


N_FOX = 2


def build_fused():
    nc = bass.Bass("TRN2", target_bir_lowering=False)

    def din(name, shape, dt=F32):
        return nc.dram_tensor(name, list(shape), dt, kind="ExternalInput").ap()

    def dint(name, shape, dt):
        return nc.dram_tensor(name, list(shape), dt).ap()

    xT = din("xT", [D, S])
    cT = din("cT", [128, KC])
    ada_w = din("ada_w", [DEPTH, D, 6 * D])
    ada_b = din("ada_b", [DEPTH, 6 * D])
    lnv = din("lnv", [DEPTH, 128, 64])
    fox_wq = din("fox_wq", [N_FOX, D, D])
    fox_wk = din("fox_wk", [N_FOX, D, D])
    fox_wv = din("fox_wv", [N_FOX, D, D])
    fox_wg = din("fox_wg", [N_FOX, D, D])
    fox_wf = din("fox_wf", [N_FOX, D, 16])
    fox_bfb = din("fox_bfb", [N_FOX, 128, 16])
    fox_wo = din("fox_wo", [N_FOX, D, D])
    fox_cf = din("fox_cf", [128, 896])
    gla_wq = din("gla_wq", [1, D, 1024])
    gla_wk = din("gla_wk", [1, D, 1024])
    gla_wv = din("gla_wv", [1, D, D])
    gla_wr = din("gla_wr", [1, D, D])
    gla_wa1 = din("gla_wa1", [1, D, 16])
    gla_wa2 = din("gla_wa2", [1, 16, 1024])
    gla_ba = din("gla_ba", [1, 1024])
    gla_gbc = din("gla_gbc", [128, 512])
    gla_cg = din("gla_cg", [128, 512])
    gla_wo = din("gla_wo", [1, D, D])
    conv_w_in = din("conv_w_in", [1, D, 3 * D])
    conv_cw = din("conv_cw", [128, 48])
    conv_w_out = din("conv_w_out", [1, D, D])
    ffn_w_up = din("ffn_w_up", [DEPTH, D, 2 * DFF])
    ffn_w_down = din("ffn_w_down", [DEPTH, DFF, D])
    ffn_cw = din("ffn_cw", [DEPTH, 128, 264])
    ffn_cb = din("ffn_cb", [DEPTH, 128, 88])
    outT = nc.dram_tensor("outT", [D, S], F32, kind="ExternalOutput").ap()

    with ExitStack() as es:
        c = Ctx(nc, es)
        modts = [c.sb(f"modt{l}", [128, 96], F32) for l in range(DEPTH)]
        t_modts = [Tok(f"modt{l}") for l in range(DEPTH)]
        with c.phase("mod", "M"):
            emit_mod_own(c, cT, ada_w, ada_b, modts, t_modts)
        x_cur = xT
        conv = []
        wos = [fox_wo[0], gla_wo[0], conv_w_out[0], fox_wo[1]]
        for i in range(DEPTH):
            conv.append((dint(f"wub{i}", [NJ, 128, KC * 256], BF16), dint(f"wdb{i}", [KC, 128, NJ * 128], BF16),
                         dint(f"wob{i}", [KC, 128, KC * 128], BF16)))
        fconv = {}
        for jj in range(N_FOX):
            fwb = {k: dint(f"fwb{jj}{k}", [KC, 128, KC * 128], BF16) for k in ("q", "k", "g")}
            fvb = dint(f"fvb{jj}", [4, 128, KC * 512], BF16)
            fconv[jj] = [fwb, fvb, None]
        fconv[0][2] = emit_fox_wconv(c, "f0", fox_wq[0], fox_wk[0], fox_wv[0], fox_wg[0],
                                     fconv[0][0], fconv[0][1], lazy=False)
        c.pump_all()
        t_conv0 = emit_wconv(c, 0, ffn_w_up[0], ffn_w_down[0], conv[0][0], conv[0][1], wos[0], conv[0][2])
        conv[0] = conv[0] + (t_conv0,)
        for i in range(DEPTH):
            kind, j = i % 3, i // 3
            mod = (modts[i], t_modts[i])
            if i > 0:
                c.pump_all()
            if i == 0:
                hT_d = dint("hT0", [D, S], BF16)
                with c.phase("hprep", "H0"):
                    emit_hprep(c, x_cur, mod, hT_d)
            else:
                hT_d = hT_next
            hT_next = dint(f"hT{i + 1}", [D, S], BF16) if i + 1 < DEPTH else None
            og = dint(f"og{i}", [D, S], BF16)
            x1 = dint(f"x1_{i}", [D, S], F32)
            h2x = dint(f"h2x{i}", [D, S + 2], BF16)
            x_next = outT if i == DEPTH - 1 else dint(f"x2_{i}", [D, S], F32)
            wub, wdb, wob, t_wconv = conv[i]
            if kind == 0:
                with c.phase("fox", f"A{i}"):
                    emit_fox(c, hT_d, tuple(fconv[j]), fox_wf[j], fox_bfb[j], fox_cf, og, nheads=16)
                wo = fox_wo[j]
            elif kind == 1:
                with c.phase("gla", f"A{i}"):
                    emit_gla(c, hT_d, gla_wq[j], gla_wk[j], gla_wv[j], gla_wr[j], gla_wa1[j],
                             gla_wa2[j], gla_ba[j:j + 1, :], gla_gbc, gla_cg, og, nheads=4)
                wo = gla_wo[j]
            else:
                w_in = conv_w_in[j]
                with c.phase("sconv", f"A{i}"):
                    emit_sconv(c, hT_d, w_in[:, 0:D], w_in[:, D:2 * D], w_in[:, 2 * D:3 * D],
                               conv_cw, og, nsw=2)
                wo = conv_w_out[j]
            c.pump_all()
            if i + 1 < DEPTH:
                t_n = emit_wconv(c, i + 1, ffn_w_up[i + 1], ffn_w_down[i + 1], conv[i + 1][0], conv[i + 1][1],
                                 wos[i + 1], conv[i + 1][2])
                conv[i + 1] = conv[i + 1] + (t_n,)
                if (i + 1) % 3 == 0:
                    jn = (i + 1) // 3
                    fconv[jn][2] = emit_fox_wconv(c, f"f{jn}", fox_wq[jn], fox_wk[jn], fox_wv[jn], fox_wg[jn],
                                                  fconv[jn][0], fconv[jn][1], lazy=True)
            with c.phase("oproj", f"B{i}"):
                emit_oproj(c, og, x_cur, (wob, t_wconv), mod, lnv[i][:, 0:16], lnv[i][:, 16:32], x1, h2x, ntok=S, h2off=2)
            with c.phase("ffn", f"F{i}"):
                emit_ffn(c, h2x, x1, wub, wdb, t_wconv, ffn_cw[i], ffn_cb[i], mod,
                         lnv[i][:, 32:48], lnv[i][:, 48:64], x_next, FUSED_PASSES,
                         hnext=(hT_next, (modts[i + 1], t_modts[i + 1])) if i + 1 < DEPTH else None)
            x_cur = x_next
        c.barrier()
        c.finish()
        print("fused instructions:", c.n_inst, {k: v for k, v in c.cnt.items() if v and not k.startswith("d_")},
              "sems", len(c.sem))
    return nc


_NC_CACHE = {}


def kernel(**inp):
    f32 = np.float32
    x = np.asarray(inp["x"], dtype=f32)
    cvec = np.asarray(inp["c"], dtype=f32)
    if "fused" not in _NC_CACHE:
        _NC_CACHE["fused"] = build_fused()
    nc = _NC_CACHE["fused"]
    A = lambda k: np.ascontiguousarray(np.asarray(inp[k], dtype=f32))
    lnv = np.ascontiguousarray(np.stack(
        [np.concatenate([vec_pm(inp["ln1_g"][l]), vec_pm(inp["ln1_b"][l]),
                         vec_pm(inp["ln2_g"][l]), vec_pm(inp["ln2_b"][l])], axis=1) for l in range(DEPTH)], axis=0))
    shared = dict(
        ada_w=A("ada_w"), ada_b=A("ada_b"), lnv=lnv,
        fox_wq=A("fox_wq"), fox_wk=A("fox_wk"), fox_wv=A("fox_wv"), fox_wg=A("fox_wg"), fox_wf=A("fox_wf"),
        fox_bfb=np.ascontiguousarray(np.stack([bcast_rows(inp["fox_bf"][j]) for j in range(N_FOX)], axis=0)),
        fox_wo=A("fox_wo"), fox_cf=fox_consts(),
        gla_wq=A("gla_wq"), gla_wk=A("gla_wk"), gla_wv=A("gla_wv"), gla_wr=A("gla_wr"), gla_wa1=A("gla_wa1"),
        gla_wa2=A("gla_wa2"), gla_ba=A("gla_ba"), gla_gbc=bcast_rows(inp["gla_norm_g"][0]), gla_cg=gla_consts(),
        gla_wo=A("gla_wo"), conv_w_in=A("conv_w_in"), conv_cw=vec_pm(inp["conv_w"][0]), conv_w_out=A("conv_w_out"),
        ffn_w_up=A("ffn_w_up"), ffn_w_down=A("ffn_w_down"),
        ffn_cw=np.ascontiguousarray(np.stack([vec_pm(inp["ffn_conv_w"][l]) for l in range(DEPTH)], axis=0)),
        ffn_cb=np.ascontiguousarray(np.stack([vec_pm(inp["ffn_conv_b"][l]) for l in range(DEPTH)], axis=0)),
    )
    maps = []
    for core in range(NCORES):
        b = core // 2
        m = dict(shared)
        m["xT"] = np.ascontiguousarray(x[b].T)
        m["cT"] = np.ascontiguousarray(cvec[b].reshape(KC, 128).T)
        maps.append(m)
    res = run_bass_kernel_spmd(nc, maps, core_ids=list(range(NCORES))).results
    out = np.stack([res[2 * b]["outT"].T for b in range(NB)], axis=0).astype(f32)
    return np.ascontiguousarray(out)
```
